# Optimizing a Trainium2 kernel written in Bass

```python
import jax, jax.numpy as jnp
from jax import lax
import numpy as np

D_MODEL = 1024
BATCH = 4
SEQ = 8192
DEPTH = 1

D_INNER = 2 * D_MODEL
M_WIDTH = D_INNER // 2
M_HEADS = 4
M_DV = M_WIDTH // M_HEADS
M_DQK = M_DV // 2
M_QK = M_HEADS * M_DQK
M_CHUNK = 128
GATE_CAP = 15.0
S_WIDTH = D_INNER - M_WIDTH
S_HEADDIM = 64
S_HEADS = S_WIDTH // S_HEADDIM
S_GROUPS = 2
S_STATE = 128
S_CONV = 4
S_CHUNK = 128
S_CONV_CH = S_WIDTH + 2 * S_GROUPS * S_STATE
EPS = 1e-6

IN_SIZES = (M_QK, M_QK, M_WIDTH, M_HEADS, M_HEADS, M_WIDTH, M_WIDTH, S_WIDTH, S_CONV_CH, S_HEADS)
D_IN_PROJ = sum(IN_SIZES)
SPLIT_POINTS = tuple(int(p) for p in np.cumsum(IN_SIZES)[:-1])

kernel_name = "hymba_mlstm_ssd_parallel_heads"


def rms_norm(x, w):
    xf = x.astype(jnp.float32)
    y = xf * lax.rsqrt(jnp.mean(xf * xf, axis=-1, keepdims=True) + EPS)
    return (y * w.astype(jnp.float32)).astype(x.dtype)


def soft_cap(u):
    return GATE_CAP * jnp.tanh(u / GATE_CAP)


def mlstm_chunkwise(q, k, v, log_i, log_f):
    f32 = jnp.float32
    Bsz, T = q.shape[0], q.shape[1]
    L = M_CHUNK
    NC = T // L
    q = q.astype(f32).reshape(Bsz, NC, L, M_HEADS, M_DQK) * (M_DQK ** -0.5)
    k = k.astype(f32).reshape(Bsz, NC, L, M_HEADS, M_DQK)
    v = v.astype(f32).reshape(Bsz, NC, L, M_HEADS, M_DV)
    li = log_i.reshape(Bsz, NC, L, M_HEADS).transpose(0, 3, 1, 2)
    lf = log_f.reshape(Bsz, NC, L, M_HEADS).transpose(0, 3, 1, 2)
    b = jnp.cumsum(lf, axis=-1)
    b_last = b[..., -1]
    g = b_last[..., None] - b + li
    g_max = jnp.max(g, axis=-1)
    w = jnp.exp(g - g_max[..., None])
    C_loc = jnp.einsum('bhcs,bcshk,bcshv->cbhkv', w, k, v)
    n_loc = jnp.einsum('bhcs,bcshk->cbhk', w, k)

    def step(carry, inp):
        C, n, m = carry
        a, gm, Cl, nl = inp
        m_new = jnp.maximum(a + m, gm)
        s_old = jnp.exp(a + m - m_new)
        s_loc = jnp.exp(gm - m_new)
        C_new = s_old[..., None, None] * C + s_loc[..., None, None] * Cl
        n_new = s_old[..., None] * n + s_loc[..., None] * nl
        return (C_new, n_new, m_new), (C, n, m)

    init = (jnp.zeros((Bsz, M_HEADS, M_DQK, M_DV), f32),
            jnp.zeros((Bsz, M_HEADS, M_DQK), f32),
            jnp.zeros((Bsz, M_HEADS), f32))
    xs = (jnp.moveaxis(b_last, 2, 0), jnp.moveaxis(g_max, 2, 0), C_loc, n_loc)
    _, (C_prev, n_prev, m_prev) = lax.scan(step, init, xs)
    m_prev = jnp.moveaxis(m_prev, 0, 2)

    causal = jnp.tril(jnp.ones((L, L), dtype=bool))
    D = b[..., :, None] - b[..., None, :] + li[..., None, :]
    D = jnp.where(causal, D, -jnp.inf)
    m_inter = b + m_prev[..., None]
    m_j = jnp.maximum(m_inter, jnp.max(D, axis=-1))
    S = jnp.einsum('bcjhk,bcshk->bhcjs', q, k) * jnp.exp(D - m_j[..., None])
    num = jnp.einsum('bhcjs,bcshv->bcjhv', S, v)
    den = jnp.sum(S, axis=-1)
    inter_scale = jnp.exp(m_inter - m_j)
    num = num + jnp.einsum('bcjhk,cbhkv->bcjhv', q, C_prev) * inter_scale.transpose(0, 2, 3, 1)[..., None]
    den = den + jnp.einsum('bcjhk,cbhk->bhcj', q, n_prev) * inter_scale
    denom = jnp.maximum(jnp.abs(den), jnp.exp(-m_j))
    h = num / denom.transpose(0, 2, 3, 1)[..., None]
    return h.reshape(Bsz, T, M_HEADS, M_DV)


def causal_depthwise_conv(u, w, b):
    kern = jnp.transpose(w)[:, None, :].astype(u.dtype)
    out = lax.conv_general_dilated(u, kern, window_strides=(1,), padding=[(S_CONV - 1, 0)],
                                   dimension_numbers=('NWC', 'WIO', 'NWC'),
                                   feature_group_count=u.shape[-1])
    return out + b.astype(u.dtype)


def ssd_chunked(x, dt, A, Bm, Cm):
    f32 = jnp.float32
    Bsz, T = x.shape[0], x.shape[1]
    L = S_CHUNK
    NC = T // L
    R = S_HEADS // S_GROUPS
    x = x.astype(f32).reshape(Bsz, NC, L, S_GROUPS, R, S_HEADDIM)
    dt = dt.reshape(Bsz, NC, L, S_GROUPS, R)
    Bm = Bm.astype(f32).reshape(Bsz, NC, L, S_GROUPS, S_STATE)
    Cm = Cm.astype(f32).reshape(Bsz, NC, L, S_GROUPS, S_STATE)
    a = (dt * A.reshape(S_GROUPS, R)).transpose(0, 3, 4, 1, 2)
    a_cum = jnp.cumsum(a, axis=-1)
    xdt = x * dt[..., None]
    causal = jnp.tril(jnp.ones((L, L), dtype=bool))
    decay = jnp.exp(jnp.where(causal, a_cum[..., :, None] - a_cum[..., None, :], -jnp.inf))
    CB = jnp.einsum('bclgn,bcsgn->bgcls', Cm, Bm)
    y_diag = jnp.einsum('bgcls,bgrcls,bcsgrp->bclgrp', CB, decay, xdt)
    decay_end = jnp.exp(a_cum[..., -1:] - a_cum)
    states = jnp.einsum('bcsgn,bgrcs,bcsgrp->cbgrpn', Bm, decay_end, xdt)
    chunk_decay = jnp.moveaxis(jnp.exp(a_cum[..., -1]), -1, 0)

    def step(h, inp):
        dA, s = inp
        return dA[..., None, None] * h + s, h

    h0 = jnp.zeros((Bsz, S_GROUPS, R, S_HEADDIM, S_STATE), f32)
    _, h_prev = lax.scan(step, h0, (chunk_decay, states))
    y_off = jnp.einsum('bclgn,cbgrpn,bgrcl->bclgrp', Cm, h_prev, jnp.exp(a_cum))
    return (y_diag + y_off).reshape(Bsz, T, S_HEADS, S_HEADDIM)


def setup_inputs(seed: int = 0) -> dict:
    key = jax.random.key(seed)
    ks = jax.random.split(key, 16)
    f32 = jnp.float32
    x = jax.random.normal(ks[0], (BATCH, SEQ, D_MODEL), f32)
    norm_w = 1.0 + 0.01 * jax.random.normal(ks[1], (DEPTH, D_MODEL), f32)
    w_in = jax.random.normal(ks[2], (DEPTH, D_MODEL, D_IN_PROJ), f32) * D_MODEL ** -0.5
    b_igate = 0.1 * jax.random.normal(ks[3], (DEPTH, M_HEADS), f32)
    b_fgate = jnp.linspace(3.0, 6.0, M_HEADS, dtype=f32)[None, :] + 0.1 * jax.random.normal(ks[4], (DEPTH, M_HEADS), f32)
    conv_w = jax.random.normal(ks[5], (DEPTH, S_CONV_CH, S_CONV), f32) * S_CONV ** -0.5
    conv_b = 0.01 * jax.random.normal(ks[6], (DEPTH, S_CONV_CH), f32)
    dt0 = jnp.exp(jax.random.uniform(ks[7], (DEPTH, S_HEADS), f32, minval=float(np.log(1e-3)), maxval=float(np.log(1e-1))))
    dt_bias = dt0 + jnp.log(-jnp.expm1(-dt0))
    a_log = jnp.log(jax.random.uniform(ks[8], (DEPTH, S_HEADS), f32, minval=1.0, maxval=16.0))
    d_skip = 1.0 + 0.1 * jax.random.normal(ks[9], (DEPTH, S_HEADS), f32)
    mlstm_norm_w = 1.0 + 0.01 * jax.random.normal(ks[10], (DEPTH, M_WIDTH), f32)
    ssd_norm_w = 1.0 + 0.01 * jax.random.normal(ks[11], (DEPTH, S_WIDTH), f32)
    w_out = jax.random.normal(ks[12], (DEPTH, D_INNER, D_MODEL), f32) * D_INNER ** -0.5
    final_norm_w = 1.0 + 0.01 * jax.random.normal(ks[13], (D_MODEL,), f32)
    return {"x": x, "norm_w": norm_w, "w_in": w_in, "b_igate": b_igate, "b_fgate": b_fgate,
            "conv_w": conv_w, "conv_b": conv_b, "dt_bias": dt_bias, "a_log": a_log,
            "d_skip": d_skip, "mlstm_norm_w": mlstm_norm_w, "ssd_norm_w": ssd_norm_w,
            "w_out": w_out, "final_norm_w": final_norm_w}


def reference(x, norm_w, w_in, b_igate, b_fgate, conv_w, conv_b, dt_bias, a_log, d_skip,
              mlstm_norm_w, ssd_norm_w, w_out, final_norm_w):
    f32 = jnp.float32
    dtype = x.dtype
    Bsz, T = x.shape[0], x.shape[1]
    for l in range(DEPTH):
        h = rms_norm(x, norm_w[l])
        proj = h @ w_in[l]
        q, k, v, i_pre, f_pre, o_pre, z_m, z_s, xbc, dt_raw = jnp.split(proj, SPLIT_POINTS, axis=-1)

        log_i = soft_cap(i_pre.astype(f32) + b_igate[l].astype(f32))
        log_f = jax.nn.log_sigmoid(soft_cap(f_pre.astype(f32) + b_fgate[l].astype(f32)))
        hm = mlstm_chunkwise(q.reshape(Bsz, T, M_HEADS, M_DQK),
                             k.reshape(Bsz, T, M_HEADS, M_DQK),
                             v.reshape(Bsz, T, M_HEADS, M_DV), log_i, log_f)
        hm = hm * lax.rsqrt(jnp.mean(hm * hm, axis=-1, keepdims=True) + EPS)
        hm = hm.reshape(Bsz, T, M_WIDTH) * mlstm_norm_w[l].astype(f32)
        hm = hm * jax.nn.sigmoid(o_pre.astype(f32)) * jax.nn.silu(z_m.astype(f32))

        xbc = jax.nn.silu(causal_depthwise_conv(xbc, conv_w[l], conv_b[l]))
        xs, Bm, Cm = jnp.split(xbc, (S_WIDTH, S_WIDTH + S_GROUPS * S_STATE), axis=-1)
        xs = xs.reshape(Bsz, T, S_HEADS, S_HEADDIM)
        dt = jax.nn.softplus(dt_raw.astype(f32) + dt_bias[l].astype(f32))
        A = -jnp.exp(a_log[l].astype(f32))
        y = ssd_chunked(xs, dt, A,
                        Bm.reshape(Bsz, T, S_GROUPS, S_STATE),
                        Cm.reshape(Bsz, T, S_GROUPS, S_STATE))
        y = y + d_skip[l].astype(f32)[:, None] * xs.astype(f32)
        y = y.reshape(Bsz, T, S_WIDTH) * jax.nn.silu(z_s.astype(f32))
        yg = y.reshape(Bsz, T, S_GROUPS, S_WIDTH // S_GROUPS)
        yg = yg * lax.rsqrt(jnp.mean(yg * yg, axis=-1, keepdims=True) + EPS)
        y = yg.reshape(Bsz, T, S_WIDTH) * ssd_norm_w[l].astype(f32)

        mix = jnp.concatenate([hm, y], axis=-1).astype(dtype)
        x = x + mix @ w_out[l]
    return rms_norm(x, final_norm_w)
```

```python
import numpy as np
from contextlib import ExitStack
import concourse.bass as bass
import concourse.mybir as mybir
from concourse.bass_utils import run_bass_kernel_spmd

F32 = mybir.dt.float32
BF16 = mybir.dt.bfloat16
AF = mybir.ActivationFunctionType
ALU = mybir.AluOpType
AX = mybir.AxisListType

D_MODEL = 1024
SEQ = 8192
BATCH = 4
NCOL = 6680
EPS = 1e-6
L = 128
NCORES = 8


class Tracker:
    def __init__(self):
        self.ops = []
        self.bufs = {}
        self.waitall_streams = set()
        self.regions = {}

    def reg(self, key, arena, off, nbytes, gran=64):
        self.regions[key] = [(arena, g) for g in range(off // gran, (off + nbytes + gran - 1) // gran)]

    def _expand(self, keys):
        out = []
        for k in keys:
            out.extend(self.regions.get(k, [k]))
        return out

    PSUM_BANKS = ("pT0", "pT1", "pm", "pb0", "pb1", "pb2", "pb3", "pb4")

    @classmethod
    def _bank(cls, k):
        if isinstance(k, str):
            if k.startswith("pm_"):
                return "pm"
            if k in cls.PSUM_BANKS:
                return k
        return None

    def add(self, eng, fn, r=(), w=(), stream=None):
        banks = [self._bank(k) for k in list(r) + list(w)]
        banks = [b for b in banks if b is not None]
        r = [k for k in r if self._bank(k) is None]
        w = [k for k in w if self._bank(k) is None] + sorted(set(banks))
        r = self._expand(r)
        w = self._expand(w)
        deps = set()
        for k in r:
            b = self.bufs.setdefault(k, [None, []])
            if b[0] is not None:
                deps.add(b[0])
        for k in w:
            b = self.bufs.setdefault(k, [None, []])
            if b[0] is not None:
                deps.add(b[0])
            deps.update(b[1])
        idx = len(self.ops)
        deps.discard(idx)
        self.ops.append(dict(eng=eng, fn=fn, deps=deps, stream=stream, idx=idx, has_dep=False))
        for k in r:
            self.bufs[k][1].append(idx)
        for k in w:
            self.bufs[k] = [idx, []]
        return idx

    def emit(self, nc, es, same_engine_sync=True):
        ops = self.ops
        for op in ops:
            nd = set()
            for d in op["deps"]:
                dop = ops[d]
                if dop["stream"] is None and dop["eng"] == "pe" and op["eng"] == "pe" and op["stream"] is None:
                    continue
                if (not same_engine_sync) and dop["stream"] is None and op["stream"] is None and dop["eng"] == op["eng"]:
                    continue
                nd.add(d)
            op["deps"] = nd
            for d in nd:
                ops[d]["has_dep"] = True
        sems = {}
        engs = ["pe", "act", "dve", "pool", "sp"]
        for e in engs:
            sems[e] = es.enter_context(nc.semaphore("s_" + e))
        streams = sorted({op["stream"] for op in ops if op["stream"] is not None})
        for s in streams:
            sems["d:" + s] = es.enter_context(nc.semaphore("d_" + s))
        cnt = {e: 0 for e in engs}
        scnt = {s: 0 for s in streams}
        for op in ops:
            if op["stream"] is not None:
                scnt[op["stream"]] += 1
                op["sig"] = ("d:" + op["stream"], 16 * scnt[op["stream"]])
            elif op["has_dep"]:
                cnt[op["eng"]] += 1
                op["sig"] = (op["eng"], cnt[op["eng"]])
            else:
                op["sig"] = None
        for op in ops:
            if op["stream"] in self.waitall_streams:
                op["sig"] = ("d:" + op["stream"], 16 * scnt[op["stream"]])
        self.final_counts = {("d:" + s): 16 * scnt[s] for s in streams}
        self.sems = sems
        block = es.enter_context(nc.Block())
        per_eng = {e: [op for op in ops if op["eng"] == e] for e in engs}

        def run(engobj, lst, extra_tail=None):
            waited = {}
            for op in lst:
                need = {}
                for d in op["deps"]:
                    sg = ops[d]["sig"]
                    assert sg is not None
                    if sg[1] > need.get(sg[0], 0):
                        need[sg[0]] = sg[1]
                for sk, v in need.items():
                    if v > waited.get(sk, 0):
                        engobj.wait_ge(sems[sk], v)
                        waited[sk] = v
                ins = op["fn"](engobj)
                if op["stream"] is not None:
                    ins.then_inc(sems["d:" + op["stream"]], 16)
                elif op["sig"] is not None:
                    ins.then_inc(sems[op["eng"]], 1)
            if extra_tail is not None:
                extra_tail(engobj, waited)

        def sp_tail(engobj, waited):
            for sk, v in self.final_counts.items():
                if v > waited.get(sk, 0):
                    engobj.wait_ge(sems[sk], v)

        @block.sync
        def _(e):
            run(e, per_eng["sp"], sp_tail)

        @block.tensor
        def _(e):
            run(e, per_eng["pe"])

        @block.scalar
        def _(e):
            run(e, per_eng["act"])

        @block.vector
        def _(e):
            run(e, per_eng["dve"])

        @block.gpsimd
        def _(e):
            run(e, per_eng["pool"])


PROJ_GROUPS = [
    ("if", 2048, 8), ("dt", 6664, 16), ("k", 512, 512), ("v0", 1024, 512), ("v1", 1536, 512),
    ("xbc0", 5128, 512), ("xbc1", 5640, 512), ("xbc2", 6152, 512), ("q", 0, 512),
    ("zm0", 3080, 512), ("zm1", 3592, 512), ("o0", 2056, 512), ("o1", 2568, 512),
    ("zs0", 4104, 512), ("zs1", 4616, 512),
]
STATE_ONLY = {"if", "dt", "k", "v0", "v1", "xbc0", "xbc1", "xbc2"}


def build(n_pre, n_full, debug=()):
    nc = bass.Bass("TRN2", target_bir_lowering=False)
    T_pre, T_full = n_pre * L, n_full * L
    dr = {}
    dr["x"] = nc.dram_tensor("x", [max(T_full, 1), D_MODEL], F32, kind="ExternalInput").ap()
    if n_pre:
        dr["xpre"] = nc.dram_tensor("xpre", [T_pre, D_MODEL], F32, kind="ExternalInput").ap()
    for nm, shp in [("norm_w", [1024]), ("w_in", [1024, NCOL]), ("b_igate", [4]), ("b_fgate", [4]),
                    ("conv_w", [1536, 4]), ("conv_b", [1536]), ("dt_bias", [16]), ("a_log", [16]),
                    ("d_skip", [16]), ("normcat", [2048]), ("w_out", [2048, 1024]),
                    ("final_norm_w", [1024]), ("flag", [1])]:
        dr[nm] = nc.dram_tensor(nm, shp, F32, kind="ExternalInput").ap()
    out_d = nc.dram_tensor("out", [T_full, D_MODEL], F32, kind="ExternalOutput").ap()
    dbg_out = {}

    T = Tracker()
    T.waitall_streams.add("const")
    es = ExitStack()
    with es:
        def sb(name, shape, dt):
            return es.enter_context(nc.sbuf_tensor(name, shape, dt))

        def ps(name, shape, dt):
            return es.enter_context(nc.psum_tensor(name, shape, dt))

        win = sb("win", [128, 8 * NCOL], BF16)
        wout = sb("wout", [128, 16 * 1024], BF16)
        win3 = win[:].rearrange("p (k n) -> p k n", k=8)
        wout3 = wout[:].rearrange("p (k n) -> p k n", k=16)
        identb = sb("identb", [128, 128], BF16)
        identf = sb("identf", [128, 128], F32)
        tri = sb("tri", [128, 128], F32)
        maskb = sb("maskb", [128, 128], BF16)
        onesf = sb("onesf", [128, 128], F32)
        oh48 = sb("oh48", [48, 16 * 128], BF16)
        oh48_3 = oh48[:].rearrange("p (r s) -> p r s", r=16)
        finalw = sb("finalw", [128, 1024], F32)
        cst = sb("cst", [128, 256], F32)
        cw = cst[:, 0:48].rearrange("p (t w) -> p t w", t=12)
        cb = cst[:, 48:60]
        bias8 = cst[:, 60:68]
        dtb = cst[:, 68:84]
        arep = cst[:, 84:100]
        drep = cst[:, 100:116]
        normw_fm = cst[:, 116:124]
        normcat = cst[:, 124:140]
        m05 = cst[:, 140:148]
        flag = cst[:, 148:149]
        alog = cst[:, 152:168]

        xbuf = [sb("xbuf0", [128, 1024], F32), sb("xbuf1", [128, 1024], F32)]
        ARENA_BYTES = 25664
        arena = sb("arena", [128, ARENA_BYTES // 4], F32)

        def carve(layout_off, name, nbytes, dt, subkeys=None):
            assert layout_off[0] % 4 == 0
            o = layout_off[0]
            layout_off[0] += (nbytes + 3) // 4 * 4
            assert layout_off[0] <= ARENA_BYTES, (name, layout_off[0])
            v = arena[:, o // 4:(o + (nbytes + 3) // 4 * 4) // 4]
            if dt != F32:
                v = v.bitcast(dt)
            if subkeys is None:
                T.reg(name, "A", o, nbytes)
            else:
                n = len(subkeys)
                for i, sk in enumerate(subkeys):
                    T.reg(sk, "A", o + i * (nbytes // n), nbytes // n)
            return v

        lo1 = [0]
        xn = carve(lo1, "xn", 2048, BF16)
        junk = xn
        xnT = carve(lo1, "xnT", 2048, BF16)
        xnT3 = xnT[:].rearrange("p (k t) -> p k t", k=8)
        xbc_tm = carve(lo1, "xbc_tm", 3072, BF16, ["xbc_tm0", "xbc_tm1", "xbc_tm2"])
        cacc = [carve(lo1, "cacc0", 2048, F32, ["cacc0_%d" % i for i in range(4)]),
                carve(lo1, "cacc1", 2048, F32, ["cacc1_%d" % i for i in range(4)])]
        cth = [carve(lo1, "cth0", 1024, BF16), carve(lo1, "cth1", 1024, BF16)]
        q_tm = carve(lo1, "q_tm", 1024, BF16)
        tz = carve(lo1, "tz", 2048, BF16, ["tz0", "tz1"])
        k_tm = carve(lo1, "k_tm", 1024, BF16)
        kp_tm = carve(lo1, "kp_tm", 1024, BF16)
        qT = carve(lo1, "qT", 1024, BF16)
        kT = carve(lo1, "kT", 1024, BF16)
        Pm = carve(lo1, "Pm", 1024, BF16, ["Pm_%d" % i for i in range(4)])
        Pm3 = Pm[:].rearrange("p (h j) -> p h j", h=4)
        Cbf = carve(lo1, "Cbf", 2064, BF16)
        Cbf3 = Cbf[:].rearrange("p (h c) -> p h c", h=4)
        g1 = carve(lo1, "g1", 2048, BF16, ["g1_0", "g1_1"])
        lo2 = [0]
        dec = carve(lo2, "dec", 4096, BF16, ["dec_%d" % i for i in range(4)])
        dec3 = dec[:].rearrange("p (r l) -> p r l", r=16)
        rl = carve(lo2, "rl", 2048, F32)
        CBm = carve(lo2, "CBm", 512, BF16)
        CBm3 = CBm[:].rearrange("p (g l) -> p g l", g=2)
        yo = carve(lo2, "yo", 4096, F32, ["yo_0", "yo_1"])
        ytmp = carve(lo2, "ytmp", 2048, F32)
        x_tm = carve(lo2, "x_tm", 2048, BF16)
        B_tm = carve(lo2, "B_tm", 512, BF16)
        xdt = carve(lo2, "xdt", 2048, BF16)
        xde = carve(lo2, "xde", 2048, BF16)
        mixT = carve(lo2, "mixT", 4096, BF16, ["mixT_0", "mixT_1"])
        mixT3 = mixT[:].rearrange("p (k t) -> p k t", k=16)

        vaug = sb("vaug", [128, 4 * 258], BF16)
        vaug3 = vaug[:].rearrange("p (h c) -> p h c", h=4)
        gs = sb("gs", [128, 1024], BF16)
        xbcT = sb("xbcT", [128, 12 * 132], BF16)
        xbcT3 = xbcT[:].rearrange("p (t c) -> p t c", t=12)
        xcT = sb("xcT", [128, 1536], BF16)
        xcT3 = xcT[:].rearrange("p (t c) -> p t c", t=12)
        Cst = sb("Cst", [128, 4 * 258], F32)
        Cst3 = Cst[:].rearrange("p (h c) -> p h c", h=4)
        hst = sb("hst", [128, 1024], F32)
        hbf = sb("hbf", [128, 1024], BF16)
        mix = sb("mix", [128, 2048], BF16)
        a3 = sb("a3", [128, 48], BF16)
        A48 = sb("A48", [48, 128], BF16)
        sm = sb("sm", [128, 256], F32)
        g8 = sm[:, 0:8]
        t8 = sm[:, 8:16]
        e4 = sm[:, 16:20]
        nlf = sm[:, 20:24]
        li = sm[:, 24:28]
        T1 = sm[:, 28:36]
        T2 = sm[:, 36:44]
        wf = sm[:, 44:52]
        soldb = sm[:, 52:56]
        absden = sm[:, 56:60]
        ssqr = sm[:, 60:64]
        dn4 = sm[:, 64:68]
        rd4 = sm[:, 68:72]
        t4 = sm[:, 72:76]
        rs4 = sm[:, 76:80]
        sc4 = sm[:, 80:84]
        ssq_x = sm[:, 84:85]
        rstd_x = sm[:, 85:86]
        tx1 = sm[:, 86:87]
        ssq_o = sm[:, 87:88]
        rstd_o = sm[:, 88:89]
        to1 = sm[:, 89:90]
        ssq_s = sm[:, 90:92]
        rstd_s = sm[:, 92:94]
        ts2 = sm[:, 94:96]
        dtp = sm[:, 96:112]
        edt = sm[:, 112:128]
        dt16 = sm[:, 128:144]
        a_tm = sm[:, 144:160]
        acum_sb = sm[:, 160:176]
        ea = sm[:, 176:192]
        dEa = sm[:, 192:208]
        dE = sm[:, 208:224]
        cdb = sm[:, 224:240]
        r1 = sm[:, 240:256]
        sm2 = sb("sm2", [128, 16], F32)
        r2 = sm2[:, 0:16]
        gm = sb("gm", [4, 32], F32)
        mst = gm[:, 0:1]
        umax = gm[:, 1:2]
        Rg = gm[:, 2:3]
        dg = gm[:, 3:4]
        soldg = gm[:, 4:5]
        Dg = gm[:, 8:16]
        pT = [ps("pT0", [128, 1024], BF16), ps("pT1", [128, 1024], BF16)]
        pm = ps("pm", [128, 512], F32)
        pb = [ps("pb%d" % i, [128, 512], F32) for i in range(5)]
        print("sbuf bytes remaining:", nc.sbuf_bytes_remaining)

        bank_rr = [0]

        def next_bank():
            i = bank_rr[0] % 5
            bank_rr[0] += 1
            return pb[i], "pb%d" % i

        pt_rr = [0]

        def next_pT():
            i = pt_rr[0] % 2
            pt_rr[0] += 1
            return pT[i], "pT%d" % i

        def setup_pool(e):
            e.memset(onesf[:], 1.0)
            e.memset(identf[:], 1.0)
            e.affine_select(identf[:], identf[:], [[-1, 128]], ALU.is_equal, 0.0, base=0, channel_multiplier=1)
            e.tensor_copy(identb[:], identf[:])
            e.memset(tri[:], 1.0)
            e.affine_select(tri[:], tri[:], [[1, 128]], ALU.is_ge, 0.0, base=0, channel_multiplier=-1)
            e.tensor_copy(maskb[:], tri[:])
            e.memset(m05, -0.5)
            e.memset(vaug[:], 0.0)
            e.memset(vaug3[:, :, 256:257], 1.0)
            e.memset(Cst[:], 0.0)
            e.memset(hst[:], 0.0)
            e.memset(hbf[:], 0.0)
            e.memset(Cbf[:], 0.0)
            e.memset(gm[:], 0.0)
            e.memset(xbcT[:], 0.0)
            e.memset(oh48[:], 1.0)
            e.affine_select(oh48_3, oh48_3, [[-1, 16], [0, 128]], ALU.is_equal, 0.0, base=0, channel_multiplier=1)
            e.affine_select(oh48_3, oh48_3, [[-1, 16], [0, 128]], ALU.not_equal, 1.0, base=-16, channel_multiplier=1)
            return e.affine_select(oh48_3, oh48_3, [[-1, 16], [0, 128]], ALU.not_equal, 1.0, base=-32, channel_multiplier=1)

        T.add("pool", setup_pool, w=["identb", "identf", "tri", "maskb", "onesf", "m05", "vaug", "Cst", "hst",
                                     "hbf0", "hbf1", "Cbf", "mst", "xbcT_carry", "oh48", "gm"])

        def cdma(out_ap, in_ap, key, noncontig=False):
            T.add("sp", lambda e: e.dma_start(out=out_ap, in_=in_ap, allow_slow_non_contiguous=noncontig),
                  w=[key], stream="const")

        cdma(normw_fm, dr["norm_w"].rearrange("(k p) -> p k", p=128), "normw_fm", True)
        cdma(normcat, dr["normcat"].rearrange("(k p) -> p k", p=128), "normcat", True)
        cdma(cw, dr["conv_w"].rearrange("(t p) w -> p t w", p=128), "cw")
        cdma(cb, dr["conv_b"].rearrange("(t p) -> p t", p=128), "cb", True)
        cdma(bias8[:, 0:4], dr["b_igate"].partition_broadcast(128), "bias8a")
        cdma(bias8[:, 4:8], dr["b_fgate"].partition_broadcast(128), "bias8b")
        cdma(dtb, dr["dt_bias"].partition_broadcast(128), "dtb")
        cdma(alog, dr["a_log"].partition_broadcast(128), "alog")
        cdma(drep, dr["d_skip"].partition_broadcast(128), "drep")
        cdma(finalw[:], dr["final_norm_w"].partition_broadcast(128), "finalw")
        cdma(flag, dr["flag"].partition_broadcast(128), "flag")

        T.add("act", lambda e: e.activation(out=arep, in_=alog, func=AF.Exp), r=["alog"], w=["arep0"])
        T.add("dve", lambda e: e.tensor_scalar_mul(arep, arep, -1.0), r=["arep0"], w=["arep"])
        T.add("dve", lambda e: e.tensor_scalar_mul(cst[:, 0:60], cst[:, 0:60], 0.5), r=["cw", "cb"], w=["cwb"])
        T.add("dve", lambda e: e.tensor_scalar_mul(finalw[:], finalw[:], 32.0), r=["finalw"], w=["finalw2"])
        T.add("dve", lambda e: e.tensor_scalar_mul(normw_fm, normw_fm, 32.0), r=["normw_fm"], w=["normw2"])
        T.add("dve", lambda e: e.tensor_scalar_mul(normcat[:, 0:8], normcat[:, 0:8], 4.0), r=["normcat"], w=["normcat_a"])
        T.add("dve", lambda e: e.tensor_scalar_mul(normcat[:, 8:16], normcat[:, 8:16], float(np.sqrt(2048.0) / 2.0)),
              r=["normcat"], w=["normcat_b"])

        piece = [0]
        wres_keys = []
        sc_engs = ["dve", "act", "pool"]

        def wpiece(src_ap, dst_ap, scale_ap, scale_key, ncols):
            i = piece[0]
            piece[0] += 1
            slot = i % 2
            stg = xbuf[slot]
            T.add("sp", lambda e: e.dma_start(out=stg[:, 0:ncols], in_=src_ap), w=["xbuf%d" % slot], stream="stg%d" % slot)
            eng = sc_engs[i % 3]
            if eng == "act":
                T.add("act", lambda e: e.activation(out=dst_ap, in_=stg[:, 0:ncols], func=AF.Copy, scale=scale_ap),
                      r=["xbuf%d" % slot, scale_key], w=["wres_%d" % i])
            else:
                T.add(eng, lambda e: e.tensor_scalar_mul(dst_ap, stg[:, 0:ncols], scale_ap),
                      r=["xbuf%d" % slot, scale_key], w=["wres_%d" % i])
            wres_keys.append("wres_%d" % i)

        for k in range(8):
            c0 = 0
            while c0 < NCOL:
                n = min(1024, NCOL - c0)
                wpiece(dr["w_in"][k * 128:(k + 1) * 128, c0:c0 + n], win3[:, k, c0:c0 + n], normw_fm[:, k:k + 1], "normw2", n)
                c0 += n
        for kc in range(16):
            key = "normcat_a" if kc < 8 else "normcat_b"
            wpiece(dr["w_out"][kc * 128:(kc + 1) * 128, :], wout3[:, kc, :], normcat[:, kc:kc + 1], key, 1024)

        def load_x(src, ci, slot):
            T.add("sp", lambda e: e.dma_start(out=xbuf[slot][:], in_=src[ci * L:(ci + 1) * L, :]),
                  w=["xbuf%d" % slot], stream="xin%d" % slot)

        chunk_list = [("pre", i) for i in range(n_pre)] + [("full", i) for i in range(n_full)]

        def src_of(kind):
            return dr["xpre"] if kind == "pre" else dr["x"]

        def chunk(gi):
            kind, ci = chunk_list[gi]
            full = kind == "full"
            slot = gi % 2
            xb = xbuf[slot]
            xk = "xbuf%d" % slot
            if gi + 1 < len(chunk_list):
                nk, nci = chunk_list[gi + 1]
                load_x(src_of(nk), nci, (gi + 1) % 2)
            T.add("act", lambda e: e.activation(out=junk[:], in_=xb[:], func=AF.Square, accum_out=ssq_x), r=[xk], w=["xn", "ssq_x"])
            T.add("pool", lambda e: e.tensor_scalar(tx1, ssq_x, 1024.0 * EPS, None, ALU.add), r=["ssq_x"], w=["tx1"])
            T.add("pool", lambda e: e.tensor_tensor(rstd_x, tx1, m05[:, 0:1], ALU.pow), r=["tx1", "m05"], w=["rstd_x"])
            T.add("dve", lambda e: e.tensor_scalar_mul(xn[:], xb[:], rstd_x), r=[xk, "rstd_x"], w=["xn"])
            p, pk = next_pT()

            def tr_x(e, p=p):
                for k in range(8):
                    ins = e.transpose(p[:, k * 128:(k + 1) * 128], xn[:, k * 128:(k + 1) * 128], identb[:])
                return ins
            T.add("pe", tr_x, r=["xn", "identb"], w=[pk])
            T.add("act", lambda e, p=p: e.activation(out=xnT[:], in_=p[:, 0:1024], func=AF.Copy), r=[pk], w=["xnT"])

            def proj(name, c0, n, out_ap, okey, extra_r=()):
                def f(e):
                    for k in range(8):
                        ins = e.matmul(out_ap, xnT3[:, k, :], win3[:, k, c0:c0 + n], start=(k == 0), stop=(k == 7))
                    return ins
                T.add("pe", f, r=["xnT"] + wres_keys + list(extra_r), w=[okey])

            for name, c0, n in PROJ_GROUPS:
                if not full and name not in STATE_ONLY:
                    continue
                if name == "if":
                    proj(name, c0, n, pm[:, 0:8], "pm_if")
                    T.add("dve", lambda e: e.tensor_tensor(g8, pm[:, 0:8], bias8, ALU.add), r=["pm_if", "bias8a", "bias8b"], w=["g8"])
                    T.add("act", lambda e: e.activation(out=t8, in_=g8, func=AF.Tanh, scale=1.0 / 15.0), r=["g8"], w=["t8"])
                    T.add("act", lambda e: e.activation(out=e4, in_=t8[:, 4:8], func=AF.Exp, scale=-15.0), r=["t8"], w=["e4"])
                    T.add("act", lambda e: e.activation(out=nlf, in_=e4, func=AF.Ln, bias=1.0), r=["e4"], w=["nlf"])
                    T.add("dve", lambda e: e.tensor_scalar_mul(li, t8[:, 0:4], 15.0), r=["t8"], w=["li"])
                elif name == "dt":
                    proj(name, c0, n, pm[:, 8:24], "pm_dt")
                    T.add("dve", lambda e: e.tensor_tensor(dtp, pm[:, 8:24], dtb, ALU.add), r=["pm_dt", "dtb"], w=["dtp"])
                    T.add("act", lambda e: e.activation(out=edt, in_=dtp, func=AF.Exp), r=["dtp"], w=["edt"])
                    T.add("act", lambda e: e.activation(out=dt16, in_=edt, func=AF.Ln, bias=1.0), r=["edt"], w=["dt16"])
                    T.add("dve", lambda e: e.tensor_tensor(a_tm, dt16, arep, ALU.mult), r=["dt16", "arep"], w=["a_tm"])
                else:
                    b, bk = next_bank()
                    proj(name, c0, n, b[:, 0:n], bk)
                    if name == "k":
                        T.add("dve", lambda e, b=b: e.tensor_copy(k_tm[:], b[:, 0:512]), r=[bk], w=["k_tm"])
                    elif name in ("v0", "v1"):
                        h0 = 0 if name == "v0" else 2
                        T.add("act", lambda e, b=b, h0=h0: e.activation(
                            out=vaug3[:, h0:h0 + 2, 0:256], in_=b[:, 0:512].rearrange("p (h c) -> p h c", h=2), func=AF.Copy),
                            r=[bk, "vaug"], w=["vaug_%d" % h0])
                    elif name.startswith("xbc"):
                        j = int(name[3])
                        eng = "dve" if j == 1 else "act"
                        if eng == "act":
                            T.add("act", lambda e, b=b, j=j: e.activation(out=xbc_tm[:, j * 512:(j + 1) * 512], in_=b[:, 0:512], func=AF.Copy),
                                  r=[bk], w=["xbc_tm%d" % j])
                        else:
                            T.add("dve", lambda e, b=b, j=j: e.tensor_copy(xbc_tm[:, j * 512:(j + 1) * 512], b[:, 0:512]),
                                  r=[bk], w=["xbc_tm%d" % j])
                    elif name == "q":
                        T.add("act", lambda e, b=b: e.activation(out=q_tm[:], in_=b[:, 0:512], func=AF.Copy, scale=float(128 ** -0.5)),
                              r=[bk], w=["q_tm"])
                    elif name in ("zm0", "zm1"):
                        hh = int(name[2])
                        sl = slice(hh * 512, (hh + 1) * 512)
                        T.add("act", lambda e, b=b, sl=sl: e.activation(out=tz[:, sl], in_=b[:, 0:512], func=AF.Tanh, scale=0.5),
                              r=[bk], w=["tz%d" % hh])
                        T.add("dve", lambda e, b=b, sl=sl: e.scalar_tensor_tensor(g1[:, sl], tz[:, sl], 1.0, b[:, 0:512], ALU.add, ALU.mult),
                              r=[bk, "tz%d" % hh], w=["g1_%d" % hh])
                    elif name in ("o0", "o1"):
                        hh = int(name[1])
                        sl = slice(hh * 512, (hh + 1) * 512)
                        T.add("act", lambda e, b=b, sl=sl: e.activation(out=tz[:, sl], in_=b[:, 0:512], func=AF.Tanh, scale=0.5),
                              r=[bk], w=["tz%d" % hh])
                        T.add("dve", lambda e, sl=sl: e.scalar_tensor_tensor(g1[:, sl], tz[:, sl], 1.0, g1[:, sl], ALU.add, ALU.mult),
                              r=["tz%d" % hh, "g1_%d" % hh], w=["g1_%d" % hh])
                    elif name in ("zs0", "zs1"):
                        hh = int(name[2])
                        sl = slice(hh * 512, (hh + 1) * 512)
                        T.add("act", lambda e, b=b, sl=sl: e.activation(out=tz[:, sl], in_=b[:, 0:512], func=AF.Tanh, scale=0.5),
                              r=[bk], w=["tz%d" % hh])
                        T.add("dve", lambda e, b=b, sl=sl: e.scalar_tensor_tensor(gs[:, sl], tz[:, sl], 1.0, b[:, 0:512], ALU.add, ALU.mult),
                              r=[bk, "tz%d" % hh], w=["gs_%d" % hh])

            def tr4(src, dst, skey, dkey):
                p, pk = next_pT()

                def f(e, p=p):
                    for h in range(4):
                        ins = e.transpose(p[:, h * 128:(h + 1) * 128], src[:, h * 128:(h + 1) * 128], identb[:])
                    return ins
                T.add("pe", f, r=[skey, "identb"], w=[pk])
                T.add("dve", lambda e, p=p: e.tensor_copy(dst[:], p[:, 0:512]), r=[pk], w=[dkey])

            if full:
                tr4(k_tm, kT, "k_tm", "kT")
                tr4(q_tm, qT, "q_tm", "qT")
            for half, (t0, nt) in enumerate([(0, 8), (8, 4)]):
                p, pk = next_pT()

                def f(e, p=p, t0=t0, nt=nt):
                    for t in range(nt):
                        ins = e.transpose(p[:, t * 128:(t + 1) * 128], xbc_tm[:, (t0 + t) * 128:(t0 + t + 1) * 128], identb[:])
                    return ins
                T.add("pe", f, r=["xbc_tm0", "xbc_tm1", "xbc_tm2", "identb"], w=[pk])
                T.add("act", lambda e, p=p, t0=t0, nt=nt: e.activation(
                    out=xbcT3[:, t0:t0 + nt, 4:132], in_=p[:, 0:nt * 128].rearrange("p (t c) -> p t c", t=nt), func=AF.Copy),
                    r=[pk, "xbcT_carry"], w=["xbcT_%d" % half])

            T.add("pe", lambda e: e.matmul(pm[:, 24:28], tri[:], nlf, start=True, stop=True), r=["tri", "nlf"], w=["pm_nb"])

            def f_ugm(e):
                e.matmul(pm[0:4, 128:256], li, identf[:], start=True, stop=False)
                return e.matmul(pm[0:4, 128:256], nlf, tri[:], start=False, stop=True)
            T.add("pe", f_ugm, r=["li", "nlf", "identf", "tri"], w=["pm_ugm"])
            T.add("pe", lambda e: e.matmul(pm[0:4, 120:121], nlf, onesf[:, 0:1], start=True, stop=True), r=["nlf", "onesf"], w=["pm_nbl"])
            T.add("dve", lambda e: e.reduce_max(umax, pm[0:4, 128:256], AX.X), r=["pm_ugm"], w=["umax"])
            T.add("dve", lambda e: e.tensor_tensor(Rg, umax, mst, ALU.max), r=["umax", "mst"], w=["Rg"])
            T.add("dve", lambda e: e.tensor_tensor(dg, mst, Rg, ALU.subtract), r=["mst", "Rg"], w=["dg"])
            T.add("act", lambda e: e.activation(out=soldg, in_=dg, func=AF.Exp), r=["dg"], w=["soldg"])
            T.add("dve", lambda e: e.tensor_tensor(mst, Rg, pm[0:4, 120:121], ALU.subtract), r=["Rg", "pm_nbl", "dg"], w=["mst"])
            T.add("dve", lambda e: e.tensor_scalar_mul(Dg[:, 0:4], identf[0:4, 0:4], Rg), r=["Rg", "identf"], w=["Dg_a"])
            T.add("dve", lambda e: e.tensor_scalar_mul(Dg[:, 4:8], identf[0:4, 0:4], soldg), r=["soldg", "identf"], w=["Dg_b"])
            T.add("pe", lambda e: e.matmul(pm[:, 28:36], onesf[0:4, :], Dg, start=True, stop=True), r=["onesf", "Dg_a", "Dg_b"], w=["pm_rs"])
            T.add("dve", lambda e: e.tensor_tensor(T1[:, 0:4], li, pm[:, 24:28], ALU.add), r=["li", "pm_nb"], w=["T1a"])
            T.add("dve", lambda e: e.tensor_copy(T1[:, 4:8], pm[:, 24:28]), r=["pm_nb"], w=["T1b"])
            T.add("dve", lambda e: e.tensor_tensor(
                T2.rearrange("p (a h) -> p a h", a=2), T1.rearrange("p (a h) -> p a h", a=2),
                pm[:, 28:32].unsqueeze(1).broadcast_to([128, 2, 4]), ALU.subtract), r=["T1a", "T1b", "pm_rs"], w=["T2"])
            T.add("act", lambda e: e.activation(out=wf, in_=T2, func=AF.Exp), r=["T2"], w=["wf"])
            T.add("dve", lambda e: e.tensor_copy(soldb, pm[:, 32:36]), r=["pm_rs"], w=["soldb"])

            T.add("pe", lambda e: e.matmul(pm[:, 40:56], tri[:], a_tm, start=True, stop=True), r=["tri", "a_tm"], w=["pm_acum"])
            T.add("pe", lambda e: e.matmul(pm[:, 56:72], onesf[:], a_tm, start=True, stop=True), r=["onesf", "a_tm"], w=["pm_alast"])
            T.add("dve", lambda e: e.tensor_copy(acum_sb, pm[:, 40:56]), r=["pm_acum"], w=["acum_sb"])
            T.add("act", lambda e: e.activation(out=ea, in_=pm[:, 40:56], func=AF.Exp), r=["pm_acum"], w=["ea"])
            T.add("dve", lambda e: e.tensor_tensor(dEa, pm[:, 56:72], acum_sb, ALU.subtract), r=["pm_alast", "acum_sb"], w=["dEa"])
            T.add("act", lambda e: e.activation(out=dE, in_=dEa, func=AF.Exp), r=["dEa"], w=["dE"])
            T.add("act", lambda e: e.activation(out=cdb, in_=pm[:, 56:72], func=AF.Exp), r=["pm_alast"], w=["cdb"])
            if full:
                T.add("dve", lambda e: e.tensor_copy(a3[:, 0:16], acum_sb), r=["acum_sb"], w=["a3_0"])
                T.add("dve", lambda e: e.tensor_tensor(r1, acum_sb, a3[:, 0:16], ALU.subtract), r=["acum_sb", "a3_0"], w=["r1"])
                T.add("dve", lambda e: e.tensor_copy(a3[:, 16:32], r1), r=["r1"], w=["a3_1"])
                T.add("dve", lambda e: e.tensor_tensor(r2, r1, a3[:, 16:32], ALU.subtract), r=["r1", "a3_1"], w=["r2"])
                T.add("dve", lambda e: e.tensor_copy(a3[:, 32:48], r2), r=["r2"], w=["a3_2"])
                p, pk = next_pT()
                T.add("pe", lambda e, p=p: e.transpose(p[0:48, 0:128], a3[:, 0:48], identb[:]), r=["a3_0", "a3_1", "a3_2", "identb"], w=[pk])
                T.add("dve", lambda e, p=p: e.tensor_copy(A48[:], p[0:48, 0:128]), r=[pk], w=["A48"])

            for rnd in range(3):
                ca = cacc[rnd % 2]
                ct = cth[rnd % 2]
                cak = "cacc%d" % (rnd % 2)
                ctk = "cth%d" % (rnd % 2)
                src_key = "xbcT_0" if rnd < 2 else "xbcT_1"
                for tt in range(4):
                    t = rnd * 4 + tt
                    eng = "dve"
                    cslice = ca[:, tt * 128:(tt + 1) * 128]

                    ckey = cak + "_%d" % tt
                    T.add(eng, lambda e, t=t, cslice=cslice: e.tensor_scalar(
                        cslice, xbcT3[:, t, 1:129], cw[:, t, 0:1], cb[:, t:t + 1], ALU.mult, ALU.add),
                        r=[src_key, "xbcT_carry", "cwb"], w=[ckey])
                    for w_ in range(1, 4):
                        T.add(eng, lambda e, t=t, cslice=cslice, w_=w_: e.scalar_tensor_tensor(
                            cslice, xbcT3[:, t, 1 + w_:129 + w_], cw[:, t, w_:w_ + 1], cslice, ALU.mult, ALU.add),
                            r=[src_key, "xbcT_carry", "cwb", ckey], w=[ckey])
                T.add("act", lambda e, ca=ca, ct=ct: e.activation(out=ct[:], in_=ca[:], func=AF.Tanh),
                      r=[cak + "_%d" % i for i in range(4)], w=[ctk])
                T.add("dve", lambda e, ca=ca, ct=ct, rnd=rnd: e.scalar_tensor_tensor(
                    xcT[:, rnd * 512:(rnd + 1) * 512], ct[:], 1.0, ca[:], ALU.add, ALU.mult),
                    r=[ctk] + [cak + "_%d" % i for i in range(4)], w=["xcT_%d" % rnd])
            T.add("pool", lambda e: e.tensor_copy(xbcT3[:, :, 1:4], xbcT3[:, :, 129:132]), r=["xbcT_0", "xbcT_1"], w=["xbcT_carry"])

            T.add("pool", lambda e: e.tensor_tensor(
                kp_tm[:].rearrange("p (h d) -> p h d", h=4), k_tm[:].rearrange("p (h d) -> p h d", h=4),
                wf[:, 0:4].unsqueeze(2).broadcast_to([128, 4, 128]), ALU.mult), r=["k_tm", "wf"], w=["kp_tm"])
            T.add("dve", lambda e: e.tensor_tensor(Cst3[:, :, 0:257], Cst3[:, :, 0:257],
                                                   soldb.unsqueeze(2).broadcast_to([128, 4, 257]), ALU.mult),
                  r=["Cst", "soldb"], w=["Cst"])
            if full:
                T.add("act", lambda e: e.activation(out=Cbf3[:, :, 0:257], in_=Cst3[:, :, 0:257], func=AF.Copy), r=["Cst"], w=["Cbf"])
                sb_, sbk = next_bank()

                def f_st(e, sb_=sb_):
                    for h in range(4):
                        ins = e.matmul(sb_[:, h * 128:(h + 1) * 128], kT[:, h * 128:(h + 1) * 128], qT[:, h * 128:(h + 1) * 128],
                                       start=True, stop=True)
                    return ins
                T.add("pe", f_st, r=["kT", "qT"], w=[sbk])
                for h in range(4):
                    T.add("dve", lambda e, h=h, sb_=sb_: e.scalar_tensor_tensor(
                        Pm3[:, h, :], sb_[:, h * 128:(h + 1) * 128], wf[:, h:h + 1], maskb[:], ALU.mult, ALU.mult),
                        r=[sbk, "wf", "maskb"], w=["Pm_%d" % h])
                brs = []
                for h in range(4):
                    bb, bbk = next_bank()
                    brs.append((bb, bbk))

                    def f_br(e, h=h, bb=bb):
                        e.matmul(bb[:, 0:257], Pm3[:, h, :], vaug3[:, h, 0:257], start=True, stop=False)
                        return e.matmul(bb[:, 0:257], qT[:, h * 128:(h + 1) * 128], Cbf3[:, h, 0:257], start=False, stop=True)
                    T.add("pe", f_br, r=["Pm_%d" % h, "vaug_0", "vaug_2", "qT", "Cbf"], w=[bbk])
                    T.add("act", lambda e, h=h, bb=bb: e.activation(out=absden[:, h:h + 1], in_=bb[:, 256:257], func=AF.Abs),
                          r=[bbk], w=["absden_%d" % h])
                    T.add("act", lambda e, h=h, bb=bb: e.activation(out=junk[:, 0:256], in_=bb[:, 0:256], func=AF.Square,
                                                                     accum_out=ssqr[:, h:h + 1]), r=[bbk], w=["xn", "ssqr_%d" % h])
                hk = ["absden_%d" % h for h in range(4)]
                sk = ["ssqr_%d" % h for h in range(4)]
                T.add("dve", lambda e: e.tensor_tensor(dn4, absden, wf[:, 4:8], ALU.max), r=hk + ["wf"], w=["dn4"])
                T.add("dve", lambda e: e.reciprocal(rd4, dn4), r=["dn4"], w=["rd4"])
                T.add("dve", lambda e: e.tensor_tensor(t4, rd4, rd4, ALU.mult), r=["rd4"], w=["t4"])
                T.add("dve", lambda e: e.tensor_tensor(t4, t4, ssqr, ALU.mult), r=["t4"] + sk, w=["t4"])
                T.add("pool", lambda e: e.tensor_scalar(t4, t4, 256.0 * EPS, None, ALU.add), r=["t4"], w=["t4"])
                T.add("pool", lambda e: e.tensor_tensor(rs4, t4, m05[:, 0:4], ALU.pow), r=["t4", "m05"], w=["rs4"])
                T.add("dve", lambda e: e.tensor_tensor(sc4, rd4, rs4, ALU.mult), r=["rd4", "rs4"], w=["sc4"])
                for h in range(4):
                    bb, bbk = brs[h]
                    T.add("dve", lambda e, h=h, bb=bb: e.scalar_tensor_tensor(
                        mix[:, h * 256:(h + 1) * 256], bb[:, 0:256], sc4[:, h:h + 1], g1[:, h * 256:(h + 1) * 256], ALU.mult, ALU.mult),
                        r=[bbk, "sc4", "g1_%d" % (h // 2)], w=["mix_m%d" % h])
            for h in range(4):
                cb_, cbk = next_bank()
                T.add("pe", lambda e, h=h, cb_=cb_: e.matmul(cb_[:, 0:257], kp_tm[:, h * 128:(h + 1) * 128], vaug3[:, h, 0:257],
                                                             start=True, stop=True), r=["kp_tm", "vaug_0", "vaug_2"], w=[cbk])
                T.add("dve", lambda e, h=h, cb_=cb_: e.tensor_tensor(Cst3[:, h, 0:257], Cst3[:, h, 0:257], cb_[:, 0:257], ALU.add),
                      r=[cbk, "Cst", "Cbf"], w=["Cst"])

            p, pk = next_pT()

            def tr_xc(e, p=p):
                for t in range(8):
                    ins = e.transpose(p[:, t * 128:(t + 1) * 128], xcT3[:, t, :], identb[:])
                return ins
            T.add("pe", tr_xc, r=["xcT_0", "xcT_1", "identb"], w=[pk])
            T.add("act", lambda e, p=p: e.activation(out=x_tm[:], in_=p[:, 0:1024], func=AF.Copy), r=[pk], w=["x_tm"])
            p, pk = next_pT()

            def tr_B(e, p=p):
                for t in range(2):
                    ins = e.transpose(p[:, t * 128:(t + 1) * 128], xcT3[:, 8 + t, :], identb[:])
                return ins
            T.add("pe", tr_B, r=["xcT_2", "identb"], w=[pk])
            T.add("dve", lambda e, p=p: e.tensor_copy(B_tm[:], p[:, 0:256]), r=[pk], w=["B_tm"])
            T.add("pool", lambda e: e.tensor_tensor(
                xdt[:].rearrange("p (r c) -> p r c", r=16), x_tm[:].rearrange("p (r c) -> p r c", r=16),
                dt16.unsqueeze(2).broadcast_to([128, 16, 64]), ALU.mult), r=["x_tm", "dt16"], w=["xdt"])
            T.add("pool", lambda e: e.tensor_tensor(
                xde[:].rearrange("p (r c) -> p r c", r=16), xdt[:].rearrange("p (r c) -> p r c", r=16),
                dE.unsqueeze(2).broadcast_to([128, 16, 64]), ALU.mult), r=["xdt", "dE"], w=["xde"])

            if full:
                T.add("pe", lambda e: (e.matmul(pm[:, 256:384], xcT3[:, 8, :], xcT3[:, 10, :], start=True, stop=True),
                                       e.matmul(pm[:, 384:512], xcT3[:, 9, :], xcT3[:, 11, :], start=True, stop=True))[1],
                      r=["xcT_2"], w=["pm_cb"])
                T.add("dve", lambda e: e.tensor_tensor(CBm3, pm[:, 256:512].rearrange("p (g l) -> p g l", g=2),
                                                       maskb[:].unsqueeze(1).broadcast_to([128, 2, 128]), ALU.mult),
                      r=["pm_cb", "maskb"], w=["CBm"])
                for bq in range(4):
                    ab, abk = next_bank()

                    def f_arg(e, bq=bq, ab=ab):
                        for rr in range(4):
                            ins = e.matmul(ab[:, rr * 128:(rr + 1) * 128], oh48_3[:, bq * 4 + rr, :], A48[:], start=True, stop=True)
                        return ins
                    T.add("pe", f_arg, r=["oh48", "A48"], w=[abk])

                    def f_relu(e, bq=bq, ab=ab):
                        for rr in range(4):
                            hd = bq * 4 + rr
                            ins = e.activation(out=rl[:, rr * 128:(rr + 1) * 128], in_=ab[:, rr * 128:(rr + 1) * 128],
                                               func=AF.Relu, bias=acum_sb[:, hd:hd + 1], scale=-1.0)
                        return ins
                    T.add("act", f_relu, r=[abk, "acum_sb"], w=["rl"])
                    T.add("act", lambda e, bq=bq: e.activation(out=dec[:, bq * 512:(bq + 1) * 512], in_=rl[:], func=AF.Exp, scale=-1.0),
                          r=["rl"], w=["dec_%d" % bq])
                    g = bq // 2
                    T.add("pool", lambda e, bq=bq, g=g: e.tensor_tensor(
                        dec3[:, bq * 4:bq * 4 + 4, :], dec3[:, bq * 4:bq * 4 + 4, :],
                        CBm3[:, g:g + 1, :].broadcast_to([128, 4, 128]), ALU.mult), r=["dec_%d" % bq, "CBm"], w=["dec_%d" % bq])
                T.add("pool", lambda e: e.tensor_tensor(
                    yo[:].rearrange("p (r c) -> p r c", r=16), x_tm[:].rearrange("p (r c) -> p r c", r=16),
                    drep.unsqueeze(2).broadcast_to([128, 16, 64]), ALU.mult), r=["x_tm", "drep"], w=["yo_0", "yo_1"])
                for g in range(2):
                    yd, ydk = next_bank()

                    def f_yd(e, g=g, yd=yd):
                        for rr in range(8):
                            hd = g * 8 + rr
                            ins = e.matmul(yd[:, rr * 64:(rr + 1) * 64], dec3[:, hd, :], xdt[:, hd * 64:(hd + 1) * 64], start=True, stop=True)
                        return ins
                    T.add("pe", f_yd, r=["dec_%d" % (2 * g), "dec_%d" % (2 * g + 1), "xdt"], w=[ydk])
                    yf, yfk = next_bank()
                    T.add("pe", lambda e, g=g, yf=yf: e.matmul(yf[:, 0:512], xcT3[:, 10 + g, :], hbf[:, g * 512:(g + 1) * 512], start=True, stop=True),
                          r=["xcT_2", "hbf%d" % g], w=[yfk])
                    gsl = slice(g * 512, (g + 1) * 512)
                    T.add("dve", lambda e, g=g, yf=yf: e.tensor_tensor(
                        ytmp[:].rearrange("p (r c) -> p r c", r=8), yf[:, 0:512].rearrange("p (r c) -> p r c", r=8),
                        ea[:, g * 8:(g + 1) * 8].unsqueeze(2).broadcast_to([128, 8, 64]), ALU.mult), r=[yfk, "ea"], w=["ytmp"])
                    T.add("dve", lambda e, yd=yd: e.tensor_tensor(ytmp[:], ytmp[:], yd[:, 0:512], ALU.add), r=[ydk, "ytmp"], w=["ytmp"])
                    T.add("pool", lambda e, gsl=gsl: e.tensor_tensor(yo[:, gsl], yo[:, gsl], ytmp[:], ALU.add), r=["ytmp", "yo_%d" % g], w=["yo_%d" % g])
                    T.add("pool", lambda e, gsl=gsl: e.tensor_tensor(yo[:, gsl], yo[:, gsl], gs[:, gsl], ALU.mult),
                          r=["yo_%d" % g, "gs_%d" % g], w=["yo_%d" % g])
                    T.add("act", lambda e, g=g, gsl=gsl: e.activation(out=junk[:, 0:512], in_=yo[:, gsl], func=AF.Square, accum_out=ssq_s[:, g:g + 1]),
                          r=["yo_%d" % g], w=["xn", "ssq_s%d" % g])
                T.add("pool", lambda e: e.tensor_scalar(ts2, ssq_s, 2048.0 * EPS, None, ALU.add), r=["ssq_s0", "ssq_s1"], w=["ts2"])
                T.add("pool", lambda e: e.tensor_tensor(rstd_s, ts2, m05[:, 0:2], ALU.pow), r=["ts2", "m05"], w=["rstd_s"])
                for g in range(2):
                    gsl = slice(g * 512, (g + 1) * 512)
                    T.add("dve", lambda e, g=g, gsl=gsl: e.tensor_scalar_mul(mix[:, 1024 + g * 512:1024 + (g + 1) * 512], yo[:, gsl], rstd_s[:, g:g + 1]),
                          r=["yo_%d" % g, "rstd_s"], w=["mix_s%d" % g])
            for g in range(2):
                st, stk = next_bank()
                gsl = slice(g * 512, (g + 1) * 512)
                T.add("pe", lambda e, g=g, st=st, gsl=gsl: e.matmul(st[:, 0:512], B_tm[:, g * 128:(g + 1) * 128], xde[:, gsl], start=True, stop=True),
                      r=["B_tm", "xde"], w=[stk])
                T.add("dve", lambda e, g=g, gsl=gsl: e.tensor_tensor(
                    hst[:, gsl].rearrange("p (r c) -> p r c", r=8), hst[:, gsl].rearrange("p (r c) -> p r c", r=8),
                    cdb[:, g * 8:(g + 1) * 8].unsqueeze(2).broadcast_to([128, 8, 64]), ALU.mult), r=["hst", "cdb", "hbf%d" % g], w=["hst%d" % g])
                T.add("dve", lambda e, st=st, gsl=gsl: e.tensor_tensor(hst[:, gsl], hst[:, gsl], st[:, 0:512], ALU.add), r=[stk, "hst%d" % g], w=["hst%d" % g])
                T.add("act", lambda e, gsl=gsl: e.activation(out=hbf[:, gsl], in_=hst[:, gsl], func=AF.Copy), r=["hst%d" % g], w=["hbf%d" % g])

            if not full:
                return
            mkeys = ["mix_m%d" % h for h in range(4)] + ["mix_s0", "mix_s1"]
            for half in range(2):
                p, pk = next_pT()

                def tr_m(e, p=p, half=half):
                    for t in range(8):
                        kc = half * 8 + t
                        ins = e.transpose(p[:, t * 128:(t + 1) * 128], mix[:, kc * 128:(kc + 1) * 128], identb[:])
                    return ins
                T.add("pe", tr_m, r=mkeys + ["identb"], w=[pk])
                if half == 0:
                    T.add("act", lambda e, p=p: e.activation(out=mixT[:, 0:1024], in_=p[:, 0:1024], func=AF.Copy), r=[pk], w=["mixT_0"])
                else:
                    T.add("dve", lambda e, p=p: e.tensor_copy(mixT[:, 1024:2048], p[:, 0:1024]), r=[pk], w=["mixT_1"])
            for half in range(2):
                ob, obk = next_bank()

                def f_o(e, ob=ob, half=half):
                    for kc in range(16):
                        ins = e.matmul(ob[:, 0:512], mixT3[:, kc, :], wout3[:, kc, half * 512:(half + 1) * 512], start=(kc == 0), stop=(kc == 15))
                    return ins
                T.add("pe", f_o, r=["mixT_0", "mixT_1"] + wres_keys, w=[obk])
                hsl = slice(half * 512, (half + 1) * 512)
                T.add("dve", lambda e, ob=ob, hsl=hsl: e.tensor_tensor(xb[:, hsl], xb[:, hsl], ob[:, 0:512], ALU.add), r=[obk, xk], w=[xk])
            T.add("act", lambda e: e.activation(out=junk[:], in_=xb[:], func=AF.Square, accum_out=ssq_o), r=[xk], w=["xn", "ssq_o"])
            T.add("pool", lambda e: e.tensor_scalar(to1, ssq_o, 1024.0 * EPS, None, ALU.add), r=["ssq_o"], w=["to1"])
            T.add("pool", lambda e: e.tensor_tensor(rstd_o, to1, m05[:, 0:1], ALU.pow), r=["to1", "m05"], w=["rstd_o"])
            T.add("dve", lambda e: e.scalar_tensor_tensor(xb[:], xb[:], rstd_o, finalw[:], ALU.mult, ALU.mult), r=[xk, "rstd_o", "finalw2"], w=[xk])
            T.add("sp", lambda e: e.dma_start(out=out_d[ci * L:(ci + 1) * L, :], in_=xb[:]), r=[xk], w=["out_dram"], stream="out%d" % slot)

        k0, c0_ = chunk_list[0]
        load_x(src_of(k0), c0_, 0)
        for gi in range(len(chunk_list)):
            chunk(gi)
            if n_pre and gi == n_pre - 1:
                T.add("dve", lambda e: e.tensor_scalar_mul(Cst[:], Cst[:], flag), r=["Cst", "flag"], w=["Cst"])
                T.add("dve", lambda e: e.tensor_scalar_mul(hst[:], hst[:], flag), r=["hst0", "hst1", "flag"], w=["hst0", "hst1", "hst"])
                T.add("dve", lambda e: e.tensor_scalar_mul(hbf[:], hbf[:], flag), r=["hbf0", "hbf1", "flag"], w=["hbf0", "hbf1"])
                T.add("dve", lambda e: e.tensor_scalar_mul(mst, mst, flag[0:4, :]), r=["mst", "flag"], w=["mst"])
                T.add("dve", lambda e: e.tensor_scalar_mul(xbcT3[:, :, 1:4], xbcT3[:, :, 1:4], flag), r=["xbcT_carry", "flag"], w=["xbcT_carry"])

        for nm in debug:
            tile_ap, shape, rkeys = {
                "xnT": (xnT[:], [128, 1024], ["xnT"]),
                "k_tm": (k_tm[:], [128, 512], ["k_tm"]),
                "mix": (mix[:], [128, 2048], ["mix_m0", "mix_m1", "mix_m2", "mix_m3", "mix_s0", "mix_s1"]),
                "Cst": (Cst[:], [128, 4 * 258], ["Cst"]),
                "hst": (hst[:], [128, 1024], ["hst0", "hst1"]),
                "x_tm": (x_tm[:], [128, 1024], ["x_tm"]),
                "sm": (sm[:], [128, 256], ["wf", "dE", "ea", "cdb", "dt16", "acum_sb"]),
                "yo": (yo[:], [128, 1024], ["yo_0", "yo_1"]),
                "dec": (dec[:], [128, 2048], ["dec_0", "dec_1", "dec_2", "dec_3"]),
                "xcT": (xcT[:], [128, 1536], ["xcT_0", "xcT_1", "xcT_2"]),
                "g1": (g1[:], [128, 1024], ["g1_0", "g1_1"]),
                "vaug": (vaug[:], [128, 4 * 258], ["vaug_0", "vaug_2"]),
            }[nm]
            d = nc.dram_tensor("dbg_" + nm, shape, F32, kind="ExternalOutput").ap()
            dbg_out[nm] = d
            q_eng = "sp" if tile_ap.dtype == F32 else "pool"
            T.add(q_eng, lambda e, d=d, tile_ap=tile_ap: e.dma_start(out=d, in_=tile_ap), r=rkeys, w=["dbg_" + nm], stream="dbg")

        T.emit(nc, es)
    return nc


_CACHE = {}


def kernel(x, norm_w, w_in, b_igate, b_fgate, conv_w, conv_b, dt_bias, a_log, d_skip,
           mlstm_norm_w, ssd_norm_w, w_out, final_norm_w):
    f = lambda a: np.ascontiguousarray(np.asarray(a, dtype=np.float32))
    x = f(x)
    n_half = SEQ // 2 // L
    key = "main"
    if key not in _CACHE:
        _CACHE[key] = build(n_half, n_half)
    nc = _CACHE[key]
    common = {
        "norm_w": f(norm_w)[0], "w_in": f(w_in)[0], "b_igate": f(b_igate)[0], "b_fgate": f(b_fgate)[0],
        "conv_w": f(conv_w)[0], "conv_b": f(conv_b)[0], "dt_bias": f(dt_bias)[0], "a_log": f(a_log)[0],
        "d_skip": f(d_skip)[0],
        "normcat": np.ascontiguousarray(np.concatenate([f(mlstm_norm_w)[0], f(ssd_norm_w)[0]])),
        "w_out": f(w_out)[0], "final_norm_w": f(final_norm_w),
    }
    half = SEQ // 2
    zeros = np.zeros((half, D_MODEL), np.float32)
    in_maps = []
    for core in range(NCORES):
        b, hf = core // 2, core % 2
        m = dict(common)
        m["x"] = np.ascontiguousarray(x[b, hf * half:(hf + 1) * half])
        m["xpre"] = zeros if hf == 0 else np.ascontiguousarray(x[b, 0:half])
        m["flag"] = np.array([float(hf)], np.float32)
        in_maps.append(m)
    res = run_bass_kernel_spmd(nc, in_maps, core_ids=list(range(NCORES)))
    out = np.empty((BATCH, SEQ, D_MODEL), np.float32)
    for core in range(NCORES):
        b, hf = core // 2, core % 2
        out[b, hf * half:(hf + 1) * half] = res.results[core]["out"]
    return out
```

```python
import numpy as np
from contextlib import ExitStack
import concourse.bass as bass
import concourse.mybir as mybir
from concourse.bass_utils import run_bass_kernel_spmd

F32 = mybir.dt.float32
BF16 = mybir.dt.bfloat16
AF = mybir.ActivationFunctionType
ALU = mybir.AluOpType
AX = mybir.AxisListType

D_MODEL = 1024
SEQ = 8192
BATCH = 4
NCOL = 6680
EPS = 1e-6
L = 128
NCORES = 8


class Tracker:
    def __init__(self):
        self.ops = []
        self.bufs = {}
        self.waitall_streams = set()
        self.regions = {}

    def reg(self, key, arena, off, nbytes, gran=64):
        self.regions[key] = [(arena, g) for g in range(off // gran, (off + nbytes + gran - 1) // gran)]

    def _expand(self, keys):
        out = []
        for k in keys:
            out.extend(self.regions.get(k, [k]))
        return out

    PSUM_BANKS = ("pT0", "pT1", "pm", "pb0", "pb1", "pb2", "pb3", "pb4")

    @classmethod
    def _bank(cls, k):
        if isinstance(k, str):
            if k.startswith("pm_"):
                return "pm"
            if k in cls.PSUM_BANKS:
                return k
        return None

    def add(self, eng, fn, r=(), w=(), stream=None):
        banks = [self._bank(k) for k in list(r) + list(w)]
        banks = [b for b in banks if b is not None]
        r = [k for k in r if self._bank(k) is None]
        w = [k for k in w if self._bank(k) is None] + sorted(set(banks))
        r = self._expand(r)
        w = self._expand(w)
        deps = set()
        for k in r:
            b = self.bufs.setdefault(k, [None, []])
            if b[0] is not None:
                deps.add(b[0])
        for k in w:
            b = self.bufs.setdefault(k, [None, []])
            if b[0] is not None:
                deps.add(b[0])
            deps.update(b[1])
        idx = len(self.ops)
        deps.discard(idx)
        self.ops.append(dict(eng=eng, fn=fn, deps=deps, stream=stream, idx=idx, has_dep=False))
        for k in r:
            self.bufs[k][1].append(idx)
        for k in w:
            self.bufs[k] = [idx, []]
        return idx

    class _Ins:
        def then_inc(self, *a, **k):
            return self

    class _Probe:
        def __init__(self):
            self.calls = []

        def __getattr__(self, name):
            def f(*args, **kw):
                self.calls.append((name, args, kw))
                return Tracker._Ins()
            return f

    @staticmethod
    def _free(ap):
        n = 1
        for d in ap.shape[1:]:
            n *= int(d)
        return n

    def _cost(self, op):
        pr = Tracker._Probe()
        op["fn"](pr)
        eng = op["eng"]
        dur = 0.0
        lat = 0.0
        for name, args, kw in pr.calls:
            out = kw.get("out", args[0] if args else None)
            if name == "dma_start":
                src = kw.get("in_", args[1] if len(args) > 1 else None)
                nbytes = self._free(out) * int(out.shape[0]) * 4
                dur += 80.0
                lat = 2500.0 + nbytes / 160.0
                continue
            n = self._free(out) if out is not None and hasattr(out, "shape") else 64
            if eng == "pe":
                lhs = args[1] if len(args) > 1 else None
                mult = 4.0 if (lhs is not None and lhs.dtype == F32 and name == "matmul") else 1.0
                dur += 16.0 + max(n, 64) * mult / 1.95
            elif eng == "act":
                dur += 230.0 + n / 1.15
            elif eng == "dve":
                dur += 170.0 + n / 0.96
            elif eng == "pool":
                dur += 300.0 + n * 2.0
            else:
                dur += 50.0
        op["dur"] = max(dur, 20.0)
        op["lat"] = lat

    def schedule(self):
        import heapq
        ops = self.ops
        n = len(ops)
        for op in ops:
            self._cost(op)
        succ = [[] for _ in range(n)]
        for op in ops:
            for d in op["deps"]:
                succ[d].append(op["idx"])
        bl = [0.0] * n
        for i in range(n - 1, -1, -1):
            m = 0.0
            for sidx in succ[i]:
                if bl[sidx] > m:
                    m = bl[sidx]
            bl[i] = m + ops[i]["dur"] + ops[i]["lat"]
        ndeps = [len(op["deps"]) for op in ops]
        ready_at = [0.0] * n
        engs = ["pe", "act", "dve", "pool", "sp"]
        avail = {e: [] for e in engs}
        for i in range(n):
            if ndeps[i] == 0:
                avail[ops[i]["eng"]].append(i)
        free_at = {e: 0.0 for e in engs}
        order = []
        finish = [0.0] * n
        done = 0
        WINDOW = 4000
        lowest_unscheduled = 0
        scheduled = [False] * n
        while done < n:
            best = None
            while lowest_unscheduled < n and scheduled[lowest_unscheduled]:
                lowest_unscheduled += 1
            for e in engs:
                lst = avail[e]
                if not lst:
                    continue
                fa = free_at[e]
                cand = None
                for i in lst:
                    if i > lowest_unscheduled + WINDOW:
                        continue
                    st = ready_at[i] if ready_at[i] > fa else fa
                    key = (st, -bl[i], i)
                    if cand is None or key < cand[0]:
                        cand = (key, i, st)
                if cand is None:
                    continue
                if best is None or cand[0] < best[0]:
                    best = cand + (e,)
            assert best is not None, "scheduler stuck"
            _, i, st, e = best
            avail[e].remove(i)
            op = ops[i]
            fin = st + op["dur"]
            free_at[e] = fin
            finish[i] = fin + op["lat"]
            op["t_start"] = st
            order.append(i)
            scheduled[i] = True
            done += 1
            for sidx in succ[i]:
                if finish[i] > ready_at[sidx]:
                    ready_at[sidx] = finish[i]
                ndeps[sidx] -= 1
                if ndeps[sidx] == 0:
                    avail[ops[sidx]["eng"]].append(sidx)
        self.est_makespan_us = max(finish) / 1e3
        busy = {e: 0.0 for e in engs}
        for op in ops:
            busy[op["eng"]] += op["dur"]
        print("[sched] est makespan %.1f us; busy us: %s" % (self.est_makespan_us, {e: round(v / 1e3) for e, v in busy.items()}))
        remap = {old: new for new, old in enumerate(order)}
        new_ops = []
        for new, old in enumerate(order):
            op = ops[old]
            op["deps"] = {remap[d] for d in op["deps"]}
            op["idx"] = new
            new_ops.append(op)
        self.ops = new_ops

    def emit(self, nc, es, same_engine_sync=True, do_schedule=True):
        if do_schedule:
            self.schedule()
        ops = self.ops
        for op in ops:
            nd = set()
            for d in op["deps"]:
                dop = ops[d]
                if dop["stream"] is None and dop["eng"] == "pe" and op["eng"] == "pe" and op["stream"] is None:
                    continue
                if (not same_engine_sync) and dop["stream"] is None and op["stream"] is None and dop["eng"] == op["eng"]:
                    continue
                nd.add(d)
            op["deps"] = nd
            for d in nd:
                ops[d]["has_dep"] = True
        sems = {}
        engs = ["pe", "act", "dve", "pool", "sp"]
        for e in engs:
            sems[e] = es.enter_context(nc.semaphore("s_" + e))
        streams = sorted({op["stream"] for op in ops if op["stream"] is not None})
        for s in streams:
            sems["d:" + s] = es.enter_context(nc.semaphore("d_" + s))
        cnt = {e: 0 for e in engs}
        scnt = {s: 0 for s in streams}
        for op in ops:
            if op["stream"] is not None:
                scnt[op["stream"]] += 1
                op["sig"] = ("d:" + op["stream"], 16 * scnt[op["stream"]])
            elif op["has_dep"]:
                cnt[op["eng"]] += 1
                op["sig"] = (op["eng"], cnt[op["eng"]])
            else:
                op["sig"] = None
        for op in ops:
            if op["stream"] in self.waitall_streams:
                op["sig"] = ("d:" + op["stream"], 16 * scnt[op["stream"]])
        self.final_counts = {("d:" + s): 16 * scnt[s] for s in streams}
        self.sems = sems
        block = es.enter_context(nc.Block())
        per_eng = {e: [op for op in ops if op["eng"] == e] for e in engs}

        def run(engobj, lst, extra_tail=None):
            waited = {}
            for op in lst:
                need = {}
                for d in op["deps"]:
                    sg = ops[d]["sig"]
                    assert sg is not None
                    if sg[1] > need.get(sg[0], 0):
                        need[sg[0]] = sg[1]
                for sk, v in need.items():
                    if v > waited.get(sk, 0):
                        engobj.wait_ge(sems[sk], v)
                        waited[sk] = v
                ins = op["fn"](engobj)
                if op["stream"] is not None:
                    ins.then_inc(sems["d:" + op["stream"]], 16)
                elif op["sig"] is not None:
                    ins.then_inc(sems[op["eng"]], 1)
            if extra_tail is not None:
                extra_tail(engobj, waited)

        def sp_tail(engobj, waited):
            for sk, v in self.final_counts.items():
                if v > waited.get(sk, 0):
                    engobj.wait_ge(sems[sk], v)

        @block.sync
        def _(e):
            run(e, per_eng["sp"], sp_tail)

        @block.tensor
        def _(e):
            run(e, per_eng["pe"])

        @block.scalar
        def _(e):
            run(e, per_eng["act"])

        @block.vector
        def _(e):
            run(e, per_eng["dve"])

        @block.gpsimd
        def _(e):
            run(e, per_eng["pool"])


PROJ_GROUPS = [
    ("if", 2048, 8), ("dt", 6664, 16), ("k", 512, 512), ("v0", 1024, 512), ("v1", 1536, 512),
    ("xbc0", 5128, 512), ("xbc1", 5640, 512), ("xbc2", 6152, 512), ("q", 0, 512),
    ("zm0", 3080, 512), ("zm1", 3592, 512), ("o0", 2056, 512), ("o1", 2568, 512),
    ("zs0", 4104, 512), ("zs1", 4616, 512),
]
STATE_ONLY = {"if", "dt", "k", "v0", "v1", "xbc0", "xbc1", "xbc2"}


def build(n_pre, n_full, debug=()):
    nc = bass.Bass("TRN2", target_bir_lowering=False)
    T_pre, T_full = n_pre * L, n_full * L
    dr = {}
    dr["x"] = nc.dram_tensor("x", [max(T_full, 1), D_MODEL], F32, kind="ExternalInput").ap()
    if n_pre:
        dr["xpre"] = nc.dram_tensor("xpre", [T_pre, D_MODEL], F32, kind="ExternalInput").ap()
    for nm, shp in [("norm_w", [1024]), ("w_in", [1024, NCOL]), ("b_igate", [4]), ("b_fgate", [4]),
                    ("conv_w", [1536, 4]), ("conv_b", [1536]), ("dt_bias", [16]), ("a_log", [16]),
                    ("d_skip", [16]), ("normcat", [2048]), ("w_out", [2048, 1024]),
                    ("final_norm_w", [1024]), ("flag", [1])]:
        dr[nm] = nc.dram_tensor(nm, shp, F32, kind="ExternalInput").ap()
    out_d = nc.dram_tensor("out", [T_full, D_MODEL], F32, kind="ExternalOutput").ap()
    dbg_out = {}

    T = Tracker()
    T.waitall_streams.add("const")
    es = ExitStack()
    with es:
        def sb(name, shape, dt):
            return es.enter_context(nc.sbuf_tensor(name, shape, dt))

        def ps(name, shape, dt):
            return es.enter_context(nc.psum_tensor(name, shape, dt))

        win = sb("win", [128, 8 * NCOL], BF16)
        wout = sb("wout", [128, 16 * 1024], BF16)
        win3 = win[:].rearrange("p (k n) -> p k n", k=8)
        wout3 = wout[:].rearrange("p (k n) -> p k n", k=16)
        identb = sb("identb", [128, 128], BF16)
        identf = sb("identf", [128, 128], F32)
        tri = sb("tri", [128, 128], F32)
        maskb = sb("maskb", [128, 128], BF16)
        onesf = sb("onesf", [128, 128], F32)
        oh48 = sb("oh48", [48, 16 * 128], BF16)
        oh48_3 = oh48[:].rearrange("p (r s) -> p r s", r=16)
        finalw = sb("finalw", [128, 1024], F32)
        cst = sb("cst", [128, 256], F32)
        cw = cst[:, 0:48].rearrange("p (t w) -> p t w", t=12)
        cb = cst[:, 48:60]
        bias8 = cst[:, 60:68]
        dtb = cst[:, 68:84]
        arep = cst[:, 84:100]
        drep = cst[:, 100:116]
        normw_fm = cst[:, 116:124]
        normcat = cst[:, 124:140]
        m05 = cst[:, 140:148]
        flag = cst[:, 148:149]
        alog = cst[:, 152:168]

        xbuf = [sb("xbuf0", [128, 1024], F32), sb("xbuf1", [128, 1024], F32)]
        ARENA_BYTES = 25664
        arena = sb("arena", [128, ARENA_BYTES // 4], F32)

        def carve(layout_off, name, nbytes, dt, subkeys=None):
            assert layout_off[0] % 4 == 0
            o = layout_off[0]
            layout_off[0] += (nbytes + 3) // 4 * 4
            assert layout_off[0] <= ARENA_BYTES, (name, layout_off[0])
            v = arena[:, o // 4:(o + (nbytes + 3) // 4 * 4) // 4]
            if dt != F32:
                v = v.bitcast(dt)
            if subkeys is None:
                T.reg(name, "A", o, nbytes)
            else:
                n = len(subkeys)
                for i, sk in enumerate(subkeys):
                    T.reg(sk, "A", o + i * (nbytes // n), nbytes // n)
            return v

        lo1 = [0]
        xn = carve(lo1, "xn", 2048, BF16)
        junk = xn
        xnT = carve(lo1, "xnT", 2048, BF16)
        xnT3 = xnT[:].rearrange("p (k t) -> p k t", k=8)
        xbc_tm = carve(lo1, "xbc_tm", 3072, BF16, ["xbc_tm0", "xbc_tm1", "xbc_tm2"])
        cacc = [carve(lo1, "cacc0", 2048, F32, ["cacc0_%d" % i for i in range(4)]),
                carve(lo1, "cacc1", 2048, F32, ["cacc1_%d" % i for i in range(4)])]
        cth = [carve(lo1, "cth0", 1024, BF16), carve(lo1, "cth1", 1024, BF16)]
        q_tm = carve(lo1, "q_tm", 1024, BF16)
        tz = carve(lo1, "tz", 2048, BF16, ["tz0", "tz1"])
        k_tm = carve(lo1, "k_tm", 1024, BF16)
        kp_tm = carve(lo1, "kp_tm", 1024, BF16)
        qT = carve(lo1, "qT", 1024, BF16)
        kT = carve(lo1, "kT", 1024, BF16)
        Pm = carve(lo1, "Pm", 1024, BF16, ["Pm_%d" % i for i in range(4)])
        Pm3 = Pm[:].rearrange("p (h j) -> p h j", h=4)
        Cbf = carve(lo1, "Cbf", 2064, BF16)
        Cbf3 = Cbf[:].rearrange("p (h c) -> p h c", h=4)
        g1 = carve(lo1, "g1", 2048, BF16, ["g1_0", "g1_1"])
        lo2 = [0]
        dec = carve(lo2, "dec", 4096, BF16, ["dec_%d" % i for i in range(4)])
        dec3 = dec[:].rearrange("p (r l) -> p r l", r=16)
        rl = carve(lo2, "rl", 2048, F32)
        CBm = carve(lo2, "CBm", 512, BF16)
        CBm3 = CBm[:].rearrange("p (g l) -> p g l", g=2)
        yo = carve(lo2, "yo", 4096, F32, ["yo_0", "yo_1"])
        ytmp = carve(lo2, "ytmp", 2048, F32)
        x_tm = carve(lo2, "x_tm", 2048, BF16)
        B_tm = carve(lo2, "B_tm", 512, BF16)
        xdt = carve(lo2, "xdt", 2048, BF16)
        xde = carve(lo2, "xde", 2048, BF16)
        mixT = carve(lo2, "mixT", 4096, BF16, ["mixT_0", "mixT_1"])
        mixT3 = mixT[:].rearrange("p (k t) -> p k t", k=16)

        vaug = sb("vaug", [128, 4 * 258], BF16)
        vaug3 = vaug[:].rearrange("p (h c) -> p h c", h=4)
        gs = sb("gs", [128, 1024], BF16)
        xbcT = sb("xbcT", [128, 12 * 132], BF16)
        xbcT3 = xbcT[:].rearrange("p (t c) -> p t c", t=12)
        xcT = sb("xcT", [128, 1536], BF16)
        xcT3 = xcT[:].rearrange("p (t c) -> p t c", t=12)
        Cst = sb("Cst", [128, 4 * 258], F32)
        Cst3 = Cst[:].rearrange("p (h c) -> p h c", h=4)
        hst = sb("hst", [128, 1024], F32)
        hbf = sb("hbf", [128, 1024], BF16)
        mix = sb("mix", [128, 2048], BF16)
        a3 = sb("a3", [128, 48], BF16)
        A48 = sb("A48", [48, 128], BF16)
        sm = sb("sm", [128, 256], F32)
        g8 = sm[:, 0:8]
        t8 = sm[:, 8:16]
        e4 = sm[:, 16:20]
        nlf = sm[:, 20:24]
        li = sm[:, 24:28]
        T1 = sm[:, 28:36]
        T2 = sm[:, 36:44]
        wf = sm[:, 44:52]
        soldb = sm[:, 52:56]
        absden = sm[:, 56:60]
        ssqr = sm[:, 60:64]
        dn4 = sm[:, 64:68]
        rd4 = sm[:, 68:72]
        t4 = sm[:, 72:76]
        rs4 = sm[:, 76:80]
        sc4 = sm[:, 80:84]
        ssq_x = sm[:, 84:85]
        rstd_x = sm[:, 85:86]
        tx1 = sm[:, 86:87]
        ssq_o = sm[:, 87:88]
        rstd_o = sm[:, 88:89]
        to1 = sm[:, 89:90]
        ssq_s = sm[:, 90:92]
        rstd_s = sm[:, 92:94]
        ts2 = sm[:, 94:96]
        dtp = sm[:, 96:112]
        edt = sm[:, 112:128]
        dt16 = sm[:, 128:144]
        a_tm = sm[:, 144:160]
        acum_sb = sm[:, 160:176]
        ea = sm[:, 176:192]
        dEa = sm[:, 192:208]
        dE = sm[:, 208:224]
        cdb = sm[:, 224:240]
        r1 = sm[:, 240:256]
        sm2 = sb("sm2", [128, 16], F32)
        r2 = sm2[:, 0:16]
        gm = sb("gm", [4, 32], F32)
        mst = gm[:, 0:1]
        umax = gm[:, 1:2]
        Rg = gm[:, 2:3]
        dg = gm[:, 3:4]
        soldg = gm[:, 4:5]
        Dg = gm[:, 8:16]
        pT = [ps("pT0", [128, 1024], BF16), ps("pT1", [128, 1024], BF16)]
        pm = ps("pm", [128, 512], F32)
        pb = [ps("pb%d" % i, [128, 512], F32) for i in range(5)]
        print("sbuf bytes remaining:", nc.sbuf_bytes_remaining)

        bank_rr = [0]

        def next_bank():
            i = bank_rr[0] % 5
            bank_rr[0] += 1
            return pb[i], "pb%d" % i

        pt_rr = [0]

        def next_pT():
            i = pt_rr[0] % 2
            pt_rr[0] += 1
            return pT[i], "pT%d" % i

        def setup_pool(e):
            e.memset(onesf[:], 1.0)
            e.memset(identf[:], 1.0)
            e.affine_select(identf[:], identf[:], [[-1, 128]], ALU.is_equal, 0.0, base=0, channel_multiplier=1)
            e.tensor_copy(identb[:], identf[:])
            e.memset(tri[:], 1.0)
            e.affine_select(tri[:], tri[:], [[1, 128]], ALU.is_ge, 0.0, base=0, channel_multiplier=-1)
            e.tensor_copy(maskb[:], tri[:])
            e.memset(m05, -0.5)
            e.memset(vaug[:], 0.0)
            e.memset(vaug3[:, :, 256:257], 1.0)
            e.memset(Cst[:], 0.0)
            e.memset(hst[:], 0.0)
            e.memset(hbf[:], 0.0)
            e.memset(Cbf[:], 0.0)
            e.memset(gm[:], 0.0)
            e.memset(xbcT[:], 0.0)
            e.memset(oh48[:], 1.0)
            e.affine_select(oh48_3, oh48_3, [[-1, 16], [0, 128]], ALU.is_equal, 0.0, base=0, channel_multiplier=1)
            e.affine_select(oh48_3, oh48_3, [[-1, 16], [0, 128]], ALU.not_equal, 1.0, base=-16, channel_multiplier=1)
            return e.affine_select(oh48_3, oh48_3, [[-1, 16], [0, 128]], ALU.not_equal, 1.0, base=-32, channel_multiplier=1)

        T.add("pool", setup_pool, w=["identb", "identf", "tri", "maskb", "onesf", "m05", "vaug", "Cst", "hst",
                                     "hbf0", "hbf1", "Cbf", "mst", "xbcT_carry", "oh48", "gm"])

        def cdma(out_ap, in_ap, key, noncontig=False):
            T.add("sp", lambda e: e.dma_start(out=out_ap, in_=in_ap, allow_slow_non_contiguous=noncontig),
                  w=[key], stream="const")

        cdma(normw_fm, dr["norm_w"].rearrange("(k p) -> p k", p=128), "normw_fm", True)
        cdma(normcat, dr["normcat"].rearrange("(k p) -> p k", p=128), "normcat", True)
        cdma(cw, dr["conv_w"].rearrange("(t p) w -> p t w", p=128), "cw")
        cdma(cb, dr["conv_b"].rearrange("(t p) -> p t", p=128), "cb", True)
        cdma(bias8[:, 0:4], dr["b_igate"].partition_broadcast(128), "bias8a")
        cdma(bias8[:, 4:8], dr["b_fgate"].partition_broadcast(128), "bias8b")
        cdma(dtb, dr["dt_bias"].partition_broadcast(128), "dtb")
        cdma(alog, dr["a_log"].partition_broadcast(128), "alog")
        cdma(drep, dr["d_skip"].partition_broadcast(128), "drep")
        cdma(finalw[:], dr["final_norm_w"].partition_broadcast(128), "finalw")
        cdma(flag, dr["flag"].partition_broadcast(128), "flag")

        T.add("act", lambda e: e.activation(out=arep, in_=alog, func=AF.Exp), r=["alog"], w=["arep0"])
        T.add("dve", lambda e: e.tensor_scalar_mul(arep, arep, -1.0), r=["arep0"], w=["arep"])
        T.add("dve", lambda e: e.tensor_scalar_mul(cst[:, 0:60], cst[:, 0:60], 0.5), r=["cw", "cb"], w=["cwb"])
        T.add("dve", lambda e: e.tensor_scalar_mul(finalw[:], finalw[:], 32.0), r=["finalw"], w=["finalw2"])
        T.add("dve", lambda e: e.tensor_scalar_mul(normw_fm, normw_fm, 32.0), r=["normw_fm"], w=["normw2"])
        T.add("dve", lambda e: e.tensor_scalar_mul(normcat[:, 0:8], normcat[:, 0:8], 4.0), r=["normcat"], w=["normcat_a"])
        T.add("dve", lambda e: e.tensor_scalar_mul(normcat[:, 8:16], normcat[:, 8:16], float(np.sqrt(2048.0) / 2.0)),
              r=["normcat"], w=["normcat_b"])

        W_PRE = [(512, 1024), (1024, 2048), (2048, 2056), (5128, 6152), (6152, 6680)]
        W_FULL = [(0, 512), (2056, 3080), (3080, 4104), (4104, 5128)]
        wkeys = {}
        piece = [0]
        NSTG = 8
        wout_f = wout[:].bitcast(F32)

        def wpiece(src_ap, dst_ap, scale_ap, scale_key, ncols, stg_ap, stg_key, stream, wkey, extra_w=()):
            i = piece[0]
            piece[0] += 1
            T.add("sp", lambda e: e.dma_start(out=stg_ap[:, 0:ncols], in_=src_ap), w=[stg_key], stream=stream)
            if i % 2 == 1:
                T.add("act", lambda e: e.activation(out=dst_ap, in_=stg_ap[:, 0:ncols], func=AF.Copy, scale=scale_ap),
                      r=[stg_key, scale_key], w=[wkey] + list(extra_w))
            else:
                T.add("dve", lambda e: e.tensor_scalar_mul(dst_ap, stg_ap[:, 0:ncols], scale_ap),
                      r=[stg_key, scale_key], w=[wkey] + list(extra_w))

        for (c0, c1) in W_PRE + W_FULL:
            wkeys[(c0, c1)] = []
            for k in range(8):
                sl_ = piece[0] % NSTG
                wk = "w_%d_%d" % (c0, k)
                wkeys[(c0, c1)].append(wk)
                wpiece(dr["w_in"][k * 128:(k + 1) * 128, c0:c1], win3[:, k, c0:c1], normw_fm[:, k:k + 1], "normw2", c1 - c0,
                       wout_f[:, sl_ * 1024:(sl_ + 1) * 1024], "wstg%d" % sl_, "stg%d" % sl_, wk)

        def wkeys_for(c0, n):
            out = []
            for (a0, a1), ks in wkeys.items():
                if a0 < c0 + n and c0 < a1:
                    out += ks
            return out

        def load_wout():
            for kc in range(16):
                key = "normcat_a" if kc < 8 else "normcat_b"
                slot = kc % 2
                wpiece(dr["w_out"][kc * 128:(kc + 1) * 128, :], wout3[:, kc, :], normcat[:, kc:kc + 1], key, 1024,
                       xbuf[slot], "xbuf%d" % slot, "xin%d" % slot, "wout_%d" % kc,
                       extra_w=["wstg%d" % j for j in range(NSTG)])
        wout_keys = ["wout_%d" % kc for kc in range(16)]

        def load_x(src, ci, slot):
            T.add("sp", lambda e: e.dma_start(out=xbuf[slot][:], in_=src[ci * L:(ci + 1) * L, :]),
                  w=["xbuf%d" % slot], stream="xin%d" % slot)

        chunk_list = [("pre", i) for i in range(n_pre)] + [("full", i) for i in range(n_full)]

        def src_of(kind):
            return dr["xpre"] if kind == "pre" else dr["x"]

        def chunk(gi):
            kind, ci = chunk_list[gi]
            full = kind == "full"
            slot = gi % 2
            xb = xbuf[slot]
            xk = "xbuf%d" % slot
            if gi + 1 < len(chunk_list) and not (n_pre and gi + 1 == n_pre):
                nk, nci = chunk_list[gi + 1]
                load_x(src_of(nk), nci, (gi + 1) % 2)
            T.add("act", lambda e: e.activation(out=junk[:], in_=xb[:], func=AF.Square, accum_out=ssq_x), r=[xk], w=["xn", "ssq_x"])
            T.add("pool", lambda e: e.tensor_scalar(tx1, ssq_x, 1024.0 * EPS, None, ALU.add), r=["ssq_x"], w=["tx1"])
            T.add("pool", lambda e: e.tensor_tensor(rstd_x, tx1, m05[:, 0:1], ALU.pow), r=["tx1", "m05"], w=["rstd_x"])
            T.add("dve", lambda e: e.tensor_scalar_mul(xn[:], xb[:], rstd_x), r=[xk, "rstd_x"], w=["xn"])
            p, pk = next_pT()

            def tr_x(e, p=p):
                for k in range(8):
                    ins = e.transpose(p[:, k * 128:(k + 1) * 128], xn[:, k * 128:(k + 1) * 128], identb[:])
                return ins
            T.add("pe", tr_x, r=["xn", "identb"], w=[pk])
            T.add("act", lambda e, p=p: e.activation(out=xnT[:], in_=p[:, 0:1024], func=AF.Copy), r=[pk], w=["xnT"])

            def proj(name, c0, n, out_ap, okey, extra_r=()):
                def f(e):
                    for k in range(8):
                        ins = e.matmul(out_ap, xnT3[:, k, :], win3[:, k, c0:c0 + n], start=(k == 0), stop=(k == 7))
                    return ins
                T.add("pe", f, r=["xnT"] + wkeys_for(c0, n) + list(extra_r), w=[okey])

            for name, c0, n in PROJ_GROUPS:
                if not full and name not in STATE_ONLY:
                    continue
                if name == "if":
                    proj(name, c0, n, pm[:, 0:8], "pm_if")
                    T.add("dve", lambda e: e.tensor_tensor(g8, pm[:, 0:8], bias8, ALU.add), r=["pm_if", "bias8a", "bias8b"], w=["g8"])
                    T.add("act", lambda e: e.activation(out=t8, in_=g8, func=AF.Tanh, scale=1.0 / 15.0), r=["g8"], w=["t8"])
                    T.add("act", lambda e: e.activation(out=e4, in_=t8[:, 4:8], func=AF.Exp, scale=-15.0), r=["t8"], w=["e4"])
                    T.add("act", lambda e: e.activation(out=nlf, in_=e4, func=AF.Ln, bias=1.0), r=["e4"], w=["nlf"])
                    T.add("dve", lambda e: e.tensor_scalar_mul(li, t8[:, 0:4], 15.0), r=["t8"], w=["li"])
                elif name == "dt":
                    proj(name, c0, n, pm[:, 8:24], "pm_dt")
                    T.add("dve", lambda e: e.tensor_tensor(dtp, pm[:, 8:24], dtb, ALU.add), r=["pm_dt", "dtb"], w=["dtp"])
                    T.add("act", lambda e: e.activation(out=edt, in_=dtp, func=AF.Exp), r=["dtp"], w=["edt"])
                    T.add("act", lambda e: e.activation(out=dt16, in_=edt, func=AF.Ln, bias=1.0), r=["edt"], w=["dt16"])
                    T.add("dve", lambda e: e.tensor_tensor(a_tm, dt16, arep, ALU.mult), r=["dt16", "arep"], w=["a_tm"])
                else:
                    b, bk = next_bank()
                    proj(name, c0, n, b[:, 0:n], bk)
                    if name == "k":
                        T.add("dve", lambda e, b=b: e.tensor_copy(k_tm[:], b[:, 0:512]), r=[bk], w=["k_tm"])
                    elif name in ("v0", "v1"):
                        h0 = 0 if name == "v0" else 2
                        T.add("act", lambda e, b=b, h0=h0: e.activation(
                            out=vaug3[:, h0:h0 + 2, 0:256], in_=b[:, 0:512].rearrange("p (h c) -> p h c", h=2), func=AF.Copy),
                            r=[bk, "vaug"], w=["vaug_%d" % h0])
                    elif name.startswith("xbc"):
                        j = int(name[3])
                        eng = "dve" if j == 1 else "act"
                        if eng == "act":
                            T.add("act", lambda e, b=b, j=j: e.activation(out=xbc_tm[:, j * 512:(j + 1) * 512], in_=b[:, 0:512], func=AF.Copy),
                                  r=[bk], w=["xbc_tm%d" % j])
                        else:
                            T.add("dve", lambda e, b=b, j=j: e.tensor_copy(xbc_tm[:, j * 512:(j + 1) * 512], b[:, 0:512]),
                                  r=[bk], w=["xbc_tm%d" % j])
                    elif name == "q":
                        T.add("act", lambda e, b=b: e.activation(out=q_tm[:], in_=b[:, 0:512], func=AF.Copy, scale=float(128 ** -0.5)),
                              r=[bk], w=["q_tm"])
                    elif name in ("zm0", "zm1"):
                        hh = int(name[2])
                        sl = slice(hh * 512, (hh + 1) * 512)
                        T.add("act", lambda e, b=b, sl=sl: e.activation(out=tz[:, sl], in_=b[:, 0:512], func=AF.Tanh, scale=0.5),
                              r=[bk], w=["tz%d" % hh])
                        T.add("dve", lambda e, b=b, sl=sl: e.scalar_tensor_tensor(g1[:, sl], tz[:, sl], 1.0, b[:, 0:512], ALU.add, ALU.mult),
                              r=[bk, "tz%d" % hh], w=["g1_%d" % hh])
                    elif name in ("o0", "o1"):
                        hh = int(name[1])
                        sl = slice(hh * 512, (hh + 1) * 512)
                        T.add("act", lambda e, b=b, sl=sl: e.activation(out=tz[:, sl], in_=b[:, 0:512], func=AF.Tanh, scale=0.5),
                              r=[bk], w=["tz%d" % hh])
                        T.add("dve", lambda e, sl=sl: e.scalar_tensor_tensor(g1[:, sl], tz[:, sl], 1.0, g1[:, sl], ALU.add, ALU.mult),
                              r=["tz%d" % hh, "g1_%d" % hh], w=["g1_%d" % hh])
                    elif name in ("zs0", "zs1"):
                        hh = int(name[2])
                        sl = slice(hh * 512, (hh + 1) * 512)
                        T.add("act", lambda e, b=b, sl=sl: e.activation(out=tz[:, sl], in_=b[:, 0:512], func=AF.Tanh, scale=0.5),
                              r=[bk], w=["tz%d" % hh])
                        T.add("dve", lambda e, b=b, sl=sl: e.scalar_tensor_tensor(gs[:, sl], tz[:, sl], 1.0, b[:, 0:512], ALU.add, ALU.mult),
                              r=[bk, "tz%d" % hh], w=["gs_%d" % hh])

            def tr4(src, dst, skey, dkey):
                p, pk = next_pT()

                def f(e, p=p):
                    for h in range(4):
                        ins = e.transpose(p[:, h * 128:(h + 1) * 128], src[:, h * 128:(h + 1) * 128], identb[:])
                    return ins
                T.add("pe", f, r=[skey, "identb"], w=[pk])
                T.add("dve", lambda e, p=p: e.tensor_copy(dst[:], p[:, 0:512]), r=[pk], w=[dkey])

            if full:
                tr4(k_tm, kT, "k_tm", "kT")
                tr4(q_tm, qT, "q_tm", "qT")
            for half, (t0, nt) in enumerate([(0, 8), (8, 4)]):
                p, pk = next_pT()

                def f(e, p=p, t0=t0, nt=nt):
                    for t in range(nt):
                        ins = e.transpose(p[:, t * 128:(t + 1) * 128], xbc_tm[:, (t0 + t) * 128:(t0 + t + 1) * 128], identb[:])
                    return ins
                T.add("pe", f, r=["xbc_tm0", "xbc_tm1", "xbc_tm2", "identb"], w=[pk])
                T.add("act", lambda e, p=p, t0=t0, nt=nt: e.activation(
                    out=xbcT3[:, t0:t0 + nt, 4:132], in_=p[:, 0:nt * 128].rearrange("p (t c) -> p t c", t=nt), func=AF.Copy),
                    r=[pk, "xbcT_carry"], w=["xbcT_%d" % half])

            T.add("pe", lambda e: e.matmul(pm[:, 24:28], tri[:], nlf, start=True, stop=True), r=["tri", "nlf"], w=["pm_nb"])

            def f_ugm(e):
                e.matmul(pm[0:4, 128:256], li, identf[:], start=True, stop=False)
                return e.matmul(pm[0:4, 128:256], nlf, tri[:], start=False, stop=True)
            T.add("pe", f_ugm, r=["li", "nlf", "identf", "tri"], w=["pm_ugm"])
            T.add("pe", lambda e: e.matmul(pm[0:4, 120:121], nlf, onesf[:, 0:1], start=True, stop=True), r=["nlf", "onesf"], w=["pm_nbl"])
            T.add("dve", lambda e: e.reduce_max(umax, pm[0:4, 128:256], AX.X), r=["pm_ugm"], w=["umax"])
            T.add("dve", lambda e: e.tensor_tensor(Rg, umax, mst, ALU.max), r=["umax", "mst"], w=["Rg"])
            T.add("dve", lambda e: e.tensor_tensor(dg, mst, Rg, ALU.subtract), r=["mst", "Rg"], w=["dg"])
            T.add("act", lambda e: e.activation(out=soldg, in_=dg, func=AF.Exp), r=["dg"], w=["soldg"])
            T.add("dve", lambda e: e.tensor_tensor(mst, Rg, pm[0:4, 120:121], ALU.subtract), r=["Rg", "pm_nbl", "dg"], w=["mst"])
            T.add("dve", lambda e: e.tensor_scalar_mul(Dg[:, 0:4], identf[0:4, 0:4], Rg), r=["Rg", "identf"], w=["Dg_a"])
            T.add("dve", lambda e: e.tensor_scalar_mul(Dg[:, 4:8], identf[0:4, 0:4], soldg), r=["soldg", "identf"], w=["Dg_b"])
            T.add("pe", lambda e: e.matmul(pm[:, 28:36], onesf[0:4, :], Dg, start=True, stop=True), r=["onesf", "Dg_a", "Dg_b"], w=["pm_rs"])
            T.add("dve", lambda e: e.tensor_tensor(T1[:, 0:4], li, pm[:, 24:28], ALU.add), r=["li", "pm_nb"], w=["T1a"])
            T.add("dve", lambda e: e.tensor_copy(T1[:, 4:8], pm[:, 24:28]), r=["pm_nb"], w=["T1b"])
            T.add("dve", lambda e: e.tensor_tensor(
                T2.rearrange("p (a h) -> p a h", a=2), T1.rearrange("p (a h) -> p a h", a=2),
                pm[:, 28:32].unsqueeze(1).broadcast_to([128, 2, 4]), ALU.subtract), r=["T1a", "T1b", "pm_rs"], w=["T2"])
            T.add("act", lambda e: e.activation(out=wf, in_=T2, func=AF.Exp), r=["T2"], w=["wf"])
            T.add("dve", lambda e: e.tensor_copy(soldb, pm[:, 32:36]), r=["pm_rs"], w=["soldb"])

            T.add("pe", lambda e: e.matmul(pm[:, 40:56], tri[:], a_tm, start=True, stop=True), r=["tri", "a_tm"], w=["pm_acum"])
            T.add("pe", lambda e: e.matmul(pm[:, 56:72], onesf[:], a_tm, start=True, stop=True), r=["onesf", "a_tm"], w=["pm_alast"])
            T.add("dve", lambda e: e.tensor_copy(acum_sb, pm[:, 40:56]), r=["pm_acum"], w=["acum_sb"])
            T.add("act", lambda e: e.activation(out=ea, in_=pm[:, 40:56], func=AF.Exp), r=["pm_acum"], w=["ea"])
            T.add("dve", lambda e: e.tensor_tensor(dEa, pm[:, 56:72], acum_sb, ALU.subtract), r=["pm_alast", "acum_sb"], w=["dEa"])
            T.add("act", lambda e: e.activation(out=dE, in_=dEa, func=AF.Exp), r=["dEa"], w=["dE"])
            T.add("act", lambda e: e.activation(out=cdb, in_=pm[:, 56:72], func=AF.Exp), r=["pm_alast"], w=["cdb"])
            if full:
                T.add("dve", lambda e: e.tensor_copy(a3[:, 0:16], acum_sb), r=["acum_sb"], w=["a3_0"])
                T.add("dve", lambda e: e.tensor_tensor(r1, acum_sb, a3[:, 0:16], ALU.subtract), r=["acum_sb", "a3_0"], w=["r1"])
                T.add("dve", lambda e: e.tensor_copy(a3[:, 16:32], r1), r=["r1"], w=["a3_1"])
                T.add("dve", lambda e: e.tensor_tensor(r2, r1, a3[:, 16:32], ALU.subtract), r=["r1", "a3_1"], w=["r2"])
                T.add("dve", lambda e: e.tensor_copy(a3[:, 32:48], r2), r=["r2"], w=["a3_2"])
                p, pk = next_pT()
                T.add("pe", lambda e, p=p: e.transpose(p[0:48, 0:128], a3[:, 0:48], identb[:]), r=["a3_0", "a3_1", "a3_2", "identb"], w=[pk])
                T.add("dve", lambda e, p=p: e.tensor_copy(A48[:], p[0:48, 0:128]), r=[pk], w=["A48"])

            for rnd in range(3):
                ca = cacc[rnd % 2]
                ct = cth[rnd % 2]
                cak = "cacc%d" % (rnd % 2)
                ctk = "cth%d" % (rnd % 2)
                src_key = "xbcT_0" if rnd < 2 else "xbcT_1"
                for tt in range(4):
                    t = rnd * 4 + tt
                    eng = "dve"
                    cslice = ca[:, tt * 128:(tt + 1) * 128]

                    ckey = cak + "_%d" % tt
                    T.add(eng, lambda e, t=t, cslice=cslice: e.tensor_scalar(
                        cslice, xbcT3[:, t, 1:129], cw[:, t, 0:1], cb[:, t:t + 1], ALU.mult, ALU.add),
                        r=[src_key, "xbcT_carry", "cwb"], w=[ckey])
                    for w_ in range(1, 4):
                        T.add(eng, lambda e, t=t, cslice=cslice, w_=w_: e.scalar_tensor_tensor(
                            cslice, xbcT3[:, t, 1 + w_:129 + w_], cw[:, t, w_:w_ + 1], cslice, ALU.mult, ALU.add),
                            r=[src_key, "xbcT_carry", "cwb", ckey], w=[ckey])
                T.add("act", lambda e, ca=ca, ct=ct: e.activation(out=ct[:], in_=ca[:], func=AF.Tanh),
                      r=[cak + "_%d" % i for i in range(4)], w=[ctk])
                T.add("dve", lambda e, ca=ca, ct=ct, rnd=rnd: e.scalar_tensor_tensor(
                    xcT[:, rnd * 512:(rnd + 1) * 512], ct[:], 1.0, ca[:], ALU.add, ALU.mult),
                    r=[ctk] + [cak + "_%d" % i for i in range(4)], w=["xcT_%d" % rnd])
            T.add("pool", lambda e: e.tensor_copy(xbcT3[:, :, 1:4], xbcT3[:, :, 129:132]), r=["xbcT_0", "xbcT_1"], w=["xbcT_carry"])

            T.add("pool", lambda e: e.tensor_tensor(
                kp_tm[:].rearrange("p (h d) -> p h d", h=4), k_tm[:].rearrange("p (h d) -> p h d", h=4),
                wf[:, 0:4].unsqueeze(2).broadcast_to([128, 4, 128]), ALU.mult), r=["k_tm", "wf"], w=["kp_tm"])
            T.add("dve", lambda e: e.tensor_tensor(Cst3[:, :, 0:257], Cst3[:, :, 0:257],
                                                   soldb.unsqueeze(2).broadcast_to([128, 4, 257]), ALU.mult),
                  r=["Cst", "soldb"], w=["Cst"])
            if full:
                T.add("act", lambda e: e.activation(out=Cbf3[:, :, 0:257], in_=Cst3[:, :, 0:257], func=AF.Copy), r=["Cst"], w=["Cbf"])
                sb_, sbk = next_bank()

                def f_st(e, sb_=sb_):
                    for h in range(4):
                        ins = e.matmul(sb_[:, h * 128:(h + 1) * 128], kT[:, h * 128:(h + 1) * 128], qT[:, h * 128:(h + 1) * 128],
                                       start=True, stop=True)
                    return ins
                T.add("pe", f_st, r=["kT", "qT"], w=[sbk])
                for h in range(4):
                    T.add("dve", lambda e, h=h, sb_=sb_: e.scalar_tensor_tensor(
                        Pm3[:, h, :], sb_[:, h * 128:(h + 1) * 128], wf[:, h:h + 1], maskb[:], ALU.mult, ALU.mult),
                        r=[sbk, "wf", "maskb"], w=["Pm_%d" % h])
                brs = []
                for h in range(4):
                    bb, bbk = next_bank()
                    brs.append((bb, bbk))

                    def f_br(e, h=h, bb=bb):
                        e.matmul(bb[:, 0:257], Pm3[:, h, :], vaug3[:, h, 0:257], start=True, stop=False)
                        return e.matmul(bb[:, 0:257], qT[:, h * 128:(h + 1) * 128], Cbf3[:, h, 0:257], start=False, stop=True)
                    T.add("pe", f_br, r=["Pm_%d" % h, "vaug_0", "vaug_2", "qT", "Cbf"], w=[bbk])
                    T.add("act", lambda e, h=h, bb=bb: e.activation(out=absden[:, h:h + 1], in_=bb[:, 256:257], func=AF.Abs),
                          r=[bbk], w=["absden_%d" % h])
                    T.add("act", lambda e, h=h, bb=bb: e.activation(out=junk[:, 0:256], in_=bb[:, 0:256], func=AF.Square,
                                                                     accum_out=ssqr[:, h:h + 1]), r=[bbk], w=["xn", "ssqr_%d" % h])
                hk = ["absden_%d" % h for h in range(4)]
                sk = ["ssqr_%d" % h for h in range(4)]
                T.add("dve", lambda e: e.tensor_tensor(dn4, absden, wf[:, 4:8], ALU.max), r=hk + ["wf"], w=["dn4"])
                T.add("dve", lambda e: e.reciprocal(rd4, dn4), r=["dn4"], w=["rd4"])
                T.add("dve", lambda e: e.tensor_tensor(t4, rd4, rd4, ALU.mult), r=["rd4"], w=["t4"])
                T.add("dve", lambda e: e.tensor_tensor(t4, t4, ssqr, ALU.mult), r=["t4"] + sk, w=["t4"])
                T.add("pool", lambda e: e.tensor_scalar(t4, t4, 256.0 * EPS, None, ALU.add), r=["t4"], w=["t4"])
                T.add("pool", lambda e: e.tensor_tensor(rs4, t4, m05[:, 0:4], ALU.pow), r=["t4", "m05"], w=["rs4"])
                T.add("dve", lambda e: e.tensor_tensor(sc4, rd4, rs4, ALU.mult), r=["rd4", "rs4"], w=["sc4"])
                for h in range(4):
                    bb, bbk = brs[h]
                    T.add("dve", lambda e, h=h, bb=bb: e.scalar_tensor_tensor(
                        mix[:, h * 256:(h + 1) * 256], bb[:, 0:256], sc4[:, h:h + 1], g1[:, h * 256:(h + 1) * 256], ALU.mult, ALU.mult),
                        r=[bbk, "sc4", "g1_%d" % (h // 2)], w=["mix_m%d" % h])
            for h in range(4):
                cb_, cbk = next_bank()
                T.add("pe", lambda e, h=h, cb_=cb_: e.matmul(cb_[:, 0:257], kp_tm[:, h * 128:(h + 1) * 128], vaug3[:, h, 0:257],
                                                             start=True, stop=True), r=["kp_tm", "vaug_0", "vaug_2"], w=[cbk])
                T.add("dve", lambda e, h=h, cb_=cb_: e.tensor_tensor(Cst3[:, h, 0:257], Cst3[:, h, 0:257], cb_[:, 0:257], ALU.add),
                      r=[cbk, "Cst", "Cbf"], w=["Cst"])

            p, pk = next_pT()

            def tr_xc(e, p=p):
                for t in range(8):
                    ins = e.transpose(p[:, t * 128:(t + 1) * 128], xcT3[:, t, :], identb[:])
                return ins
            T.add("pe", tr_xc, r=["xcT_0", "xcT_1", "identb"], w=[pk])
            T.add("act", lambda e, p=p: e.activation(out=x_tm[:], in_=p[:, 0:1024], func=AF.Copy), r=[pk], w=["x_tm"])
            p, pk = next_pT()

            def tr_B(e, p=p):
                for t in range(2):
                    ins = e.transpose(p[:, t * 128:(t + 1) * 128], xcT3[:, 8 + t, :], identb[:])
                return ins
            T.add("pe", tr_B, r=["xcT_2", "identb"], w=[pk])
            T.add("dve", lambda e, p=p: e.tensor_copy(B_tm[:], p[:, 0:256]), r=[pk], w=["B_tm"])
            T.add("pool", lambda e: e.tensor_tensor(
                xdt[:].rearrange("p (r c) -> p r c", r=16), x_tm[:].rearrange("p (r c) -> p r c", r=16),
                dt16.unsqueeze(2).broadcast_to([128, 16, 64]), ALU.mult), r=["x_tm", "dt16"], w=["xdt"])
            T.add("pool", lambda e: e.tensor_tensor(
                xde[:].rearrange("p (r c) -> p r c", r=16), xdt[:].rearrange("p (r c) -> p r c", r=16),
                dE.unsqueeze(2).broadcast_to([128, 16, 64]), ALU.mult), r=["xdt", "dE"], w=["xde"])

            if full:
                T.add("pe", lambda e: (e.matmul(pm[:, 256:384], xcT3[:, 8, :], xcT3[:, 10, :], start=True, stop=True),
                                       e.matmul(pm[:, 384:512], xcT3[:, 9, :], xcT3[:, 11, :], start=True, stop=True))[1],
                      r=["xcT_2"], w=["pm_cb"])
                T.add("dve", lambda e: e.tensor_tensor(CBm3, pm[:, 256:512].rearrange("p (g l) -> p g l", g=2),
                                                       maskb[:].unsqueeze(1).broadcast_to([128, 2, 128]), ALU.mult),
                      r=["pm_cb", "maskb"], w=["CBm"])
                for bq in range(4):
                    ab, abk = next_bank()

                    def f_arg(e, bq=bq, ab=ab):
                        for rr in range(4):
                            ins = e.matmul(ab[:, rr * 128:(rr + 1) * 128], oh48_3[:, bq * 4 + rr, :], A48[:], start=True, stop=True)
                        return ins
                    T.add("pe", f_arg, r=["oh48", "A48"], w=[abk])

                    def f_relu(e, bq=bq, ab=ab):
                        for rr in range(4):
                            hd = bq * 4 + rr
                            ins = e.activation(out=rl[:, rr * 128:(rr + 1) * 128], in_=ab[:, rr * 128:(rr + 1) * 128],
                                               func=AF.Relu, bias=acum_sb[:, hd:hd + 1], scale=-1.0)
                        return ins
                    T.add("act", f_relu, r=[abk, "acum_sb"], w=["rl"])
                    T.add("act", lambda e, bq=bq: e.activation(out=dec[:, bq * 512:(bq + 1) * 512], in_=rl[:], func=AF.Exp, scale=-1.0),
                          r=["rl"], w=["dec_%d" % bq])
                    g = bq // 2
                    T.add("pool", lambda e, bq=bq, g=g: e.tensor_tensor(
                        dec3[:, bq * 4:bq * 4 + 4, :], dec3[:, bq * 4:bq * 4 + 4, :],
                        CBm3[:, g:g + 1, :].broadcast_to([128, 4, 128]), ALU.mult), r=["dec_%d" % bq, "CBm"], w=["dec_%d" % bq])
                T.add("pool", lambda e: e.tensor_tensor(
                    yo[:].rearrange("p (r c) -> p r c", r=16), x_tm[:].rearrange("p (r c) -> p r c", r=16),
                    drep.unsqueeze(2).broadcast_to([128, 16, 64]), ALU.mult), r=["x_tm", "drep"], w=["yo_0", "yo_1"])
                for g in range(2):
                    yd, ydk = next_bank()

                    def f_yd(e, g=g, yd=yd):
                        for rr in range(8):
                            hd = g * 8 + rr
                            ins = e.matmul(yd[:, rr * 64:(rr + 1) * 64], dec3[:, hd, :], xdt[:, hd * 64:(hd + 1) * 64], start=True, stop=True)
                        return ins
                    T.add("pe", f_yd, r=["dec_%d" % (2 * g), "dec_%d" % (2 * g + 1), "xdt"], w=[ydk])
                    yf, yfk = next_bank()
                    T.add("pe", lambda e, g=g, yf=yf: e.matmul(yf[:, 0:512], xcT3[:, 10 + g, :], hbf[:, g * 512:(g + 1) * 512], start=True, stop=True),
                          r=["xcT_2", "hbf%d" % g], w=[yfk])
                    gsl = slice(g * 512, (g + 1) * 512)
                    T.add("dve", lambda e, g=g, yf=yf: e.tensor_tensor(
                        ytmp[:].rearrange("p (r c) -> p r c", r=8), yf[:, 0:512].rearrange("p (r c) -> p r c", r=8),
                        ea[:, g * 8:(g + 1) * 8].unsqueeze(2).broadcast_to([128, 8, 64]), ALU.mult), r=[yfk, "ea"], w=["ytmp"])
                    T.add("dve", lambda e, yd=yd: e.tensor_tensor(ytmp[:], ytmp[:], yd[:, 0:512], ALU.add), r=[ydk, "ytmp"], w=["ytmp"])
                    T.add("pool", lambda e, gsl=gsl: e.tensor_tensor(yo[:, gsl], yo[:, gsl], ytmp[:], ALU.add), r=["ytmp", "yo_%d" % g], w=["yo_%d" % g])
                    T.add("pool", lambda e, gsl=gsl: e.tensor_tensor(yo[:, gsl], yo[:, gsl], gs[:, gsl], ALU.mult),
                          r=["yo_%d" % g, "gs_%d" % g], w=["yo_%d" % g])
                    T.add("act", lambda e, g=g, gsl=gsl: e.activation(out=junk[:, 0:512], in_=yo[:, gsl], func=AF.Square, accum_out=ssq_s[:, g:g + 1]),
                          r=["yo_%d" % g], w=["xn", "ssq_s%d" % g])
                T.add("pool", lambda e: e.tensor_scalar(ts2, ssq_s, 2048.0 * EPS, None, ALU.add), r=["ssq_s0", "ssq_s1"], w=["ts2"])
                T.add("pool", lambda e: e.tensor_tensor(rstd_s, ts2, m05[:, 0:2], ALU.pow), r=["ts2", "m05"], w=["rstd_s"])
                for g in range(2):
                    gsl = slice(g * 512, (g + 1) * 512)
                    T.add("dve", lambda e, g=g, gsl=gsl: e.tensor_scalar_mul(mix[:, 1024 + g * 512:1024 + (g + 1) * 512], yo[:, gsl], rstd_s[:, g:g + 1]),
                          r=["yo_%d" % g, "rstd_s"], w=["mix_s%d" % g])
            for g in range(2):
                st, stk = next_bank()
                gsl = slice(g * 512, (g + 1) * 512)
                T.add("pe", lambda e, g=g, st=st, gsl=gsl: e.matmul(st[:, 0:512], B_tm[:, g * 128:(g + 1) * 128], xde[:, gsl], start=True, stop=True),
                      r=["B_tm", "xde"], w=[stk])
                T.add("dve", lambda e, g=g, gsl=gsl: e.tensor_tensor(
                    hst[:, gsl].rearrange("p (r c) -> p r c", r=8), hst[:, gsl].rearrange("p (r c) -> p r c", r=8),
                    cdb[:, g * 8:(g + 1) * 8].unsqueeze(2).broadcast_to([128, 8, 64]), ALU.mult), r=["hst", "cdb", "hbf%d" % g], w=["hst%d" % g])
                T.add("dve", lambda e, st=st, gsl=gsl: e.tensor_tensor(hst[:, gsl], hst[:, gsl], st[:, 0:512], ALU.add), r=[stk, "hst%d" % g], w=["hst%d" % g])
                T.add("act", lambda e, gsl=gsl: e.activation(out=hbf[:, gsl], in_=hst[:, gsl], func=AF.Copy), r=["hst%d" % g], w=["hbf%d" % g])

            if not full:
                return
            mkeys = ["mix_m%d" % h for h in range(4)] + ["mix_s0", "mix_s1"]
            for half in range(2):
                p, pk = next_pT()

                def tr_m(e, p=p, half=half):
                    for t in range(8):
                        kc = half * 8 + t
                        ins = e.transpose(p[:, t * 128:(t + 1) * 128], mix[:, kc * 128:(kc + 1) * 128], identb[:])
                    return ins
                T.add("pe", tr_m, r=mkeys + ["identb"], w=[pk])
                if half == 0:
                    T.add("act", lambda e, p=p: e.activation(out=mixT[:, 0:1024], in_=p[:, 0:1024], func=AF.Copy), r=[pk], w=["mixT_0"])
                else:
                    T.add("dve", lambda e, p=p: e.tensor_copy(mixT[:, 1024:2048], p[:, 0:1024]), r=[pk], w=["mixT_1"])
            for half in range(2):
                ob, obk = next_bank()

                def f_o(e, ob=ob, half=half):
                    for kc in range(16):
                        ins = e.matmul(ob[:, 0:512], mixT3[:, kc, :], wout3[:, kc, half * 512:(half + 1) * 512], start=(kc == 0), stop=(kc == 15))
                    return ins
                T.add("pe", f_o, r=["mixT_0", "mixT_1"] + wout_keys, w=[obk])
                hsl = slice(half * 512, (half + 1) * 512)
                T.add("dve", lambda e, ob=ob, hsl=hsl: e.tensor_tensor(xb[:, hsl], xb[:, hsl], ob[:, 0:512], ALU.add), r=[obk, xk], w=[xk])
            T.add("act", lambda e: e.activation(out=junk[:], in_=xb[:], func=AF.Square, accum_out=ssq_o), r=[xk], w=["xn", "ssq_o"])
            T.add("pool", lambda e: e.tensor_scalar(to1, ssq_o, 1024.0 * EPS, None, ALU.add), r=["ssq_o"], w=["to1"])
            T.add("pool", lambda e: e.tensor_tensor(rstd_o, to1, m05[:, 0:1], ALU.pow), r=["to1", "m05"], w=["rstd_o"])
            T.add("dve", lambda e: e.scalar_tensor_tensor(xb[:], xb[:], rstd_o, finalw[:], ALU.mult, ALU.mult), r=[xk, "rstd_o", "finalw2"], w=[xk])
            T.add("sp", lambda e: e.dma_start(out=out_d[ci * L:(ci + 1) * L, :], in_=xb[:]), r=[xk], w=["out_dram"], stream="out%d" % slot)

        k0, c0_ = chunk_list[0]
        if n_pre == 0:
            load_wout()
        load_x(src_of(k0), c0_, 0)
        for gi in range(len(chunk_list)):
            if n_pre and gi == n_pre:
                load_wout()
                load_x(src_of("full"), 0, gi % 2)
            chunk(gi)
            if n_pre and gi == n_pre - 1:
                T.add("dve", lambda e: e.tensor_scalar_mul(Cst[:], Cst[:], flag), r=["Cst", "flag"], w=["Cst"])
                T.add("dve", lambda e: e.tensor_scalar_mul(hst[:], hst[:], flag), r=["hst0", "hst1", "flag"], w=["hst0", "hst1", "hst"])
                T.add("dve", lambda e: e.tensor_scalar_mul(hbf[:], hbf[:], flag), r=["hbf0", "hbf1", "flag"], w=["hbf0", "hbf1"])
                T.add("dve", lambda e: e.tensor_scalar_mul(mst, mst, flag[0:4, :]), r=["mst", "flag"], w=["mst"])
                T.add("dve", lambda e: e.tensor_scalar_mul(xbcT3[:, :, 1:4], xbcT3[:, :, 1:4], flag), r=["xbcT_carry", "flag"], w=["xbcT_carry"])

        for nm in debug:
            tile_ap, shape, rkeys = {
                "xnT": (xnT[:], [128, 1024], ["xnT"]),
                "k_tm": (k_tm[:], [128, 512], ["k_tm"]),
                "mix": (mix[:], [128, 2048], ["mix_m0", "mix_m1", "mix_m2", "mix_m3", "mix_s0", "mix_s1"]),
                "Cst": (Cst[:], [128, 4 * 258], ["Cst"]),
                "hst": (hst[:], [128, 1024], ["hst0", "hst1"]),
                "x_tm": (x_tm[:], [128, 1024], ["x_tm"]),
                "sm": (sm[:], [128, 256], ["wf", "dE", "ea", "cdb", "dt16", "acum_sb"]),
                "yo": (yo[:], [128, 1024], ["yo_0", "yo_1"]),
                "dec": (dec[:], [128, 2048], ["dec_0", "dec_1", "dec_2", "dec_3"]),
                "xcT": (xcT[:], [128, 1536], ["xcT_0", "xcT_1", "xcT_2"]),
                "g1": (g1[:], [128, 1024], ["g1_0", "g1_1"]),
                "vaug": (vaug[:], [128, 4 * 258], ["vaug_0", "vaug_2"]),
            }[nm]
            d = nc.dram_tensor("dbg_" + nm, shape, F32, kind="ExternalOutput").ap()
            dbg_out[nm] = d
            q_eng = "sp" if tile_ap.dtype == F32 else "pool"
            T.add(q_eng, lambda e, d=d, tile_ap=tile_ap: e.dma_start(out=d, in_=tile_ap), r=rkeys, w=["dbg_" + nm], stream="dbg")

        T.emit(nc, es)
    return nc


_CACHE = {}


def kernel(x, norm_w, w_in, b_igate, b_fgate, conv_w, conv_b, dt_bias, a_log, d_skip,
           mlstm_norm_w, ssd_norm_w, w_out, final_norm_w):
    f = lambda a: np.ascontiguousarray(np.asarray(a, dtype=np.float32))
    x = f(x)
    n_half = SEQ // 2 // L
    key = "main"
    if key not in _CACHE:
        _CACHE[key] = build(n_half, n_half)
    nc = _CACHE[key]
    common = {
        "norm_w": f(norm_w)[0], "w_in": f(w_in)[0], "b_igate": f(b_igate)[0], "b_fgate": f(b_fgate)[0],
        "conv_w": f(conv_w)[0], "conv_b": f(conv_b)[0], "dt_bias": f(dt_bias)[0], "a_log": f(a_log)[0],
        "d_skip": f(d_skip)[0],
        "normcat": np.ascontiguousarray(np.concatenate([f(mlstm_norm_w)[0], f(ssd_norm_w)[0]])),
        "w_out": f(w_out)[0], "final_norm_w": f(final_norm_w),
    }
    half = SEQ // 2
    zeros = np.zeros((half, D_MODEL), np.float32)
    in_maps = []
    for core in range(NCORES):
        b, hf = core // 2, core % 2
        m = dict(common)
        m["x"] = np.ascontiguousarray(x[b, hf * half:(hf + 1) * half])
        m["xpre"] = zeros if hf == 0 else np.ascontiguousarray(x[b, 0:half])
        m["flag"] = np.array([float(hf)], np.float32)
        in_maps.append(m)
    res = run_bass_kernel_spmd(nc, in_maps, core_ids=list(range(NCORES)))
    out = np.empty((BATCH, SEQ, D_MODEL), np.float32)
    for core in range(NCORES):
        b, hf = core // 2, core % 2
        out[b, hf * half:(hf + 1) * half] = res.results[core]["out"]
    return out
```

```python
import numpy as np
from contextlib import ExitStack
import concourse.bass as bass
import concourse.mybir as mybir
from concourse.bass_utils import run_bass_kernel_spmd

F32 = mybir.dt.float32
BF16 = mybir.dt.bfloat16
AF = mybir.ActivationFunctionType
ALU = mybir.AluOpType
AX = mybir.AxisListType

D_MODEL = 1024
SEQ = 8192
BATCH = 4
NCOL = 6680
EPS = 1e-6
L = 128
NCORES = 8


class Tracker:
    def __init__(self):
        self.ops = []
        self.bufs = {}
        self.waitall_streams = set()
        self.regions = {}

    def reg(self, key, arena, off, nbytes, gran=64):
        import os
        if os.environ.get("K_NOALIAS"):
            return
        self.regions[key] = [(arena, g) for g in range(off // gran, (off + nbytes + gran - 1) // gran)]

    def _expand(self, keys):
        out = []
        for k in keys:
            out.extend(self.regions.get(k, [k]))
        return out

    PSUM_BANKS = ("pT0", "pT1", "pm", "pb0", "pb1", "pb2", "pb3", "pb4")

    @classmethod
    def _bank(cls, k):
        if isinstance(k, str):
            if k.startswith("pm_"):
                return "pm"
            if k in cls.PSUM_BANKS:
                return k
        return None

    def add(self, eng, fn, r=(), w=(), stream=None):
        self._lbl = "%s:%s" % (eng, (list(w) + ["?"])[0])
        banks = [self._bank(k) for k in list(r) + list(w)]
        banks = [b for b in banks if b is not None]
        r = [k for k in r if self._bank(k) is None]
        w = [k for k in w if self._bank(k) is None] + sorted(set(banks))
        r = self._expand(r)
        w = self._expand(w)
        deps = set()
        for k in r:
            b = self.bufs.setdefault(k, [None, []])
            if b[0] is not None:
                deps.add(b[0])
        for k in w:
            b = self.bufs.setdefault(k, [None, []])
            if b[0] is not None:
                deps.add(b[0])
            deps.update(b[1])
        idx = len(self.ops)
        deps.discard(idx)
        self.ops.append(dict(eng=eng, fn=fn, deps=deps, stream=stream, idx=idx, has_dep=False, label=self._lbl))
        for k in r:
            self.bufs[k][1].append(idx)
        for k in w:
            self.bufs[k] = [idx, []]
        return idx

    class _Ins:
        def then_inc(self, *a, **k):
            return self

    class _Probe:
        def __init__(self):
            self.calls = []

        def __getattr__(self, name):
            def f(*args, **kw):
                self.calls.append((name, args, kw))
                return Tracker._Ins()
            return f

    @staticmethod
    def _free(ap):
        n = 1
        for d in ap.shape[1:]:
            n *= int(d)
        return n

    def _cost(self, op):
        pr = Tracker._Probe()
        op["fn"](pr)
        eng = op["eng"]
        dur = 0.0
        lat = 0.0
        for name, args, kw in pr.calls:
            out = kw.get("out", args[0] if args else None)
            if name == "dma_start":
                src = kw.get("in_", args[1] if len(args) > 1 else None)
                nbytes = self._free(out) * int(out.shape[0]) * 4
                dur += 80.0
                lat = 2500.0 + nbytes / 160.0
                continue
            n = self._free(out) if out is not None and hasattr(out, "shape") else 64
            if eng == "pe":
                lhs = args[1] if len(args) > 1 else None
                mult = 4.0 if (lhs is not None and lhs.dtype == F32 and name == "matmul") else 1.0
                dur += 16.0 + max(n, 64) * mult / 1.95
            elif eng == "act":
                dur += 230.0 + n / 1.15
            elif eng == "dve":
                dur += 170.0 + n / 0.96
            elif eng == "pool":
                dur += 300.0 + n * 2.0
            else:
                dur += 50.0
        op["dur"] = max(dur, 20.0)
        op["lat"] = lat

    def schedule(self):
        import heapq
        ops = self.ops
        n = len(ops)
        for op in ops:
            self._cost(op)
        succ = [[] for _ in range(n)]
        for op in ops:
            for d in op["deps"]:
                succ[d].append(op["idx"])
        bl = [0.0] * n
        for i in range(n - 1, -1, -1):
            m = 0.0
            for sidx in succ[i]:
                if bl[sidx] > m:
                    m = bl[sidx]
            bl[i] = m + ops[i]["dur"] + ops[i]["lat"]
        import os
        if os.environ.get("K_CRIT"):
            i = max(range(n), key=lambda j: bl[j])
            print("[crit] DAG critical path %.1f us" % (bl[i] / 1e3))
            agg = {}
            while True:
                agg[ops[i]["label"]] = agg.get(ops[i]["label"], 0.0) + ops[i]["dur"] + ops[i]["lat"]
                nxt = None
                for sidx in succ[i]:
                    if nxt is None or bl[sidx] > bl[nxt]:
                        nxt = sidx
                if nxt is None:
                    break
                i = nxt
            for k, v in sorted(agg.items(), key=lambda kv: -kv[1])[:40]:
                print("     %-28s %.1f us" % (k, v / 1e3))
        ndeps = [len(op["deps"]) for op in ops]
        ready_at = [0.0] * n
        engs = ["pe", "act", "dve", "pool", "sp"]
        avail = {e: [] for e in engs}
        for i in range(n):
            if ndeps[i] == 0:
                avail[ops[i]["eng"]].append(i)
        free_at = {e: 0.0 for e in engs}
        fa_prev = {e: 0.0 for e in engs}
        order = []
        finish = [0.0] * n
        done = 0
        WINDOW = 4000
        lowest_unscheduled = 0
        scheduled = [False] * n
        while done < n:
            best = None
            while lowest_unscheduled < n and scheduled[lowest_unscheduled]:
                lowest_unscheduled += 1
            for e in engs:
                lst = avail[e]
                if not lst:
                    continue
                fa = free_at[e]
                cand = None
                for i in lst:
                    if i > lowest_unscheduled + WINDOW:
                        continue
                    st = ready_at[i] if ready_at[i] > fa else fa
                    key = (st, -bl[i], i)
                    if cand is None or key < cand[0]:
                        cand = (key, i, st)
                if cand is None:
                    continue
                if best is None or cand[0] < best[0]:
                    best = cand + (e,)
            assert best is not None, "scheduler stuck"
            _, i, st, e = best
            avail[e].remove(i)
            op = ops[i]
            fin = st + op["dur"]
            free_at[e] = fin
            finish[i] = fin + op["lat"]
            op["t_start"] = st
            op["stall"] = st - fa_prev[e]
            crit = None
            for d in op["deps"]:
                if crit is None or finish[d] > finish[crit]:
                    crit = d
            op["crit"] = crit
            fa_prev[e] = fin
            order.append(i)
            scheduled[i] = True
            done += 1
            for sidx in succ[i]:
                if finish[i] > ready_at[sidx]:
                    ready_at[sidx] = finish[i]
                ndeps[sidx] -= 1
                if ndeps[sidx] == 0:
                    avail[ops[sidx]["eng"]].append(sidx)
        self.est_makespan_us = max(finish) / 1e3
        busy = {e: 0.0 for e in engs}
        for op in ops:
            busy[op["eng"]] += op["dur"]
        print("[sched] est makespan %.1f us; busy us: %s" % (self.est_makespan_us, {e: round(v / 1e3) for e, v in busy.items()}))
        import os
        if os.environ.get("K_STALLS"):
            t_lo, t_hi = [float(v) * 1e3 for v in os.environ["K_STALLS"].split(",")]
            for e in engs:
                agg = {}
                tot = 0.0
                for op in ops:
                    if op["eng"] == e and t_lo <= op["t_start"] < t_hi and op["stall"] > 1.0 and op["crit"] is not None:
                        k = (op["label"], ops[op["crit"]]["label"])
                        agg[k] = agg.get(k, 0.0) + op["stall"]
                        tot += op["stall"]
                print("[stalls] %s total %.1f us" % (e, tot / 1e3))
                for k, v in sorted(agg.items(), key=lambda kv: -kv[1])[:12]:
                    print("     %-28s waits on %-28s %.1f us" % (k[0], k[1], v / 1e3))
        if os.environ.get("K_BUSY"):
            for e in engs:
                agg = {}
                for op in ops:
                    if op["eng"] == e:
                        k = op["label"].rstrip("0123456789_")
                        agg[k] = agg.get(k, 0.0) + op["dur"]
                print("[busy] %s" % e)
                for k, v in sorted(agg.items(), key=lambda kv: -kv[1])[:22]:
                    print("     %-24s %.1f us" % (k, v / 1e3))
        if os.environ.get("K_TL"):
            eng_, t_lo, t_hi = os.environ["K_TL"].split(",")
            t_lo, t_hi = float(t_lo) * 1e3, float(t_hi) * 1e3
            for i in order:
                op = ops[i]
                if op["eng"] == eng_ and t_lo <= op["t_start"] < t_hi:
                    c = ops[op["crit"]]["label"] if op["crit"] is not None else "-"
                    print("[tl] %8.1f +%6.2f stall %6.2f  %-22s <- %s" % (op["t_start"] / 1e3, op["dur"] / 1e3, op["stall"] / 1e3, op["label"], c))
        remap = {old: new for new, old in enumerate(order)}
        new_ops = []
        for new, old in enumerate(order):
            op = ops[old]
            op["deps"] = {remap[d] for d in op["deps"]}
            op["idx"] = new
            new_ops.append(op)
        self.ops = new_ops

    def emit(self, nc, es, same_engine_sync=True, do_schedule=True):
        if do_schedule:
            self.schedule()
        ops = self.ops
        for op in ops:
            nd = set()
            for d in op["deps"]:
                dop = ops[d]
                if dop["stream"] is None and dop["eng"] == "pe" and op["eng"] == "pe" and op["stream"] is None:
                    continue
                if (not same_engine_sync) and dop["stream"] is None and op["stream"] is None and dop["eng"] == op["eng"]:
                    continue
                nd.add(d)
            op["deps"] = nd
            for d in nd:
                ops[d]["has_dep"] = True
        sems = {}
        engs = ["pe", "act", "dve", "pool", "sp"]
        for e in engs:
            sems[e] = es.enter_context(nc.semaphore("s_" + e))
        streams = sorted({op["stream"] for op in ops if op["stream"] is not None})
        for s in streams:
            sems["d:" + s] = es.enter_context(nc.semaphore("d_" + s))
        cnt = {e: 0 for e in engs}
        scnt = {s: 0 for s in streams}
        for op in ops:
            if op["stream"] is not None:
                scnt[op["stream"]] += 1
                op["sig"] = ("d:" + op["stream"], 16 * scnt[op["stream"]])
            elif op["has_dep"]:
                cnt[op["eng"]] += 1
                op["sig"] = (op["eng"], cnt[op["eng"]])
            else:
                op["sig"] = None
        for op in ops:
            if op["stream"] in self.waitall_streams:
                op["sig"] = ("d:" + op["stream"], 16 * scnt[op["stream"]])
        self.final_counts = {("d:" + s): 16 * scnt[s] for s in streams}
        self.sems = sems
        block = es.enter_context(nc.Block())
        per_eng = {e: [op for op in ops if op["eng"] == e] for e in engs}

        def run(engobj, lst, extra_tail=None):
            waited = {}
            for op in lst:
                need = {}
                for d in op["deps"]:
                    sg = ops[d]["sig"]
                    assert sg is not None
                    if sg[1] > need.get(sg[0], 0):
                        need[sg[0]] = sg[1]
                for sk, v in need.items():
                    if v > waited.get(sk, 0):
                        engobj.wait_ge(sems[sk], v)
                        waited[sk] = v
                ins = op["fn"](engobj)
                if op["stream"] is not None:
                    ins.then_inc(sems["d:" + op["stream"]], 16)
                elif op["sig"] is not None:
                    ins.then_inc(sems[op["eng"]], 1)
            if extra_tail is not None:
                extra_tail(engobj, waited)

        def sp_tail(engobj, waited):
            for sk, v in self.final_counts.items():
                if v > waited.get(sk, 0):
                    engobj.wait_ge(sems[sk], v)

        @block.sync
        def _(e):
            run(e, per_eng["sp"], sp_tail)

        @block.tensor
        def _(e):
            run(e, per_eng["pe"])

        @block.scalar
        def _(e):
            run(e, per_eng["act"])

        @block.vector
        def _(e):
            run(e, per_eng["dve"])

        @block.gpsimd
        def _(e):
            run(e, per_eng["pool"])


PROJ_GROUPS = [
    ("if", 2048, 8), ("dt", 6664, 16), ("k", 512, 512), ("v0", 1024, 512), ("v1", 1536, 512),
    ("xbc0", 5128, 512), ("xbc1", 5640, 512), ("xbc2", 6152, 512), ("q", 0, 512),
    ("zm0", 3080, 512), ("zm1", 3592, 512), ("o0", 2056, 512), ("o1", 2568, 512),
    ("zs0", 4104, 512), ("zs1", 4616, 512),
]
STATE_ONLY = {"if", "dt", "k", "v0", "v1", "xbc0", "xbc1", "xbc2"}


def build(n_pre, n_full, debug=()):
    nc = bass.Bass("TRN2", target_bir_lowering=False)
    T_pre, T_full = n_pre * L, n_full * L
    dr = {}
    dr["x"] = nc.dram_tensor("x", [max(T_full, 1), D_MODEL], F32, kind="ExternalInput").ap()
    if n_pre:
        dr["xpre"] = nc.dram_tensor("xpre", [T_pre, D_MODEL], F32, kind="ExternalInput").ap()
    for nm, shp in [("norm_w", [1024]), ("w_in", [1024, NCOL]), ("b_igate", [4]), ("b_fgate", [4]),
                    ("conv_w", [1536, 4]), ("conv_b", [1536]), ("dt_bias", [16]), ("a_log", [16]),
                    ("d_skip", [16]), ("normcat", [2048]), ("w_out", [2048, 1024]),
                    ("final_norm_w", [1024]), ("flag", [1])]:
        dr[nm] = nc.dram_tensor(nm, shp, F32, kind="ExternalInput").ap()
    out_d = nc.dram_tensor("out", [T_full, D_MODEL], F32, kind="ExternalOutput").ap()
    dbg_out = {}

    T = Tracker()
    T.waitall_streams.add("const")
    es = ExitStack()
    with es:
        def sb(name, shape, dt):
            return es.enter_context(nc.sbuf_tensor(name, shape, dt))

        def ps(name, shape, dt):
            return es.enter_context(nc.psum_tensor(name, shape, dt))

        win = sb("win", [128, 8 * NCOL], BF16)
        wout = sb("wout", [128, 16 * 1024], BF16)
        win3 = win[:].rearrange("p (k n) -> p k n", k=8)
        wout3 = wout[:].rearrange("p (k n) -> p k n", k=16)
        identb = sb("identb", [128, 128], BF16)
        identf = sb("identf", [128, 128], F32)
        tri = sb("tri", [128, 128], F32)
        maskb = sb("maskb", [128, 128], BF16)
        onesf = sb("onesf", [128, 128], F32)
        oh48 = sb("oh48", [48, 16 * 128], BF16)
        oh48_3 = oh48[:].rearrange("p (r s) -> p r s", r=16)
        finalw = sb("finalw", [128, 1024], F32)
        cst = sb("cst", [128, 256], F32)
        cw = cst[:, 0:48].rearrange("p (t w) -> p t w", t=12)
        cb = cst[:, 48:60]
        bias8 = cst[:, 60:68]
        dtb = cst[:, 68:84]
        arep = cst[:, 84:100]
        drep = cst[:, 100:116]
        normw_fm = cst[:, 116:124]
        normcat = cst[:, 124:140]
        m05 = cst[:, 140:148]
        flag = cst[:, 148:149]
        alog = cst[:, 152:168]

        xbuf = [sb("xbuf0", [128, 1024], F32), sb("xbuf1", [128, 1024], F32)]
        ARENA_BYTES = 23552
        arena = sb("arena", [128, ARENA_BYTES // 4], F32)

        def carve(layout_off, name, nbytes, dt, subkeys=None):
            assert layout_off[0] % 4 == 0
            o = layout_off[0]
            layout_off[0] += (nbytes + 3) // 4 * 4
            assert layout_off[0] <= ARENA_BYTES, (name, layout_off[0])
            v = arena[:, o // 4:(o + (nbytes + 3) // 4 * 4) // 4]
            if dt != F32:
                v = v.bitcast(dt)
            if subkeys is None:
                T.reg(name, "A", o, nbytes)
            else:
                n = len(subkeys)
                for i, sk in enumerate(subkeys):
                    T.reg(sk, "A", o + i * (nbytes // n), nbytes // n)
            return v

        lo1 = [0]
        xn = carve(lo1, "xn", 2048, BF16)
        xnT = carve(lo1, "xnT", 2048, BF16)
        xnT3 = xnT[:].rearrange("p (k t) -> p k t", k=8)
        xbc_tm = carve(lo1, "xbc_tm", 3072, BF16, ["xbc_tm0", "xbc_tm1", "xbc_tm2"])
        _c0 = carve(lo1, "cacc0", 2048, F32, ["cacc0_%d" % i for i in range(4)])
        cacc = [_c0, _c0]
        _t0 = carve(lo1, "cth0", 1024, BF16)
        cth = [_t0, _t0]
        q_tm = carve(lo1, "q_tm", 1024, BF16)
        tz = carve(lo1, "tz", 2048, BF16, ["tz0", "tz1"])
        k_tm = carve(lo1, "k_tm", 1024, BF16)
        kp_tm = carve(lo1, "kp_tm", 1024, BF16)
        qT = carve(lo1, "qT", 1024, BF16)
        kT = carve(lo1, "kT", 1024, BF16)
        Pm = carve(lo1, "Pm", 1024, BF16, ["Pm_%d" % i for i in range(4)])
        Pm3 = Pm[:].rearrange("p (h j) -> p h j", h=4)
        Cbf = carve(lo1, "Cbf", 2064, BF16)
        Cbf3 = Cbf[:].rearrange("p (h c) -> p h c", h=4)
        g1 = carve(lo1, "g1", 2048, BF16, ["g1_0", "g1_1"])
        lo2 = [0]
        dec = carve(lo2, "dec", 4096, BF16, ["dec_%d" % i for i in range(4)])
        dec3 = dec[:].rearrange("p (r l) -> p r l", r=16)
        rl = carve(lo2, "rl", 2048, F32)
        CBm = carve(lo2, "CBm", 512, BF16)
        CBm3 = CBm[:].rearrange("p (g l) -> p g l", g=2)
        yo = carve(lo2, "yo", 4096, F32, ["yo_0", "yo_1"])
        ytmp = carve(lo2, "ytmp", 2048, F32)
        x_tm = carve(lo2, "x_tm", 2048, BF16)
        B_tm = carve(lo2, "B_tm", 512, BF16)
        xdt = carve(lo2, "xdt", 2048, BF16)
        xde = carve(lo2, "xde", 2048, BF16)
        mixT = carve(lo2, "mixT", 4096, BF16, ["mixT_0", "mixT_1"])
        mixT3 = mixT[:].rearrange("p (k t) -> p k t", k=16)

        junk = sb("junk", [128, 1024], BF16)
        vaug = sb("vaug", [128, 4 * 258], BF16)
        vaug3 = vaug[:].rearrange("p (h c) -> p h c", h=4)
        gs = sb("gs", [128, 1024], BF16)
        xbcT = sb("xbcT", [128, 12 * 132], BF16)
        xbcT3 = xbcT[:].rearrange("p (t c) -> p t c", t=12)
        xcT = sb("xcT", [128, 1536], BF16)
        xcT3 = xcT[:].rearrange("p (t c) -> p t c", t=12)
        Cst = sb("Cst", [128, 4 * 258], F32)
        Cst3 = Cst[:].rearrange("p (h c) -> p h c", h=4)
        hst = sb("hst", [128, 1024], F32)
        hbf = sb("hbf", [128, 1024], BF16)
        mix = sb("mix", [128, 2048], BF16)
        a3 = sb("a3", [128, 48], BF16)
        A48 = sb("A48", [48, 128], BF16)
        sm = sb("sm", [128, 256], F32)
        g8 = sm[:, 0:8]
        t8 = sm[:, 8:16]
        e4 = sm[:, 16:20]
        nlf = sm[:, 20:24]
        li = sm[:, 24:28]
        T1 = sm[:, 28:36]
        T2 = sm[:, 36:44]
        wf = sm[:, 44:52]
        soldb = sm[:, 52:56]
        absden = sm[:, 56:60]
        ssqr = sm[:, 60:64]
        dn4 = sm[:, 64:68]
        rd4 = sm[:, 68:72]
        t4 = sm[:, 72:76]
        rs4 = sm[:, 76:80]
        sc4 = sm[:, 80:84]
        ssq_x = sm[:, 84:85]
        rstd_x = sm[:, 85:86]
        tx1 = sm[:, 86:87]
        ssq_o = sm[:, 87:88]
        rstd_o = sm[:, 88:89]
        to1 = sm[:, 89:90]
        ssq_s = sm[:, 90:92]
        rstd_s = sm[:, 92:94]
        ts2 = sm[:, 94:96]
        dtp = sm[:, 96:112]
        edt = sm[:, 112:128]
        dt16 = sm[:, 128:144]
        a_tm = sm[:, 144:160]
        acum_sb = sm[:, 160:176]
        ea = sm[:, 176:192]
        dEa = sm[:, 192:208]
        dE = sm[:, 208:224]
        cdb = sm[:, 224:240]
        r1 = sm[:, 240:256]
        sm2 = sb("sm2", [128, 16], F32)
        r2 = sm2[:, 0:16]
        gm = sb("gm", [4, 32], F32)
        mst = gm[:, 0:1]
        umax = gm[:, 1:2]
        Rg = gm[:, 2:3]
        dg = gm[:, 3:4]
        soldg = gm[:, 4:5]
        Dg = gm[:, 8:16]
        pT = [ps("pT0", [128, 1024], BF16), ps("pT1", [128, 1024], BF16)]
        pm = ps("pm", [128, 512], F32)
        pb = [ps("pb%d" % i, [128, 512], F32) for i in range(5)]
        print("sbuf bytes remaining:", nc.sbuf_bytes_remaining)

        bank_rr = {"A": 0, "B": 0}
        BANKS = {"A": [0, 1], "B": [2, 3, 4]}

        def next_bank(ph="B"):
            lst = BANKS[ph]
            i = lst[bank_rr[ph] % len(lst)]
            bank_rr[ph] += 1
            return pb[i], "pb%d" % i

        def next_pT(ph="B"):
            i = 0 if ph == "A" else 1
            return pT[i], "pT%d" % i

        def setup_pool(e):
            e.memset(onesf[:], 1.0)
            e.memset(identf[:], 1.0)
            e.affine_select(identf[:], identf[:], [[-1, 128]], ALU.is_equal, 0.0, base=0, channel_multiplier=1)
            e.tensor_copy(identb[:], identf[:])
            e.memset(tri[:], 1.0)
            e.affine_select(tri[:], tri[:], [[1, 128]], ALU.is_ge, 0.0, base=0, channel_multiplier=-1)
            e.tensor_copy(maskb[:], tri[:])
            e.memset(m05, -0.5)
            e.memset(vaug[:], 0.0)
            e.memset(vaug3[:, :, 256:257], 1.0)
            e.memset(Cst[:], 0.0)
            e.memset(hst[:], 0.0)
            e.memset(hbf[:], 0.0)
            e.memset(Cbf[:], 0.0)
            e.memset(gm[:], 0.0)
            e.memset(xbcT[:], 0.0)
            e.memset(oh48[:], 1.0)
            e.affine_select(oh48_3, oh48_3, [[-1, 16], [0, 128]], ALU.is_equal, 0.0, base=0, channel_multiplier=1)
            e.affine_select(oh48_3, oh48_3, [[-1, 16], [0, 128]], ALU.not_equal, 1.0, base=-16, channel_multiplier=1)
            return e.affine_select(oh48_3, oh48_3, [[-1, 16], [0, 128]], ALU.not_equal, 1.0, base=-32, channel_multiplier=1)

        T.add("pool", setup_pool, w=["identb", "identf", "tri", "maskb", "onesf", "m05", "vaug", "Cst", "hst",
                                     "hbf0", "hbf1", "Cbf", "mst", "xbcT_carry", "oh48", "gm"])

        def cdma(out_ap, in_ap, key, noncontig=False):
            T.add("sp", lambda e: e.dma_start(out=out_ap, in_=in_ap, allow_slow_non_contiguous=noncontig),
                  w=[key], stream="const")

        cdma(normw_fm, dr["norm_w"].rearrange("(k p) -> p k", p=128), "normw_fm", True)
        cdma(normcat, dr["normcat"].rearrange("(k p) -> p k", p=128), "normcat", True)
        cdma(cw, dr["conv_w"].rearrange("(t p) w -> p t w", p=128), "cw")
        cdma(cb, dr["conv_b"].rearrange("(t p) -> p t", p=128), "cb", True)
        cdma(bias8[:, 0:4], dr["b_igate"].partition_broadcast(128), "bias8a")
        cdma(bias8[:, 4:8], dr["b_fgate"].partition_broadcast(128), "bias8b")
        cdma(dtb, dr["dt_bias"].partition_broadcast(128), "dtb")
        cdma(alog, dr["a_log"].partition_broadcast(128), "alog")
        cdma(drep, dr["d_skip"].partition_broadcast(128), "drep")
        cdma(finalw[:], dr["final_norm_w"].partition_broadcast(128), "finalw")
        cdma(flag, dr["flag"].partition_broadcast(128), "flag")

        T.add("act", lambda e: e.activation(out=arep, in_=alog, func=AF.Exp), r=["alog"], w=["arep0"])
        T.add("dve", lambda e: e.tensor_scalar_mul(arep, arep, -1.0), r=["arep0"], w=["arep"])
        T.add("dve", lambda e: e.tensor_scalar_mul(cst[:, 0:60], cst[:, 0:60], 0.5), r=["cw", "cb"], w=["cwb"])
        T.add("dve", lambda e: e.tensor_scalar_mul(finalw[:], finalw[:], 32.0), r=["finalw"], w=["finalw2"])
        T.add("dve", lambda e: e.tensor_scalar_mul(normw_fm, normw_fm, 32.0), r=["normw_fm"], w=["normw2"])
        T.add("dve", lambda e: e.tensor_scalar_mul(normcat[:, 0:8], normcat[:, 0:8], 4.0), r=["normcat"], w=["normcat_a"])
        T.add("dve", lambda e: e.tensor_scalar_mul(normcat[:, 8:16], normcat[:, 8:16], float(np.sqrt(2048.0) / 2.0)),
              r=["normcat"], w=["normcat_b"])

        W_PRE = [(512, 1024), (1024, 2048), (2048, 2056), (5128, 6152), (6152, 6680)]
        W_FULL = [(0, 512), (2056, 3080), (3080, 4104), (4104, 5128)]
        wkeys = {}
        piece = [0]
        NSTG = 8
        wout_f = wout[:].bitcast(F32)

        def wpiece(src_ap, dst_ap, scale_ap, scale_key, ncols, stg_ap, stg_key, stream, wkey, extra_w=()):
            i = piece[0]
            piece[0] += 1
            T.add("sp", lambda e: e.dma_start(out=stg_ap[:, 0:ncols], in_=src_ap), w=[stg_key], stream=stream)
            if i % 2 == 1:
                T.add("act", lambda e: e.activation(out=dst_ap, in_=stg_ap[:, 0:ncols], func=AF.Copy, scale=scale_ap),
                      r=[stg_key, scale_key], w=[wkey] + list(extra_w))
            else:
                T.add("dve", lambda e: e.tensor_scalar_mul(dst_ap, stg_ap[:, 0:ncols], scale_ap),
                      r=[stg_key, scale_key], w=[wkey] + list(extra_w))

        for (c0, c1) in W_PRE + W_FULL:
            wkeys[(c0, c1)] = []
            for k in range(8):
                sl_ = piece[0] % NSTG
                wk = "w_%d_%d" % (c0, k)
                wkeys[(c0, c1)].append(wk)
                wpiece(dr["w_in"][k * 128:(k + 1) * 128, c0:c1], win3[:, k, c0:c1], normw_fm[:, k:k + 1], "normw2", c1 - c0,
                       wout_f[:, sl_ * 1024:(sl_ + 1) * 1024], "wstg%d" % sl_, "stg%d" % sl_, wk)

        def wkeys_for(c0, n):
            out = []
            for (a0, a1), ks in wkeys.items():
                if a0 < c0 + n and c0 < a1:
                    out += ks
            return out

        def load_wout():
            for kc in range(16):
                key = "normcat_a" if kc < 8 else "normcat_b"
                slot = kc % 2
                wpiece(dr["w_out"][kc * 128:(kc + 1) * 128, :], wout3[:, kc, :], normcat[:, kc:kc + 1], key, 1024,
                       xbuf[slot], "xbuf%d" % slot, "xin%d" % slot, "wout_%d" % kc,
                       extra_w=["wstg%d" % j for j in range(NSTG)])
        wout_keys = ["wout_%d" % kc for kc in range(16)]

        def load_x(src, ci, slot):
            T.add("sp", lambda e: e.dma_start(out=xbuf[slot][:], in_=src[ci * L:(ci + 1) * L, :]),
                  w=["xbuf%d" % slot], stream="xin%d" % slot)

        chunk_list = [("pre", i) for i in range(n_pre)] + [("full", i) for i in range(n_full)]

        def src_of(kind):
            return dr["xpre"] if kind == "pre" else dr["x"]

        import os as _os
        _DBL = set((_os.environ.get("K_DBL") or "").split(",")) - {""}
        _PERSIST = {"Cst", "hst", "hst0", "hst1", "hbf0", "hbf1", "mst", "xbcT_carry", "identb", "identf", "tri", "maskb", "onesf",
                    "m05", "oh48", "cwb", "arep", "drep", "dtb", "bias8a", "bias8b", "finalw2", "flag", "out_dram", "xbuf0", "xbuf1"}
        _par = [0]

        def _km(k):
            if not isinstance(k, str) or Tracker._bank(k) is not None or k in _PERSIST or k.startswith("w_") or k.startswith("wout_"):
                return k
            if "ALL" in _DBL or k in _DBL or k.rstrip("0123456789_") in _DBL:
                return "%s@%d" % (k, _par[0])
            return k

        def TA(eng, fn, r=(), w=(), stream=None):
            return T.add(eng, fn, r=[_km(k) for k in r], w=[_km(k) for k in w], stream=stream)

        def chunk(gi):
            kind, ci = chunk_list[gi]
            _par[0] = gi % 2
            full = kind == "full"
            slot = gi % 2
            xb = xbuf[slot]
            xk = "xbuf%d" % slot
            if gi + 1 < len(chunk_list) and not (n_pre and gi + 1 == n_pre):
                nk, nci = chunk_list[gi + 1]
                load_x(src_of(nk), nci, (gi + 1) % 2)
            TA("act", lambda e: e.activation(out=junk[:], in_=xb[:], func=AF.Square, accum_out=ssq_x), r=[xk], w=["ssq_x"])
            TA("pool", lambda e: e.tensor_scalar(tx1, ssq_x, 1024.0 * EPS, None, ALU.add), r=["ssq_x"], w=["tx1"])
            TA("pool", lambda e: e.tensor_tensor(rstd_x, tx1, m05[:, 0:1], ALU.pow), r=["tx1", "m05"], w=["rstd_x"])
            TA("dve", lambda e: e.tensor_scalar_mul(xn[:], xb[:], rstd_x), r=[xk, "rstd_x"], w=["xn"])
            p, pk = next_pT("A")

            def tr_x(e, p=p):
                for k in range(8):
                    ins = e.transpose(p[:, k * 128:(k + 1) * 128], xn[:, k * 128:(k + 1) * 128], identb[:])
                return ins
            TA("pe", tr_x, r=["xn", "identb"], w=[pk])
            TA("act", lambda e, p=p: e.activation(out=xnT[:], in_=p[:, 0:1024], func=AF.Copy), r=[pk], w=["xnT"])

            def proj(name, c0, n, out_ap, okey, extra_r=()):
                def f(e):
                    for k in range(8):
                        ins = e.matmul(out_ap, xnT3[:, k, :], win3[:, k, c0:c0 + n], start=(k == 0), stop=(k == 7))
                    return ins
                TA("pe", f, r=["xnT"] + wkeys_for(c0, n) + list(extra_r), w=[okey])

            for name, c0, n in PROJ_GROUPS:
                if not full and name not in STATE_ONLY:
                    continue
                if name == "if":
                    b, bk = next_bank("A")
                    proj(name, c0, n, b[:, 0:8], bk)
                    TA("dve", lambda e, b=b: e.tensor_tensor(g8, b[:, 0:8], bias8, ALU.add), r=[bk, "bias8a", "bias8b"], w=["g8"])
                    TA("act", lambda e: e.activation(out=t8, in_=g8, func=AF.Tanh, scale=1.0 / 15.0), r=["g8"], w=["t8"])
                    TA("act", lambda e: e.activation(out=e4, in_=t8[:, 4:8], func=AF.Exp, scale=-15.0), r=["t8"], w=["e4"])
                    TA("act", lambda e: e.activation(out=nlf, in_=e4, func=AF.Ln, bias=1.0), r=["e4"], w=["nlf"])
                    TA("dve", lambda e: e.tensor_scalar_mul(li, t8[:, 0:4], 15.0), r=["t8"], w=["li"])
                elif name == "dt":
                    b, bk = next_bank("A")
                    proj(name, c0, n, b[:, 0:16], bk)
                    TA("dve", lambda e, b=b: e.tensor_tensor(dtp, b[:, 0:16], dtb, ALU.add), r=[bk, "dtb"], w=["dtp"])
                    TA("act", lambda e: e.activation(out=edt, in_=dtp, func=AF.Exp), r=["dtp"], w=["edt"])
                    TA("act", lambda e: e.activation(out=dt16, in_=edt, func=AF.Ln, bias=1.0), r=["edt"], w=["dt16"])
                    TA("dve", lambda e: e.tensor_tensor(a_tm, dt16, arep, ALU.mult), r=["dt16", "arep"], w=["a_tm"])
                else:
                    b, bk = next_bank("A")
                    proj(name, c0, n, b[:, 0:n], bk)
                    if name == "k":
                        TA("dve", lambda e, b=b: e.tensor_copy(k_tm[:], b[:, 0:512]), r=[bk], w=["k_tm"])
                    elif name in ("v0", "v1"):
                        h0 = 0 if name == "v0" else 2
                        TA("act", lambda e, b=b, h0=h0: e.activation(
                            out=vaug3[:, h0:h0 + 2, 0:256], in_=b[:, 0:512].rearrange("p (h c) -> p h c", h=2), func=AF.Copy),
                            r=[bk, "vaug"], w=["vaug_%d" % h0])
                    elif name.startswith("xbc"):
                        j = int(name[3])
                        eng = "dve" if j == 1 else "act"
                        if eng == "act":
                            TA("act", lambda e, b=b, j=j: e.activation(out=xbc_tm[:, j * 512:(j + 1) * 512], in_=b[:, 0:512], func=AF.Copy),
                                  r=[bk], w=["xbc_tm%d" % j])
                        else:
                            TA("dve", lambda e, b=b, j=j: e.tensor_copy(xbc_tm[:, j * 512:(j + 1) * 512], b[:, 0:512]),
                                  r=[bk], w=["xbc_tm%d" % j])
                    elif name == "q":
                        TA("act", lambda e, b=b: e.activation(out=q_tm[:], in_=b[:, 0:512], func=AF.Copy, scale=float(128 ** -0.5)),
                              r=[bk], w=["q_tm"])
                    elif name in ("zm0", "zm1"):
                        hh = int(name[2])
                        sl = slice(hh * 512, (hh + 1) * 512)
                        TA("act", lambda e, b=b, sl=sl: e.activation(out=tz[:, sl], in_=b[:, 0:512], func=AF.Tanh, scale=0.5),
                              r=[bk], w=["tz%d" % hh])
                        TA("dve", lambda e, b=b, sl=sl: e.scalar_tensor_tensor(g1[:, sl], tz[:, sl], 1.0, b[:, 0:512], ALU.add, ALU.mult),
                              r=[bk, "tz%d" % hh], w=["g1_%d" % hh])
                    elif name in ("o0", "o1"):
                        hh = int(name[1])
                        sl = slice(hh * 512, (hh + 1) * 512)
                        TA("act", lambda e, b=b, sl=sl: e.activation(out=tz[:, sl], in_=b[:, 0:512], func=AF.Tanh, scale=0.5),
                              r=[bk], w=["tz%d" % hh])
                        TA("dve", lambda e, sl=sl: e.scalar_tensor_tensor(g1[:, sl], tz[:, sl], 1.0, g1[:, sl], ALU.add, ALU.mult),
                              r=["tz%d" % hh, "g1_%d" % hh], w=["g1_%d" % hh])
                    elif name in ("zs0", "zs1"):
                        hh = int(name[2])
                        sl = slice(hh * 512, (hh + 1) * 512)
                        TA("act", lambda e, b=b, sl=sl: e.activation(out=tz[:, sl], in_=b[:, 0:512], func=AF.Tanh, scale=0.5),
                              r=[bk], w=["tz%d" % hh])
                        TA("dve", lambda e, b=b, sl=sl: e.scalar_tensor_tensor(gs[:, sl], tz[:, sl], 1.0, b[:, 0:512], ALU.add, ALU.mult),
                              r=[bk, "tz%d" % hh], w=["gs_%d" % hh])

            def tr4(src, dst, skey, dkey):
                p, pk = next_pT("A")

                def f(e, p=p):
                    for h in range(4):
                        ins = e.transpose(p[:, h * 128:(h + 1) * 128], src[:, h * 128:(h + 1) * 128], identb[:])
                    return ins
                TA("pe", f, r=[skey, "identb"], w=[pk])
                TA("dve", lambda e, p=p: e.tensor_copy(dst[:], p[:, 0:512]), r=[pk], w=[dkey])

            if full:
                tr4(k_tm, kT, "k_tm", "kT")
                tr4(q_tm, qT, "q_tm", "qT")
            for half, (t0, nt) in enumerate([(0, 8), (8, 4)]):
                p, pk = next_pT("A")

                def f(e, p=p, t0=t0, nt=nt):
                    for t in range(nt):
                        ins = e.transpose(p[:, t * 128:(t + 1) * 128], xbc_tm[:, (t0 + t) * 128:(t0 + t + 1) * 128], identb[:])
                    return ins
                TA("pe", f, r=["xbc_tm0", "xbc_tm1", "xbc_tm2", "identb"], w=[pk])
                TA("act", lambda e, p=p, t0=t0, nt=nt: e.activation(
                    out=xbcT3[:, t0:t0 + nt, 4:132], in_=p[:, 0:nt * 128].rearrange("p (t c) -> p t c", t=nt), func=AF.Copy),
                    r=[pk, "xbcT_carry"], w=["xbcT_%d" % half])

            TA("pe", lambda e: e.matmul(pm[:, 24:28], tri[:], nlf, start=True, stop=True), r=["tri", "nlf"], w=["pm_nb"])

            def f_ugm(e):
                e.matmul(pm[0:4, 128:256], li, identf[:], start=True, stop=False)
                return e.matmul(pm[0:4, 128:256], nlf, tri[:], start=False, stop=True)
            TA("pe", f_ugm, r=["li", "nlf", "identf", "tri"], w=["pm_ugm"])
            TA("pe", lambda e: e.matmul(pm[0:4, 120:121], nlf, onesf[:, 0:1], start=True, stop=True), r=["nlf", "onesf"], w=["pm_nbl"])
            TA("dve", lambda e: e.reduce_max(umax, pm[0:4, 128:256], AX.X), r=["pm_ugm"], w=["umax"])
            TA("dve", lambda e: e.tensor_tensor(Rg, umax, mst, ALU.max), r=["umax", "mst"], w=["Rg"])
            TA("dve", lambda e: e.tensor_tensor(dg, mst, Rg, ALU.subtract), r=["mst", "Rg"], w=["dg"])
            TA("act", lambda e: e.activation(out=soldg, in_=dg, func=AF.Exp), r=["dg"], w=["soldg"])
            TA("dve", lambda e: e.tensor_tensor(mst, Rg, pm[0:4, 120:121], ALU.subtract), r=["Rg", "pm_nbl", "dg"], w=["mst"])
            TA("dve", lambda e: e.tensor_scalar_mul(Dg[:, 0:4], identf[0:4, 0:4], Rg), r=["Rg", "identf"], w=["Dg_a"])
            TA("dve", lambda e: e.tensor_scalar_mul(Dg[:, 4:8], identf[0:4, 0:4], soldg), r=["soldg", "identf"], w=["Dg_b"])
            TA("pe", lambda e: e.matmul(pm[:, 28:36], onesf[0:4, :], Dg, start=True, stop=True), r=["onesf", "Dg_a", "Dg_b"], w=["pm_rs"])
            TA("dve", lambda e: e.tensor_tensor(T1[:, 0:4], li, pm[:, 24:28], ALU.add), r=["li", "pm_nb"], w=["T1a"])
            TA("dve", lambda e: e.tensor_copy(T1[:, 4:8], pm[:, 24:28]), r=["pm_nb"], w=["T1b"])
            TA("dve", lambda e: e.tensor_tensor(
                T2.rearrange("p (a h) -> p a h", a=2), T1.rearrange("p (a h) -> p a h", a=2),
                pm[:, 28:32].unsqueeze(1).broadcast_to([128, 2, 4]), ALU.subtract), r=["T1a", "T1b", "pm_rs"], w=["T2"])
            TA("act", lambda e: e.activation(out=wf, in_=T2, func=AF.Exp), r=["T2"], w=["wf"])
            TA("dve", lambda e: e.tensor_copy(soldb, pm[:, 32:36]), r=["pm_rs"], w=["soldb"])

            TA("pe", lambda e: e.matmul(pm[:, 40:56], tri[:], a_tm, start=True, stop=True), r=["tri", "a_tm"], w=["pm_acum"])
            TA("pe", lambda e: e.matmul(pm[:, 56:72], onesf[:], a_tm, start=True, stop=True), r=["onesf", "a_tm"], w=["pm_alast"])
            TA("dve", lambda e: e.tensor_copy(acum_sb, pm[:, 40:56]), r=["pm_acum"], w=["acum_sb"])
            TA("act", lambda e: e.activation(out=ea, in_=pm[:, 40:56], func=AF.Exp), r=["pm_acum"], w=["ea"])
            TA("dve", lambda e: e.tensor_tensor(dEa, pm[:, 56:72], acum_sb, ALU.subtract), r=["pm_alast", "acum_sb"], w=["dEa"])
            TA("act", lambda e: e.activation(out=dE, in_=dEa, func=AF.Exp), r=["dEa"], w=["dE"])
            TA("act", lambda e: e.activation(out=cdb, in_=pm[:, 56:72], func=AF.Exp), r=["pm_alast"], w=["cdb"])
            if full:
                TA("dve", lambda e: e.tensor_copy(a3[:, 0:16], acum_sb), r=["acum_sb"], w=["a3_0"])
                TA("dve", lambda e: e.tensor_tensor(r1, acum_sb, a3[:, 0:16], ALU.subtract), r=["acum_sb", "a3_0"], w=["r1"])
                TA("dve", lambda e: e.tensor_copy(a3[:, 16:32], r1), r=["r1"], w=["a3_1"])
                TA("dve", lambda e: e.tensor_tensor(r2, r1, a3[:, 16:32], ALU.subtract), r=["r1", "a3_1"], w=["r2"])
                TA("dve", lambda e: e.tensor_copy(a3[:, 32:48], r2), r=["r2"], w=["a3_2"])
                p, pk = next_pT()
                TA("pe", lambda e, p=p: e.transpose(p[0:48, 0:128], a3[:, 0:48], identb[:]), r=["a3_0", "a3_1", "a3_2", "identb"], w=[pk])
                TA("dve", lambda e, p=p: e.tensor_copy(A48[:], p[0:48, 0:128]), r=[pk], w=["A48"])

            for rnd in range(3):
                ca = cacc[rnd % 2]
                ct = cth[rnd % 2]
                cak = "cacc0"
                ctk = "cth0"
                src_key = "xbcT_0" if rnd < 2 else "xbcT_1"
                for tt in range(4):
                    t = rnd * 4 + tt
                    eng = "dve"
                    cslice = ca[:, tt * 128:(tt + 1) * 128]

                    ckey = cak + "_%d" % tt
                    TA(eng, lambda e, t=t, cslice=cslice: e.tensor_scalar(
                        cslice, xbcT3[:, t, 1:129], cw[:, t, 0:1], cb[:, t:t + 1], ALU.mult, ALU.add),
                        r=[src_key, "xbcT_carry", "cwb"], w=[ckey])
                    for w_ in range(1, 4):
                        TA(eng, lambda e, t=t, cslice=cslice, w_=w_: e.scalar_tensor_tensor(
                            cslice, xbcT3[:, t, 1 + w_:129 + w_], cw[:, t, w_:w_ + 1], cslice, ALU.mult, ALU.add),
                            r=[src_key, "xbcT_carry", "cwb", ckey], w=[ckey])
                TA("act", lambda e, ca=ca, ct=ct: e.activation(out=ct[:], in_=ca[:], func=AF.Tanh),
                      r=[cak + "_%d" % i for i in range(4)], w=[ctk])
                TA("dve", lambda e, ca=ca, ct=ct, rnd=rnd: e.scalar_tensor_tensor(
                    xcT[:, rnd * 512:(rnd + 1) * 512], ct[:], 1.0, ca[:], ALU.add, ALU.mult),
                    r=[ctk] + [cak + "_%d" % i for i in range(4)], w=["xcT_%d" % rnd])
            TA("pool", lambda e: e.tensor_copy(xbcT3[:, :, 1:4], xbcT3[:, :, 129:132]), r=["xbcT_0", "xbcT_1"], w=["xbcT_carry"])

            TA("pool", lambda e: e.tensor_tensor(
                kp_tm[:].rearrange("p (h d) -> p h d", h=4), k_tm[:].rearrange("p (h d) -> p h d", h=4),
                wf[:, 0:4].unsqueeze(2).broadcast_to([128, 4, 128]), ALU.mult), r=["k_tm", "wf"], w=["kp_tm"])
            TA("dve", lambda e: e.tensor_tensor(Cst3[:, :, 0:257], Cst3[:, :, 0:257],
                                                   soldb.unsqueeze(2).broadcast_to([128, 4, 257]), ALU.mult),
                  r=["Cst", "soldb"], w=["Cst"])
            if full:
                TA("act", lambda e: e.activation(out=Cbf3[:, :, 0:257], in_=Cst3[:, :, 0:257], func=AF.Copy), r=["Cst"], w=["Cbf"])
                sb_, sbk = next_bank()

                def f_st(e, sb_=sb_):
                    for h in range(4):
                        ins = e.matmul(sb_[:, h * 128:(h + 1) * 128], kT[:, h * 128:(h + 1) * 128], qT[:, h * 128:(h + 1) * 128],
                                       start=True, stop=True)
                    return ins
                TA("pe", f_st, r=["kT", "qT"], w=[sbk])
                for h in range(4):
                    TA("dve", lambda e, h=h, sb_=sb_: e.scalar_tensor_tensor(
                        Pm3[:, h, :], sb_[:, h * 128:(h + 1) * 128], wf[:, h:h + 1], maskb[:], ALU.mult, ALU.mult),
                        r=[sbk, "wf", "maskb"], w=["Pm_%d" % h])
                brs = []
                for hp in range(2):
                    bb, bbk = next_bank()
                    for h in (2 * hp, 2 * hp + 1):
                        brs.append((bb, bbk, (h % 2) * 256))

                    def f_br(e, hp=hp, bb=bb):
                        for h in (2 * hp, 2 * hp + 1):
                            o_ = (h % 2) * 256
                            e.matmul(bb[:, o_:o_ + 256], Pm3[:, h, :], vaug3[:, h, 0:256], start=True, stop=False)
                            ins = e.matmul(bb[:, o_:o_ + 256], qT[:, h * 128:(h + 1) * 128], Cbf3[:, h, 0:256], start=False, stop=True)
                        return ins
                    TA("pe", f_br, r=["Pm_%d" % (2 * hp), "Pm_%d" % (2 * hp + 1), "vaug_0", "vaug_2", "qT", "Cbf"], w=[bbk])
                    for h in (2 * hp, 2 * hp + 1):
                        o_ = (h % 2) * 256
                        TA("act", lambda e, h=h, bb=bb, o_=o_: e.activation(out=junk[:, 0:256], in_=bb[:, o_:o_ + 256], func=AF.Square,
                                                                            accum_out=ssqr[:, h:h + 1]), r=[bbk], w=["ssqr_%d" % h])

                def f_den(e):
                    for h in range(4):
                        e.matmul(pm[:, 64 + h:65 + h], Pm3[:, h, :], vaug3[:, h, 256:257], start=True, stop=False)
                        ins = e.matmul(pm[:, 64 + h:65 + h], qT[:, h * 128:(h + 1) * 128], Cbf3[:, h, 256:257], start=False, stop=True)
                    return ins
                TA("pe", f_den, r=["Pm_0", "Pm_1", "Pm_2", "Pm_3", "vaug_0", "vaug_2", "qT", "Cbf"], w=["pm_den"])
                TA("act", lambda e: e.activation(out=absden, in_=pm[:, 64:68], func=AF.Abs), r=["pm_den"], w=["absden"])
                hk = ["absden"]
                sk = ["ssqr_%d" % h for h in range(4)]
                TA("dve", lambda e: e.tensor_tensor(dn4, absden, wf[:, 4:8], ALU.max), r=hk + ["wf"], w=["dn4"])
                TA("dve", lambda e: e.reciprocal(rd4, dn4), r=["dn4"], w=["rd4"])
                TA("dve", lambda e: e.tensor_tensor(t4, rd4, rd4, ALU.mult), r=["rd4"], w=["t4"])
                TA("dve", lambda e: e.tensor_tensor(t4, t4, ssqr, ALU.mult), r=["t4"] + sk, w=["t4"])
                TA("pool", lambda e: e.tensor_scalar(t4, t4, 256.0 * EPS, None, ALU.add), r=["t4"], w=["t4"])
                TA("pool", lambda e: e.tensor_tensor(rs4, t4, m05[:, 0:4], ALU.pow), r=["t4", "m05"], w=["rs4"])
                TA("dve", lambda e: e.tensor_tensor(sc4, rd4, rs4, ALU.mult), r=["rd4", "rs4"], w=["sc4"])
                for h in range(4):
                    bb, bbk, o_ = brs[h]
                    TA("dve", lambda e, h=h, bb=bb, o_=o_: e.scalar_tensor_tensor(
                        mix[:, h * 256:(h + 1) * 256], bb[:, o_:o_ + 256], sc4[:, h:h + 1], g1[:, h * 256:(h + 1) * 256], ALU.mult, ALU.mult),
                        r=[bbk, "sc4", "g1_%d" % (h // 2)], w=["mix_m%d" % h])
            for h in range(4):
                cb_, cbk = next_bank()
                TA("pe", lambda e, h=h, cb_=cb_: e.matmul(cb_[:, 0:257], kp_tm[:, h * 128:(h + 1) * 128], vaug3[:, h, 0:257],
                                                             start=True, stop=True), r=["kp_tm", "vaug_0", "vaug_2"], w=[cbk])
                TA("dve", lambda e, h=h, cb_=cb_: e.tensor_tensor(Cst3[:, h, 0:257], Cst3[:, h, 0:257], cb_[:, 0:257], ALU.add),
                      r=[cbk, "Cst", "Cbf"], w=["Cst"])

            p, pk = next_pT()

            def tr_xc(e, p=p):
                for t in range(8):
                    ins = e.transpose(p[:, t * 128:(t + 1) * 128], xcT3[:, t, :], identb[:])
                return ins
            TA("pe", tr_xc, r=["xcT_0", "xcT_1", "identb"], w=[pk])
            TA("act", lambda e, p=p: e.activation(out=x_tm[:], in_=p[:, 0:1024], func=AF.Copy), r=[pk], w=["x_tm"])
            p, pk = next_pT()

            def tr_B(e, p=p):
                for t in range(2):
                    ins = e.transpose(p[:, t * 128:(t + 1) * 128], xcT3[:, 8 + t, :], identb[:])
                return ins
            TA("pe", tr_B, r=["xcT_2", "identb"], w=[pk])
            TA("dve", lambda e, p=p: e.tensor_copy(B_tm[:], p[:, 0:256]), r=[pk], w=["B_tm"])
            TA("pool", lambda e: e.tensor_tensor(
                xdt[:].rearrange("p (r c) -> p r c", r=16), x_tm[:].rearrange("p (r c) -> p r c", r=16),
                dt16.unsqueeze(2).broadcast_to([128, 16, 64]), ALU.mult), r=["x_tm", "dt16"], w=["xdt"])
            TA("pool", lambda e: e.tensor_tensor(
                xde[:].rearrange("p (r c) -> p r c", r=16), xdt[:].rearrange("p (r c) -> p r c", r=16),
                dE.unsqueeze(2).broadcast_to([128, 16, 64]), ALU.mult), r=["xdt", "dE"], w=["xde"])

            if full:
                TA("pe", lambda e: (e.matmul(pm[:, 256:384], xcT3[:, 8, :], xcT3[:, 10, :], start=True, stop=True),
                                       e.matmul(pm[:, 384:512], xcT3[:, 9, :], xcT3[:, 11, :], start=True, stop=True))[1],
                      r=["xcT_2"], w=["pm_cb"])
                TA("dve", lambda e: e.tensor_tensor(CBm3, pm[:, 256:512].rearrange("p (g l) -> p g l", g=2),
                                                       maskb[:].unsqueeze(1).broadcast_to([128, 2, 128]), ALU.mult),
                      r=["pm_cb", "maskb"], w=["CBm"])
                for bq in range(4):
                    ab, abk = next_bank()

                    def f_arg(e, bq=bq, ab=ab):
                        for rr in range(4):
                            ins = e.matmul(ab[:, rr * 128:(rr + 1) * 128], oh48_3[:, bq * 4 + rr, :], A48[:], start=True, stop=True)
                        return ins
                    TA("pe", f_arg, r=["oh48", "A48"], w=[abk])

                    def f_relu(e, bq=bq, ab=ab):
                        for rr in range(4):
                            hd = bq * 4 + rr
                            ins = e.activation(out=rl[:, rr * 128:(rr + 1) * 128], in_=ab[:, rr * 128:(rr + 1) * 128],
                                               func=AF.Relu, bias=acum_sb[:, hd:hd + 1], scale=-1.0)
                        return ins
                    TA("act", f_relu, r=[abk, "acum_sb"], w=["rl"])
                    TA("act", lambda e, bq=bq: e.activation(out=dec[:, bq * 512:(bq + 1) * 512], in_=rl[:], func=AF.Exp, scale=-1.0),
                          r=["rl"], w=["dec_%d" % bq])
                    g = bq // 2
                    TA("pool", lambda e, bq=bq, g=g: e.tensor_tensor(
                        dec3[:, bq * 4:bq * 4 + 4, :], dec3[:, bq * 4:bq * 4 + 4, :],
                        CBm3[:, g:g + 1, :].broadcast_to([128, 4, 128]), ALU.mult), r=["dec_%d" % bq, "CBm"], w=["dec_%d" % bq])
                TA("pool", lambda e: e.tensor_tensor(
                    yo[:].rearrange("p (r c) -> p r c", r=16), x_tm[:].rearrange("p (r c) -> p r c", r=16),
                    drep.unsqueeze(2).broadcast_to([128, 16, 64]), ALU.mult), r=["x_tm", "drep"], w=["yo_0", "yo_1"])
                for g in range(2):
                    yd, ydk = next_bank()

                    def f_yd(e, g=g, yd=yd):
                        for rr in range(8):
                            hd = g * 8 + rr
                            ins = e.matmul(yd[:, rr * 64:(rr + 1) * 64], dec3[:, hd, :], xdt[:, hd * 64:(hd + 1) * 64], start=True, stop=True)
                        return ins
                    TA("pe", f_yd, r=["dec_%d" % (2 * g), "dec_%d" % (2 * g + 1), "xdt"], w=[ydk])
                    yf, yfk = next_bank()
                    TA("pe", lambda e, g=g, yf=yf: e.matmul(yf[:, 0:512], xcT3[:, 10 + g, :], hbf[:, g * 512:(g + 1) * 512], start=True, stop=True),
                          r=["xcT_2", "hbf%d" % g], w=[yfk])
                    gsl = slice(g * 512, (g + 1) * 512)
                    TA("dve", lambda e, g=g, yf=yf: e.tensor_tensor(
                        ytmp[:].rearrange("p (r c) -> p r c", r=8), yf[:, 0:512].rearrange("p (r c) -> p r c", r=8),
                        ea[:, g * 8:(g + 1) * 8].unsqueeze(2).broadcast_to([128, 8, 64]), ALU.mult), r=[yfk, "ea"], w=["ytmp"])
                    TA("dve", lambda e, yd=yd: e.tensor_tensor(ytmp[:], ytmp[:], yd[:, 0:512], ALU.add), r=[ydk, "ytmp"], w=["ytmp"])
                    TA("pool", lambda e, gsl=gsl: e.tensor_tensor(yo[:, gsl], yo[:, gsl], ytmp[:], ALU.add), r=["ytmp", "yo_%d" % g], w=["yo_%d" % g])
                    TA("pool", lambda e, gsl=gsl: e.tensor_tensor(yo[:, gsl], yo[:, gsl], gs[:, gsl], ALU.mult),
                          r=["yo_%d" % g, "gs_%d" % g], w=["yo_%d" % g])
                    TA("act", lambda e, g=g, gsl=gsl: e.activation(out=junk[:, 0:512], in_=yo[:, gsl], func=AF.Square, accum_out=ssq_s[:, g:g + 1]),
                          r=["yo_%d" % g], w=["ssq_s%d" % g])
                TA("pool", lambda e: e.tensor_scalar(ts2, ssq_s, 2048.0 * EPS, None, ALU.add), r=["ssq_s0", "ssq_s1"], w=["ts2"])
                TA("pool", lambda e: e.tensor_tensor(rstd_s, ts2, m05[:, 0:2], ALU.pow), r=["ts2", "m05"], w=["rstd_s"])
                for g in range(2):
                    gsl = slice(g * 512, (g + 1) * 512)
                    TA("dve", lambda e, g=g, gsl=gsl: e.tensor_scalar_mul(mix[:, 1024 + g * 512:1024 + (g + 1) * 512], yo[:, gsl], rstd_s[:, g:g + 1]),
                          r=["yo_%d" % g, "rstd_s"], w=["mix_s%d" % g])
            for g in range(2):
                st, stk = next_bank()
                gsl = slice(g * 512, (g + 1) * 512)
                TA("pe", lambda e, g=g, st=st, gsl=gsl: e.matmul(st[:, 0:512], B_tm[:, g * 128:(g + 1) * 128], xde[:, gsl], start=True, stop=True),
                      r=["B_tm", "xde"], w=[stk])
                TA("dve", lambda e, g=g, gsl=gsl: e.tensor_tensor(
                    hst[:, gsl].rearrange("p (r c) -> p r c", r=8), hst[:, gsl].rearrange("p (r c) -> p r c", r=8),
                    cdb[:, g * 8:(g + 1) * 8].unsqueeze(2).broadcast_to([128, 8, 64]), ALU.mult), r=["hst", "cdb", "hbf%d" % g], w=["hst%d" % g])
                TA("dve", lambda e, st=st, gsl=gsl: e.tensor_tensor(hst[:, gsl], hst[:, gsl], st[:, 0:512], ALU.add), r=[stk, "hst%d" % g], w=["hst%d" % g])
                TA("act", lambda e, gsl=gsl: e.activation(out=hbf[:, gsl], in_=hst[:, gsl], func=AF.Copy), r=["hst%d" % g], w=["hbf%d" % g])

            if not full:
                return
            mkeys = ["mix_m%d" % h for h in range(4)] + ["mix_s0", "mix_s1"]
            for half in range(2):
                p, pk = next_pT()

                def tr_m(e, p=p, half=half):
                    for t in range(8):
                        kc = half * 8 + t
                        ins = e.transpose(p[:, t * 128:(t + 1) * 128], mix[:, kc * 128:(kc + 1) * 128], identb[:])
                    return ins
                TA("pe", tr_m, r=mkeys + ["identb"], w=[pk])
                if half == 0:
                    TA("act", lambda e, p=p: e.activation(out=mixT[:, 0:1024], in_=p[:, 0:1024], func=AF.Copy), r=[pk], w=["mixT_0"])
                else:
                    TA("dve", lambda e, p=p: e.tensor_copy(mixT[:, 1024:2048], p[:, 0:1024]), r=[pk], w=["mixT_1"])
            for half in range(2):
                ob, obk = next_bank()

                def f_o(e, ob=ob, half=half):
                    for kc in range(16):
                        ins = e.matmul(ob[:, 0:512], mixT3[:, kc, :], wout3[:, kc, half * 512:(half + 1) * 512], start=(kc == 0), stop=(kc == 15))
                    return ins
                TA("pe", f_o, r=["mixT_0", "mixT_1"] + wout_keys, w=[obk])
                hsl = slice(half * 512, (half + 1) * 512)
                TA("dve", lambda e, ob=ob, hsl=hsl: e.tensor_tensor(xb[:, hsl], xb[:, hsl], ob[:, 0:512], ALU.add), r=[obk, xk], w=[xk])
            TA("act", lambda e: e.activation(out=junk[:], in_=xb[:], func=AF.Square, accum_out=ssq_o), r=[xk], w=["ssq_o"])
            TA("pool", lambda e: e.tensor_scalar(to1, ssq_o, 1024.0 * EPS, None, ALU.add), r=["ssq_o"], w=["to1"])
            TA("pool", lambda e: e.tensor_tensor(rstd_o, to1, m05[:, 0:1], ALU.pow), r=["to1", "m05"], w=["rstd_o"])
            TA("dve", lambda e: e.scalar_tensor_tensor(xb[:], xb[:], rstd_o, finalw[:], ALU.mult, ALU.mult), r=[xk, "rstd_o", "finalw2"], w=[xk])
            TA("sp", lambda e: e.dma_start(out=out_d[ci * L:(ci + 1) * L, :], in_=xb[:]), r=[xk], w=["out_dram"], stream="out%d" % slot)

        k0, c0_ = chunk_list[0]
        if n_pre == 0:
            load_wout()
        load_x(src_of(k0), c0_, 0)
        for gi in range(len(chunk_list)):
            if n_pre and gi == n_pre:
                load_wout()
                load_x(src_of("full"), 0, gi % 2)
            chunk(gi)
            if n_pre and gi == n_pre - 1:
                T.add("dve", lambda e: e.tensor_scalar_mul(Cst[:], Cst[:], flag), r=["Cst", "flag"], w=["Cst"])
                T.add("dve", lambda e: e.tensor_scalar_mul(hst[:], hst[:], flag), r=["hst0", "hst1", "flag"], w=["hst0", "hst1", "hst"])
                T.add("dve", lambda e: e.tensor_scalar_mul(hbf[:], hbf[:], flag), r=["hbf0", "hbf1", "flag"], w=["hbf0", "hbf1"])
                T.add("dve", lambda e: e.tensor_scalar_mul(mst, mst, flag[0:4, :]), r=["mst", "flag"], w=["mst"])
                T.add("dve", lambda e: e.tensor_scalar_mul(xbcT3[:, :, 1:4], xbcT3[:, :, 1:4], flag), r=["xbcT_carry", "flag"], w=["xbcT_carry"])

        for nm in debug:
            tile_ap, shape, rkeys = {
                "xnT": (xnT[:], [128, 1024], ["xnT"]),
                "k_tm": (k_tm[:], [128, 512], ["k_tm"]),
                "mix": (mix[:], [128, 2048], ["mix_m0", "mix_m1", "mix_m2", "mix_m3", "mix_s0", "mix_s1"]),
                "Cst": (Cst[:], [128, 4 * 258], ["Cst"]),
                "hst": (hst[:], [128, 1024], ["hst0", "hst1"]),
                "x_tm": (x_tm[:], [128, 1024], ["x_tm"]),
                "sm": (sm[:], [128, 256], ["wf", "dE", "ea", "cdb", "dt16", "acum_sb"]),
                "yo": (yo[:], [128, 1024], ["yo_0", "yo_1"]),
                "dec": (dec[:], [128, 2048], ["dec_0", "dec_1", "dec_2", "dec_3"]),
                "xcT": (xcT[:], [128, 1536], ["xcT_0", "xcT_1", "xcT_2"]),
                "g1": (g1[:], [128, 1024], ["g1_0", "g1_1"]),
                "vaug": (vaug[:], [128, 4 * 258], ["vaug_0", "vaug_2"]),
            }[nm]
            d = nc.dram_tensor("dbg_" + nm, shape, F32, kind="ExternalOutput").ap()
            dbg_out[nm] = d
            q_eng = "sp" if tile_ap.dtype == F32 else "pool"
            T.add(q_eng, lambda e, d=d, tile_ap=tile_ap: e.dma_start(out=d, in_=tile_ap), r=rkeys, w=["dbg_" + nm], stream="dbg")

        T.emit(nc, es)
    return nc


_CACHE = {}


def kernel(x, norm_w, w_in, b_igate, b_fgate, conv_w, conv_b, dt_bias, a_log, d_skip,
           mlstm_norm_w, ssd_norm_w, w_out, final_norm_w):
    f = lambda a: np.ascontiguousarray(np.asarray(a, dtype=np.float32))
    x = f(x)
    n_half = SEQ // 2 // L
    key = "main"
    if key not in _CACHE:
        _CACHE[key] = build(n_half, n_half)
    nc = _CACHE[key]
    common = {
        "norm_w": f(norm_w)[0], "w_in": f(w_in)[0], "b_igate": f(b_igate)[0], "b_fgate": f(b_fgate)[0],
        "conv_w": f(conv_w)[0], "conv_b": f(conv_b)[0], "dt_bias": f(dt_bias)[0], "a_log": f(a_log)[0],
        "d_skip": f(d_skip)[0],
        "normcat": np.ascontiguousarray(np.concatenate([f(mlstm_norm_w)[0], f(ssd_norm_w)[0]])),
        "w_out": f(w_out)[0], "final_norm_w": f(final_norm_w),
    }
    half = SEQ // 2
    zeros = np.zeros((half, D_MODEL), np.float32)
    in_maps = []
    for core in range(NCORES):
        b, hf = core // 2, core % 2
        m = dict(common)
        m["x"] = np.ascontiguousarray(x[b, hf * half:(hf + 1) * half])
        m["xpre"] = zeros if hf == 0 else np.ascontiguousarray(x[b, 0:half])
        m["flag"] = np.array([float(hf)], np.float32)
        in_maps.append(m)
    res = run_bass_kernel_spmd(nc, in_maps, core_ids=list(range(NCORES)))
    out = np.empty((BATCH, SEQ, D_MODEL), np.float32)
    for core in range(NCORES):
        b, hf = core // 2, core % 2
        out[b, hf * half:(hf + 1) * half] = res.results[core]["out"]
    return out
```

```python
import numpy as np
from contextlib import ExitStack
import concourse.bass as bass
import concourse.mybir as mybir
from concourse.bass_utils import run_bass_kernel_spmd

F32 = mybir.dt.float32
BF16 = mybir.dt.bfloat16
AF = mybir.ActivationFunctionType
ALU = mybir.AluOpType
AX = mybir.AxisListType

D_MODEL = 1024
SEQ = 8192
BATCH = 4
NCOL = 6680
EPS = 1e-6
L = 128
NCORES = 8


class Tracker:
    def __init__(self):
        self.ops = []
        self.bufs = {}
        self.waitall_streams = set()
        self.regions = {}

    def reg(self, key, arena, off, nbytes, gran=64):
        import os
        if os.environ.get("K_NOALIAS"):
            return
        self.regions[key] = [(arena, g) for g in range(off // gran, (off + nbytes + gran - 1) // gran)]

    def _expand(self, keys):
        out = []
        for k in keys:
            out.extend(self.regions.get(k, [k]))
        return out

    PSUM_BANKS = ("pT0", "pT1", "pm", "pb0", "pb1", "pb2", "pb3", "pb4")

    @classmethod
    def _bank(cls, k):
        if isinstance(k, str):
            if k.startswith("pm_"):
                return "pm"
            if k in cls.PSUM_BANKS:
                return k
        return None

    def add(self, eng, fn, r=(), w=(), stream=None):
        self._lbl = "%s:%s" % (eng, (list(w) + ["?"])[0])
        banks = [self._bank(k) for k in list(r) + list(w)]
        banks = [b for b in banks if b is not None]
        r = [k for k in r if self._bank(k) is None]
        w = [k for k in w if self._bank(k) is None] + sorted(set(banks))
        r = self._expand(r)
        w = self._expand(w)
        deps = set()
        for k in r:
            b = self.bufs.setdefault(k, [None, []])
            if b[0] is not None:
                deps.add(b[0])
        for k in w:
            b = self.bufs.setdefault(k, [None, []])
            if b[0] is not None:
                deps.add(b[0])
            deps.update(b[1])
        idx = len(self.ops)
        deps.discard(idx)
        self.ops.append(dict(eng=eng, fn=fn, deps=deps, stream=stream, idx=idx, has_dep=False, label=self._lbl))
        for k in r:
            self.bufs[k][1].append(idx)
        for k in w:
            self.bufs[k] = [idx, []]
        return idx

    class _Ins:
        def then_inc(self, *a, **k):
            return self

    class _Probe:
        def __init__(self):
            self.calls = []

        def __getattr__(self, name):
            def f(*args, **kw):
                self.calls.append((name, args, kw))
                return Tracker._Ins()
            return f

    @staticmethod
    def _free(ap):
        n = 1
        for d in ap.shape[1:]:
            n *= int(d)
        return n

    def _cost(self, op):
        pr = Tracker._Probe()
        op["fn"](pr)
        eng = op["eng"]
        dur = 0.0
        lat = 0.0
        for name, args, kw in pr.calls:
            out = kw.get("out", args[0] if args else None)
            if name == "dma_start":
                src = kw.get("in_", args[1] if len(args) > 1 else None)
                nbytes = self._free(out) * int(out.shape[0]) * 4
                dur += 80.0
                lat = 2500.0 + nbytes / 160.0
                continue
            n = self._free(out) if out is not None and hasattr(out, "shape") else 64
            if eng == "pe":
                lhs = args[1] if len(args) > 1 else None
                mult = 4.0 if (lhs is not None and lhs.dtype == F32 and name == "matmul") else 1.0
                dur += 16.0 + max(n, 64) * mult / 1.95
            elif eng == "act":
                dur += 230.0 + n / 1.15
            elif eng == "dve":
                dur += 170.0 + n / 0.96
            elif eng == "pool":
                dur += 300.0 + n * 2.0
            else:
                dur += 50.0
        import os
        fr = os.environ.get("K_FREE")
        if fr and any(op["label"].startswith(p) for p in fr.split(",")):
            dur = 20.0
        op["dur"] = max(dur, 20.0)
        op["lat"] = lat

    def schedule(self):
        import heapq
        ops = self.ops
        n = len(ops)
        for op in ops:
            self._cost(op)
        succ = [[] for _ in range(n)]
        for op in ops:
            for d in op["deps"]:
                succ[d].append(op["idx"])
        bl = [0.0] * n
        for i in range(n - 1, -1, -1):
            m = 0.0
            for sidx in succ[i]:
                if bl[sidx] > m:
                    m = bl[sidx]
            bl[i] = m + ops[i]["dur"] + ops[i]["lat"]
        import os
        if os.environ.get("K_CRIT"):
            i = max(range(n), key=lambda j: bl[j])
            print("[crit] DAG critical path %.1f us" % (bl[i] / 1e3))
            agg = {}
            while True:
                agg[ops[i]["label"]] = agg.get(ops[i]["label"], 0.0) + ops[i]["dur"] + ops[i]["lat"]
                nxt = None
                for sidx in succ[i]:
                    if nxt is None or bl[sidx] > bl[nxt]:
                        nxt = sidx
                if nxt is None:
                    break
                i = nxt
            for k, v in sorted(agg.items(), key=lambda kv: -kv[1])[:40]:
                print("     %-28s %.1f us" % (k, v / 1e3))
        ndeps = [len(op["deps"]) for op in ops]
        ready_at = [0.0] * n
        engs = ["pe", "act", "dve", "pool", "sp"]
        avail = {e: [] for e in engs}
        for i in range(n):
            if ndeps[i] == 0:
                avail[ops[i]["eng"]].append(i)
        free_at = {e: 0.0 for e in engs}
        fa_prev = {e: 0.0 for e in engs}
        order = []
        finish = [0.0] * n
        done = 0
        WINDOW = 4000
        lowest_unscheduled = 0
        scheduled = [False] * n
        while done < n:
            best = None
            while lowest_unscheduled < n and scheduled[lowest_unscheduled]:
                lowest_unscheduled += 1
            for e in engs:
                lst = avail[e]
                if not lst:
                    continue
                fa = free_at[e]
                cand = None
                for i in lst:
                    if i > lowest_unscheduled + WINDOW:
                        continue
                    st = ready_at[i] if ready_at[i] > fa else fa
                    key = (st, -bl[i], i)
                    if cand is None or key < cand[0]:
                        cand = (key, i, st)
                if cand is None:
                    continue
                if best is None or cand[0] < best[0]:
                    best = cand + (e,)
            assert best is not None, "scheduler stuck"
            _, i, st, e = best
            avail[e].remove(i)
            op = ops[i]
            fin = st + op["dur"]
            free_at[e] = fin
            finish[i] = fin + op["lat"]
            op["t_start"] = st
            op["stall"] = st - fa_prev[e]
            crit = None
            for d in op["deps"]:
                if crit is None or finish[d] > finish[crit]:
                    crit = d
            op["crit"] = crit
            fa_prev[e] = fin
            order.append(i)
            scheduled[i] = True
            done += 1
            for sidx in succ[i]:
                if finish[i] > ready_at[sidx]:
                    ready_at[sidx] = finish[i]
                ndeps[sidx] -= 1
                if ndeps[sidx] == 0:
                    avail[ops[sidx]["eng"]].append(sidx)
        self.est_makespan_us = max(finish) / 1e3
        busy = {e: 0.0 for e in engs}
        for op in ops:
            busy[op["eng"]] += op["dur"]
        print("[sched] est makespan %.1f us; busy us: %s" % (self.est_makespan_us, {e: round(v / 1e3) for e, v in busy.items()}))
        import os
        if os.environ.get("K_STALLS"):
            t_lo, t_hi = [float(v) * 1e3 for v in os.environ["K_STALLS"].split(",")]
            for e in engs:
                agg = {}
                tot = 0.0
                for op in ops:
                    if op["eng"] == e and t_lo <= op["t_start"] < t_hi and op["stall"] > 1.0 and op["crit"] is not None:
                        k = (op["label"], ops[op["crit"]]["label"])
                        agg[k] = agg.get(k, 0.0) + op["stall"]
                        tot += op["stall"]
                print("[stalls] %s total %.1f us" % (e, tot / 1e3))
                for k, v in sorted(agg.items(), key=lambda kv: -kv[1])[:12]:
                    print("     %-28s waits on %-28s %.1f us" % (k[0], k[1], v / 1e3))
        if os.environ.get("K_BUSY"):
            for e in engs:
                agg = {}
                for op in ops:
                    if op["eng"] == e:
                        k = op["label"].rstrip("0123456789_")
                        agg[k] = agg.get(k, 0.0) + op["dur"]
                print("[busy] %s" % e)
                for k, v in sorted(agg.items(), key=lambda kv: -kv[1])[:22]:
                    print("     %-24s %.1f us" % (k, v / 1e3))
        if os.environ.get("K_TL"):
            eng_, t_lo, t_hi = os.environ["K_TL"].split(",")
            t_lo, t_hi = float(t_lo) * 1e3, float(t_hi) * 1e3
            for i in order:
                op = ops[i]
                if op["eng"] == eng_ and t_lo <= op["t_start"] < t_hi:
                    c = ops[op["crit"]]["label"] if op["crit"] is not None else "-"
                    print("[tl] %8.1f +%6.2f stall %6.2f  %-22s <- %s" % (op["t_start"] / 1e3, op["dur"] / 1e3, op["stall"] / 1e3, op["label"], c))
        remap = {old: new for new, old in enumerate(order)}
        new_ops = []
        for new, old in enumerate(order):
            op = ops[old]
            op["deps"] = {remap[d] for d in op["deps"]}
            op["idx"] = new
            new_ops.append(op)
        self.ops = new_ops

    def emit(self, nc, es, same_engine_sync=True, do_schedule=True):
        if do_schedule:
            self.schedule()
        ops = self.ops
        for op in ops:
            nd = set()
            for d in op["deps"]:
                dop = ops[d]
                if dop["stream"] is None and dop["eng"] == "pe" and op["eng"] == "pe" and op["stream"] is None:
                    continue
                if (not same_engine_sync) and dop["stream"] is None and op["stream"] is None and dop["eng"] == op["eng"]:
                    continue
                nd.add(d)
            op["deps"] = nd
            for d in nd:
                ops[d]["has_dep"] = True
        sems = {}
        engs = ["pe", "act", "dve", "pool", "sp"]
        for e in engs:
            sems[e] = es.enter_context(nc.semaphore("s_" + e))
        streams = sorted({op["stream"] for op in ops if op["stream"] is not None})
        for s in streams:
            sems["d:" + s] = es.enter_context(nc.semaphore("d_" + s))
        cnt = {e: 0 for e in engs}
        scnt = {s: 0 for s in streams}
        for op in ops:
            if op["stream"] is not None:
                scnt[op["stream"]] += 1
                op["sig"] = ("d:" + op["stream"], 16 * scnt[op["stream"]])
            elif op["has_dep"]:
                cnt[op["eng"]] += 1
                op["sig"] = (op["eng"], cnt[op["eng"]])
            else:
                op["sig"] = None
        for op in ops:
            if op["stream"] in self.waitall_streams:
                op["sig"] = ("d:" + op["stream"], 16 * scnt[op["stream"]])
        self.final_counts = {("d:" + s): 16 * scnt[s] for s in streams}
        self.sems = sems
        block = es.enter_context(nc.Block())
        per_eng = {e: [op for op in ops if op["eng"] == e] for e in engs}

        def run(engobj, lst, extra_tail=None):
            waited = {}
            for op in lst:
                need = {}
                for d in op["deps"]:
                    sg = ops[d]["sig"]
                    assert sg is not None
                    if sg[1] > need.get(sg[0], 0):
                        need[sg[0]] = sg[1]
                for sk, v in need.items():
                    if v > waited.get(sk, 0):
                        engobj.wait_ge(sems[sk], v)
                        waited[sk] = v
                ins = op["fn"](engobj)
                if op["stream"] is not None:
                    ins.then_inc(sems["d:" + op["stream"]], 16)
                elif op["sig"] is not None:
                    ins.then_inc(sems[op["eng"]], 1)
            if extra_tail is not None:
                extra_tail(engobj, waited)

        def sp_tail(engobj, waited):
            for sk, v in self.final_counts.items():
                if v > waited.get(sk, 0):
                    engobj.wait_ge(sems[sk], v)

        @block.sync
        def _(e):
            run(e, per_eng["sp"], sp_tail)

        @block.tensor
        def _(e):
            run(e, per_eng["pe"])

        @block.scalar
        def _(e):
            run(e, per_eng["act"])

        @block.vector
        def _(e):
            run(e, per_eng["dve"])

        @block.gpsimd
        def _(e):
            run(e, per_eng["pool"])


PROJ_GROUPS = [
    ("if", 2048, 8), ("dt", 6664, 16), ("k", 512, 512), ("v0", 1024, 512), ("v1", 1536, 512),
    ("xbc0", 5128, 512), ("xbc1", 5640, 512), ("xbc2", 6152, 512), ("q", 0, 512),
    ("zm0", 3080, 512), ("zm1", 3592, 512), ("o0", 2056, 512), ("o1", 2568, 512),
    ("zs0", 4104, 512), ("zs1", 4616, 512),
]
STATE_ONLY = {"if", "dt", "k", "v0", "v1", "xbc0", "xbc1", "xbc2"}


def build(n_pre, n_full, debug=()):
    nc = bass.Bass("TRN2", target_bir_lowering=False)
    T_pre, T_full = n_pre * L, n_full * L
    dr = {}
    dr["x"] = nc.dram_tensor("x", [max(T_full, 1), D_MODEL], F32, kind="ExternalInput").ap()
    if n_pre:
        dr["xpre"] = nc.dram_tensor("xpre", [T_pre, D_MODEL], F32, kind="ExternalInput").ap()
    for nm, shp in [("norm_w", [1024]), ("w_in", [1024, NCOL]), ("b_igate", [4]), ("b_fgate", [4]),
                    ("conv_w", [1536, 4]), ("conv_b", [1536]), ("dt_bias", [16]), ("a_log", [16]),
                    ("d_skip", [16]), ("normcat", [2048]), ("w_out", [2048, 1024]),
                    ("final_norm_w", [1024]), ("flag", [1])]:
        dr[nm] = nc.dram_tensor(nm, shp, F32, kind="ExternalInput").ap()
    out_d = nc.dram_tensor("out", [T_full, D_MODEL], F32, kind="ExternalOutput").ap()
    dbg_out = {}

    T = Tracker()
    T.waitall_streams.add("const")
    es = ExitStack()
    with es:
        def sb(name, shape, dt):
            return es.enter_context(nc.sbuf_tensor(name, shape, dt))

        def ps(name, shape, dt):
            return es.enter_context(nc.psum_tensor(name, shape, dt))

        win = sb("win", [128, 8 * NCOL], BF16)
        wout = sb("wout", [128, 16 * 1024], BF16)
        win3 = win[:].rearrange("p (k n) -> p k n", k=8)
        wout3 = wout[:].rearrange("p (k n) -> p k n", k=16)
        identb = sb("identb", [128, 128], BF16)
        identf = sb("identf", [128, 128], F32)
        tri = sb("tri", [128, 128], F32)
        maskb = sb("maskb", [128, 128], BF16)
        onesf = sb("onesf", [128, 128], F32)
        oh48 = sb("oh48", [128, 16 * 128], BF16)
        oh48_3 = oh48[0:48, :].rearrange("p (r s) -> p r s", r=16)
        cbrow = oh48[64:65, 0:1024]
        onesrow = oh48[64:65, 1024:1152]
        NPE_R = 1
        dgc = sb("dgc", [128, NPE_R * 4 * 4 * 128], BF16)
        dgc4 = dgc[:].rearrange("p (t w c) -> p t w c", t=NPE_R * 4, w=4)
        finalw = sb("finalw", [128, 1024], F32)
        cst = sb("cst", [128, 172], F32)
        cw = cst[:, 0:48].rearrange("p (t w) -> p t w", t=12)
        cb = cst[:, 48:60]
        bias8 = cst[:, 60:68]
        dtb = cst[:, 68:84]
        arep = cst[:, 84:100]
        drep = cst[:, 100:116]
        normw_fm = cst[:, 116:124]
        normcat = cst[:, 124:140]
        m05 = cst[:, 140:148]
        flag = cst[:, 148:149]
        alog = cst[:, 152:168]

        xbuf = [sb("xbuf0", [128, 1024], F32), sb("xbuf1", [128, 1024], F32)]
        ARENA_BYTES = 23552
        arena = sb("arena", [128, ARENA_BYTES // 4], F32)

        def carve(layout_off, name, nbytes, dt, subkeys=None):
            assert layout_off[0] % 4 == 0
            o = layout_off[0]
            layout_off[0] += (nbytes + 3) // 4 * 4
            assert layout_off[0] <= ARENA_BYTES, (name, layout_off[0])
            v = arena[:, o // 4:(o + (nbytes + 3) // 4 * 4) // 4]
            if dt != F32:
                v = v.bitcast(dt)
            if subkeys is None:
                T.reg(name, "A", o, nbytes)
            else:
                n = len(subkeys)
                for i, sk in enumerate(subkeys):
                    T.reg(sk, "A", o + i * (nbytes // n), nbytes // n)
            return v

        lo1 = [0]
        xn = carve(lo1, "xn", 2048, BF16)
        xnT = carve(lo1, "xnT", 2048, BF16)
        xnT3 = xnT[:].rearrange("p (k t) -> p k t", k=8)
        xbc_tm = carve(lo1, "xbc_tm", 3072, BF16, ["xbc_tm0", "xbc_tm1", "xbc_tm2"])
        _c0 = carve(lo1, "cacc0", 2048, F32, ["cacc0_%d" % i for i in range(4)])
        cacc = [_c0, _c0]
        _t0 = carve(lo1, "cth0", 1024, BF16)
        cth = [_t0, _t0]
        q_tm = carve(lo1, "q_tm", 1024, BF16)
        tz = carve(lo1, "tz", 2048, BF16, ["tz0", "tz1"])
        k_tm = carve(lo1, "k_tm", 1024, BF16)
        kp_tm = carve(lo1, "kp_tm", 1024, BF16)
        qT = carve(lo1, "qT", 1024, BF16)
        kT = carve(lo1, "kT", 1024, BF16)
        Pm = carve(lo1, "Pm", 1024, BF16, ["Pm_%d" % i for i in range(4)])
        Pm3 = Pm[:].rearrange("p (h j) -> p h j", h=4)
        Cbf = carve(lo1, "Cbf", 2064, BF16)
        Cbf3 = Cbf[:].rearrange("p (h c) -> p h c", h=4)
        g1 = carve(lo1, "g1", 2048, BF16, ["g1_0", "g1_1"])
        lo2 = [0]
        dec = carve(lo2, "dec", 4096, BF16, ["dec_%d" % i for i in range(4)])
        dec3 = dec[:].rearrange("p (r l) -> p r l", r=16)
        rl = carve(lo2, "rl", 2048, F32)
        CBm = carve(lo2, "CBm", 512, BF16)
        CBm3 = CBm[:].rearrange("p (g l) -> p g l", g=2)
        yo = carve(lo2, "yo", 4096, F32, ["yo_0", "yo_1"])
        ytmp = carve(lo2, "ytmp", 2048, F32)
        x_tm = carve(lo2, "x_tm", 2048, BF16)
        B_tm = carve(lo2, "B_tm", 512, BF16)
        xdt = carve(lo2, "xdt", 2048, BF16)
        xde = carve(lo2, "xde", 2048, BF16)
        mixT = carve(lo2, "mixT", 4096, BF16, ["mixT_0", "mixT_1"])
        mixT3 = mixT[:].rearrange("p (k t) -> p k t", k=16)

        junk = sb("junk", [128, 2], BF16)
        vaug = sb("vaug", [128, 4 * 258], BF16)
        vaug3 = vaug[:].rearrange("p (h c) -> p h c", h=4)
        gs = sb("gs", [128, 1024], BF16)
        xbcT = sb("xbcT", [128, 12 * 132], BF16)
        xbcT3 = xbcT[:].rearrange("p (t c) -> p t c", t=12)
        xcT = sb("xcT", [128, 1536], BF16)
        xcT3 = xcT[:].rearrange("p (t c) -> p t c", t=12)
        Cst = sb("Cst", [128, 4 * 258], F32)
        Cst3 = Cst[:].rearrange("p (h c) -> p h c", h=4)
        hst = sb("hst", [128, 1024], F32)
        hbf = sb("hbf", [128, 1024], BF16)
        mix = sb("mix", [128, 2048], BF16)
        a3 = sb("a3", [128, 48], BF16)
        A48 = sb("A48", [48, 128], BF16)
        sm = sb("sm", [128, 256], F32)
        g8 = sm[:, 0:8]
        t8 = sm[:, 8:16]
        e4 = sm[:, 16:20]
        nlf = sm[:, 20:24]
        li = sm[:, 24:28]
        T1 = sm[:, 28:36]
        T2 = sm[:, 36:44]
        wf = sm[:, 44:52]
        soldb = sm[:, 52:56]
        absden = sm[:, 56:60]
        ssqr = sm[:, 60:64]
        dn4 = sm[:, 64:68]
        rd4 = sm[:, 68:72]
        t4 = sm[:, 72:76]
        rs4 = sm[:, 76:80]
        sc4 = sm[:, 80:84]
        ssq_x = sm[:, 84:85]
        rstd_x = sm[:, 85:86]
        tx1 = sm[:, 86:87]
        ssq_o = sm[:, 87:88]
        rstd_o = sm[:, 88:89]
        to1 = sm[:, 89:90]
        ssq_s = sm[:, 90:92]
        rstd_s = sm[:, 92:94]
        ts2 = sm[:, 94:96]
        dtp = sm[:, 96:112]
        edt = sm[:, 112:128]
        dt16 = sm[:, 128:144]
        a_tm = sm[:, 144:160]
        acum_sb = sm[:, 160:176]
        ea = sm[:, 176:192]
        dEa = sm[:, 192:208]
        dE = sm[:, 208:224]
        cdb = sm[:, 224:240]
        r1 = sm[:, 240:256]
        sm2 = sb("sm2", [128, 16], F32)
        r2 = sm2[:, 0:16]
        gm = sb("gm", [4, 32], F32)
        mst = gm[:, 0:1]
        umax = gm[:, 1:2]
        Rg = gm[:, 2:3]
        dg = gm[:, 3:4]
        soldg = gm[:, 4:5]
        Dg = gm[:, 8:16]
        pT = [ps("pT0", [128, 1024], BF16), ps("pT1", [128, 1024], BF16)]
        pm = ps("pm", [128, 512], F32)
        pb = [ps("pb%d" % i, [128, 512], F32) for i in range(5)]
        print("sbuf bytes remaining:", nc.sbuf_bytes_remaining)

        bank_rr = {"A": 0, "B": 0}
        BANKS = {"A": [0, 1], "B": [2, 3, 4]}

        def next_bank(ph="B"):
            lst = BANKS[ph]
            i = lst[bank_rr[ph] % len(lst)]
            bank_rr[ph] += 1
            return pb[i], "pb%d" % i

        def next_pT(ph="B"):
            i = 0 if ph == "A" else 1
            return pT[i], "pT%d" % i

        def setup_pool(e):
            e.memset(onesf[:], 1.0)
            e.memset(identf[:], 1.0)
            e.affine_select(identf[:], identf[:], [[-1, 128]], ALU.is_equal, 0.0, base=0, channel_multiplier=1)
            e.tensor_copy(identb[:], identf[:])
            e.memset(tri[:], 1.0)
            e.affine_select(tri[:], tri[:], [[1, 128]], ALU.is_ge, 0.0, base=0, channel_multiplier=-1)
            e.tensor_copy(maskb[:], tri[:])
            e.memset(m05, -0.5)
            e.memset(vaug[:], 0.0)
            e.memset(vaug3[:, :, 256:257], 1.0)
            e.memset(Cst[:], 0.0)
            e.memset(hst[:], 0.0)
            e.memset(hbf[:], 0.0)
            e.memset(Cbf[:], 0.0)
            e.memset(gm[:], 0.0)
            e.memset(xbcT[:], 0.0)
            e.memset(oh48[0:48, :], 1.0)
            e.memset(onesrow, 1.0)
            e.affine_select(oh48_3, oh48_3, [[-1, 16], [0, 128]], ALU.is_equal, 0.0, base=0, channel_multiplier=1)
            e.affine_select(oh48_3, oh48_3, [[-1, 16], [0, 128]], ALU.not_equal, 1.0, base=-16, channel_multiplier=1)
            return e.affine_select(oh48_3, oh48_3, [[-1, 16], [0, 128]], ALU.not_equal, 1.0, base=-32, channel_multiplier=1)

        T.add("pool", setup_pool, w=["identb", "identf", "tri", "maskb", "onesf", "m05", "vaug", "Cst", "hst",
                                     "hbf0", "hbf1", "Cbf", "mst", "xbcT_carry", "oh48", "gm"])

        def cdma(out_ap, in_ap, key, noncontig=False):
            T.add("sp", lambda e: e.dma_start(out=out_ap, in_=in_ap, allow_slow_non_contiguous=noncontig),
                  w=[key], stream="const")

        cdma(normw_fm, dr["norm_w"].rearrange("(k p) -> p k", p=128), "normw_fm", True)
        cdma(normcat, dr["normcat"].rearrange("(k p) -> p k", p=128), "normcat", True)
        cdma(cw, dr["conv_w"].rearrange("(t p) w -> p t w", p=128), "cw")
        cdma(cb, dr["conv_b"].rearrange("(t p) -> p t", p=128), "cb", True)
        cdma(bias8[:, 0:4], dr["b_igate"].partition_broadcast(128), "bias8a")
        cdma(bias8[:, 4:8], dr["b_fgate"].partition_broadcast(128), "bias8b")
        cdma(dtb, dr["dt_bias"].partition_broadcast(128), "dtb")
        cdma(alog, dr["a_log"].partition_broadcast(128), "alog")
        cdma(drep, dr["d_skip"].partition_broadcast(128), "drep")
        cdma(finalw[:], dr["final_norm_w"].partition_broadcast(128), "finalw")
        cdma(flag, dr["flag"].partition_broadcast(128), "flag")

        T.add("act", lambda e: e.activation(out=arep, in_=alog, func=AF.Exp), r=["alog"], w=["arep0"])
        T.add("dve", lambda e: e.tensor_scalar_mul(arep, arep, -1.0), r=["arep0"], w=["arep"])
        T.add("dve", lambda e: e.tensor_scalar_mul(cst[:, 0:60], cst[:, 0:60], 0.5), r=["cw", "cb"], w=["cwb"])
        T.add("dve", lambda e: e.tensor_scalar_mul(finalw[:], finalw[:], 32.0), r=["finalw"], w=["finalw2"])
        T.add("dve", lambda e: e.tensor_scalar_mul(normw_fm, normw_fm, 32.0), r=["normw_fm"], w=["normw2"])
        T.add("dve", lambda e: e.tensor_scalar_mul(normcat[:, 0:8], normcat[:, 0:8], 4.0), r=["normcat"], w=["normcat_a"])
        T.add("dve", lambda e: e.tensor_scalar_mul(normcat[:, 8:16], normcat[:, 8:16], float(np.sqrt(2048.0) / 2.0)),
              r=["normcat"], w=["normcat_b"])

        T.add("pool", lambda e: e.dma_start(out=cbrow, in_=dr["conv_b"][0:1024].rearrange("(o n) -> o n", o=1)), w=["cbrow0"], stream="const2")
        T.add("pool", lambda e: e.tensor_scalar(cbrow, cbrow, 0.5, None, ALU.mult), r=["cbrow0", "oh48"], w=["cbrow"])
        for t_ in range(NPE_R * 4):
            for w_ in range(4):
                T.add("dve" if (t_ * 4 + w_) % 2 == 0 else "pool",
                      lambda e, t_=t_, w_=w_: e.tensor_scalar(dgc4[:, t_, w_, :], identf[:], cw[:, t_, w_:w_ + 1], None, ALU.mult),
                      r=["cwb", "identf"], w=["dgc_%d_%d" % (t_, w_)])
        dgc_keys = ["dgc_%d_%d" % (t_, w_) for t_ in range(NPE_R * 4) for w_ in range(4)]

        W_PRE = [(512, 1024), (1024, 2048), (2048, 2056), (5128, 6152), (6152, 6680)]
        W_FULL = [(0, 512), (2056, 3080), (3080, 4104), (4104, 5128)]
        wkeys = {}
        piece = [0]
        NSTG = 8
        wout_f = wout[:].bitcast(F32)

        def wpiece(src_ap, dst_ap, scale_ap, scale_key, ncols, stg_ap, stg_key, stream, wkey, extra_w=()):
            i = piece[0]
            piece[0] += 1
            T.add("sp", lambda e: e.dma_start(out=stg_ap[:, 0:ncols], in_=src_ap), w=[stg_key], stream=stream)
            if i % 2 == 1:
                T.add("act", lambda e: e.activation(out=dst_ap, in_=stg_ap[:, 0:ncols], func=AF.Copy, scale=scale_ap),
                      r=[stg_key, scale_key], w=[wkey] + list(extra_w))
            else:
                T.add("dve", lambda e: e.tensor_scalar_mul(dst_ap, stg_ap[:, 0:ncols], scale_ap),
                      r=[stg_key, scale_key], w=[wkey] + list(extra_w))

        for (c0, c1) in W_PRE + W_FULL:
            wkeys[(c0, c1)] = []
            for k in range(8):
                sl_ = piece[0] % NSTG
                wk = "w_%d_%d" % (c0, k)
                wkeys[(c0, c1)].append(wk)
                wpiece(dr["w_in"][k * 128:(k + 1) * 128, c0:c1], win3[:, k, c0:c1], normw_fm[:, k:k + 1], "normw2", c1 - c0,
                       wout_f[:, sl_ * 1024:(sl_ + 1) * 1024], "wstg%d" % sl_, "stg%d" % sl_, wk)

        def wkeys_for(c0, n):
            out = []
            for (a0, a1), ks in wkeys.items():
                if a0 < c0 + n and c0 < a1:
                    out += ks
            return out

        def load_wout():
            for kc in range(16):
                key = "normcat_a" if kc < 8 else "normcat_b"
                slot = kc % 2
                wpiece(dr["w_out"][kc * 128:(kc + 1) * 128, :], wout3[:, kc, :], normcat[:, kc:kc + 1], key, 1024,
                       xbuf[slot], "xbuf%d" % slot, "xin%d" % slot, "wout_%d" % kc,
                       extra_w=["wstg%d" % j for j in range(NSTG)])
        wout_keys = ["wout_%d" % kc for kc in range(16)]

        def load_x(src, ci, slot):
            T.add("sp", lambda e: e.dma_start(out=xbuf[slot][:], in_=src[ci * L:(ci + 1) * L, :]),
                  w=["xbuf%d" % slot], stream="xin%d" % slot)

        chunk_list = [("pre", i) for i in range(n_pre)] + [("full", i) for i in range(n_full)]

        def src_of(kind):
            return dr["xpre"] if kind == "pre" else dr["x"]

        import os as _os
        _DBL = set((_os.environ.get("K_DBL") or "").split(",")) - {""}
        _PERSIST = {"Cst", "hst", "hst0", "hst1", "hbf0", "hbf1", "mst", "xbcT_carry", "identb", "identf", "tri", "maskb", "onesf",
                    "m05", "oh48", "cwb", "arep", "drep", "dtb", "bias8a", "bias8b", "finalw2", "flag", "out_dram", "xbuf0", "xbuf1"}
        _par = [0]

        def _km(k):
            if not isinstance(k, str) or Tracker._bank(k) is not None or k in _PERSIST or k.startswith("w_") or k.startswith("wout_"):
                return k
            if "ALL" in _DBL or k in _DBL or k.rstrip("0123456789_") in _DBL:
                return "%s@%d" % (k, _par[0])
            return k

        def TA(eng, fn, r=(), w=(), stream=None):
            return T.add(eng, fn, r=[_km(k) for k in r], w=[_km(k) for k in w], stream=stream)

        def chunk(gi):
            kind, ci = chunk_list[gi]
            _par[0] = gi % 2
            full = kind == "full"
            slot = gi % 2
            xb = xbuf[slot]
            xk = "xbuf%d" % slot
            if gi + 1 < len(chunk_list) and not (n_pre and gi + 1 == n_pre):
                nk, nci = chunk_list[gi + 1]
                load_x(src_of(nk), nci, (gi + 1) % 2)
            TA("act", lambda e: e.activation(out=junk[:, 0:1].broadcast_to([128, 1024]), in_=xb[:], func=AF.Square, accum_out=ssq_x), r=[xk], w=["ssq_x"])
            TA("pool", lambda e: e.tensor_scalar(tx1, ssq_x, 1024.0 * EPS, None, ALU.add), r=["ssq_x"], w=["tx1"])
            TA("pool", lambda e: e.tensor_tensor(rstd_x, tx1, m05[:, 0:1], ALU.pow), r=["tx1", "m05"], w=["rstd_x"])
            TA("dve", lambda e: e.tensor_scalar_mul(xn[:], xb[:], rstd_x), r=[xk, "rstd_x"], w=["xn"])
            p, pk = next_pT("A")

            def tr_x(e, p=p):
                for k in range(8):
                    ins = e.transpose(p[:, k * 128:(k + 1) * 128], xn[:, k * 128:(k + 1) * 128], identb[:])
                return ins
            TA("pe", tr_x, r=["xn", "identb"], w=[pk])
            TA("act", lambda e, p=p: e.activation(out=xnT[:], in_=p[:, 0:1024], func=AF.Copy), r=[pk], w=["xnT"])

            def proj(name, c0, n, out_ap, okey, extra_r=()):
                def f(e):
                    for k in range(8):
                        ins = e.matmul(out_ap, xnT3[:, k, :], win3[:, k, c0:c0 + n], start=(k == 0), stop=(k == 7))
                    return ins
                TA("pe", f, r=["xnT"] + wkeys_for(c0, n) + list(extra_r), w=[okey])

            for name, c0, n in PROJ_GROUPS:
                if not full and name not in STATE_ONLY:
                    continue
                if name == "if":
                    b, bk = next_bank("A")
                    proj(name, c0, n, b[:, 0:8], bk)
                    TA("dve", lambda e, b=b: e.tensor_tensor(g8, b[:, 0:8], bias8, ALU.add), r=[bk, "bias8a", "bias8b"], w=["g8"])
                    TA("act", lambda e: e.activation(out=t8, in_=g8, func=AF.Tanh, scale=1.0 / 15.0), r=["g8"], w=["t8"])
                    TA("act", lambda e: e.activation(out=e4, in_=t8[:, 4:8], func=AF.Exp, scale=-15.0), r=["t8"], w=["e4"])
                    TA("act", lambda e: e.activation(out=nlf, in_=e4, func=AF.Ln, bias=1.0), r=["e4"], w=["nlf"])
                    TA("dve", lambda e: e.tensor_scalar_mul(li, t8[:, 0:4], 15.0), r=["t8"], w=["li"])
                elif name == "dt":
                    b, bk = next_bank("A")
                    proj(name, c0, n, b[:, 0:16], bk)
                    TA("dve", lambda e, b=b: e.tensor_tensor(dtp, b[:, 0:16], dtb, ALU.add), r=[bk, "dtb"], w=["dtp"])
                    TA("act", lambda e: e.activation(out=edt, in_=dtp, func=AF.Exp), r=["dtp"], w=["edt"])
                    TA("act", lambda e: e.activation(out=dt16, in_=edt, func=AF.Ln, bias=1.0), r=["edt"], w=["dt16"])
                    TA("dve", lambda e: e.tensor_tensor(a_tm, dt16, arep, ALU.mult), r=["dt16", "arep"], w=["a_tm"])
                else:
                    b, bk = next_bank("A")
                    proj(name, c0, n, b[:, 0:n], bk)
                    if name == "k":
                        TA("act", lambda e, b=b: e.activation(out=k_tm[:], in_=b[:, 0:512], func=AF.Copy), r=[bk], w=["k_tm"])
                    elif name in ("v0", "v1"):
                        h0 = 0 if name == "v0" else 2
                        TA("act", lambda e, b=b, h0=h0: e.activation(
                            out=vaug3[:, h0:h0 + 2, 0:256], in_=b[:, 0:512].rearrange("p (h c) -> p h c", h=2), func=AF.Copy),
                            r=[bk, "vaug"], w=["vaug_%d" % h0])
                    elif name.startswith("xbc"):
                        j = int(name[3])
                        eng = "act"
                        if eng == "act":
                            TA("act", lambda e, b=b, j=j: e.activation(out=xbc_tm[:, j * 512:(j + 1) * 512], in_=b[:, 0:512], func=AF.Copy),
                                  r=[bk], w=["xbc_tm%d" % j])
                        else:
                            TA("dve", lambda e, b=b, j=j: e.tensor_copy(xbc_tm[:, j * 512:(j + 1) * 512], b[:, 0:512]),
                                  r=[bk], w=["xbc_tm%d" % j])
                    elif name == "q":
                        TA("act", lambda e, b=b: e.activation(out=q_tm[:], in_=b[:, 0:512], func=AF.Copy, scale=float(128 ** -0.5)),
                              r=[bk], w=["q_tm"])
                    elif name in ("zm0", "zm1"):
                        hh = int(name[2])
                        sl = slice(hh * 512, (hh + 1) * 512)
                        TA("act", lambda e, b=b, sl=sl: e.activation(out=tz[:, sl], in_=b[:, 0:512], func=AF.Tanh, scale=0.5),
                              r=[bk], w=["tz%d" % hh])
                        TA("dve", lambda e, b=b, sl=sl: e.scalar_tensor_tensor(g1[:, sl], tz[:, sl], 1.0, b[:, 0:512], ALU.add, ALU.mult),
                              r=[bk, "tz%d" % hh], w=["g1_%d" % hh])
                    elif name in ("o0", "o1"):
                        hh = int(name[1])
                        sl = slice(hh * 512, (hh + 1) * 512)
                        TA("act", lambda e, b=b, sl=sl: e.activation(out=tz[:, sl], in_=b[:, 0:512], func=AF.Tanh, scale=0.5),
                              r=[bk], w=["tz%d" % hh])
                        TA("dve", lambda e, sl=sl: e.scalar_tensor_tensor(g1[:, sl], tz[:, sl], 1.0, g1[:, sl], ALU.add, ALU.mult),
                              r=["tz%d" % hh, "g1_%d" % hh], w=["g1_%d" % hh])
                    elif name in ("zs0", "zs1"):
                        hh = int(name[2])
                        sl = slice(hh * 512, (hh + 1) * 512)
                        TA("act", lambda e, b=b, sl=sl: e.activation(out=tz[:, sl], in_=b[:, 0:512], func=AF.Tanh, scale=0.5),
                              r=[bk], w=["tz%d" % hh])
                        TA("dve", lambda e, b=b, sl=sl: e.scalar_tensor_tensor(gs[:, sl], tz[:, sl], 1.0, b[:, 0:512], ALU.add, ALU.mult),
                              r=[bk, "tz%d" % hh], w=["gs_%d" % hh])

            def tr4(src, dst, skey, dkey):
                p, pk = next_pT("A")

                def f(e, p=p):
                    for h in range(4):
                        ins = e.transpose(p[:, h * 128:(h + 1) * 128], src[:, h * 128:(h + 1) * 128], identb[:])
                    return ins
                TA("pe", f, r=[skey, "identb"], w=[pk])
                TA("act", lambda e, p=p: e.activation(out=dst[:], in_=p[:, 0:512], func=AF.Copy), r=[pk], w=[dkey])

            if full:
                tr4(k_tm, kT, "k_tm", "kT")
                tr4(q_tm, qT, "q_tm", "qT")
            for half, (t0, nt) in enumerate([(0, 8), (8, 4)]):
                p, pk = next_pT("A")

                def f(e, p=p, t0=t0, nt=nt):
                    for t in range(nt):
                        ins = e.transpose(p[:, t * 128:(t + 1) * 128], xbc_tm[:, (t0 + t) * 128:(t0 + t + 1) * 128], identb[:])
                    return ins
                TA("pe", f, r=["xbc_tm0", "xbc_tm1", "xbc_tm2", "identb"], w=[pk])
                TA("act", lambda e, p=p, t0=t0, nt=nt: e.activation(
                    out=xbcT3[:, t0:t0 + nt, 4:132], in_=p[:, 0:nt * 128].rearrange("p (t c) -> p t c", t=nt), func=AF.Copy),
                    r=[pk, "xbcT_carry"], w=["xbcT_%d" % half])

            TA("pe", lambda e: e.matmul(pm[:, 24:28], tri[:], nlf, start=True, stop=True), r=["tri", "nlf"], w=["pm_nb"])

            def f_ugm(e):
                e.matmul(pm[0:4, 128:256], li, identf[:], start=True, stop=False)
                return e.matmul(pm[0:4, 128:256], nlf, tri[:], start=False, stop=True)
            TA("pe", f_ugm, r=["li", "nlf", "identf", "tri"], w=["pm_ugm"])
            TA("pe", lambda e: e.matmul(pm[0:4, 120:121], nlf, onesf[:, 0:1], start=True, stop=True), r=["nlf", "onesf"], w=["pm_nbl"])
            TA("dve", lambda e: e.reduce_max(umax, pm[0:4, 128:256], AX.X), r=["pm_ugm"], w=["umax"])
            TA("dve", lambda e: e.tensor_tensor(Rg, umax, mst, ALU.max), r=["umax", "mst"], w=["Rg"])
            TA("dve", lambda e: e.tensor_tensor(dg, mst, Rg, ALU.subtract), r=["mst", "Rg"], w=["dg"])
            TA("act", lambda e: e.activation(out=soldg, in_=dg, func=AF.Exp), r=["dg"], w=["soldg"])
            TA("dve", lambda e: e.tensor_tensor(mst, Rg, pm[0:4, 120:121], ALU.subtract), r=["Rg", "pm_nbl", "dg"], w=["mst"])
            TA("dve", lambda e: e.tensor_scalar_mul(Dg[:, 0:4], identf[0:4, 0:4], Rg), r=["Rg", "identf"], w=["Dg_a"])
            TA("dve", lambda e: e.tensor_scalar_mul(Dg[:, 4:8], identf[0:4, 0:4], soldg), r=["soldg", "identf"], w=["Dg_b"])
            TA("pe", lambda e: e.matmul(pm[:, 28:36], onesf[0:4, :], Dg, start=True, stop=True), r=["onesf", "Dg_a", "Dg_b"], w=["pm_rs"])
            TA("dve", lambda e: e.tensor_tensor(T1[:, 0:4], li, pm[:, 24:28], ALU.add), r=["li", "pm_nb"], w=["T1a"])
            TA("dve", lambda e: e.tensor_copy(T1[:, 4:8], pm[:, 24:28]), r=["pm_nb"], w=["T1b"])
            TA("dve", lambda e: e.tensor_tensor(
                T2.rearrange("p (a h) -> p a h", a=2), T1.rearrange("p (a h) -> p a h", a=2),
                pm[:, 28:32].unsqueeze(1).broadcast_to([128, 2, 4]), ALU.subtract), r=["T1a", "T1b", "pm_rs"], w=["T2"])
            TA("act", lambda e: e.activation(out=wf, in_=T2, func=AF.Exp), r=["T2"], w=["wf"])
            TA("dve", lambda e: e.tensor_copy(soldb, pm[:, 32:36]), r=["pm_rs"], w=["soldb"])

            TA("pe", lambda e: e.matmul(pm[:, 40:56], tri[:], a_tm, start=True, stop=True), r=["tri", "a_tm"], w=["pm_acum"])
            TA("pe", lambda e: e.matmul(pm[:, 56:72], onesf[:], a_tm, start=True, stop=True), r=["onesf", "a_tm"], w=["pm_alast"])
            TA("dve", lambda e: e.tensor_copy(acum_sb, pm[:, 40:56]), r=["pm_acum"], w=["acum_sb"])
            TA("act", lambda e: e.activation(out=ea, in_=pm[:, 40:56], func=AF.Exp), r=["pm_acum"], w=["ea"])
            TA("dve", lambda e: e.tensor_tensor(dEa, pm[:, 56:72], acum_sb, ALU.subtract), r=["pm_alast", "acum_sb"], w=["dEa"])
            TA("act", lambda e: e.activation(out=dE, in_=dEa, func=AF.Exp), r=["dEa"], w=["dE"])
            TA("act", lambda e: e.activation(out=cdb, in_=pm[:, 56:72], func=AF.Exp), r=["pm_alast"], w=["cdb"])
            if full:
                TA("dve", lambda e: e.tensor_copy(a3[:, 0:16], acum_sb), r=["acum_sb"], w=["a3_0"])
                TA("dve", lambda e: e.tensor_tensor(r1, acum_sb, a3[:, 0:16], ALU.subtract), r=["acum_sb", "a3_0"], w=["r1"])
                TA("dve", lambda e: e.tensor_copy(a3[:, 16:32], r1), r=["r1"], w=["a3_1"])
                TA("dve", lambda e: e.tensor_tensor(r2, r1, a3[:, 16:32], ALU.subtract), r=["r1", "a3_1"], w=["r2"])
                TA("dve", lambda e: e.tensor_copy(a3[:, 32:48], r2), r=["r2"], w=["a3_2"])
                p, pk = next_pT()
                TA("pe", lambda e, p=p: e.transpose(p[0:48, 0:128], a3[:, 0:48], identb[:]), r=["a3_0", "a3_1", "a3_2", "identb"], w=[pk])
                TA("dve", lambda e, p=p: e.tensor_copy(A48[:], p[0:48, 0:128]), r=[pk], w=["A48"])

            for rnd in range(NPE_R):
                cvb, cvk = next_bank()

                def f_cv(e, rnd=rnd, cvb=cvb):
                    for tt in range(4):
                        t = rnd * 4 + tt
                        o_ = cvb[:, tt * 128:(tt + 1) * 128]
                        for w_ in range(4):
                            e.matmul(o_, dgc4[:, t, w_, :], xbcT3[:, t, 1 + w_:129 + w_], start=(w_ == 0), stop=False)
                        ins = e.matmul(o_, cbrow[:, t * 128:(t + 1) * 128], onesrow, start=False, stop=True)
                    return ins
                TA("pe", f_cv, r=["xbcT_0", "xbcT_carry", "cbrow", "oh48"] + dgc_keys, w=[cvk])
                ct = cth[0]
                TA("act", lambda e, cvb=cvb, ct=ct: e.activation(out=ct[:], in_=cvb[:, 0:512], func=AF.Tanh), r=[cvk], w=["cth0"])
                TA("dve", lambda e, cvb=cvb, ct=ct, rnd=rnd: e.scalar_tensor_tensor(
                    xcT[:, rnd * 512:(rnd + 1) * 512], ct[:], 1.0, cvb[:, 0:512], ALU.add, ALU.mult),
                    r=["cth0", cvk], w=["xcT_%d" % rnd])
            for rnd in range(NPE_R, 3):
                ca = cacc[0]
                ct = cth[0]
                cak = "cacc0"
                ctk = "cth0"
                src_key = "xbcT_0" if rnd < 2 else "xbcT_1"
                for tt in range(4):
                    t = rnd * 4 + tt
                    if not full and t >= 10:
                        continue
                    cslice = ca[:, tt * 128:(tt + 1) * 128]
                    ckey = cak + "_%d" % tt
                    TA("dve", lambda e, t=t, cslice=cslice: e.tensor_scalar(
                        cslice, xbcT3[:, t, 1:129], cw[:, t, 0:1], cb[:, t:t + 1], ALU.mult, ALU.add),
                        r=[src_key, "xbcT_carry", "cwb"], w=[ckey])
                    for w_ in range(1, 4):
                        TA("dve", lambda e, t=t, cslice=cslice, w_=w_: e.scalar_tensor_tensor(
                            cslice, xbcT3[:, t, 1 + w_:129 + w_], cw[:, t, w_:w_ + 1], cslice, ALU.mult, ALU.add),
                            r=[src_key, "xbcT_carry", "cwb", ckey], w=[ckey])
                nv = 512 if (full or rnd < 2) else 256
                TA("act", lambda e, ca=ca, ct=ct, nv=nv: e.activation(out=ct[:, 0:nv], in_=ca[:, 0:nv], func=AF.Tanh),
                   r=[cak + "_%d" % i for i in range(4)], w=[ctk])
                TA("dve", lambda e, ca=ca, ct=ct, rnd=rnd, nv=nv: e.scalar_tensor_tensor(
                    xcT[:, rnd * 512:rnd * 512 + nv], ct[:, 0:nv], 1.0, ca[:, 0:nv], ALU.add, ALU.mult),
                    r=[ctk] + [cak + "_%d" % i for i in range(4)], w=["xcT_%d" % rnd])
            TA("pool", lambda e: e.tensor_copy(xbcT3[:, :, 1:4], xbcT3[:, :, 129:132]), r=["xbcT_0", "xbcT_1"], w=["xbcT_carry"])

            TA("pool", lambda e: e.tensor_tensor(
                kp_tm[:].rearrange("p (h d) -> p h d", h=4), k_tm[:].rearrange("p (h d) -> p h d", h=4),
                wf[:, 0:4].unsqueeze(2).broadcast_to([128, 4, 128]), ALU.mult), r=["k_tm", "wf"], w=["kp_tm"])
            TA("pool", lambda e: e.tensor_tensor(Cst3[:, :, 0:257], Cst3[:, :, 0:257],
                                                    soldb.unsqueeze(2).broadcast_to([128, 4, 257]), ALU.mult),
                  r=["Cst", "soldb"], w=["Cst"])
            if full:
                TA("act", lambda e: e.activation(out=Cbf3[:, :, 0:257], in_=Cst3[:, :, 0:257], func=AF.Copy), r=["Cst"], w=["Cbf"])
                sb_, sbk = next_bank()

                def f_st(e, sb_=sb_):
                    for h in range(4):
                        ins = e.matmul(sb_[:, h * 128:(h + 1) * 128], kT[:, h * 128:(h + 1) * 128], qT[:, h * 128:(h + 1) * 128],
                                       start=True, stop=True)
                    return ins
                TA("pe", f_st, r=["kT", "qT"], w=[sbk])
                for h in range(4):
                    TA("dve", lambda e, h=h, sb_=sb_: e.scalar_tensor_tensor(
                        Pm3[:, h, :], sb_[:, h * 128:(h + 1) * 128], wf[:, h:h + 1], maskb[:], ALU.mult, ALU.mult),
                        r=[sbk, "wf", "maskb"], w=["Pm_%d" % h])
                brs = []
                for hp in range(2):
                    bb, bbk = next_bank()
                    for h in (2 * hp, 2 * hp + 1):
                        brs.append((bb, bbk, (h % 2) * 256))

                    def f_br(e, hp=hp, bb=bb):
                        for h in (2 * hp, 2 * hp + 1):
                            o_ = (h % 2) * 256
                            e.matmul(bb[:, o_:o_ + 256], Pm3[:, h, :], vaug3[:, h, 0:256], start=True, stop=False)
                            ins = e.matmul(bb[:, o_:o_ + 256], qT[:, h * 128:(h + 1) * 128], Cbf3[:, h, 0:256], start=False, stop=True)
                        return ins
                    TA("pe", f_br, r=["Pm_%d" % (2 * hp), "Pm_%d" % (2 * hp + 1), "vaug_0", "vaug_2", "qT", "Cbf"], w=[bbk])
                    for h in (2 * hp, 2 * hp + 1):
                        o_ = (h % 2) * 256
                        TA("act", lambda e, h=h, bb=bb, o_=o_: e.activation(out=junk[:, 0:1].broadcast_to([128, 256]), in_=bb[:, o_:o_ + 256], func=AF.Square,
                                                                            accum_out=ssqr[:, h:h + 1]), r=[bbk], w=["ssqr_%d" % h])

                def f_den(e):
                    for h in range(4):
                        e.matmul(pm[:, 64 + h:65 + h], Pm3[:, h, :], vaug3[:, h, 256:257], start=True, stop=False)
                        ins = e.matmul(pm[:, 64 + h:65 + h], qT[:, h * 128:(h + 1) * 128], Cbf3[:, h, 256:257], start=False, stop=True)
                    return ins
                TA("pe", f_den, r=["Pm_0", "Pm_1", "Pm_2", "Pm_3", "vaug_0", "vaug_2", "qT", "Cbf"], w=["pm_den"])
                TA("act", lambda e: e.activation(out=absden, in_=pm[:, 64:68], func=AF.Abs), r=["pm_den"], w=["absden"])
                hk = ["absden"]
                sk = ["ssqr_%d" % h for h in range(4)]
                TA("dve", lambda e: e.tensor_tensor(dn4, absden, wf[:, 4:8], ALU.max), r=hk + ["wf"], w=["dn4"])
                TA("dve", lambda e: e.reciprocal(rd4, dn4), r=["dn4"], w=["rd4"])
                TA("dve", lambda e: e.tensor_tensor(t4, rd4, rd4, ALU.mult), r=["rd4"], w=["t4"])
                TA("dve", lambda e: e.tensor_tensor(t4, t4, ssqr, ALU.mult), r=["t4"] + sk, w=["t4"])
                TA("pool", lambda e: e.tensor_scalar(t4, t4, 256.0 * EPS, None, ALU.add), r=["t4"], w=["t4"])
                TA("pool", lambda e: e.tensor_tensor(rs4, t4, m05[:, 0:4], ALU.pow), r=["t4", "m05"], w=["rs4"])
                TA("dve", lambda e: e.tensor_tensor(sc4, rd4, rs4, ALU.mult), r=["rd4", "rs4"], w=["sc4"])
                for h in range(4):
                    bb, bbk, o_ = brs[h]
                    TA("dve", lambda e, h=h, bb=bb, o_=o_: e.scalar_tensor_tensor(
                        mix[:, h * 256:(h + 1) * 256], bb[:, o_:o_ + 256], sc4[:, h:h + 1], g1[:, h * 256:(h + 1) * 256], ALU.mult, ALU.mult),
                        r=[bbk, "sc4", "g1_%d" % (h // 2)], w=["mix_m%d" % h])
            for h in range(4):
                cb_, cbk = next_bank()
                TA("pe", lambda e, h=h, cb_=cb_: e.matmul(cb_[:, 0:257], kp_tm[:, h * 128:(h + 1) * 128], vaug3[:, h, 0:257],
                                                             start=True, stop=True), r=["kp_tm", "vaug_0", "vaug_2"], w=[cbk])
                TA("dve", lambda e, h=h, cb_=cb_: e.tensor_tensor(Cst3[:, h, 0:257], Cst3[:, h, 0:257], cb_[:, 0:257], ALU.add),
                      r=[cbk, "Cst", "Cbf"], w=["Cst"])

            p, pk = next_pT()

            def tr_xc(e, p=p):
                for t in range(8):
                    ins = e.transpose(p[:, t * 128:(t + 1) * 128], xcT3[:, t, :], identb[:])
                return ins
            TA("pe", tr_xc, r=["xcT_0", "xcT_1", "identb"], w=[pk])
            TA("act", lambda e, p=p: e.activation(out=x_tm[:], in_=p[:, 0:1024], func=AF.Copy), r=[pk], w=["x_tm"])
            p, pk = next_pT()

            def tr_B(e, p=p):
                for t in range(2):
                    ins = e.transpose(p[:, t * 128:(t + 1) * 128], xcT3[:, 8 + t, :], identb[:])
                return ins
            TA("pe", tr_B, r=["xcT_2", "identb"], w=[pk])
            TA("act", lambda e, p=p: e.activation(out=B_tm[:], in_=p[:, 0:256], func=AF.Copy), r=[pk], w=["B_tm"])
            TA("pool", lambda e: e.tensor_tensor(
                xdt[:].rearrange("p (r c) -> p r c", r=16), x_tm[:].rearrange("p (r c) -> p r c", r=16),
                dt16.unsqueeze(2).broadcast_to([128, 16, 64]), ALU.mult), r=["x_tm", "dt16"], w=["xdt"])
            TA("pool", lambda e: e.tensor_tensor(
                xde[:].rearrange("p (r c) -> p r c", r=16), xdt[:].rearrange("p (r c) -> p r c", r=16),
                dE.unsqueeze(2).broadcast_to([128, 16, 64]), ALU.mult), r=["xdt", "dE"], w=["xde"])

            if full:
                TA("pe", lambda e: (e.matmul(pm[:, 256:384], xcT3[:, 8, :], xcT3[:, 10, :], start=True, stop=True),
                                       e.matmul(pm[:, 384:512], xcT3[:, 9, :], xcT3[:, 11, :], start=True, stop=True))[1],
                      r=["xcT_2"], w=["pm_cb"])
                TA("dve", lambda e: e.tensor_tensor(CBm3, pm[:, 256:512].rearrange("p (g l) -> p g l", g=2),
                                                       maskb[:].unsqueeze(1).broadcast_to([128, 2, 128]), ALU.mult),
                      r=["pm_cb", "maskb"], w=["CBm"])
                for bq in range(4):
                    ab, abk = next_bank()

                    def f_arg(e, bq=bq, ab=ab):
                        for rr in range(4):
                            ins = e.matmul(ab[:, rr * 128:(rr + 1) * 128], oh48_3[:, bq * 4 + rr, :], A48[:], start=True, stop=True)
                        return ins
                    TA("pe", f_arg, r=["oh48", "A48"], w=[abk])

                    def f_relu(e, bq=bq, ab=ab):
                        for rr in range(4):
                            hd = bq * 4 + rr
                            ins = e.activation(out=rl[:, rr * 128:(rr + 1) * 128], in_=ab[:, rr * 128:(rr + 1) * 128],
                                               func=AF.Relu, bias=acum_sb[:, hd:hd + 1], scale=-1.0)
                        return ins
                    TA("act", f_relu, r=[abk, "acum_sb"], w=["rl"])
                    TA("act", lambda e, bq=bq: e.activation(out=dec[:, bq * 512:(bq + 1) * 512], in_=rl[:], func=AF.Exp, scale=-1.0),
                          r=["rl"], w=["dec_%d" % bq])
                    g = bq // 2
                    TA("pool", lambda e, bq=bq, g=g: e.tensor_tensor(
                        dec3[:, bq * 4:bq * 4 + 4, :], dec3[:, bq * 4:bq * 4 + 4, :],
                        CBm3[:, g:g + 1, :].broadcast_to([128, 4, 128]), ALU.mult), r=["dec_%d" % bq, "CBm"], w=["dec_%d" % bq])
                TA("pool", lambda e: e.tensor_tensor(
                    yo[:].rearrange("p (r c) -> p r c", r=16), x_tm[:].rearrange("p (r c) -> p r c", r=16),
                    drep.unsqueeze(2).broadcast_to([128, 16, 64]), ALU.mult), r=["x_tm", "drep"], w=["yo_0", "yo_1"])
                for g in range(2):
                    yd, ydk = next_bank()

                    def f_yd(e, g=g, yd=yd):
                        for rr in range(8):
                            hd = g * 8 + rr
                            ins = e.matmul(yd[:, rr * 64:(rr + 1) * 64], dec3[:, hd, :], xdt[:, hd * 64:(hd + 1) * 64], start=True, stop=True)
                        return ins
                    TA("pe", f_yd, r=["dec_%d" % (2 * g), "dec_%d" % (2 * g + 1), "xdt"], w=[ydk])
                    yf, yfk = next_bank()
                    TA("pe", lambda e, g=g, yf=yf: e.matmul(yf[:, 0:512], xcT3[:, 10 + g, :], hbf[:, g * 512:(g + 1) * 512], start=True, stop=True),
                          r=["xcT_2", "hbf%d" % g], w=[yfk])
                    gsl = slice(g * 512, (g + 1) * 512)
                    TA("dve", lambda e, g=g, yf=yf: e.tensor_tensor(
                        ytmp[:].rearrange("p (r c) -> p r c", r=8), yf[:, 0:512].rearrange("p (r c) -> p r c", r=8),
                        ea[:, g * 8:(g + 1) * 8].unsqueeze(2).broadcast_to([128, 8, 64]), ALU.mult), r=[yfk, "ea"], w=["ytmp"])
                    TA("dve", lambda e, yd=yd: e.tensor_tensor(ytmp[:], ytmp[:], yd[:, 0:512], ALU.add), r=[ydk, "ytmp"], w=["ytmp"])
                    TA("pool", lambda e, gsl=gsl: e.tensor_tensor(yo[:, gsl], yo[:, gsl], ytmp[:], ALU.add), r=["ytmp", "yo_%d" % g], w=["yo_%d" % g])
                    TA("pool", lambda e, gsl=gsl: e.tensor_tensor(yo[:, gsl], yo[:, gsl], gs[:, gsl], ALU.mult),
                          r=["yo_%d" % g, "gs_%d" % g], w=["yo_%d" % g])
                    TA("act", lambda e, g=g, gsl=gsl: e.activation(out=junk[:, 0:1].broadcast_to([128, 512]), in_=yo[:, gsl], func=AF.Square, accum_out=ssq_s[:, g:g + 1]),
                          r=["yo_%d" % g], w=["ssq_s%d" % g])
                TA("pool", lambda e: e.tensor_scalar(ts2, ssq_s, 2048.0 * EPS, None, ALU.add), r=["ssq_s0", "ssq_s1"], w=["ts2"])
                TA("pool", lambda e: e.tensor_tensor(rstd_s, ts2, m05[:, 0:2], ALU.pow), r=["ts2", "m05"], w=["rstd_s"])
                for g in range(2):
                    gsl = slice(g * 512, (g + 1) * 512)
                    TA("dve", lambda e, g=g, gsl=gsl: e.tensor_scalar_mul(mix[:, 1024 + g * 512:1024 + (g + 1) * 512], yo[:, gsl], rstd_s[:, g:g + 1]),
                          r=["yo_%d" % g, "rstd_s"], w=["mix_s%d" % g])
            for g in range(2):
                st, stk = next_bank()
                gsl = slice(g * 512, (g + 1) * 512)
                TA("pe", lambda e, g=g, st=st, gsl=gsl: e.matmul(st[:, 0:512], B_tm[:, g * 128:(g + 1) * 128], xde[:, gsl], start=True, stop=True),
                      r=["B_tm", "xde"], w=[stk])
                TA("pool", lambda e, g=g, gsl=gsl: e.tensor_tensor(
                    hst[:, gsl].rearrange("p (r c) -> p r c", r=8), hst[:, gsl].rearrange("p (r c) -> p r c", r=8),
                    cdb[:, g * 8:(g + 1) * 8].unsqueeze(2).broadcast_to([128, 8, 64]), ALU.mult), r=["hst", "cdb", "hbf%d" % g], w=["hst%d" % g])
                TA("dve", lambda e, st=st, gsl=gsl: e.tensor_tensor(hst[:, gsl], hst[:, gsl], st[:, 0:512], ALU.add), r=[stk, "hst%d" % g], w=["hst%d" % g])
                TA("act", lambda e, gsl=gsl: e.activation(out=hbf[:, gsl], in_=hst[:, gsl], func=AF.Copy), r=["hst%d" % g], w=["hbf%d" % g])

            if not full:
                return
            mkeys = ["mix_m%d" % h for h in range(4)] + ["mix_s0", "mix_s1"]
            for half in range(2):
                p, pk = next_pT()

                def tr_m(e, p=p, half=half):
                    for t in range(8):
                        kc = half * 8 + t
                        ins = e.transpose(p[:, t * 128:(t + 1) * 128], mix[:, kc * 128:(kc + 1) * 128], identb[:])
                    return ins
                TA("pe", tr_m, r=mkeys + ["identb"], w=[pk])
                if half == 0:
                    TA("act", lambda e, p=p: e.activation(out=mixT[:, 0:1024], in_=p[:, 0:1024], func=AF.Copy), r=[pk], w=["mixT_0"])
                else:
                    TA("dve", lambda e, p=p: e.tensor_copy(mixT[:, 1024:2048], p[:, 0:1024]), r=[pk], w=["mixT_1"])
            for half in range(2):
                ob, obk = next_bank()

                def f_o(e, ob=ob, half=half):
                    for kc in range(16):
                        ins = e.matmul(ob[:, 0:512], mixT3[:, kc, :], wout3[:, kc, half * 512:(half + 1) * 512], start=(kc == 0), stop=(kc == 15))
                    return ins
                TA("pe", f_o, r=["mixT_0", "mixT_1"] + wout_keys, w=[obk])
                hsl = slice(half * 512, (half + 1) * 512)
                TA("dve", lambda e, ob=ob, hsl=hsl: e.tensor_tensor(xb[:, hsl], xb[:, hsl], ob[:, 0:512], ALU.add), r=[obk, xk], w=[xk])
            TA("act", lambda e: e.activation(out=junk[:, 0:1].broadcast_to([128, 1024]), in_=xb[:], func=AF.Square, accum_out=ssq_o), r=[xk], w=["ssq_o"])
            TA("pool", lambda e: e.tensor_scalar(to1, ssq_o, 1024.0 * EPS, None, ALU.add), r=["ssq_o"], w=["to1"])
            TA("pool", lambda e: e.tensor_tensor(rstd_o, to1, m05[:, 0:1], ALU.pow), r=["to1", "m05"], w=["rstd_o"])
            TA("dve", lambda e: e.scalar_tensor_tensor(xb[:], xb[:], rstd_o, finalw[:], ALU.mult, ALU.mult), r=[xk, "rstd_o", "finalw2"], w=[xk])
            TA("sp", lambda e: e.dma_start(out=out_d[ci * L:(ci + 1) * L, :], in_=xb[:]), r=[xk], w=["out_dram"], stream="out%d" % slot)

        k0, c0_ = chunk_list[0]
        if n_pre == 0:
            load_wout()
        load_x(src_of(k0), c0_, 0)
        for gi in range(len(chunk_list)):
            if n_pre and gi == n_pre:
                load_wout()
                load_x(src_of("full"), 0, gi % 2)
            chunk(gi)
            if n_pre and gi == n_pre - 1:
                T.add("dve", lambda e: e.tensor_scalar_mul(Cst[:], Cst[:], flag), r=["Cst", "flag"], w=["Cst"])
                T.add("dve", lambda e: e.tensor_scalar_mul(hst[:], hst[:], flag), r=["hst0", "hst1", "flag"], w=["hst0", "hst1", "hst"])
                T.add("dve", lambda e: e.tensor_scalar_mul(hbf[:], hbf[:], flag), r=["hbf0", "hbf1", "flag"], w=["hbf0", "hbf1"])
                T.add("dve", lambda e: e.tensor_scalar_mul(mst, mst, flag[0:4, :]), r=["mst", "flag"], w=["mst"])
                T.add("dve", lambda e: e.tensor_scalar_mul(xbcT3[:, :, 1:4], xbcT3[:, :, 1:4], flag), r=["xbcT_carry", "flag"], w=["xbcT_carry"])

        for nm in debug:
            tile_ap, shape, rkeys = {
                "xnT": (xnT[:], [128, 1024], ["xnT"]),
                "k_tm": (k_tm[:], [128, 512], ["k_tm"]),
                "mix": (mix[:], [128, 2048], ["mix_m0", "mix_m1", "mix_m2", "mix_m3", "mix_s0", "mix_s1"]),
                "Cst": (Cst[:], [128, 4 * 258], ["Cst"]),
                "hst": (hst[:], [128, 1024], ["hst0", "hst1"]),
                "x_tm": (x_tm[:], [128, 1024], ["x_tm"]),
                "sm": (sm[:], [128, 256], ["wf", "dE", "ea", "cdb", "dt16", "acum_sb"]),
                "yo": (yo[:], [128, 1024], ["yo_0", "yo_1"]),
                "dec": (dec[:], [128, 2048], ["dec_0", "dec_1", "dec_2", "dec_3"]),
                "xcT": (xcT[:], [128, 1536], ["xcT_0", "xcT_1", "xcT_2"]),
                "g1": (g1[:], [128, 1024], ["g1_0", "g1_1"]),
                "vaug": (vaug[:], [128, 4 * 258], ["vaug_0", "vaug_2"]),
            }[nm]
            d = nc.dram_tensor("dbg_" + nm, shape, F32, kind="ExternalOutput").ap()
            dbg_out[nm] = d
            q_eng = "sp" if tile_ap.dtype == F32 else "pool"
            T.add(q_eng, lambda e, d=d, tile_ap=tile_ap: e.dma_start(out=d, in_=tile_ap), r=rkeys, w=["dbg_" + nm], stream="dbg")

        T.emit(nc, es)
    return nc


_CACHE = {}


def kernel(x, norm_w, w_in, b_igate, b_fgate, conv_w, conv_b, dt_bias, a_log, d_skip,
           mlstm_norm_w, ssd_norm_w, w_out, final_norm_w):
    f = lambda a: np.ascontiguousarray(np.asarray(a, dtype=np.float32))
    x = f(x)
    n_half = SEQ // 2 // L
    key = "main"
    if key not in _CACHE:
        _CACHE[key] = build(n_half, n_half)
    nc = _CACHE[key]
    common = {
        "norm_w": f(norm_w)[0], "w_in": f(w_in)[0], "b_igate": f(b_igate)[0], "b_fgate": f(b_fgate)[0],
        "conv_w": f(conv_w)[0], "conv_b": f(conv_b)[0], "dt_bias": f(dt_bias)[0], "a_log": f(a_log)[0],
        "d_skip": f(d_skip)[0],
        "normcat": np.ascontiguousarray(np.concatenate([f(mlstm_norm_w)[0], f(ssd_norm_w)[0]])),
        "w_out": f(w_out)[0], "final_norm_w": f(final_norm_w),
    }
    half = SEQ // 2
    zeros = np.zeros((half, D_MODEL), np.float32)
    in_maps = []
    for core in range(NCORES):
        b, hf = core // 2, core % 2
        m = dict(common)
        m["x"] = np.ascontiguousarray(x[b, hf * half:(hf + 1) * half])
        m["xpre"] = zeros if hf == 0 else np.ascontiguousarray(x[b, 0:half])
        m["flag"] = np.array([float(hf)], np.float32)
        in_maps.append(m)
    res = run_bass_kernel_spmd(nc, in_maps, core_ids=list(range(NCORES)))
    out = np.empty((BATCH, SEQ, D_MODEL), np.float32)
    for core in range(NCORES):
        b, hf = core // 2, core % 2
        out[b, hf * half:(hf + 1) * half] = res.results[core]["out"]
    return out
```

```python
import numpy as np
from contextlib import ExitStack
import concourse.bass as bass
import concourse.mybir as mybir
from concourse.bass_utils import run_bass_kernel_spmd

F32 = mybir.dt.float32
BF16 = mybir.dt.bfloat16
AF = mybir.ActivationFunctionType
ALU = mybir.AluOpType
AX = mybir.AxisListType

D_MODEL = 1024
SEQ = 8192
BATCH = 4
NCOL = 6680
EPS = 1e-6
L = 128
NCORES = 8


class Tracker:
    def __init__(self):
        self.ops = []
        self.bufs = {}
        self.waitall_streams = set()
        self.regions = {}

    def reg(self, key, arena, off, nbytes, gran=64):
        import os
        if os.environ.get("K_NOALIAS"):
            return
        self.regions[key] = [(arena, g) for g in range(off // gran, (off + nbytes + gran - 1) // gran)]

    def _expand(self, keys):
        out = []
        for k in keys:
            out.extend(self.regions.get(k, [k]))
        return out

    PSUM_BANKS = ("pT0", "pT1", "pm", "pb0", "pb1", "pb2", "pb3", "pb4")

    @classmethod
    def _bank(cls, k):
        if isinstance(k, str):
            if k.startswith("pm_"):
                return "pm"
            if k in cls.PSUM_BANKS:
                return k
        return None

    def add(self, eng, fn, r=(), w=(), stream=None):
        self._lbl = "%s:%s" % (eng, (list(w) + ["?"])[0])
        banks = [self._bank(k) for k in list(r) + list(w)]
        banks = [b for b in banks if b is not None]
        r = [k for k in r if self._bank(k) is None]
        w = [k for k in w if self._bank(k) is None] + sorted(set(banks))
        r = self._expand(r)
        w = self._expand(w)
        deps = set()
        for k in r:
            b = self.bufs.setdefault(k, [None, []])
            if b[0] is not None:
                deps.add(b[0])
        for k in w:
            b = self.bufs.setdefault(k, [None, []])
            if b[0] is not None:
                deps.add(b[0])
            deps.update(b[1])
        idx = len(self.ops)
        deps.discard(idx)
        self.ops.append(dict(eng=eng, fn=fn, deps=deps, stream=stream, idx=idx, has_dep=False, label=self._lbl))
        for k in r:
            self.bufs[k][1].append(idx)
        for k in w:
            self.bufs[k] = [idx, []]
        return idx

    class _Ins:
        def then_inc(self, *a, **k):
            return self

    class _Probe:
        def __init__(self):
            self.calls = []

        def __getattr__(self, name):
            def f(*args, **kw):
                self.calls.append((name, args, kw))
                return Tracker._Ins()
            return f

    @staticmethod
    def _free(ap):
        n = 1
        for d in ap.shape[1:]:
            n *= int(d)
        return n

    def _cost(self, op):
        pr = Tracker._Probe()
        op["fn"](pr)
        eng = op["eng"]
        dur = 0.0
        lat = 0.0
        for name, args, kw in pr.calls:
            out = kw.get("out", args[0] if args else None)
            if name == "dma_start":
                src = kw.get("in_", args[1] if len(args) > 1 else None)
                nbytes = self._free(out) * int(out.shape[0]) * 4
                dur += 80.0
                lat = 2500.0 + nbytes / 160.0
                continue
            n = self._free(out) if out is not None and hasattr(out, "shape") else 64
            if eng == "pe":
                lhs = args[1] if len(args) > 1 else None
                mult = 4.0 if (lhs is not None and lhs.dtype == F32 and name == "matmul") else 1.0
                dur += 16.0 + max(n, 64) * mult / 1.95
            elif eng == "act":
                dur += 230.0 + n / 1.15
            elif eng == "dve":
                dur += 170.0 + n / 0.96
            elif eng == "pool":
                dur += 300.0 + n * 2.0
            else:
                dur += 50.0
        import os
        fr = os.environ.get("K_FREE")
        if fr and any(op["label"].startswith(p) for p in fr.split(",")):
            dur = 20.0
        op["dur"] = max(dur, 20.0)
        op["lat"] = lat

    def schedule(self):
        import heapq
        ops = self.ops
        n = len(ops)
        for op in ops:
            self._cost(op)
        succ = [[] for _ in range(n)]
        for op in ops:
            for d in op["deps"]:
                succ[d].append(op["idx"])
        bl = [0.0] * n
        for i in range(n - 1, -1, -1):
            m = 0.0
            for sidx in succ[i]:
                if bl[sidx] > m:
                    m = bl[sidx]
            bl[i] = m + ops[i]["dur"] + ops[i]["lat"]
        import os
        if os.environ.get("K_CRIT"):
            i = max(range(n), key=lambda j: bl[j])
            print("[crit] DAG critical path %.1f us" % (bl[i] / 1e3))
            agg = {}
            while True:
                agg[ops[i]["label"]] = agg.get(ops[i]["label"], 0.0) + ops[i]["dur"] + ops[i]["lat"]
                nxt = None
                for sidx in succ[i]:
                    if nxt is None or bl[sidx] > bl[nxt]:
                        nxt = sidx
                if nxt is None:
                    break
                i = nxt
            for k, v in sorted(agg.items(), key=lambda kv: -kv[1])[:40]:
                print("     %-28s %.1f us" % (k, v / 1e3))
        ndeps = [len(op["deps"]) for op in ops]
        ready_at = [0.0] * n
        engs = ["pe", "act", "dve", "pool", "sp"]
        avail = {e: [] for e in engs}
        for i in range(n):
            if ndeps[i] == 0:
                avail[ops[i]["eng"]].append(i)
        free_at = {e: 0.0 for e in engs}
        fa_prev = {e: 0.0 for e in engs}
        order = []
        finish = [0.0] * n
        done = 0
        WINDOW = 4000
        lowest_unscheduled = 0
        scheduled = [False] * n
        while done < n:
            best = None
            while lowest_unscheduled < n and scheduled[lowest_unscheduled]:
                lowest_unscheduled += 1
            for e in engs:
                lst = avail[e]
                if not lst:
                    continue
                fa = free_at[e]
                cand = None
                for i in lst:
                    if i > lowest_unscheduled + WINDOW:
                        continue
                    st = ready_at[i] if ready_at[i] > fa else fa
                    key = (st, -bl[i], i)
                    if cand is None or key < cand[0]:
                        cand = (key, i, st)
                if cand is None:
                    continue
                if best is None or cand[0] < best[0]:
                    best = cand + (e,)
            assert best is not None, "scheduler stuck"
            _, i, st, e = best
            avail[e].remove(i)
            op = ops[i]
            fin = st + op["dur"]
            free_at[e] = fin
            finish[i] = fin + op["lat"]
            op["t_start"] = st
            op["stall"] = st - fa_prev[e]
            crit = None
            for d in op["deps"]:
                if crit is None or finish[d] > finish[crit]:
                    crit = d
            op["crit"] = crit
            fa_prev[e] = fin
            order.append(i)
            scheduled[i] = True
            done += 1
            for sidx in succ[i]:
                if finish[i] > ready_at[sidx]:
                    ready_at[sidx] = finish[i]
                ndeps[sidx] -= 1
                if ndeps[sidx] == 0:
                    avail[ops[sidx]["eng"]].append(sidx)
        self.est_makespan_us = max(finish) / 1e3
        busy = {e: 0.0 for e in engs}
        for op in ops:
            busy[op["eng"]] += op["dur"]
        print("[sched] est makespan %.1f us; busy us: %s" % (self.est_makespan_us, {e: round(v / 1e3) for e, v in busy.items()}))
        import os
        if os.environ.get("K_STALLS"):
            t_lo, t_hi = [float(v) * 1e3 for v in os.environ["K_STALLS"].split(",")]
            for e in engs:
                agg = {}
                tot = 0.0
                for op in ops:
                    if op["eng"] == e and t_lo <= op["t_start"] < t_hi and op["stall"] > 1.0 and op["crit"] is not None:
                        k = (op["label"], ops[op["crit"]]["label"])
                        agg[k] = agg.get(k, 0.0) + op["stall"]
                        tot += op["stall"]
                print("[stalls] %s total %.1f us" % (e, tot / 1e3))
                for k, v in sorted(agg.items(), key=lambda kv: -kv[1])[:12]:
                    print("     %-28s waits on %-28s %.1f us" % (k[0], k[1], v / 1e3))
        if os.environ.get("K_BUSY"):
            for e in engs:
                agg = {}
                for op in ops:
                    if op["eng"] == e:
                        k = op["label"].rstrip("0123456789_")
                        agg[k] = agg.get(k, 0.0) + op["dur"]
                print("[busy] %s" % e)
                for k, v in sorted(agg.items(), key=lambda kv: -kv[1])[:22]:
                    print("     %-24s %.1f us" % (k, v / 1e3))
        if os.environ.get("K_TL"):
            eng_, t_lo, t_hi = os.environ["K_TL"].split(",")
            t_lo, t_hi = float(t_lo) * 1e3, float(t_hi) * 1e3
            for i in order:
                op = ops[i]
                if op["eng"] == eng_ and t_lo <= op["t_start"] < t_hi:
                    c = ops[op["crit"]]["label"] if op["crit"] is not None else "-"
                    print("[tl] %8.1f +%6.2f stall %6.2f  %-22s <- %s" % (op["t_start"] / 1e3, op["dur"] / 1e3, op["stall"] / 1e3, op["label"], c))
        remap = {old: new for new, old in enumerate(order)}
        new_ops = []
        for new, old in enumerate(order):
            op = ops[old]
            op["deps"] = {remap[d] for d in op["deps"]}
            op["idx"] = new
            new_ops.append(op)
        self.ops = new_ops

    def emit(self, nc, es, same_engine_sync=True, do_schedule=True):
        if do_schedule:
            self.schedule()
        ops = self.ops
        for op in ops:
            nd = set()
            for d in op["deps"]:
                dop = ops[d]
                if dop["stream"] is None and dop["eng"] == "pe" and op["eng"] == "pe" and op["stream"] is None:
                    continue
                if (not same_engine_sync) and dop["stream"] is None and op["stream"] is None and dop["eng"] == op["eng"]:
                    continue
                nd.add(d)
            op["deps"] = nd
            for d in nd:
                ops[d]["has_dep"] = True
        sems = {}
        engs = ["pe", "act", "dve", "pool", "sp"]
        for e in engs:
            sems[e] = es.enter_context(nc.semaphore("s_" + e))
        streams = sorted({op["stream"] for op in ops if op["stream"] is not None})
        for s in streams:
            sems["d:" + s] = es.enter_context(nc.semaphore("d_" + s))
        cnt = {e: 0 for e in engs}
        scnt = {s: 0 for s in streams}
        for op in ops:
            if op["stream"] is not None:
                scnt[op["stream"]] += 1
                op["sig"] = ("d:" + op["stream"], 16 * scnt[op["stream"]])
            elif op["has_dep"]:
                cnt[op["eng"]] += 1
                op["sig"] = (op["eng"], cnt[op["eng"]])
            else:
                op["sig"] = None
        for op in ops:
            if op["stream"] in self.waitall_streams:
                op["sig"] = ("d:" + op["stream"], 16 * scnt[op["stream"]])
        self.final_counts = {("d:" + s): 16 * scnt[s] for s in streams}
        self.sems = sems
        block = es.enter_context(nc.Block())
        per_eng = {e: [op for op in ops if op["eng"] == e] for e in engs}

        def run(engobj, lst, extra_tail=None):
            waited = {}
            for op in lst:
                need = {}
                for d in op["deps"]:
                    sg = ops[d]["sig"]
                    assert sg is not None
                    if sg[1] > need.get(sg[0], 0):
                        need[sg[0]] = sg[1]
                for sk, v in need.items():
                    if v > waited.get(sk, 0):
                        engobj.wait_ge(sems[sk], v)
                        waited[sk] = v
                ins = op["fn"](engobj)
                if op["stream"] is not None:
                    ins.then_inc(sems["d:" + op["stream"]], 16)
                elif op["sig"] is not None:
                    ins.then_inc(sems[op["eng"]], 1)
            if extra_tail is not None:
                extra_tail(engobj, waited)

        def sp_tail(engobj, waited):
            for sk, v in self.final_counts.items():
                if v > waited.get(sk, 0):
                    engobj.wait_ge(sems[sk], v)

        @block.sync
        def _(e):
            run(e, per_eng["sp"], sp_tail)

        @block.tensor
        def _(e):
            run(e, per_eng["pe"])

        @block.scalar
        def _(e):
            run(e, per_eng["act"])

        @block.vector
        def _(e):
            run(e, per_eng["dve"])

        @block.gpsimd
        def _(e):
            run(e, per_eng["pool"])


PROJ_GROUPS = [
    ("if", 2048, 8), ("dt", 6664, 16), ("k", 512, 512), ("v0", 1024, 512), ("v1", 1536, 512),
    ("xbc0", 5128, 512), ("xbc1", 5640, 512), ("xbc2", 6152, 512), ("q", 0, 512),
    ("zm0", 3080, 512), ("zm1", 3592, 512), ("o0", 2056, 512), ("o1", 2568, 512),
    ("zs0", 4104, 512), ("zs1", 4616, 512),
]
STATE_ONLY = {"if", "dt", "k", "v0", "v1", "xbc0", "xbc1", "xbc2"}


def build(n_pre, n_full, debug=()):
    nc = bass.Bass("TRN2", target_bir_lowering=False)
    T_pre, T_full = n_pre * L, n_full * L
    dr = {}
    dr["x"] = nc.dram_tensor("x", [max(T_full, 1), D_MODEL], F32, kind="ExternalInput").ap()
    if n_pre:
        dr["xpre"] = nc.dram_tensor("xpre", [T_pre, D_MODEL], F32, kind="ExternalInput").ap()
    for nm, shp in [("norm_w", [1024]), ("w_in", [1024, NCOL]), ("b_igate", [4]), ("b_fgate", [4]),
                    ("conv_w", [1536, 4]), ("conv_b", [1536]), ("dt_bias", [16]), ("a_log", [16]),
                    ("d_skip", [16]), ("normcat", [2048]), ("w_out", [2048, 1024]),
                    ("final_norm_w", [1024]), ("flag", [1])]:
        dr[nm] = nc.dram_tensor(nm, shp, F32, kind="ExternalInput").ap()
    out_d = nc.dram_tensor("out", [T_full, D_MODEL], F32, kind="ExternalOutput").ap()
    dbg_out = {}

    T = Tracker()
    T.waitall_streams.add("const")
    es = ExitStack()
    with es:
        def sb(name, shape, dt):
            return es.enter_context(nc.sbuf_tensor(name, shape, dt))

        def ps(name, shape, dt):
            return es.enter_context(nc.psum_tensor(name, shape, dt))

        win = sb("win", [128, 8 * NCOL], BF16)
        wout = sb("wout", [128, 16 * 1024], BF16)
        win3 = win[:].rearrange("p (k n) -> p k n", k=8)
        wout3 = wout[:].rearrange("p (k n) -> p k n", k=16)
        identb = sb("identb", [128, 128], BF16)
        identf = sb("identf", [128, 128], F32)
        tri = sb("tri", [128, 128], F32)
        maskb = sb("maskb", [128, 128], BF16)
        onesf = sb("onesf", [128, 128], F32)
        oh48 = sb("oh48", [128, 16 * 128], BF16)
        oh48_3 = oh48[0:48, :].rearrange("p (r s) -> p r s", r=16)
        cbrow = oh48[64:65, 0:1024]
        onesrow = oh48[64:65, 1024:1152]
        NPE_R = 1
        dgc = sb("dgc", [128, NPE_R * 4 * 4 * 128], BF16)
        dgc4 = dgc[:].rearrange("p (t w c) -> p t w c", t=NPE_R * 4, w=4)
        finalw = sb("finalw", [128, 1024], F32)
        cst = sb("cst", [128, 172], F32)
        cw = cst[:, 0:48].rearrange("p (t w) -> p t w", t=12)
        cb = cst[:, 48:60]
        bias8 = cst[:, 60:68]
        dtb = cst[:, 68:84]
        arep = cst[:, 84:100]
        drep = cst[:, 100:116]
        normw_fm = cst[:, 116:124]
        normcat = cst[:, 124:140]
        m05 = cst[:, 140:148]
        flag = cst[:, 148:149]
        alog = cst[:, 152:168]

        xbuf = [sb("xbuf0", [128, 1024], F32), sb("xbuf1", [128, 1024], F32)]
        ARENA_BYTES = 23552
        arena = sb("arena", [128, ARENA_BYTES // 4], F32)

        def carve(layout_off, name, nbytes, dt, subkeys=None):
            assert layout_off[0] % 4 == 0
            o = layout_off[0]
            layout_off[0] += (nbytes + 3) // 4 * 4
            assert layout_off[0] <= ARENA_BYTES, (name, layout_off[0])
            v = arena[:, o // 4:(o + (nbytes + 3) // 4 * 4) // 4]
            if dt != F32:
                v = v.bitcast(dt)
            if subkeys is None:
                T.reg(name, "A", o, nbytes)
            else:
                n = len(subkeys)
                for i, sk in enumerate(subkeys):
                    T.reg(sk, "A", o + i * (nbytes // n), nbytes // n)
            return v

        lo1 = [0]
        xn = carve(lo1, "xn", 2048, BF16)
        xnT = carve(lo1, "xnT", 2048, BF16)
        xnT3 = xnT[:].rearrange("p (k t) -> p k t", k=8)
        xbc_tm = carve(lo1, "xbc_tm", 3072, BF16, ["xbc_tm0", "xbc_tm1", "xbc_tm2"])
        _c0 = carve(lo1, "cacc0", 2048, F32, ["cacc0_%d" % i for i in range(4)])
        cacc = [_c0, _c0]
        _t0 = carve(lo1, "cth0", 1024, BF16)
        cth = [_t0, _t0]
        q_tm = carve(lo1, "q_tm", 1024, BF16)
        tz = carve(lo1, "tz", 2048, BF16, ["tz0", "tz1"])
        k_tm = carve(lo1, "k_tm", 1024, BF16)
        kp_tm = carve(lo1, "kp_tm", 1024, BF16)
        qT = carve(lo1, "qT", 1024, BF16)
        kT = carve(lo1, "kT", 1024, BF16)
        Pm = carve(lo1, "Pm", 1024, BF16, ["Pm_%d" % i for i in range(4)])
        Pm3 = Pm[:].rearrange("p (h j) -> p h j", h=4)
        Cbf = carve(lo1, "Cbf", 2064, BF16)
        Cbf3 = Cbf[:].rearrange("p (h c) -> p h c", h=4)
        g1 = carve(lo1, "g1", 2048, BF16, ["g1_0", "g1_1"])
        lo2 = [0]
        dec = carve(lo2, "dec", 4096, BF16, ["dec_%d" % i for i in range(4)])
        dec3 = dec[:].rearrange("p (r l) -> p r l", r=16)
        rl = carve(lo2, "rl", 2048, F32)
        CBm = carve(lo2, "CBm", 512, BF16)
        CBm3 = CBm[:].rearrange("p (g l) -> p g l", g=2)
        yo = carve(lo2, "yo", 4096, F32, ["yo_0", "yo_1"])
        ytmp = carve(lo2, "ytmp", 2048, F32)
        x_tm = carve(lo2, "x_tm", 2048, BF16)
        B_tm = carve(lo2, "B_tm", 512, BF16)
        xdt = carve(lo2, "xdt", 2048, BF16)
        xde = carve(lo2, "xde", 2048, BF16)
        mixT = carve(lo2, "mixT", 4096, BF16, ["mixT_0", "mixT_1"])
        mixT3 = mixT[:].rearrange("p (k t) -> p k t", k=16)

        junk = sb("junk", [128, 2], BF16)
        vaug = sb("vaug", [128, 4 * 258], BF16)
        vaug3 = vaug[:].rearrange("p (h c) -> p h c", h=4)
        gs = sb("gs", [128, 1024], BF16)
        xbcT = sb("xbcT", [128, 12 * 132], BF16)
        xbcT3 = xbcT[:].rearrange("p (t c) -> p t c", t=12)
        xcT = sb("xcT", [128, 1536], BF16)
        xcT3 = xcT[:].rearrange("p (t c) -> p t c", t=12)
        Cst = sb("Cst", [128, 4 * 258], F32)
        Cst3 = Cst[:].rearrange("p (h c) -> p h c", h=4)
        hst = sb("hst", [128, 1024], F32)
        hbf = sb("hbf", [128, 1024], BF16)
        mix = sb("mix", [128, 2048], BF16)
        a3 = sb("a3", [128, 48], BF16)
        A48 = sb("A48", [48, 128], BF16)
        sm = sb("sm", [128, 256], F32)
        g8 = sm[:, 0:8]
        t8 = sm[:, 8:16]
        e4 = sm[:, 16:20]
        nlf = sm[:, 20:24]
        li = sm[:, 24:28]
        T1 = sm[:, 28:36]
        T2 = sm[:, 36:44]
        wf = sm[:, 44:52]
        soldb = sm[:, 52:56]
        absden = sm[:, 56:60]
        ssqr = sm[:, 60:64]
        dn4 = sm[:, 64:68]
        rd4 = sm[:, 68:72]
        t4 = sm[:, 72:76]
        rs4 = sm[:, 76:80]
        sc4 = sm[:, 80:84]
        ssq_x = sm[:, 84:85]
        rstd_x = sm[:, 85:86]
        tx1 = sm[:, 86:87]
        ssq_o = sm[:, 87:88]
        rstd_o = sm[:, 88:89]
        to1 = sm[:, 89:90]
        ssq_s = sm[:, 90:92]
        rstd_s = sm[:, 92:94]
        ts2 = sm[:, 94:96]
        dtp = sm[:, 96:112]
        edt = sm[:, 112:128]
        dt16 = sm[:, 128:144]
        a_tm = sm[:, 144:160]
        acum_sb = sm[:, 160:176]
        ea = sm[:, 176:192]
        dEa = sm[:, 192:208]
        dE = sm[:, 208:224]
        cdb = sm[:, 224:240]
        r1 = sm[:, 240:256]
        sm2 = sb("sm2", [128, 16], F32)
        r2 = sm2[:, 0:16]
        gm = sb("gm", [4, 32], F32)
        mst = gm[:, 0:1]
        umax = gm[:, 1:2]
        Rg = gm[:, 2:3]
        dg = gm[:, 3:4]
        soldg = gm[:, 4:5]
        Dg = gm[:, 8:16]
        pT = [ps("pT0", [128, 1024], BF16), ps("pT1", [128, 1024], BF16)]
        pm = ps("pm", [128, 512], F32)
        pb = [ps("pb%d" % i, [128, 512], F32) for i in range(5)]
        print("sbuf bytes remaining:", nc.sbuf_bytes_remaining)

        bank_rr = {"A": 0, "B": 0}
        BANKS = {"A": [0, 1], "B": [2, 3, 4]}

        def next_bank(ph="B"):
            lst = BANKS[ph]
            i = lst[bank_rr[ph] % len(lst)]
            bank_rr[ph] += 1
            return pb[i], "pb%d" % i

        def next_pT(ph="B"):
            i = 0 if ph == "A" else 1
            return pT[i], "pT%d" % i

        def setup_pool(e):
            e.memset(onesf[:], 1.0)
            e.memset(identf[:], 1.0)
            e.affine_select(identf[:], identf[:], [[-1, 128]], ALU.is_equal, 0.0, base=0, channel_multiplier=1)
            e.tensor_copy(identb[:], identf[:])
            e.memset(tri[:], 1.0)
            e.affine_select(tri[:], tri[:], [[1, 128]], ALU.is_ge, 0.0, base=0, channel_multiplier=-1)
            e.tensor_copy(maskb[:], tri[:])
            e.memset(m05, -0.5)
            e.memset(vaug[:], 0.0)
            e.memset(vaug3[:, :, 256:257], 1.0)
            e.memset(Cst[:], 0.0)
            e.memset(hst[:], 0.0)
            e.memset(hbf[:], 0.0)
            e.memset(Cbf[:], 0.0)
            e.memset(gm[:], 0.0)
            e.memset(xbcT[:], 0.0)
            e.memset(oh48[0:48, :], 1.0)
            e.memset(onesrow, 1.0)
            e.affine_select(oh48_3, oh48_3, [[-1, 16], [0, 128]], ALU.is_equal, 0.0, base=0, channel_multiplier=1)
            e.affine_select(oh48_3, oh48_3, [[-1, 16], [0, 128]], ALU.not_equal, 1.0, base=-16, channel_multiplier=1)
            return e.affine_select(oh48_3, oh48_3, [[-1, 16], [0, 128]], ALU.not_equal, 1.0, base=-32, channel_multiplier=1)

        T.add("pool", setup_pool, w=["identb", "identf", "tri", "maskb", "onesf", "m05", "vaug", "Cst", "hst",
                                     "hbf0", "hbf1", "Cbf", "mst", "xbcT_carry", "oh48", "gm"])

        def cdma(out_ap, in_ap, key, noncontig=False):
            T.add("sp", lambda e: e.dma_start(out=out_ap, in_=in_ap, allow_slow_non_contiguous=noncontig),
                  w=[key], stream="const")

        cdma(normw_fm, dr["norm_w"].rearrange("(k p) -> p k", p=128), "normw_fm", True)
        cdma(normcat, dr["normcat"].rearrange("(k p) -> p k", p=128), "normcat", True)
        cdma(cw, dr["conv_w"].rearrange("(t p) w -> p t w", p=128), "cw")
        cdma(cb, dr["conv_b"].rearrange("(t p) -> p t", p=128), "cb", True)
        cdma(bias8[:, 0:4], dr["b_igate"].partition_broadcast(128), "bias8a")
        cdma(bias8[:, 4:8], dr["b_fgate"].partition_broadcast(128), "bias8b")
        cdma(dtb, dr["dt_bias"].partition_broadcast(128), "dtb")
        cdma(alog, dr["a_log"].partition_broadcast(128), "alog")
        cdma(drep, dr["d_skip"].partition_broadcast(128), "drep")
        cdma(finalw[:], dr["final_norm_w"].partition_broadcast(128), "finalw")
        cdma(flag, dr["flag"].partition_broadcast(128), "flag")

        T.add("act", lambda e: e.activation(out=arep, in_=alog, func=AF.Exp), r=["alog"], w=["arep0"])
        T.add("dve", lambda e: e.tensor_scalar_mul(arep, arep, -1.0), r=["arep0"], w=["arep"])
        T.add("dve", lambda e: e.tensor_scalar_mul(cst[:, 0:60], cst[:, 0:60], 0.5), r=["cw", "cb"], w=["cwb"])
        T.add("dve", lambda e: e.tensor_scalar_mul(finalw[:], finalw[:], 32.0), r=["finalw"], w=["finalw2"])
        T.add("dve", lambda e: e.tensor_scalar_mul(normw_fm, normw_fm, 32.0), r=["normw_fm"], w=["normw2"])
        T.add("dve", lambda e: e.tensor_scalar_mul(normcat[:, 0:8], normcat[:, 0:8], 4.0), r=["normcat"], w=["normcat_a"])
        T.add("dve", lambda e: e.tensor_scalar_mul(normcat[:, 8:16], normcat[:, 8:16], float(np.sqrt(2048.0) / 2.0)),
              r=["normcat"], w=["normcat_b"])

        T.add("pool", lambda e: e.dma_start(out=cbrow, in_=dr["conv_b"][0:1024].rearrange("(o n) -> o n", o=1)), w=["cbrow0"], stream="const2")
        T.add("pool", lambda e: e.tensor_scalar(cbrow, cbrow, 0.5, None, ALU.mult), r=["cbrow0", "oh48"], w=["cbrow"])
        for t_ in range(NPE_R * 4):
            for w_ in range(4):
                T.add("dve" if (t_ * 4 + w_) % 2 == 0 else "pool",
                      lambda e, t_=t_, w_=w_: e.tensor_scalar(dgc4[:, t_, w_, :], identf[:], cw[:, t_, w_:w_ + 1], None, ALU.mult),
                      r=["cwb", "identf"], w=["dgc_%d_%d" % (t_, w_)])
        dgc_keys = ["dgc_%d_%d" % (t_, w_) for t_ in range(NPE_R * 4) for w_ in range(4)]

        W_PRE = [(512, 1024), (1024, 2048), (2048, 2056), (5128, 6152), (6152, 6680)]
        W_FULL = [(0, 512), (2056, 3080), (3080, 4104), (4104, 5128)]
        wkeys = {}
        piece = [0]
        NSTG = 8
        wout_f = wout[:].bitcast(F32)

        def wpiece(src_ap, dst_ap, scale_ap, scale_key, ncols, stg_ap, stg_key, stream, wkey, extra_w=()):
            i = piece[0]
            piece[0] += 1
            T.add("sp", lambda e: e.dma_start(out=stg_ap[:, 0:ncols], in_=src_ap), w=[stg_key], stream=stream)
            if i % 2 == 1:
                T.add("act", lambda e: e.activation(out=dst_ap, in_=stg_ap[:, 0:ncols], func=AF.Copy, scale=scale_ap),
                      r=[stg_key, scale_key], w=[wkey] + list(extra_w))
            else:
                T.add("dve", lambda e: e.tensor_scalar_mul(dst_ap, stg_ap[:, 0:ncols], scale_ap),
                      r=[stg_key, scale_key], w=[wkey] + list(extra_w))

        for (c0, c1) in W_PRE + W_FULL:
            wkeys[(c0, c1)] = []
            for k in range(8):
                sl_ = piece[0] % NSTG
                wk = "w_%d_%d" % (c0, k)
                wkeys[(c0, c1)].append(wk)
                wpiece(dr["w_in"][k * 128:(k + 1) * 128, c0:c1], win3[:, k, c0:c1], normw_fm[:, k:k + 1], "normw2", c1 - c0,
                       wout_f[:, sl_ * 1024:(sl_ + 1) * 1024], "wstg%d" % sl_, "stg%d" % sl_, wk)

        def wkeys_for(c0, n):
            out = []
            for (a0, a1), ks in wkeys.items():
                if a0 < c0 + n and c0 < a1:
                    out += ks
            return out

        def load_wout():
            for kc in range(16):
                T.add("pool", lambda e, kc=kc: e.dma_start(out=wout3[:, kc, :], in_=dr["w_out"][kc * 128:(kc + 1) * 128, :]),
                      w=["wout_%d" % kc] + ["wstg%d" % j for j in range(NSTG)], stream="wo")
        wout_keys = ["wout_%d" % kc for kc in range(16)]
        load_wout()

        def load_x(src, ci, slot):
            T.add("sp", lambda e: e.dma_start(out=xbuf[slot][:], in_=src[ci * L:(ci + 1) * L, :]),
                  w=["xbuf%d" % slot], stream="xin%d" % slot)

        chunk_list = [("pre", i) for i in range(n_pre)] + [("full", i) for i in range(n_full)]

        def src_of(kind):
            return dr["xpre"] if kind == "pre" else dr["x"]

        import os as _os
        _DBL = set((_os.environ.get("K_DBL") or "").split(",")) - {""}
        _PERSIST = {"Cst", "hst", "hst0", "hst1", "hbf0", "hbf1", "mst", "xbcT_carry", "identb", "identf", "tri", "maskb", "onesf",
                    "m05", "oh48", "cwb", "arep", "drep", "dtb", "bias8a", "bias8b", "finalw2", "flag", "out_dram", "xbuf0", "xbuf1"}
        _par = [0]

        def _km(k):
            if not isinstance(k, str) or Tracker._bank(k) is not None or k in _PERSIST or k.startswith("w_") or k.startswith("wout_"):
                return k
            if "ALL" in _DBL or k in _DBL or k.rstrip("0123456789_") in _DBL:
                return "%s@%d" % (k, _par[0])
            return k

        def TA(eng, fn, r=(), w=(), stream=None):
            return T.add(eng, fn, r=[_km(k) for k in r], w=[_km(k) for k in w], stream=stream)

        def chunk(gi):
            kind, ci = chunk_list[gi]
            _par[0] = gi % 2
            full = kind == "full"
            slot = gi % 2
            xb = xbuf[slot]
            xk = "xbuf%d" % slot
            if gi + 1 < len(chunk_list):
                nk, nci = chunk_list[gi + 1]
                load_x(src_of(nk), nci, (gi + 1) % 2)
            TA("act", lambda e: e.activation(out=junk[:, 0:1].broadcast_to([128, 1024]), in_=xb[:], func=AF.Square, accum_out=ssq_x), r=[xk], w=["ssq_x"])
            TA("pool", lambda e: e.tensor_scalar(tx1, ssq_x, 1024.0 * EPS, None, ALU.add), r=["ssq_x"], w=["tx1"])
            TA("pool", lambda e: e.tensor_tensor(rstd_x, tx1, m05[:, 0:1], ALU.pow), r=["tx1", "m05"], w=["rstd_x"])
            TA("dve", lambda e: e.tensor_scalar_mul(xn[:], xb[:], rstd_x), r=[xk, "rstd_x"], w=["xn"])
            p, pk = next_pT("A")

            def tr_x(e, p=p):
                for k in range(8):
                    ins = e.transpose(p[:, k * 128:(k + 1) * 128], xn[:, k * 128:(k + 1) * 128], identb[:])
                return ins
            TA("pe", tr_x, r=["xn", "identb"], w=[pk])
            TA("act", lambda e, p=p: e.activation(out=xnT[:], in_=p[:, 0:1024], func=AF.Copy), r=[pk], w=["xnT"])

            def proj(name, c0, n, out_ap, okey, extra_r=()):
                def f(e):
                    for k in range(8):
                        ins = e.matmul(out_ap, xnT3[:, k, :], win3[:, k, c0:c0 + n], start=(k == 0), stop=(k == 7))
                    return ins
                TA("pe", f, r=["xnT"] + wkeys_for(c0, n) + list(extra_r), w=[okey])

            for name, c0, n in PROJ_GROUPS:
                if not full and name not in STATE_ONLY:
                    continue
                if name == "if":
                    b, bk = next_bank("A")
                    proj(name, c0, n, b[:, 0:8], bk)
                    TA("dve", lambda e, b=b: e.tensor_tensor(g8, b[:, 0:8], bias8, ALU.add), r=[bk, "bias8a", "bias8b"], w=["g8"])
                    TA("act", lambda e: e.activation(out=t8, in_=g8, func=AF.Tanh, scale=1.0 / 15.0), r=["g8"], w=["t8"])
                    TA("act", lambda e: e.activation(out=e4, in_=t8[:, 4:8], func=AF.Exp, scale=-15.0), r=["t8"], w=["e4"])
                    TA("act", lambda e: e.activation(out=nlf, in_=e4, func=AF.Ln, bias=1.0), r=["e4"], w=["nlf"])
                    TA("dve", lambda e: e.tensor_scalar_mul(li, t8[:, 0:4], 15.0), r=["t8"], w=["li"])
                elif name == "dt":
                    b, bk = next_bank("A")
                    proj(name, c0, n, b[:, 0:16], bk)
                    TA("dve", lambda e, b=b: e.tensor_tensor(dtp, b[:, 0:16], dtb, ALU.add), r=[bk, "dtb"], w=["dtp"])
                    TA("act", lambda e: e.activation(out=edt, in_=dtp, func=AF.Exp), r=["dtp"], w=["edt"])
                    TA("act", lambda e: e.activation(out=dt16, in_=edt, func=AF.Ln, bias=1.0), r=["edt"], w=["dt16"])
                    TA("dve", lambda e: e.tensor_tensor(a_tm, dt16, arep, ALU.mult), r=["dt16", "arep"], w=["a_tm"])
                else:
                    b, bk = next_bank("A")
                    proj(name, c0, n, b[:, 0:n], bk)
                    if name == "k":
                        TA("act", lambda e, b=b: e.activation(out=k_tm[:], in_=b[:, 0:512], func=AF.Copy), r=[bk], w=["k_tm"])
                    elif name in ("v0", "v1"):
                        h0 = 0 if name == "v0" else 2
                        TA("act", lambda e, b=b, h0=h0: e.activation(
                            out=vaug3[:, h0:h0 + 2, 0:256], in_=b[:, 0:512].rearrange("p (h c) -> p h c", h=2), func=AF.Copy),
                            r=[bk, "vaug"], w=["vaug_%d" % h0])
                    elif name.startswith("xbc"):
                        j = int(name[3])
                        eng = "act"
                        if eng == "act":
                            TA("act", lambda e, b=b, j=j: e.activation(out=xbc_tm[:, j * 512:(j + 1) * 512], in_=b[:, 0:512], func=AF.Copy),
                                  r=[bk], w=["xbc_tm%d" % j])
                        else:
                            TA("dve", lambda e, b=b, j=j: e.tensor_copy(xbc_tm[:, j * 512:(j + 1) * 512], b[:, 0:512]),
                                  r=[bk], w=["xbc_tm%d" % j])
                    elif name == "q":
                        TA("act", lambda e, b=b: e.activation(out=q_tm[:], in_=b[:, 0:512], func=AF.Copy, scale=float(128 ** -0.5)),
                              r=[bk], w=["q_tm"])
                    elif name in ("zm0", "zm1"):
                        hh = int(name[2])
                        sl = slice(hh * 512, (hh + 1) * 512)
                        TA("act", lambda e, b=b, sl=sl: e.activation(out=tz[:, sl], in_=b[:, 0:512], func=AF.Tanh, scale=0.5),
                              r=[bk], w=["tz%d" % hh])
                        TA("dve", lambda e, b=b, sl=sl: e.scalar_tensor_tensor(g1[:, sl], tz[:, sl], 1.0, b[:, 0:512], ALU.add, ALU.mult),
                              r=[bk, "tz%d" % hh], w=["g1_%d" % hh])
                    elif name in ("o0", "o1"):
                        hh = int(name[1])
                        sl = slice(hh * 512, (hh + 1) * 512)
                        TA("act", lambda e, b=b, sl=sl: e.activation(out=tz[:, sl], in_=b[:, 0:512], func=AF.Tanh, scale=0.5),
                              r=[bk], w=["tz%d" % hh])
                        TA("dve", lambda e, sl=sl: e.scalar_tensor_tensor(g1[:, sl], tz[:, sl], 1.0, g1[:, sl], ALU.add, ALU.mult),
                              r=["tz%d" % hh, "g1_%d" % hh], w=["g1_%d" % hh])
                    elif name in ("zs0", "zs1"):
                        hh = int(name[2])
                        sl = slice(hh * 512, (hh + 1) * 512)
                        TA("act", lambda e, b=b, sl=sl: e.activation(out=tz[:, sl], in_=b[:, 0:512], func=AF.Tanh, scale=0.5),
                              r=[bk], w=["tz%d" % hh])
                        TA("dve", lambda e, b=b, sl=sl: e.scalar_tensor_tensor(gs[:, sl], tz[:, sl], 1.0, b[:, 0:512], ALU.add, ALU.mult),
                              r=[bk, "tz%d" % hh], w=["gs_%d" % hh])

            def tr4(src, dst, skey, dkey):
                p, pk = next_pT("A")

                def f(e, p=p):
                    for h in range(4):
                        ins = e.transpose(p[:, h * 128:(h + 1) * 128], src[:, h * 128:(h + 1) * 128], identb[:])
                    return ins
                TA("pe", f, r=[skey, "identb"], w=[pk])
                TA("act", lambda e, p=p: e.activation(out=dst[:], in_=p[:, 0:512], func=AF.Copy), r=[pk], w=[dkey])

            if full:
                tr4(k_tm, kT, "k_tm", "kT")
                tr4(q_tm, qT, "q_tm", "qT")
            for half, (t0, nt) in enumerate([(0, 8), (8, 4)]):
                p, pk = next_pT("A")

                def f(e, p=p, t0=t0, nt=nt):
                    for t in range(nt):
                        ins = e.transpose(p[:, t * 128:(t + 1) * 128], xbc_tm[:, (t0 + t) * 128:(t0 + t + 1) * 128], identb[:])
                    return ins
                TA("pe", f, r=["xbc_tm0", "xbc_tm1", "xbc_tm2", "identb"], w=[pk])
                TA("act", lambda e, p=p, t0=t0, nt=nt: e.activation(
                    out=xbcT3[:, t0:t0 + nt, 4:132], in_=p[:, 0:nt * 128].rearrange("p (t c) -> p t c", t=nt), func=AF.Copy),
                    r=[pk, "xbcT_carry"], w=["xbcT_%d" % half])

            TA("pe", lambda e: e.matmul(pm[:, 24:28], tri[:], nlf, start=True, stop=True), r=["tri", "nlf"], w=["pm_nb"])

            def f_ugm(e):
                e.matmul(pm[0:4, 128:256], li, identf[:], start=True, stop=False)
                return e.matmul(pm[0:4, 128:256], nlf, tri[:], start=False, stop=True)
            TA("pe", f_ugm, r=["li", "nlf", "identf", "tri"], w=["pm_ugm"])
            TA("pe", lambda e: e.matmul(pm[0:4, 120:121], nlf, onesf[:, 0:1], start=True, stop=True), r=["nlf", "onesf"], w=["pm_nbl"])
            TA("dve", lambda e: e.reduce_max(umax, pm[0:4, 128:256], AX.X), r=["pm_ugm"], w=["umax"])
            TA("dve", lambda e: e.tensor_tensor(Rg, umax, mst, ALU.max), r=["umax", "mst"], w=["Rg"])
            TA("dve", lambda e: e.tensor_tensor(dg, mst, Rg, ALU.subtract), r=["mst", "Rg"], w=["dg"])
            TA("act", lambda e: e.activation(out=soldg, in_=dg, func=AF.Exp), r=["dg"], w=["soldg"])
            TA("dve", lambda e: e.tensor_tensor(mst, Rg, pm[0:4, 120:121], ALU.subtract), r=["Rg", "pm_nbl", "dg"], w=["mst"])
            TA("dve", lambda e: e.tensor_scalar_mul(Dg[:, 0:4], identf[0:4, 0:4], Rg), r=["Rg", "identf"], w=["Dg_a"])
            TA("dve", lambda e: e.tensor_scalar_mul(Dg[:, 4:8], identf[0:4, 0:4], soldg), r=["soldg", "identf"], w=["Dg_b"])
            TA("pe", lambda e: e.matmul(pm[:, 28:36], onesf[0:4, :], Dg, start=True, stop=True), r=["onesf", "Dg_a", "Dg_b"], w=["pm_rs"])
            TA("dve", lambda e: e.tensor_tensor(T1[:, 0:4], li, pm[:, 24:28], ALU.add), r=["li", "pm_nb"], w=["T1a"])
            TA("dve", lambda e: e.tensor_copy(T1[:, 4:8], pm[:, 24:28]), r=["pm_nb"], w=["T1b"])
            TA("dve", lambda e: e.tensor_tensor(
                T2.rearrange("p (a h) -> p a h", a=2), T1.rearrange("p (a h) -> p a h", a=2),
                pm[:, 28:32].unsqueeze(1).broadcast_to([128, 2, 4]), ALU.subtract), r=["T1a", "T1b", "pm_rs"], w=["T2"])
            TA("act", lambda e: e.activation(out=wf, in_=T2, func=AF.Exp), r=["T2"], w=["wf"])
            TA("dve", lambda e: e.tensor_copy(soldb, pm[:, 32:36]), r=["pm_rs"], w=["soldb"])

            TA("pe", lambda e: e.matmul(pm[:, 40:56], tri[:], a_tm, start=True, stop=True), r=["tri", "a_tm"], w=["pm_acum"])
            TA("pe", lambda e: e.matmul(pm[:, 56:72], onesf[:], a_tm, start=True, stop=True), r=["onesf", "a_tm"], w=["pm_alast"])
            TA("dve", lambda e: e.tensor_copy(acum_sb, pm[:, 40:56]), r=["pm_acum"], w=["acum_sb"])
            TA("act", lambda e: e.activation(out=ea, in_=pm[:, 40:56], func=AF.Exp), r=["pm_acum"], w=["ea"])
            TA("dve", lambda e: e.tensor_tensor(dEa, pm[:, 56:72], acum_sb, ALU.subtract), r=["pm_alast", "acum_sb"], w=["dEa"])
            TA("act", lambda e: e.activation(out=dE, in_=dEa, func=AF.Exp), r=["dEa"], w=["dE"])
            TA("act", lambda e: e.activation(out=cdb, in_=pm[:, 56:72], func=AF.Exp), r=["pm_alast"], w=["cdb"])
            if full:
                TA("dve", lambda e: e.tensor_copy(a3[:, 0:16], acum_sb), r=["acum_sb"], w=["a3_0"])
                TA("dve", lambda e: e.tensor_tensor(r1, acum_sb, a3[:, 0:16], ALU.subtract), r=["acum_sb", "a3_0"], w=["r1"])
                TA("dve", lambda e: e.tensor_copy(a3[:, 16:32], r1), r=["r1"], w=["a3_1"])
                TA("dve", lambda e: e.tensor_tensor(r2, r1, a3[:, 16:32], ALU.subtract), r=["r1", "a3_1"], w=["r2"])
                TA("dve", lambda e: e.tensor_copy(a3[:, 32:48], r2), r=["r2"], w=["a3_2"])
                p, pk = next_pT()
                TA("pe", lambda e, p=p: e.transpose(p[0:48, 0:128], a3[:, 0:48], identb[:]), r=["a3_0", "a3_1", "a3_2", "identb"], w=[pk])
                TA("dve", lambda e, p=p: e.tensor_copy(A48[:], p[0:48, 0:128]), r=[pk], w=["A48"])

            for rnd in range(NPE_R):
                cvb, cvk = next_bank()

                def f_cv(e, rnd=rnd, cvb=cvb):
                    for tt in range(4):
                        t = rnd * 4 + tt
                        o_ = cvb[:, tt * 128:(tt + 1) * 128]
                        for w_ in range(4):
                            e.matmul(o_, dgc4[:, t, w_, :], xbcT3[:, t, 1 + w_:129 + w_], start=(w_ == 0), stop=False)
                        ins = e.matmul(o_, cbrow[:, t * 128:(t + 1) * 128], onesrow, start=False, stop=True)
                    return ins
                TA("pe", f_cv, r=["xbcT_0", "xbcT_carry", "cbrow", "oh48"] + dgc_keys, w=[cvk])
                ct = cth[0]
                TA("act", lambda e, cvb=cvb, ct=ct: e.activation(out=ct[:], in_=cvb[:, 0:512], func=AF.Tanh), r=[cvk], w=["cth0"])
                TA("dve", lambda e, cvb=cvb, ct=ct, rnd=rnd: e.scalar_tensor_tensor(
                    xcT[:, rnd * 512:(rnd + 1) * 512], ct[:], 1.0, cvb[:, 0:512], ALU.add, ALU.mult),
                    r=["cth0", cvk], w=["xcT_%d" % rnd])
            for rnd in range(NPE_R, 3):
                ca = cacc[0]
                ct = cth[0]
                cak = "cacc0"
                ctk = "cth0"
                src_key = "xbcT_0" if rnd < 2 else "xbcT_1"
                for tt in range(4):
                    t = rnd * 4 + tt
                    if not full and t >= 10:
                        continue
                    cslice = ca[:, tt * 128:(tt + 1) * 128]
                    ckey = cak + "_%d" % tt
                    TA("dve", lambda e, t=t, cslice=cslice: e.tensor_scalar(
                        cslice, xbcT3[:, t, 1:129], cw[:, t, 0:1], cb[:, t:t + 1], ALU.mult, ALU.add),
                        r=[src_key, "xbcT_carry", "cwb"], w=[ckey])
                    for w_ in range(1, 4):
                        TA("dve", lambda e, t=t, cslice=cslice, w_=w_: e.scalar_tensor_tensor(
                            cslice, xbcT3[:, t, 1 + w_:129 + w_], cw[:, t, w_:w_ + 1], cslice, ALU.mult, ALU.add),
                            r=[src_key, "xbcT_carry", "cwb", ckey], w=[ckey])
                nv = 512 if (full or rnd < 2) else 256
                TA("act", lambda e, ca=ca, ct=ct, nv=nv: e.activation(out=ct[:, 0:nv], in_=ca[:, 0:nv], func=AF.Tanh),
                   r=[cak + "_%d" % i for i in range(4)], w=[ctk])
                TA("dve", lambda e, ca=ca, ct=ct, rnd=rnd, nv=nv: e.scalar_tensor_tensor(
                    xcT[:, rnd * 512:rnd * 512 + nv], ct[:, 0:nv], 1.0, ca[:, 0:nv], ALU.add, ALU.mult),
                    r=[ctk] + [cak + "_%d" % i for i in range(4)], w=["xcT_%d" % rnd])
            TA("pool", lambda e: e.tensor_copy(xbcT3[:, :, 1:4], xbcT3[:, :, 129:132]), r=["xbcT_0", "xbcT_1"], w=["xbcT_carry"])

            TA("pool", lambda e: e.tensor_tensor(
                kp_tm[:].rearrange("p (h d) -> p h d", h=4), k_tm[:].rearrange("p (h d) -> p h d", h=4),
                wf[:, 0:4].unsqueeze(2).broadcast_to([128, 4, 128]), ALU.mult), r=["k_tm", "wf"], w=["kp_tm"])
            TA("pool", lambda e: e.tensor_tensor(Cst3[:, :, 0:257], Cst3[:, :, 0:257],
                                                    soldb.unsqueeze(2).broadcast_to([128, 4, 257]), ALU.mult),
                  r=["Cst", "soldb"], w=["Cst"])
            if full:
                TA("act", lambda e: e.activation(out=Cbf3[:, :, 0:257], in_=Cst3[:, :, 0:257], func=AF.Copy), r=["Cst"], w=["Cbf"])
                sb_, sbk = next_bank()

                def f_st(e, sb_=sb_):
                    for h in range(4):
                        ins = e.matmul(sb_[:, h * 128:(h + 1) * 128], kT[:, h * 128:(h + 1) * 128], qT[:, h * 128:(h + 1) * 128],
                                       start=True, stop=True)
                    return ins
                TA("pe", f_st, r=["kT", "qT"], w=[sbk])
                for h in range(4):
                    TA("dve", lambda e, h=h, sb_=sb_: e.scalar_tensor_tensor(
                        Pm3[:, h, :], sb_[:, h * 128:(h + 1) * 128], wf[:, h:h + 1], maskb[:], ALU.mult, ALU.mult),
                        r=[sbk, "wf", "maskb"], w=["Pm_%d" % h])
                brs = []
                for hp in range(2):
                    bb, bbk = next_bank()
                    for h in (2 * hp, 2 * hp + 1):
                        brs.append((bb, bbk, (h % 2) * 256))

                    def f_br(e, hp=hp, bb=bb):
                        for h in (2 * hp, 2 * hp + 1):
                            o_ = (h % 2) * 256
                            e.matmul(bb[:, o_:o_ + 256], Pm3[:, h, :], vaug3[:, h, 0:256], start=True, stop=False)
                            ins = e.matmul(bb[:, o_:o_ + 256], qT[:, h * 128:(h + 1) * 128], Cbf3[:, h, 0:256], start=False, stop=True)
                        return ins
                    TA("pe", f_br, r=["Pm_%d" % (2 * hp), "Pm_%d" % (2 * hp + 1), "vaug_0", "vaug_2", "qT", "Cbf"], w=[bbk])
                    for h in (2 * hp, 2 * hp + 1):
                        o_ = (h % 2) * 256
                        TA("act", lambda e, h=h, bb=bb, o_=o_: e.activation(out=junk[:, 0:1].broadcast_to([128, 256]), in_=bb[:, o_:o_ + 256], func=AF.Square,
                                                                            accum_out=ssqr[:, h:h + 1]), r=[bbk], w=["ssqr_%d" % h])

                def f_den(e):
                    for h in range(4):
                        e.matmul(pm[:, 64 + h:65 + h], Pm3[:, h, :], vaug3[:, h, 256:257], start=True, stop=False)
                        ins = e.matmul(pm[:, 64 + h:65 + h], qT[:, h * 128:(h + 1) * 128], Cbf3[:, h, 256:257], start=False, stop=True)
                    return ins
                TA("pe", f_den, r=["Pm_0", "Pm_1", "Pm_2", "Pm_3", "vaug_0", "vaug_2", "qT", "Cbf"], w=["pm_den"])
                TA("act", lambda e: e.activation(out=absden, in_=pm[:, 64:68], func=AF.Abs), r=["pm_den"], w=["absden"])
                hk = ["absden"]
                sk = ["ssqr_%d" % h for h in range(4)]
                TA("dve", lambda e: e.tensor_tensor(dn4, absden, wf[:, 4:8], ALU.max), r=hk + ["wf"], w=["dn4"])
                TA("dve", lambda e: e.reciprocal(rd4, dn4), r=["dn4"], w=["rd4"])
                TA("dve", lambda e: e.tensor_tensor(t4, rd4, rd4, ALU.mult), r=["rd4"], w=["t4"])
                TA("dve", lambda e: e.tensor_tensor(t4, t4, ssqr, ALU.mult), r=["t4"] + sk, w=["t4"])
                TA("pool", lambda e: e.tensor_scalar(t4, t4, 256.0 * EPS, None, ALU.add), r=["t4"], w=["t4"])
                TA("pool", lambda e: e.tensor_tensor(rs4, t4, m05[:, 0:4], ALU.pow), r=["t4", "m05"], w=["rs4"])
                TA("dve", lambda e: e.tensor_tensor(sc4, rd4, rs4, ALU.mult), r=["rd4", "rs4"], w=["sc4"])
                for h in range(4):
                    bb, bbk, o_ = brs[h]
                    TA("dve", lambda e, h=h, bb=bb, o_=o_: e.scalar_tensor_tensor(
                        mix[:, h * 256:(h + 1) * 256], bb[:, o_:o_ + 256], sc4[:, h:h + 1], g1[:, h * 256:(h + 1) * 256], ALU.mult, ALU.mult),
                        r=[bbk, "sc4", "g1_%d" % (h // 2)], w=["mix_m%d" % h])
            for h in range(4):
                cb_, cbk = next_bank()
                TA("pe", lambda e, h=h, cb_=cb_: e.matmul(cb_[:, 0:257], kp_tm[:, h * 128:(h + 1) * 128], vaug3[:, h, 0:257],
                                                             start=True, stop=True), r=["kp_tm", "vaug_0", "vaug_2"], w=[cbk])
                TA("dve", lambda e, h=h, cb_=cb_: e.tensor_tensor(Cst3[:, h, 0:257], Cst3[:, h, 0:257], cb_[:, 0:257], ALU.add),
                      r=[cbk, "Cst", "Cbf"], w=["Cst"])

            p, pk = next_pT()

            def tr_xc(e, p=p):
                for t in range(8):
                    ins = e.transpose(p[:, t * 128:(t + 1) * 128], xcT3[:, t, :], identb[:])
                return ins
            TA("pe", tr_xc, r=["xcT_0", "xcT_1", "identb"], w=[pk])
            TA("act", lambda e, p=p: e.activation(out=x_tm[:], in_=p[:, 0:1024], func=AF.Copy), r=[pk], w=["x_tm"])
            p, pk = next_pT()

            def tr_B(e, p=p):
                for t in range(2):
                    ins = e.transpose(p[:, t * 128:(t + 1) * 128], xcT3[:, 8 + t, :], identb[:])
                return ins
            TA("pe", tr_B, r=["xcT_2", "identb"], w=[pk])
            TA("act", lambda e, p=p: e.activation(out=B_tm[:], in_=p[:, 0:256], func=AF.Copy), r=[pk], w=["B_tm"])
            TA("pool", lambda e: e.tensor_tensor(
                xdt[:].rearrange("p (r c) -> p r c", r=16), x_tm[:].rearrange("p (r c) -> p r c", r=16),
                dt16.unsqueeze(2).broadcast_to([128, 16, 64]), ALU.mult), r=["x_tm", "dt16"], w=["xdt"])
            TA("pool", lambda e: e.tensor_tensor(
                xde[:].rearrange("p (r c) -> p r c", r=16), xdt[:].rearrange("p (r c) -> p r c", r=16),
                dE.unsqueeze(2).broadcast_to([128, 16, 64]), ALU.mult), r=["xdt", "dE"], w=["xde"])

            if full:
                TA("pe", lambda e: (e.matmul(pm[:, 256:384], xcT3[:, 8, :], xcT3[:, 10, :], start=True, stop=True),
                                       e.matmul(pm[:, 384:512], xcT3[:, 9, :], xcT3[:, 11, :], start=True, stop=True))[1],
                      r=["xcT_2"], w=["pm_cb"])
                TA("dve", lambda e: e.tensor_tensor(CBm3, pm[:, 256:512].rearrange("p (g l) -> p g l", g=2),
                                                       maskb[:].unsqueeze(1).broadcast_to([128, 2, 128]), ALU.mult),
                      r=["pm_cb", "maskb"], w=["CBm"])
                for bq in range(4):
                    ab, abk = next_bank()

                    def f_arg(e, bq=bq, ab=ab):
                        for rr in range(4):
                            ins = e.matmul(ab[:, rr * 128:(rr + 1) * 128], oh48_3[:, bq * 4 + rr, :], A48[:], start=True, stop=True)
                        return ins
                    TA("pe", f_arg, r=["oh48", "A48"], w=[abk])

                    def f_relu(e, bq=bq, ab=ab):
                        for rr in range(4):
                            hd = bq * 4 + rr
                            ins = e.activation(out=rl[:, rr * 128:(rr + 1) * 128], in_=ab[:, rr * 128:(rr + 1) * 128],
                                               func=AF.Relu, bias=acum_sb[:, hd:hd + 1], scale=-1.0)
                        return ins
                    TA("act", f_relu, r=[abk, "acum_sb"], w=["rl"])
                    TA("act", lambda e, bq=bq: e.activation(out=dec[:, bq * 512:(bq + 1) * 512], in_=rl[:], func=AF.Exp, scale=-1.0),
                          r=["rl"], w=["dec_%d" % bq])
                    g = bq // 2
                    TA("pool", lambda e, bq=bq, g=g: e.tensor_tensor(
                        dec3[:, bq * 4:bq * 4 + 4, :], dec3[:, bq * 4:bq * 4 + 4, :],
                        CBm3[:, g:g + 1, :].broadcast_to([128, 4, 128]), ALU.mult), r=["dec_%d" % bq, "CBm"], w=["dec_%d" % bq])
                TA("pool", lambda e: e.tensor_tensor(
                    yo[:].rearrange("p (r c) -> p r c", r=16), x_tm[:].rearrange("p (r c) -> p r c", r=16),
                    drep.unsqueeze(2).broadcast_to([128, 16, 64]), ALU.mult), r=["x_tm", "drep"], w=["yo_0", "yo_1"])
                for g in range(2):
                    yd, ydk = next_bank()

                    def f_yd(e, g=g, yd=yd):
                        for rr in range(8):
                            hd = g * 8 + rr
                            ins = e.matmul(yd[:, rr * 64:(rr + 1) * 64], dec3[:, hd, :], xdt[:, hd * 64:(hd + 1) * 64], start=True, stop=True)
                        return ins
                    TA("pe", f_yd, r=["dec_%d" % (2 * g), "dec_%d" % (2 * g + 1), "xdt"], w=[ydk])
                    yf, yfk = next_bank()
                    TA("pe", lambda e, g=g, yf=yf: e.matmul(yf[:, 0:512], xcT3[:, 10 + g, :], hbf[:, g * 512:(g + 1) * 512], start=True, stop=True),
                          r=["xcT_2", "hbf%d" % g], w=[yfk])
                    gsl = slice(g * 512, (g + 1) * 512)
                    TA("dve", lambda e, g=g, yf=yf: e.tensor_tensor(
                        ytmp[:].rearrange("p (r c) -> p r c", r=8), yf[:, 0:512].rearrange("p (r c) -> p r c", r=8),
                        ea[:, g * 8:(g + 1) * 8].unsqueeze(2).broadcast_to([128, 8, 64]), ALU.mult), r=[yfk, "ea"], w=["ytmp"])
                    TA("dve", lambda e, yd=yd: e.tensor_tensor(ytmp[:], ytmp[:], yd[:, 0:512], ALU.add), r=[ydk, "ytmp"], w=["ytmp"])
                    TA("pool", lambda e, gsl=gsl: e.tensor_tensor(yo[:, gsl], yo[:, gsl], ytmp[:], ALU.add), r=["ytmp", "yo_%d" % g], w=["yo_%d" % g])
                    TA("pool", lambda e, gsl=gsl: e.tensor_tensor(yo[:, gsl], yo[:, gsl], gs[:, gsl], ALU.mult),
                          r=["yo_%d" % g, "gs_%d" % g], w=["yo_%d" % g])
                    TA("act", lambda e, g=g, gsl=gsl: e.activation(out=junk[:, 0:1].broadcast_to([128, 512]), in_=yo[:, gsl], func=AF.Square, accum_out=ssq_s[:, g:g + 1]),
                          r=["yo_%d" % g], w=["ssq_s%d" % g])
                TA("pool", lambda e: e.tensor_scalar(ts2, ssq_s, 2048.0 * EPS, None, ALU.add), r=["ssq_s0", "ssq_s1"], w=["ts2"])
                TA("pool", lambda e: e.tensor_tensor(rstd_s, ts2, m05[:, 0:2], ALU.pow), r=["ts2", "m05"], w=["rstd_s"])
                for g in range(2):
                    gsl = slice(g * 512, (g + 1) * 512)
                    TA("dve", lambda e, g=g, gsl=gsl: e.tensor_scalar_mul(mix[:, 1024 + g * 512:1024 + (g + 1) * 512], yo[:, gsl], rstd_s[:, g:g + 1]),
                          r=["yo_%d" % g, "rstd_s"], w=["mix_s%d" % g])
            for g in range(2):
                st, stk = next_bank()
                gsl = slice(g * 512, (g + 1) * 512)
                TA("pe", lambda e, g=g, st=st, gsl=gsl: e.matmul(st[:, 0:512], B_tm[:, g * 128:(g + 1) * 128], xde[:, gsl], start=True, stop=True),
                      r=["B_tm", "xde"], w=[stk])
                TA("pool", lambda e, g=g, gsl=gsl: e.tensor_tensor(
                    hst[:, gsl].rearrange("p (r c) -> p r c", r=8), hst[:, gsl].rearrange("p (r c) -> p r c", r=8),
                    cdb[:, g * 8:(g + 1) * 8].unsqueeze(2).broadcast_to([128, 8, 64]), ALU.mult), r=["hst", "cdb", "hbf%d" % g], w=["hst%d" % g])
                TA("dve", lambda e, st=st, gsl=gsl: e.tensor_tensor(hst[:, gsl], hst[:, gsl], st[:, 0:512], ALU.add), r=[stk, "hst%d" % g], w=["hst%d" % g])
                TA("act", lambda e, gsl=gsl: e.activation(out=hbf[:, gsl], in_=hst[:, gsl], func=AF.Copy), r=["hst%d" % g], w=["hbf%d" % g])

            if not full:
                return
            mkeys = ["mix_m%d" % h for h in range(4)] + ["mix_s0", "mix_s1"]
            for half in range(2):
                p, pk = next_pT()

                def tr_m(e, p=p, half=half):
                    for t in range(8):
                        kc = half * 8 + t
                        ins = e.transpose(p[:, t * 128:(t + 1) * 128], mix[:, kc * 128:(kc + 1) * 128], identb[:])
                    return ins
                TA("pe", tr_m, r=mkeys + ["identb"], w=[pk])
                TA("dve", lambda e, p=p, half=half: e.tensor_tensor(
                    mixT3[:, half * 8:(half + 1) * 8, :], p[:, 0:1024].rearrange("p (k t) -> p k t", k=8),
                    normcat[:, half * 8:(half + 1) * 8].unsqueeze(2).broadcast_to([128, 8, 128]), ALU.mult),
                    r=[pk, "normcat_a", "normcat_b"], w=["mixT_%d" % half])
            for half in range(2):
                ob, obk = next_bank()

                def f_o(e, ob=ob, half=half):
                    for kc in range(16):
                        ins = e.matmul(ob[:, 0:512], mixT3[:, kc, :], wout3[:, kc, half * 512:(half + 1) * 512], start=(kc == 0), stop=(kc == 15))
                    return ins
                TA("pe", f_o, r=["mixT_0", "mixT_1"] + wout_keys, w=[obk])
                hsl = slice(half * 512, (half + 1) * 512)
                TA("dve", lambda e, ob=ob, hsl=hsl: e.tensor_tensor(xb[:, hsl], xb[:, hsl], ob[:, 0:512], ALU.add), r=[obk, xk], w=[xk])
            TA("act", lambda e: e.activation(out=junk[:, 0:1].broadcast_to([128, 1024]), in_=xb[:], func=AF.Square, accum_out=ssq_o), r=[xk], w=["ssq_o"])
            TA("pool", lambda e: e.tensor_scalar(to1, ssq_o, 1024.0 * EPS, None, ALU.add), r=["ssq_o"], w=["to1"])
            TA("pool", lambda e: e.tensor_tensor(rstd_o, to1, m05[:, 0:1], ALU.pow), r=["to1", "m05"], w=["rstd_o"])
            TA("dve", lambda e: e.scalar_tensor_tensor(xb[:], xb[:], rstd_o, finalw[:], ALU.mult, ALU.mult), r=[xk, "rstd_o", "finalw2"], w=[xk])
            TA("sp", lambda e: e.dma_start(out=out_d[ci * L:(ci + 1) * L, :], in_=xb[:]), r=[xk], w=["out_dram"], stream="out%d" % slot)

        k0, c0_ = chunk_list[0]
        load_x(src_of(k0), c0_, 0)
        for gi in range(len(chunk_list)):
            chunk(gi)
            if n_pre and gi == n_pre - 1:
                T.add("dve", lambda e: e.tensor_scalar_mul(Cst[:], Cst[:], flag), r=["Cst", "flag"], w=["Cst"])
                T.add("dve", lambda e: e.tensor_scalar_mul(hst[:], hst[:], flag), r=["hst0", "hst1", "flag"], w=["hst0", "hst1", "hst"])
                T.add("dve", lambda e: e.tensor_scalar_mul(hbf[:], hbf[:], flag), r=["hbf0", "hbf1", "flag"], w=["hbf0", "hbf1"])
                T.add("dve", lambda e: e.tensor_scalar_mul(mst, mst, flag[0:4, :]), r=["mst", "flag"], w=["mst"])
                T.add("dve", lambda e: e.tensor_scalar_mul(xbcT3[:, :, 1:4], xbcT3[:, :, 1:4], flag), r=["xbcT_carry", "flag"], w=["xbcT_carry"])

        for nm in debug:
            tile_ap, shape, rkeys = {
                "xnT": (xnT[:], [128, 1024], ["xnT"]),
                "k_tm": (k_tm[:], [128, 512], ["k_tm"]),
                "mix": (mix[:], [128, 2048], ["mix_m0", "mix_m1", "mix_m2", "mix_m3", "mix_s0", "mix_s1"]),
                "Cst": (Cst[:], [128, 4 * 258], ["Cst"]),
                "hst": (hst[:], [128, 1024], ["hst0", "hst1"]),
                "x_tm": (x_tm[:], [128, 1024], ["x_tm"]),
                "sm": (sm[:], [128, 256], ["wf", "dE", "ea", "cdb", "dt16", "acum_sb"]),
                "yo": (yo[:], [128, 1024], ["yo_0", "yo_1"]),
                "dec": (dec[:], [128, 2048], ["dec_0", "dec_1", "dec_2", "dec_3"]),
                "xcT": (xcT[:], [128, 1536], ["xcT_0", "xcT_1", "xcT_2"]),
                "g1": (g1[:], [128, 1024], ["g1_0", "g1_1"]),
                "vaug": (vaug[:], [128, 4 * 258], ["vaug_0", "vaug_2"]),
            }[nm]
            d = nc.dram_tensor("dbg_" + nm, shape, F32, kind="ExternalOutput").ap()
            dbg_out[nm] = d
            q_eng = "sp" if tile_ap.dtype == F32 else "pool"
            T.add(q_eng, lambda e, d=d, tile_ap=tile_ap: e.dma_start(out=d, in_=tile_ap), r=rkeys, w=["dbg_" + nm], stream="dbg")

        import os as _os2
        T.emit(nc, es, same_engine_sync=not _os2.environ.get("K_NOSES"))
    return nc


_CACHE = {}


def kernel(x, norm_w, w_in, b_igate, b_fgate, conv_w, conv_b, dt_bias, a_log, d_skip,
           mlstm_norm_w, ssd_norm_w, w_out, final_norm_w):
    f = lambda a: np.ascontiguousarray(np.asarray(a, dtype=np.float32))
    x = f(x)
    n_half = SEQ // 2 // L
    key = "main"
    if key not in _CACHE:
        _CACHE[key] = build(n_half, n_half)
    nc = _CACHE[key]
    common = {
        "norm_w": f(norm_w)[0], "w_in": f(w_in)[0], "b_igate": f(b_igate)[0], "b_fgate": f(b_fgate)[0],
        "conv_w": f(conv_w)[0], "conv_b": f(conv_b)[0], "dt_bias": f(dt_bias)[0], "a_log": f(a_log)[0],
        "d_skip": f(d_skip)[0],
        "normcat": np.ascontiguousarray(np.concatenate([f(mlstm_norm_w)[0], f(ssd_norm_w)[0]])),
        "w_out": f(w_out)[0], "final_norm_w": f(final_norm_w),
    }
    half = SEQ // 2
    zeros = np.zeros((half, D_MODEL), np.float32)
    in_maps = []
    for core in range(NCORES):
        b, hf = core // 2, core % 2
        m = dict(common)
        m["x"] = np.ascontiguousarray(x[b, hf * half:(hf + 1) * half])
        m["xpre"] = zeros if hf == 0 else np.ascontiguousarray(x[b, 0:half])
        m["flag"] = np.array([float(hf)], np.float32)
        in_maps.append(m)
    res = run_bass_kernel_spmd(nc, in_maps, core_ids=list(range(NCORES)))
    out = np.empty((BATCH, SEQ, D_MODEL), np.float32)
    for core in range(NCORES):
        b, hf = core // 2, core % 2
        out[b, hf * half:(hf + 1) * half] = res.results[core]["out"]
    return out
```

```python
import numpy as np
from contextlib import ExitStack
import concourse.bass as bass
import concourse.mybir as mybir
from concourse.bass_utils import run_bass_kernel_spmd

F32 = mybir.dt.float32
BF16 = mybir.dt.bfloat16
AF = mybir.ActivationFunctionType
ALU = mybir.AluOpType
AX = mybir.AxisListType

D_MODEL = 1024
SEQ = 8192
BATCH = 4
NCOL = 6680
EPS = 1e-6
L = 128
NCORES = 8


class Tracker:
    def __init__(self):
        self.ops = []
        self.bufs = {}
        self.waitall_streams = set()
        self.regions = {}

    def reg(self, key, arena, off, nbytes, gran=64):
        import os
        if os.environ.get("K_NOALIAS"):
            return
        self.regions[key] = [(arena, g) for g in range(off // gran, (off + nbytes + gran - 1) // gran)]

    def _expand(self, keys):
        out = []
        for k in keys:
            out.extend(self.regions.get(k, [k]))
        return out

    PSUM_BANKS = ("pT0", "pT1", "pm", "pb0", "pb1", "pb2", "pb3", "pb4")

    @classmethod
    def _bank(cls, k):
        if isinstance(k, str):
            if k.startswith("pm_"):
                return "pm"
            if k in cls.PSUM_BANKS:
                return k
        return None

    def add(self, eng, fn, r=(), w=(), stream=None):
        self._lbl = "%s:%s" % (eng, (list(w) + ["?"])[0])
        banks = [self._bank(k) for k in list(r) + list(w)]
        banks = [b for b in banks if b is not None]
        r = [k for k in r if self._bank(k) is None]
        w = [k for k in w if self._bank(k) is None] + sorted(set(banks))
        r = self._expand(r)
        w = self._expand(w)
        deps = set()
        for k in r:
            b = self.bufs.setdefault(k, [None, []])
            if b[0] is not None:
                deps.add(b[0])
        for k in w:
            b = self.bufs.setdefault(k, [None, []])
            if b[0] is not None:
                deps.add(b[0])
            deps.update(b[1])
        idx = len(self.ops)
        deps.discard(idx)
        self.ops.append(dict(eng=eng, fn=fn, deps=deps, stream=stream, idx=idx, has_dep=False, label=self._lbl))
        for k in r:
            self.bufs[k][1].append(idx)
        for k in w:
            self.bufs[k] = [idx, []]
        return idx

    class _Ins:
        def then_inc(self, *a, **k):
            return self

    class _Probe:
        def __init__(self):
            self.calls = []

        def __getattr__(self, name):
            def f(*args, **kw):
                self.calls.append((name, args, kw))
                return Tracker._Ins()
            return f

    @staticmethod
    def _free(ap):
        n = 1
        for d in ap.shape[1:]:
            n *= int(d)
        return n

    def _cost(self, op):
        pr = Tracker._Probe()
        op["fn"](pr)
        eng = op["eng"]
        dur = 0.0
        lat = 0.0
        for name, args, kw in pr.calls:
            out = kw.get("out", args[0] if args else None)
            if name == "dma_start":
                src = kw.get("in_", args[1] if len(args) > 1 else None)
                nbytes = self._free(out) * int(out.shape[0]) * 4
                dur += 80.0
                lat = 2500.0 + nbytes / 160.0
                continue
            n = self._free(out) if out is not None and hasattr(out, "shape") else 64
            if eng == "pe":
                lhs = args[1] if len(args) > 1 else None
                mult = 4.0 if (lhs is not None and lhs.dtype == F32 and name == "matmul") else 1.0
                dur += 16.0 + max(n, 64) * mult / 1.95
            elif eng == "act":
                dur += 230.0 + n / 1.15
            elif eng == "dve":
                dur += 170.0 + n / 0.96
            elif eng == "pool":
                dur += 300.0 + n * 2.0
            else:
                dur += 50.0
        import os
        fr = os.environ.get("K_FREE")
        if fr and any(op["label"].startswith(p) for p in fr.split(",")):
            dur = 20.0
        op["dur"] = max(dur, 20.0)
        op["lat"] = lat

    def schedule(self):
        import heapq
        ops = self.ops
        n = len(ops)
        for op in ops:
            self._cost(op)
        succ = [[] for _ in range(n)]
        for op in ops:
            for d in op["deps"]:
                succ[d].append(op["idx"])
        bl = [0.0] * n
        for i in range(n - 1, -1, -1):
            m = 0.0
            for sidx in succ[i]:
                if bl[sidx] > m:
                    m = bl[sidx]
            bl[i] = m + ops[i]["dur"] + ops[i]["lat"]
        import os
        if os.environ.get("K_CRIT"):
            i = max(range(n), key=lambda j: bl[j])
            print("[crit] DAG critical path %.1f us" % (bl[i] / 1e3))
            agg = {}
            while True:
                agg[ops[i]["label"]] = agg.get(ops[i]["label"], 0.0) + ops[i]["dur"] + ops[i]["lat"]
                nxt = None
                for sidx in succ[i]:
                    if nxt is None or bl[sidx] > bl[nxt]:
                        nxt = sidx
                if nxt is None:
                    break
                i = nxt
            for k, v in sorted(agg.items(), key=lambda kv: -kv[1])[:40]:
                print("     %-28s %.1f us" % (k, v / 1e3))
        ndeps = [len(op["deps"]) for op in ops]
        ready_at = [0.0] * n
        engs = ["pe", "act", "dve", "pool", "sp"]
        avail = {e: [] for e in engs}
        for i in range(n):
            if ndeps[i] == 0:
                avail[ops[i]["eng"]].append(i)
        free_at = {e: 0.0 for e in engs}
        fa_prev = {e: 0.0 for e in engs}
        order = []
        finish = [0.0] * n
        done = 0
        import os
        SYNC_LAT = float(os.environ.get("K_SL", "300"))
        SYNC_LAT_SAME = float(os.environ.get("K_SLS", "150"))
        SLACK = float(os.environ.get("K_SLACK", "0"))
        WINDOW = int(os.environ.get("K_WIN", "4000"))
        lowest_unscheduled = 0
        scheduled = [False] * n
        while done < n:
            best = None
            while lowest_unscheduled < n and scheduled[lowest_unscheduled]:
                lowest_unscheduled += 1
            for e in engs:
                lst = avail[e]
                if not lst:
                    continue
                fa = free_at[e]
                cand = None
                for i in lst:
                    if i > lowest_unscheduled + WINDOW:
                        continue
                    st = ready_at[i] if ready_at[i] > fa else fa
                    stq = fa if st - fa <= SLACK else st
                    key = (stq, -bl[i], i)
                    if cand is None or key < cand[0]:
                        cand = (key, i, st)
                if cand is None:
                    continue
                if best is None or cand[0] < best[0]:
                    best = cand + (e,)
            assert best is not None, "scheduler stuck"
            _, i, st, e = best
            avail[e].remove(i)
            op = ops[i]
            fin = st + op["dur"]
            free_at[e] = fin
            finish[i] = fin + op["lat"]
            op["t_start"] = st
            op["stall"] = st - fa_prev[e]
            crit = None
            for d in op["deps"]:
                if crit is None or finish[d] > finish[crit]:
                    crit = d
            op["crit"] = crit
            fa_prev[e] = fin
            order.append(i)
            scheduled[i] = True
            done += 1
            for sidx in succ[i]:
                fx = finish[i] + (SYNC_LAT if ops[sidx]["eng"] != e else SYNC_LAT_SAME)
                if fx > ready_at[sidx]:
                    ready_at[sidx] = fx
                ndeps[sidx] -= 1
                if ndeps[sidx] == 0:
                    avail[ops[sidx]["eng"]].append(sidx)
        self.est_makespan_us = max(finish) / 1e3
        busy = {e: 0.0 for e in engs}
        for op in ops:
            busy[op["eng"]] += op["dur"]
        print("[sched] est makespan %.1f us; busy us: %s" % (self.est_makespan_us, {e: round(v / 1e3) for e, v in busy.items()}))
        import os
        if os.environ.get("K_STALLS"):
            t_lo, t_hi = [float(v) * 1e3 for v in os.environ["K_STALLS"].split(",")]
            for e in engs:
                agg = {}
                tot = 0.0
                for op in ops:
                    if op["eng"] == e and t_lo <= op["t_start"] < t_hi and op["stall"] > 1.0 and op["crit"] is not None:
                        k = (op["label"], ops[op["crit"]]["label"])
                        agg[k] = agg.get(k, 0.0) + op["stall"]
                        tot += op["stall"]
                print("[stalls] %s total %.1f us" % (e, tot / 1e3))
                for k, v in sorted(agg.items(), key=lambda kv: -kv[1])[:12]:
                    print("     %-28s waits on %-28s %.1f us" % (k[0], k[1], v / 1e3))
        if os.environ.get("K_BUSY"):
            for e in engs:
                agg = {}
                for op in ops:
                    if op["eng"] == e:
                        k = op["label"].rstrip("0123456789_")
                        agg[k] = agg.get(k, 0.0) + op["dur"]
                print("[busy] %s" % e)
                for k, v in sorted(agg.items(), key=lambda kv: -kv[1])[:22]:
                    print("     %-24s %.1f us" % (k, v / 1e3))
        if os.environ.get("K_TL"):
            eng_, t_lo, t_hi = os.environ["K_TL"].split(",")
            t_lo, t_hi = float(t_lo) * 1e3, float(t_hi) * 1e3
            for i in order:
                op = ops[i]
                if op["eng"] == eng_ and t_lo <= op["t_start"] < t_hi:
                    c = ops[op["crit"]]["label"] if op["crit"] is not None else "-"
                    print("[tl] %8.1f +%6.2f stall %6.2f  %-22s <- %s" % (op["t_start"] / 1e3, op["dur"] / 1e3, op["stall"] / 1e3, op["label"], c))
        remap = {old: new for new, old in enumerate(order)}
        new_ops = []
        for new, old in enumerate(order):
            op = ops[old]
            op["deps"] = {remap[d] for d in op["deps"]}
            op["idx"] = new
            new_ops.append(op)
        self.ops = new_ops

    def emit(self, nc, es, same_engine_sync=True, do_schedule=True):
        if do_schedule:
            self.schedule()
        ops = self.ops
        for op in ops:
            nd = set()
            for d in op["deps"]:
                dop = ops[d]
                if dop["stream"] is None and dop["eng"] == "pe" and op["eng"] == "pe" and op["stream"] is None:
                    continue
                if (not same_engine_sync) and dop["stream"] is None and op["stream"] is None and dop["eng"] == op["eng"]:
                    continue
                nd.add(d)
            op["deps"] = nd
            for d in nd:
                ops[d]["has_dep"] = True
        sems = {}
        engs = ["pe", "act", "dve", "pool", "sp"]
        for e in engs:
            sems[e] = es.enter_context(nc.semaphore("s_" + e))
        streams = sorted({op["stream"] for op in ops if op["stream"] is not None})
        for s in streams:
            sems["d:" + s] = es.enter_context(nc.semaphore("d_" + s))
        cnt = {e: 0 for e in engs}
        scnt = {s: 0 for s in streams}
        for op in ops:
            if op["stream"] is not None:
                scnt[op["stream"]] += 1
                op["sig"] = ("d:" + op["stream"], 16 * scnt[op["stream"]])
            elif op["has_dep"]:
                cnt[op["eng"]] += 1
                op["sig"] = (op["eng"], cnt[op["eng"]])
            else:
                op["sig"] = None
        for op in ops:
            if op["stream"] in self.waitall_streams:
                op["sig"] = ("d:" + op["stream"], 16 * scnt[op["stream"]])
        self.final_counts = {("d:" + s): 16 * scnt[s] for s in streams}
        self.sems = sems
        block = es.enter_context(nc.Block())
        per_eng = {e: [op for op in ops if op["eng"] == e] for e in engs}

        def run(engobj, lst, extra_tail=None):
            waited = {}
            for op in lst:
                need = {}
                for d in op["deps"]:
                    sg = ops[d]["sig"]
                    assert sg is not None
                    if sg[1] > need.get(sg[0], 0):
                        need[sg[0]] = sg[1]
                for sk, v in need.items():
                    if v > waited.get(sk, 0):
                        engobj.wait_ge(sems[sk], v)
                        waited[sk] = v
                ins = op["fn"](engobj)
                if op["stream"] is not None:
                    ins.then_inc(sems["d:" + op["stream"]], 16)
                elif op["sig"] is not None:
                    ins.then_inc(sems[op["eng"]], 1)
            if extra_tail is not None:
                extra_tail(engobj, waited)

        def sp_tail(engobj, waited):
            for sk, v in self.final_counts.items():
                if v > waited.get(sk, 0):
                    engobj.wait_ge(sems[sk], v)

        @block.sync
        def _(e):
            run(e, per_eng["sp"], sp_tail)

        @block.tensor
        def _(e):
            run(e, per_eng["pe"])

        @block.scalar
        def _(e):
            run(e, per_eng["act"])

        @block.vector
        def _(e):
            run(e, per_eng["dve"])

        @block.gpsimd
        def _(e):
            run(e, per_eng["pool"])


PROJ_GROUPS = [
    ("if", 2048, 8), ("dt", 6664, 16), ("k", 512, 512), ("v0", 1024, 512), ("v1", 1536, 512),
    ("xbc0", 5128, 512), ("xbc1", 5640, 512), ("xbc2", 6152, 512), ("q", 0, 512),
    ("zm0", 3080, 512), ("zm1", 3592, 512), ("o0", 2056, 512), ("o1", 2568, 512),
    ("zs0", 4104, 512), ("zs1", 4616, 512),
]
STATE_ONLY = {"if", "dt", "k", "v0", "v1", "xbc0", "xbc1", "xbc2"}


def build(n_pre, n_full, debug=()):
    nc = bass.Bass("TRN2", target_bir_lowering=False)
    T_pre, T_full = n_pre * L, n_full * L
    dr = {}
    dr["x"] = nc.dram_tensor("x", [max(T_full, 1), D_MODEL], F32, kind="ExternalInput").ap()
    if n_pre:
        dr["xpre"] = nc.dram_tensor("xpre", [T_pre, D_MODEL], F32, kind="ExternalInput").ap()
    for nm, shp in [("norm_w", [1024]), ("w_in", [1024, NCOL]), ("b_igate", [4]), ("b_fgate", [4]),
                    ("conv_w", [1536, 4]), ("conv_b", [1536]), ("dt_bias", [16]), ("a_log", [16]),
                    ("d_skip", [16]), ("normcat", [2048]), ("w_out", [2048, 1024]),
                    ("final_norm_w", [1024]), ("flag", [1])]:
        dr[nm] = nc.dram_tensor(nm, shp, F32, kind="ExternalInput").ap()
    out_d = nc.dram_tensor("out", [T_full, D_MODEL], F32, kind="ExternalOutput").ap()
    dbg_out = {}

    T = Tracker()
    T.waitall_streams.add("const")
    es = ExitStack()
    with es:
        def sb(name, shape, dt):
            return es.enter_context(nc.sbuf_tensor(name, shape, dt))

        def ps(name, shape, dt):
            return es.enter_context(nc.psum_tensor(name, shape, dt))

        win = sb("win", [128, 8 * NCOL], BF16)
        wout = sb("wout", [128, 16 * 1024], BF16)
        win3 = win[:].rearrange("p (k n) -> p k n", k=8)
        wout3 = wout[:].rearrange("p (k n) -> p k n", k=16)
        identb = sb("identb", [128, 128], BF16)
        identf = sb("identf", [128, 128], F32)
        tri = sb("tri", [128, 128], F32)
        maskb = sb("maskb", [128, 128], BF16)
        onesf = sb("onesf", [128, 128], F32)
        oh48 = sb("oh48", [128, 16 * 128], BF16)
        oh48_3 = oh48[0:48, :].rearrange("p (r s) -> p r s", r=16)
        cbrow = oh48[64:65, 0:1024]
        onesrow = oh48[64:65, 1024:1152]
        NPE_R = 1
        dgc = sb("dgc", [128, NPE_R * 4 * 4 * 128], BF16)
        dgc4 = dgc[:].rearrange("p (t w c) -> p t w c", t=NPE_R * 4, w=4)
        finalw = sb("finalw", [128, 1024], F32)
        cst = sb("cst", [128, 172], F32)
        cw = cst[:, 0:48].rearrange("p (t w) -> p t w", t=12)
        cb = cst[:, 48:60]
        bias8 = cst[:, 60:68]
        dtb = cst[:, 68:84]
        arep = cst[:, 84:100]
        drep = cst[:, 100:116]
        normw_fm = cst[:, 116:124]
        normcat = cst[:, 124:140]
        m05 = cst[:, 140:148]
        flag = cst[:, 148:149]
        alog = cst[:, 152:168]

        xbuf = [sb("xbuf0", [128, 1024], F32), sb("xbuf1", [128, 1024], F32)]
        ARENA_BYTES = 23552
        arena = sb("arena", [128, ARENA_BYTES // 4], F32)

        def carve(layout_off, name, nbytes, dt, subkeys=None):
            assert layout_off[0] % 4 == 0
            o = layout_off[0]
            layout_off[0] += (nbytes + 3) // 4 * 4
            assert layout_off[0] <= ARENA_BYTES, (name, layout_off[0])
            v = arena[:, o // 4:(o + (nbytes + 3) // 4 * 4) // 4]
            if dt != F32:
                v = v.bitcast(dt)
            if subkeys is None:
                T.reg(name, "A", o, nbytes)
            else:
                n = len(subkeys)
                for i, sk in enumerate(subkeys):
                    T.reg(sk, "A", o + i * (nbytes // n), nbytes // n)
            return v

        lo1 = [0]
        xn = carve(lo1, "xn", 2048, BF16)
        xnT = carve(lo1, "xnT", 2048, BF16)
        xnT3 = xnT[:].rearrange("p (k t) -> p k t", k=8)
        xbc_tm = carve(lo1, "xbc_tm", 3072, BF16, ["xbc_tm0", "xbc_tm1", "xbc_tm2"])
        _c0 = carve(lo1, "cacc0", 2048, F32, ["cacc0_%d" % i for i in range(4)])
        cacc = [_c0, _c0]
        _t0 = carve(lo1, "cth0", 1024, BF16)
        cth = [_t0, _t0]
        q_tm = carve(lo1, "q_tm", 1024, BF16)
        tz = carve(lo1, "tz", 2048, BF16, ["tz0", "tz1"])
        k_tm = carve(lo1, "k_tm", 1024, BF16)
        kp_tm = carve(lo1, "kp_tm", 1024, BF16)
        qT = carve(lo1, "qT", 1024, BF16)
        kT = carve(lo1, "kT", 1024, BF16)
        Pm = carve(lo1, "Pm", 1024, BF16, ["Pm_%d" % i for i in range(4)])
        Pm3 = Pm[:].rearrange("p (h j) -> p h j", h=4)
        Cbf = carve(lo1, "Cbf", 2064, BF16)
        Cbf3 = Cbf[:].rearrange("p (h c) -> p h c", h=4)
        g1 = carve(lo1, "g1", 2048, BF16, ["g1_0", "g1_1"])
        lo2 = [0]
        dec = carve(lo2, "dec", 4096, BF16, ["dec_%d" % i for i in range(4)])
        dec3 = dec[:].rearrange("p (r l) -> p r l", r=16)
        rl = carve(lo2, "rl", 2048, F32)
        CBm = carve(lo2, "CBm", 512, BF16)
        CBm3 = CBm[:].rearrange("p (g l) -> p g l", g=2)
        yo = carve(lo2, "yo", 4096, F32, ["yo_0", "yo_1"])
        ytmp = carve(lo2, "ytmp", 2048, F32)
        x_tm = carve(lo2, "x_tm", 2048, BF16)
        B_tm = carve(lo2, "B_tm", 512, BF16)
        xdt = carve(lo2, "xdt", 2048, BF16)
        xde = carve(lo2, "xde", 2048, BF16)
        mixT = carve(lo2, "mixT", 4096, BF16, ["mixT_0", "mixT_1"])
        mixT3 = mixT[:].rearrange("p (k t) -> p k t", k=16)

        junk = sb("junk", [128, 2], BF16)
        junk_r = sb("junk_r", [128, 2], BF16)
        junk_s = sb("junk_s", [128, 2], BF16)
        junk_o = sb("junk_o", [128, 2], BF16)
        vaug = sb("vaug", [128, 4 * 258], BF16)
        vaug3 = vaug[:].rearrange("p (h c) -> p h c", h=4)
        gs = sb("gs", [128, 1024], BF16)
        xbcT = sb("xbcT", [128, 12 * 132], BF16)
        xbcT3 = xbcT[:].rearrange("p (t c) -> p t c", t=12)
        xcT = sb("xcT", [128, 1536], BF16)
        xcT3 = xcT[:].rearrange("p (t c) -> p t c", t=12)
        Cst = sb("Cst", [128, 4 * 258], F32)
        Cst3 = Cst[:].rearrange("p (h c) -> p h c", h=4)
        hst = sb("hst", [128, 1024], F32)
        hbf = sb("hbf", [128, 1024], BF16)
        mix = sb("mix", [128, 2048], BF16)
        a3 = sb("a3", [128, 48], BF16)
        A48 = sb("A48", [48, 128], BF16)
        sm = sb("sm", [128, 256], F32)
        g8 = sm[:, 0:8]
        t8 = sm[:, 8:16]
        e4 = sm[:, 16:20]
        nlf = sm[:, 20:24]
        li = sm[:, 24:28]
        T1 = sm[:, 28:36]
        T2 = sm[:, 36:44]
        wf = sm[:, 44:52]
        soldb = sm[:, 52:56]
        absden = sm[:, 56:60]
        ssqr = sm[:, 60:64]
        dn4 = sm[:, 64:68]
        rd4 = sm[:, 68:72]
        t4 = sm[:, 72:76]
        rs4 = sm[:, 76:80]
        sc4 = sm[:, 80:84]
        ssq_x = sm[:, 84:85]
        rstd_x = sm[:, 85:86]
        tx1 = sm[:, 86:87]
        ssq_o = sm[:, 87:88]
        rstd_o = sm[:, 88:89]
        to1 = sm[:, 89:90]
        ssq_s = sm[:, 90:92]
        rstd_s = sm[:, 92:94]
        ts2 = sm[:, 94:96]
        dtp = sm[:, 96:112]
        edt = sm[:, 112:128]
        dt16 = sm[:, 128:144]
        a_tm = sm[:, 144:160]
        acum_sb = sm[:, 160:176]
        ea = sm[:, 176:192]
        dEa = sm[:, 192:208]
        dE = sm[:, 208:224]
        cdb = sm[:, 224:240]
        r1 = sm[:, 240:256]
        sm2 = sb("sm2", [128, 16], F32)
        r2 = sm2[:, 0:16]
        gm = sb("gm", [4, 16], F32)
        mst = gm[:, 0:1]
        umax = gm[:, 1:2]
        Rg = gm[:, 2:3]
        dg = gm[:, 3:4]
        soldg = gm[:, 4:5]
        Dg = gm[:, 8:16]
        pT = [ps("pT0", [128, 1024], BF16), ps("pT1", [128, 1024], BF16)]
        pm = ps("pm", [128, 512], F32)
        pb = [ps("pb%d" % i, [128, 512], F32) for i in range(5)]
        print("sbuf bytes remaining:", nc.sbuf_bytes_remaining)

        bank_rr = {"A": 0, "B": 0}
        BANKS = {"A": [0, 1], "B": [2, 3, 4]}

        def next_bank(ph="B"):
            lst = BANKS[ph]
            i = lst[bank_rr[ph] % len(lst)]
            bank_rr[ph] += 1
            return pb[i], "pb%d" % i

        def next_pT(ph="B"):
            i = 0 if ph == "A" else 1
            return pT[i], "pT%d" % i

        def P(fn, r=(), w=()):
            T.add("pool", fn, r=list(r), w=list(w))
        P(lambda e: e.memset(onesf[:], 1.0), w=["onesf"])
        P(lambda e: e.memset(identf[:], 1.0), w=["identf"])
        P(lambda e: e.affine_select(identf[:], identf[:], [[-1, 128]], ALU.is_equal, 0.0, base=0, channel_multiplier=1),
          r=["identf"], w=["identf"])
        P(lambda e: e.tensor_copy(identb[:], identf[:]), r=["identf"], w=["identb"])
        P(lambda e: e.memset(tri[:], 1.0), w=["tri"])
        P(lambda e: e.affine_select(tri[:], tri[:], [[1, 128]], ALU.is_ge, 0.0, base=0, channel_multiplier=-1), r=["tri"], w=["tri"])
        P(lambda e: e.tensor_copy(maskb[:], tri[:]), r=["tri"], w=["maskb"])
        P(lambda e: e.memset(m05, -0.5), w=["m05"])
        P(lambda e: e.memset(vaug[:], 0.0), w=["vaug"])
        P(lambda e: e.memset(vaug3[:, :, 256:257], 1.0), r=["vaug"], w=["vaug"])
        P(lambda e: e.memset(Cst[:], 0.0), w=["Cst"])
        P(lambda e: e.memset(hst[:], 0.0), w=["hst", "hst0", "hst1"])
        P(lambda e: e.memset(hbf[:], 0.0), w=["hbf0", "hbf1"])
        P(lambda e: e.memset(Cbf[:], 0.0), w=["Cbf"])
        P(lambda e: e.memset(mst, 0.0), w=["mst"])
        P(lambda e: e.memset(xbcT[:], 0.0), w=["xbcT_carry", "xbcT_0", "xbcT_1"])
        P(lambda e: e.memset(oh48[0:48, :], 1.0), w=["oh48"])
        P(lambda e: e.memset(onesrow, 1.0), w=["onesrow"])
        P(lambda e: e.affine_select(oh48_3, oh48_3, [[-1, 16], [0, 128]], ALU.is_equal, 0.0, base=0, channel_multiplier=1),
          r=["oh48"], w=["oh48"])
        P(lambda e: e.affine_select(oh48_3, oh48_3, [[-1, 16], [0, 128]], ALU.not_equal, 1.0, base=-16, channel_multiplier=1),
          r=["oh48"], w=["oh48"])
        P(lambda e: e.affine_select(oh48_3, oh48_3, [[-1, 16], [0, 128]], ALU.not_equal, 1.0, base=-32, channel_multiplier=1),
          r=["oh48"], w=["oh48"])

        def cdma(out_ap, in_ap, key, noncontig=False):
            T.add("sp", lambda e: e.dma_start(out=out_ap, in_=in_ap, allow_slow_non_contiguous=noncontig),
                  w=[key], stream="const")

        cdma(normw_fm, dr["norm_w"].rearrange("(k p) -> p k", p=128), "normw_fm", True)
        cdma(normcat, dr["normcat"].rearrange("(k p) -> p k", p=128), "normcat", True)
        cdma(cw, dr["conv_w"].rearrange("(t p) w -> p t w", p=128), "cw")
        cdma(cb, dr["conv_b"].rearrange("(t p) -> p t", p=128), "cb", True)
        cdma(bias8[:, 0:4], dr["b_igate"].partition_broadcast(128), "bias8a")
        cdma(bias8[:, 4:8], dr["b_fgate"].partition_broadcast(128), "bias8b")
        cdma(dtb, dr["dt_bias"].partition_broadcast(128), "dtb")
        cdma(alog, dr["a_log"].partition_broadcast(128), "alog")
        cdma(drep, dr["d_skip"].partition_broadcast(128), "drep")
        cdma(finalw[:], dr["final_norm_w"].partition_broadcast(128), "finalw")
        cdma(flag, dr["flag"].partition_broadcast(128), "flag")

        T.add("act", lambda e: e.activation(out=arep, in_=alog, func=AF.Exp), r=["alog"], w=["arep0"])
        T.add("dve", lambda e: e.tensor_scalar_mul(arep, arep, -1.0), r=["arep0"], w=["arep"])
        T.add("dve", lambda e: e.tensor_scalar_mul(cst[:, 0:60], cst[:, 0:60], 0.5), r=["cw", "cb"], w=["cwb"])
        T.add("dve", lambda e: e.tensor_scalar_mul(finalw[:], finalw[:], 32.0), r=["finalw"], w=["finalw2"])
        T.add("dve", lambda e: e.tensor_scalar_mul(normw_fm, normw_fm, 32.0), r=["normw_fm"], w=["normw2"])
        T.add("dve", lambda e: e.tensor_scalar_mul(normcat[:, 0:8], normcat[:, 0:8], 4.0), r=["normcat"], w=["normcat_a"])
        T.add("dve", lambda e: e.tensor_scalar_mul(normcat[:, 8:16], normcat[:, 8:16], float(np.sqrt(2048.0) / 2.0)),
              r=["normcat"], w=["normcat_b"])

        T.add("pool", lambda e: e.dma_start(out=cbrow, in_=dr["conv_b"][0:1024].rearrange("(o n) -> o n", o=1)), w=["cbrow0"], stream="const2")
        T.add("pool", lambda e: e.tensor_scalar(cbrow, cbrow, 0.5, None, ALU.mult), r=["cbrow0"], w=["cbrow"])
        for t_ in range(NPE_R * 4):
            for w_ in range(4):
                T.add("dve" if (t_ * 4 + w_) % 2 == 0 else "pool",
                      lambda e, t_=t_, w_=w_: e.tensor_scalar(dgc4[:, t_, w_, :], identf[:], cw[:, t_, w_:w_ + 1], None, ALU.mult),
                      r=["cwb", "identf"], w=["dgc_%d_%d" % (t_, w_)])
        dgc_keys = ["dgc_%d_%d" % (t_, w_) for t_ in range(NPE_R * 4) for w_ in range(4)]

        W_PRE = [(512, 1024), (1024, 2048), (2048, 2056), (5128, 6152), (6152, 6680)]
        W_FULL = [(0, 512), (2056, 3080), (3080, 4104), (4104, 5128)]
        wkeys = {}
        piece = [0]
        NSTG = 8
        wout_f = wout[:].bitcast(F32)

        def wpiece(src_ap, dst_ap, scale_ap, scale_key, ncols, stg_ap, stg_key, stream, wkey, extra_w=()):
            i = piece[0]
            piece[0] += 1
            T.add("sp", lambda e: e.dma_start(out=stg_ap[:, 0:ncols], in_=src_ap), w=[stg_key], stream=stream)
            if i % 2 == 1:
                T.add("act", lambda e: e.activation(out=dst_ap, in_=stg_ap[:, 0:ncols], func=AF.Copy, scale=scale_ap),
                      r=[stg_key, scale_key], w=[wkey] + list(extra_w))
            else:
                T.add("dve", lambda e: e.tensor_scalar_mul(dst_ap, stg_ap[:, 0:ncols], scale_ap),
                      r=[stg_key, scale_key], w=[wkey] + list(extra_w))

        for (c0, c1) in W_PRE + W_FULL:
            wkeys[(c0, c1)] = []
            for k in range(8):
                sl_ = piece[0] % NSTG
                wk = "w_%d_%d" % (c0, k)
                wkeys[(c0, c1)].append(wk)
                wpiece(dr["w_in"][k * 128:(k + 1) * 128, c0:c1], win3[:, k, c0:c1], normw_fm[:, k:k + 1], "normw2", c1 - c0,
                       wout_f[:, sl_ * 1024:(sl_ + 1) * 1024], "wstg%d" % sl_, "stg%d" % sl_, wk)

        def wkeys_for(c0, n):
            out = []
            for (a0, a1), ks in wkeys.items():
                if a0 < c0 + n and c0 < a1:
                    out += ks
            return out

        def load_wout():
            for kc in range(16):
                T.add("pool", lambda e, kc=kc: e.dma_start(out=wout3[:, kc, :], in_=dr["w_out"][kc * 128:(kc + 1) * 128, :]),
                      w=["wout_%d" % kc] + ["wstg%d" % j for j in range(NSTG)], stream="wo")
        wout_keys = ["wout_%d" % kc for kc in range(16)]
        load_wout()

        def load_x(src, ci, slot):
            T.add("sp", lambda e: e.dma_start(out=xbuf[slot][:], in_=src[ci * L:(ci + 1) * L, :]),
                  w=["xbuf%d" % slot], stream="xin%d" % slot)

        chunk_list = [("pre", i) for i in range(n_pre)] + [("full", i) for i in range(n_full)]

        def src_of(kind):
            return dr["xpre"] if kind == "pre" else dr["x"]

        import os as _os
        _DBL = set((_os.environ.get("K_DBL") or "").split(",")) - {""}
        _PERSIST = {"Cst", "hst", "hst0", "hst1", "hbf0", "hbf1", "mst", "xbcT_carry", "identb", "identf", "tri", "maskb", "onesf",
                    "m05", "oh48", "cwb", "arep", "drep", "dtb", "bias8a", "bias8b", "finalw2", "flag", "out_dram", "xbuf0", "xbuf1"}
        _par = [0]

        def _km(k):
            if not isinstance(k, str) or Tracker._bank(k) is not None or k in _PERSIST or k.startswith("w_") or k.startswith("wout_"):
                return k
            if "ALL" in _DBL or k in _DBL or k.rstrip("0123456789_") in _DBL:
                return "%s@%d" % (k, _par[0])
            return k

        def TA(eng, fn, r=(), w=(), stream=None):
            return T.add(eng, fn, r=[_km(k) for k in r], w=[_km(k) for k in w], stream=stream)

        def chunk(gi):
            kind, ci = chunk_list[gi]
            _par[0] = gi % 2
            full = kind == "full"
            need_c = full or (gi == n_pre - 1)
            slot = gi % 2
            xb = xbuf[slot]
            xk = "xbuf%d" % slot
            if gi + 1 < len(chunk_list):
                nk, nci = chunk_list[gi + 1]
                load_x(src_of(nk), nci, (gi + 1) % 2)
            TA("act", lambda e: e.activation(out=junk[:, 0:1].broadcast_to([128, 1024]), in_=xb[:], func=AF.Square, accum_out=ssq_x), r=[xk], w=["ssq_x", "junk_x"])
            TA("pool", lambda e: e.tensor_scalar(tx1, ssq_x, 1024.0 * EPS, None, ALU.add), r=["ssq_x"], w=["tx1"])
            TA("pool", lambda e: e.tensor_tensor(rstd_x, tx1, m05[:, 0:1], ALU.pow), r=["tx1", "m05"], w=["rstd_x"])
            TA("dve", lambda e: e.tensor_scalar_mul(xn[:], xb[:], rstd_x), r=[xk, "rstd_x"], w=["xn"])
            p, pk = next_pT("A")

            def tr_x(e, p=p):
                for k in range(8):
                    ins = e.transpose(p[:, k * 128:(k + 1) * 128], xn[:, k * 128:(k + 1) * 128], identb[:])
                return ins
            TA("pe", tr_x, r=["xn", "identb"], w=[pk])
            TA("act", lambda e, p=p: e.activation(out=xnT[:], in_=p[:, 0:1024], func=AF.Copy), r=[pk], w=["xnT"])

            def proj(name, c0, n, out_ap, okey, extra_r=()):
                def f(e):
                    for k in range(8):
                        ins = e.matmul(out_ap, xnT3[:, k, :], win3[:, k, c0:c0 + n], start=(k == 0), stop=(k == 7))
                    return ins
                TA("pe", f, r=["xnT"] + wkeys_for(c0, n) + list(extra_r), w=[okey])

            for name, c0, n in PROJ_GROUPS:
                if not full and name not in STATE_ONLY:
                    continue
                if not need_c and name == "xbc2":
                    n = 256
                if name == "if":
                    b, bk = next_bank("A")
                    proj(name, c0, n, b[:, 0:8], bk)
                    TA("dve", lambda e, b=b: e.tensor_tensor(g8, b[:, 0:8], bias8, ALU.add), r=[bk, "bias8a", "bias8b"], w=["g8"])
                    TA("act", lambda e: e.activation(out=t8, in_=g8, func=AF.Tanh, scale=1.0 / 15.0), r=["g8"], w=["t8"])
                    TA("act", lambda e: e.activation(out=e4, in_=t8[:, 4:8], func=AF.Exp, scale=-15.0), r=["t8"], w=["e4"])
                    TA("act", lambda e: e.activation(out=nlf, in_=e4, func=AF.Ln, bias=1.0), r=["e4"], w=["nlf"])
                    TA("dve", lambda e: e.tensor_scalar_mul(li, t8[:, 0:4], 15.0), r=["t8"], w=["li"])
                elif name == "dt":
                    b, bk = next_bank("A")
                    proj(name, c0, n, b[:, 0:16], bk)
                    TA("dve", lambda e, b=b: e.tensor_tensor(dtp, b[:, 0:16], dtb, ALU.add), r=[bk, "dtb"], w=["dtp"])
                    TA("act", lambda e: e.activation(out=edt, in_=dtp, func=AF.Exp), r=["dtp"], w=["edt"])
                    TA("act", lambda e: e.activation(out=dt16, in_=edt, func=AF.Ln, bias=1.0), r=["edt"], w=["dt16"])
                    TA("dve", lambda e: e.tensor_tensor(a_tm, dt16, arep, ALU.mult), r=["dt16", "arep"], w=["a_tm"])
                else:
                    b, bk = next_bank("A")
                    proj(name, c0, n, b[:, 0:n], bk)
                    if name == "k":
                        TA("act", lambda e, b=b: e.activation(out=k_tm[:], in_=b[:, 0:512], func=AF.Copy), r=[bk], w=["k_tm"])
                    elif name in ("v0", "v1"):
                        h0 = 0 if name == "v0" else 2
                        TA("act", lambda e, b=b, h0=h0: e.activation(
                            out=vaug3[:, h0:h0 + 2, 0:256], in_=b[:, 0:512].rearrange("p (h c) -> p h c", h=2), func=AF.Copy),
                            r=[bk, "vaug"], w=["vaug_%d" % h0])
                    elif name.startswith("xbc"):
                        j = int(name[3])
                        eng = "act"
                        if eng == "act":
                            TA("act", lambda e, b=b, j=j, n=n: e.activation(out=xbc_tm[:, j * 512:j * 512 + n], in_=b[:, 0:n], func=AF.Copy),
                                  r=[bk], w=["xbc_tm%d" % j])
                        else:
                            TA("dve", lambda e, b=b, j=j: e.tensor_copy(xbc_tm[:, j * 512:(j + 1) * 512], b[:, 0:512]),
                                  r=[bk], w=["xbc_tm%d" % j])
                    elif name == "q":
                        TA("act", lambda e, b=b: e.activation(out=q_tm[:], in_=b[:, 0:512], func=AF.Copy, scale=float(128 ** -0.5)),
                              r=[bk], w=["q_tm"])
                    elif name in ("zm0", "zm1"):
                        hh = int(name[2])
                        sl = slice(hh * 512, (hh + 1) * 512)
                        TA("act", lambda e, b=b, sl=sl: e.activation(out=tz[:, sl], in_=b[:, 0:512], func=AF.Tanh, scale=0.5),
                              r=[bk], w=["tz%d" % hh])
                        TA("dve", lambda e, b=b, sl=sl: e.scalar_tensor_tensor(g1[:, sl], tz[:, sl], 1.0, b[:, 0:512], ALU.add, ALU.mult),
                              r=[bk, "tz%d" % hh], w=["g1_%d" % hh])
                    elif name in ("o0", "o1"):
                        hh = int(name[1])
                        sl = slice(hh * 512, (hh + 1) * 512)
                        TA("act", lambda e, b=b, sl=sl: e.activation(out=tz[:, sl], in_=b[:, 0:512], func=AF.Tanh, scale=0.5),
                              r=[bk], w=["tz%d" % hh])
                        TA("dve", lambda e, sl=sl: e.scalar_tensor_tensor(g1[:, sl], tz[:, sl], 1.0, g1[:, sl], ALU.add, ALU.mult),
                              r=["tz%d" % hh, "g1_%d" % hh], w=["g1_%d" % hh])
                    elif name in ("zs0", "zs1"):
                        hh = int(name[2])
                        sl = slice(hh * 512, (hh + 1) * 512)
                        TA("act", lambda e, b=b, sl=sl: e.activation(out=tz[:, sl], in_=b[:, 0:512], func=AF.Tanh, scale=0.5),
                              r=[bk], w=["tz%d" % hh])
                        TA("dve", lambda e, b=b, sl=sl: e.scalar_tensor_tensor(gs[:, sl], tz[:, sl], 1.0, b[:, 0:512], ALU.add, ALU.mult),
                              r=[bk, "tz%d" % hh], w=["gs_%d" % hh])

            def tr4(src, dst, skey, dkey):
                p, pk = next_pT("A")

                def f(e, p=p):
                    for h in range(4):
                        ins = e.transpose(p[:, h * 128:(h + 1) * 128], src[:, h * 128:(h + 1) * 128], identb[:])
                    return ins
                TA("pe", f, r=[skey, "identb"], w=[pk])
                TA("act", lambda e, p=p: e.activation(out=dst[:], in_=p[:, 0:512], func=AF.Copy), r=[pk], w=[dkey])

            if full:
                tr4(k_tm, kT, "k_tm", "kT")
                tr4(q_tm, qT, "q_tm", "qT")
            for half, (t0, nt) in enumerate([(0, 8), (8, 4 if need_c else 2)]):
                p, pk = next_pT("A")

                def f(e, p=p, t0=t0, nt=nt):
                    for t in range(nt):
                        ins = e.transpose(p[:, t * 128:(t + 1) * 128], xbc_tm[:, (t0 + t) * 128:(t0 + t + 1) * 128], identb[:])
                    return ins
                TA("pe", f, r=["xbc_tm0", "xbc_tm1", "xbc_tm2", "identb"], w=[pk])
                TA("act", lambda e, p=p, t0=t0, nt=nt: e.activation(
                    out=xbcT3[:, t0:t0 + nt, 4:132], in_=p[:, 0:nt * 128].rearrange("p (t c) -> p t c", t=nt), func=AF.Copy),
                    r=[pk, "xbcT_carry"], w=["xbcT_%d" % half])

            TA("pe", lambda e: e.matmul(pm[:, 24:28], tri[:], nlf, start=True, stop=True), r=["tri", "nlf"], w=["pm_nb"])

            def f_ugm(e):
                e.matmul(pm[0:4, 128:256], li, identf[:], start=True, stop=False)
                return e.matmul(pm[0:4, 128:256], nlf, tri[:], start=False, stop=True)
            TA("pe", f_ugm, r=["li", "nlf", "identf", "tri"], w=["pm_ugm"])
            TA("pe", lambda e: e.matmul(pm[0:4, 120:121], nlf, onesf[:, 0:1], start=True, stop=True), r=["nlf", "onesf"], w=["pm_nbl"])
            TA("dve", lambda e: e.reduce_max(umax, pm[0:4, 128:256], AX.X), r=["pm_ugm"], w=["umax"])
            TA("dve", lambda e: e.tensor_tensor(Rg, umax, mst, ALU.max), r=["umax", "mst"], w=["Rg"])
            TA("dve", lambda e: e.tensor_tensor(dg, mst, Rg, ALU.subtract), r=["mst", "Rg"], w=["dg"])
            TA("act", lambda e: e.activation(out=soldg, in_=dg, func=AF.Exp), r=["dg"], w=["soldg"])
            TA("dve", lambda e: e.tensor_tensor(mst, Rg, pm[0:4, 120:121], ALU.subtract), r=["Rg", "pm_nbl", "dg"], w=["mst"])
            TA("dve", lambda e: e.tensor_scalar_mul(Dg[:, 0:4], identf[0:4, 0:4], Rg), r=["Rg", "identf"], w=["Dg_a"])
            TA("dve", lambda e: e.tensor_scalar_mul(Dg[:, 4:8], identf[0:4, 0:4], soldg), r=["soldg", "identf"], w=["Dg_b"])
            TA("pe", lambda e: e.matmul(pm[:, 28:36], onesf[0:4, :], Dg, start=True, stop=True), r=["onesf", "Dg_a", "Dg_b"], w=["pm_rs"])
            TA("dve", lambda e: e.tensor_tensor(T1[:, 0:4], li, pm[:, 24:28], ALU.add), r=["li", "pm_nb"], w=["T1a"])
            TA("dve", lambda e: e.tensor_copy(T1[:, 4:8], pm[:, 24:28]), r=["pm_nb"], w=["T1b"])
            TA("dve", lambda e: e.tensor_tensor(
                T2.rearrange("p (a h) -> p a h", a=2), T1.rearrange("p (a h) -> p a h", a=2),
                pm[:, 28:32].unsqueeze(1).broadcast_to([128, 2, 4]), ALU.subtract), r=["T1a", "T1b", "pm_rs"], w=["T2"])
            TA("act", lambda e: e.activation(out=wf, in_=T2, func=AF.Exp), r=["T2"], w=["wf"])
            TA("dve", lambda e: e.tensor_copy(soldb, pm[:, 32:36]), r=["pm_rs"], w=["soldb"])

            TA("pe", lambda e: e.matmul(pm[:, 40:56], tri[:], a_tm, start=True, stop=True), r=["tri", "a_tm"], w=["pm_acum"])
            TA("pe", lambda e: e.matmul(pm[:, 56:72], onesf[:], a_tm, start=True, stop=True), r=["onesf", "a_tm"], w=["pm_alast"])
            TA("dve", lambda e: e.tensor_copy(acum_sb, pm[:, 40:56]), r=["pm_acum"], w=["acum_sb"])
            TA("act", lambda e: e.activation(out=ea, in_=pm[:, 40:56], func=AF.Exp), r=["pm_acum"], w=["ea"])
            TA("dve", lambda e: e.tensor_tensor(dEa, pm[:, 56:72], acum_sb, ALU.subtract), r=["pm_alast", "acum_sb"], w=["dEa"])
            TA("act", lambda e: e.activation(out=dE, in_=dEa, func=AF.Exp), r=["dEa"], w=["dE"])
            TA("act", lambda e: e.activation(out=cdb, in_=pm[:, 56:72], func=AF.Exp), r=["pm_alast"], w=["cdb"])
            if full:
                TA("dve", lambda e: e.tensor_copy(a3[:, 0:16], acum_sb), r=["acum_sb"], w=["a3_0"])
                TA("dve", lambda e: e.tensor_tensor(r1, acum_sb, a3[:, 0:16], ALU.subtract), r=["acum_sb", "a3_0"], w=["r1"])
                TA("dve", lambda e: e.tensor_copy(a3[:, 16:32], r1), r=["r1"], w=["a3_1"])
                TA("dve", lambda e: e.tensor_tensor(r2, r1, a3[:, 16:32], ALU.subtract), r=["r1", "a3_1"], w=["r2"])
                TA("dve", lambda e: e.tensor_copy(a3[:, 32:48], r2), r=["r2"], w=["a3_2"])
                p, pk = next_pT()
                TA("pe", lambda e, p=p: e.transpose(p[0:48, 0:128], a3[:, 0:48], identb[:]), r=["a3_0", "a3_1", "a3_2", "identb"], w=[pk])
                TA("dve", lambda e, p=p: e.tensor_copy(A48[:], p[0:48, 0:128]), r=[pk], w=["A48"])

            for rnd in range(NPE_R):
                cvb, cvk = next_bank()

                def f_cv(e, rnd=rnd, cvb=cvb):
                    for tt in range(4):
                        t = rnd * 4 + tt
                        o_ = cvb[:, tt * 128:(tt + 1) * 128]
                        for w_ in range(4):
                            e.matmul(o_, dgc4[:, t, w_, :], xbcT3[:, t, 1 + w_:129 + w_], start=(w_ == 0), stop=False)
                        ins = e.matmul(o_, cbrow[:, t * 128:(t + 1) * 128], onesrow, start=False, stop=True)
                    return ins
                TA("pe", f_cv, r=["xbcT_0", "xbcT_carry", "cbrow", "onesrow"] + dgc_keys, w=[cvk])
                ct = cth[0]
                TA("act", lambda e, cvb=cvb, ct=ct: e.activation(out=ct[:], in_=cvb[:, 0:512], func=AF.Tanh), r=[cvk], w=["cth0"])
                TA("dve", lambda e, cvb=cvb, ct=ct, rnd=rnd: e.scalar_tensor_tensor(
                    xcT[:, rnd * 512:(rnd + 1) * 512], ct[:], 1.0, cvb[:, 0:512], ALU.add, ALU.mult),
                    r=["cth0", cvk], w=["xcT_%d" % rnd])
            for rnd in range(NPE_R, 3):
                ca = cacc[0]
                ct = cth[0]
                cak = "cacc0"
                ctk = "cth0"
                src_key = "xbcT_0" if rnd < 2 else "xbcT_1"
                for tt in range(4):
                    t = rnd * 4 + tt
                    if not full and t >= 10:
                        continue
                    cslice = ca[:, tt * 128:(tt + 1) * 128]
                    ckey = cak + "_%d" % tt
                    TA("dve", lambda e, t=t, cslice=cslice: e.tensor_scalar(
                        cslice, xbcT3[:, t, 1:129], cw[:, t, 0:1], cb[:, t:t + 1], ALU.mult, ALU.add),
                        r=[src_key, "xbcT_carry", "cwb"], w=[ckey])
                    for w_ in range(1, 4):
                        TA("dve", lambda e, t=t, cslice=cslice, w_=w_: e.scalar_tensor_tensor(
                            cslice, xbcT3[:, t, 1 + w_:129 + w_], cw[:, t, w_:w_ + 1], cslice, ALU.mult, ALU.add),
                            r=[src_key, "xbcT_carry", "cwb", ckey], w=[ckey])
                nv = 512 if (full or rnd < 2) else 256
                TA("act", lambda e, ca=ca, ct=ct, nv=nv: e.activation(out=ct[:, 0:nv], in_=ca[:, 0:nv], func=AF.Tanh),
                   r=[cak + "_%d" % i for i in range(4)], w=[ctk])
                TA("dve", lambda e, ca=ca, ct=ct, rnd=rnd, nv=nv: e.scalar_tensor_tensor(
                    xcT[:, rnd * 512:rnd * 512 + nv], ct[:, 0:nv], 1.0, ca[:, 0:nv], ALU.add, ALU.mult),
                    r=[ctk] + [cak + "_%d" % i for i in range(4)], w=["xcT_%d" % rnd])
            ntc = 12 if need_c else 10
            TA("pool", lambda e, ntc=ntc: e.tensor_copy(xbcT3[:, 0:ntc, 1:4], xbcT3[:, 0:ntc, 129:132]), r=["xbcT_0", "xbcT_1"], w=["xbcT_carry"])

            TA("pool", lambda e: e.tensor_tensor(
                kp_tm[:].rearrange("p (h d) -> p h d", h=4), k_tm[:].rearrange("p (h d) -> p h d", h=4),
                wf[:, 0:4].unsqueeze(2).broadcast_to([128, 4, 128]), ALU.mult), r=["k_tm", "wf"], w=["kp_tm"])
            TA("pool", lambda e: e.tensor_tensor(Cst3[:, :, 0:257], Cst3[:, :, 0:257],
                                                    soldb.unsqueeze(2).broadcast_to([128, 4, 257]), ALU.mult),
                  r=["Cst", "soldb"], w=["Cst"])
            if full:
                TA("act", lambda e: e.activation(out=Cbf3[:, :, 0:257], in_=Cst3[:, :, 0:257], func=AF.Copy), r=["Cst"], w=["Cbf"])
                sb_, sbk = next_bank()

                def f_st(e, sb_=sb_):
                    for h in range(4):
                        ins = e.matmul(sb_[:, h * 128:(h + 1) * 128], kT[:, h * 128:(h + 1) * 128], qT[:, h * 128:(h + 1) * 128],
                                       start=True, stop=True)
                    return ins
                TA("pe", f_st, r=["kT", "qT"], w=[sbk])
                for h in range(4):
                    TA("dve", lambda e, h=h, sb_=sb_: e.scalar_tensor_tensor(
                        Pm3[:, h, :], sb_[:, h * 128:(h + 1) * 128], wf[:, h:h + 1], maskb[:], ALU.mult, ALU.mult),
                        r=[sbk, "wf", "maskb"], w=["Pm_%d" % h])
                brs = []
                for hp in range(2):
                    bb, bbk = next_bank()
                    for h in (2 * hp, 2 * hp + 1):
                        brs.append((bb, bbk, (h % 2) * 256))

                    def f_br(e, hp=hp, bb=bb):
                        for h in (2 * hp, 2 * hp + 1):
                            o_ = (h % 2) * 256
                            e.matmul(bb[:, o_:o_ + 256], Pm3[:, h, :], vaug3[:, h, 0:256], start=True, stop=False)
                            ins = e.matmul(bb[:, o_:o_ + 256], qT[:, h * 128:(h + 1) * 128], Cbf3[:, h, 0:256], start=False, stop=True)
                        return ins
                    TA("pe", f_br, r=["Pm_%d" % (2 * hp), "Pm_%d" % (2 * hp + 1), "vaug_0", "vaug_2", "qT", "Cbf"], w=[bbk])
                    for h in (2 * hp, 2 * hp + 1):
                        o_ = (h % 2) * 256
                        TA("act", lambda e, h=h, bb=bb, o_=o_: e.activation(out=junk_r[:, 0:1].broadcast_to([128, 256]), in_=bb[:, o_:o_ + 256], func=AF.Square,
                                                                            accum_out=ssqr[:, h:h + 1]), r=[bbk], w=["ssqr_%d" % h, "junk_r"])

                def f_den(e):
                    for h in range(4):
                        e.matmul(pm[:, 64 + h:65 + h], Pm3[:, h, :], vaug3[:, h, 256:257], start=True, stop=False)
                        ins = e.matmul(pm[:, 64 + h:65 + h], qT[:, h * 128:(h + 1) * 128], Cbf3[:, h, 256:257], start=False, stop=True)
                    return ins
                TA("pe", f_den, r=["Pm_0", "Pm_1", "Pm_2", "Pm_3", "vaug_0", "vaug_2", "qT", "Cbf"], w=["pm_den"])
                TA("act", lambda e: e.activation(out=absden, in_=pm[:, 64:68], func=AF.Abs), r=["pm_den"], w=["absden"])
                hk = ["absden"]
                sk = ["ssqr_%d" % h for h in range(4)]
                TA("dve", lambda e: e.tensor_tensor(dn4, absden, wf[:, 4:8], ALU.max), r=hk + ["wf"], w=["dn4"])
                TA("dve", lambda e: e.reciprocal(rd4, dn4), r=["dn4"], w=["rd4"])
                TA("dve", lambda e: e.tensor_tensor(t4, rd4, rd4, ALU.mult), r=["rd4"], w=["t4"])
                TA("dve", lambda e: e.tensor_tensor(t4, t4, ssqr, ALU.mult), r=["t4"] + sk, w=["t4"])
                TA("pool", lambda e: e.tensor_scalar(t4, t4, 256.0 * EPS, None, ALU.add), r=["t4"], w=["t4"])
                TA("pool", lambda e: e.tensor_tensor(rs4, t4, m05[:, 0:4], ALU.pow), r=["t4", "m05"], w=["rs4"])
                TA("dve", lambda e: e.tensor_tensor(sc4, rd4, rs4, ALU.mult), r=["rd4", "rs4"], w=["sc4"])
                for h in range(4):
                    bb, bbk, o_ = brs[h]
                    TA("dve", lambda e, h=h, bb=bb, o_=o_: e.scalar_tensor_tensor(
                        mix[:, h * 256:(h + 1) * 256], bb[:, o_:o_ + 256], sc4[:, h:h + 1], g1[:, h * 256:(h + 1) * 256], ALU.mult, ALU.mult),
                        r=[bbk, "sc4", "g1_%d" % (h // 2)], w=["mix_m%d" % h])
            for h in range(4):
                cb_, cbk = next_bank()
                TA("pe", lambda e, h=h, cb_=cb_: e.matmul(cb_[:, 0:257], kp_tm[:, h * 128:(h + 1) * 128], vaug3[:, h, 0:257],
                                                             start=True, stop=True), r=["kp_tm", "vaug_0", "vaug_2"], w=[cbk])
                TA("dve", lambda e, h=h, cb_=cb_: e.tensor_tensor(Cst3[:, h, 0:257], Cst3[:, h, 0:257], cb_[:, 0:257], ALU.add),
                      r=[cbk, "Cst", "Cbf"], w=["Cst"])

            p, pk = next_pT()

            def tr_xc(e, p=p):
                for t in range(8):
                    ins = e.transpose(p[:, t * 128:(t + 1) * 128], xcT3[:, t, :], identb[:])
                return ins
            TA("pe", tr_xc, r=["xcT_0", "xcT_1", "identb"], w=[pk])
            TA("act", lambda e, p=p: e.activation(out=x_tm[:], in_=p[:, 0:1024], func=AF.Copy), r=[pk], w=["x_tm"])
            p, pk = next_pT()

            def tr_B(e, p=p):
                for t in range(2):
                    ins = e.transpose(p[:, t * 128:(t + 1) * 128], xcT3[:, 8 + t, :], identb[:])
                return ins
            TA("pe", tr_B, r=["xcT_2", "identb"], w=[pk])
            TA("act", lambda e, p=p: e.activation(out=B_tm[:], in_=p[:, 0:256], func=AF.Copy), r=[pk], w=["B_tm"])
            TA("pool", lambda e: e.tensor_tensor(
                xdt[:].rearrange("p (r c) -> p r c", r=16), x_tm[:].rearrange("p (r c) -> p r c", r=16),
                dt16.unsqueeze(2).broadcast_to([128, 16, 64]), ALU.mult), r=["x_tm", "dt16"], w=["xdt"])
            TA("pool", lambda e: e.tensor_tensor(
                xde[:].rearrange("p (r c) -> p r c", r=16), xdt[:].rearrange("p (r c) -> p r c", r=16),
                dE.unsqueeze(2).broadcast_to([128, 16, 64]), ALU.mult), r=["xdt", "dE"], w=["xde"])

            if full:
                TA("pe", lambda e: (e.matmul(pm[:, 256:384], xcT3[:, 8, :], xcT3[:, 10, :], start=True, stop=True),
                                       e.matmul(pm[:, 384:512], xcT3[:, 9, :], xcT3[:, 11, :], start=True, stop=True))[1],
                      r=["xcT_2"], w=["pm_cb"])
                TA("dve", lambda e: e.tensor_tensor(CBm3, pm[:, 256:512].rearrange("p (g l) -> p g l", g=2),
                                                       maskb[:].unsqueeze(1).broadcast_to([128, 2, 128]), ALU.mult),
                      r=["pm_cb", "maskb"], w=["CBm"])
                for bq in range(4):
                    ab, abk = next_bank()

                    def f_arg(e, bq=bq, ab=ab):
                        for rr in range(4):
                            ins = e.matmul(ab[:, rr * 128:(rr + 1) * 128], oh48_3[:, bq * 4 + rr, :], A48[:], start=True, stop=True)
                        return ins
                    TA("pe", f_arg, r=["oh48", "A48"], w=[abk])

                    def f_relu(e, bq=bq, ab=ab):
                        for rr in range(4):
                            hd = bq * 4 + rr
                            ins = e.activation(out=rl[:, rr * 128:(rr + 1) * 128], in_=ab[:, rr * 128:(rr + 1) * 128],
                                               func=AF.Relu, bias=acum_sb[:, hd:hd + 1], scale=-1.0)
                        return ins
                    TA("act", f_relu, r=[abk, "acum_sb"], w=["rl"])
                    TA("act", lambda e, bq=bq: e.activation(out=dec[:, bq * 512:(bq + 1) * 512], in_=rl[:], func=AF.Exp, scale=-1.0),
                          r=["rl"], w=["dec_%d" % bq])
                    g = bq // 2
                    TA("pool", lambda e, bq=bq, g=g: e.tensor_tensor(
                        dec3[:, bq * 4:bq * 4 + 4, :], dec3[:, bq * 4:bq * 4 + 4, :],
                        CBm3[:, g:g + 1, :].broadcast_to([128, 4, 128]), ALU.mult), r=["dec_%d" % bq, "CBm"], w=["dec_%d" % bq])
                TA("pool", lambda e: e.tensor_tensor(
                    yo[:].rearrange("p (r c) -> p r c", r=16), x_tm[:].rearrange("p (r c) -> p r c", r=16),
                    drep.unsqueeze(2).broadcast_to([128, 16, 64]), ALU.mult), r=["x_tm", "drep"], w=["yo_0", "yo_1"])
                for g in range(2):
                    yd, ydk = next_bank()

                    def f_yd(e, g=g, yd=yd):
                        for rr in range(8):
                            hd = g * 8 + rr
                            ins = e.matmul(yd[:, rr * 64:(rr + 1) * 64], dec3[:, hd, :], xdt[:, hd * 64:(hd + 1) * 64], start=True, stop=True)
                        return ins
                    TA("pe", f_yd, r=["dec_%d" % (2 * g), "dec_%d" % (2 * g + 1), "xdt"], w=[ydk])
                    yf, yfk = next_bank()
                    TA("pe", lambda e, g=g, yf=yf: e.matmul(yf[:, 0:512], xcT3[:, 10 + g, :], hbf[:, g * 512:(g + 1) * 512], start=True, stop=True),
                          r=["xcT_2", "hbf%d" % g], w=[yfk])
                    gsl = slice(g * 512, (g + 1) * 512)
                    TA("dve", lambda e, g=g, yf=yf: e.tensor_tensor(
                        ytmp[:].rearrange("p (r c) -> p r c", r=8), yf[:, 0:512].rearrange("p (r c) -> p r c", r=8),
                        ea[:, g * 8:(g + 1) * 8].unsqueeze(2).broadcast_to([128, 8, 64]), ALU.mult), r=[yfk, "ea"], w=["ytmp"])
                    TA("dve", lambda e, yd=yd: e.tensor_tensor(ytmp[:], ytmp[:], yd[:, 0:512], ALU.add), r=[ydk, "ytmp"], w=["ytmp"])
                    TA("pool", lambda e, gsl=gsl: e.tensor_tensor(yo[:, gsl], yo[:, gsl], ytmp[:], ALU.add), r=["ytmp", "yo_%d" % g], w=["yo_%d" % g])
                    TA("pool", lambda e, gsl=gsl: e.tensor_tensor(yo[:, gsl], yo[:, gsl], gs[:, gsl], ALU.mult),
                          r=["yo_%d" % g, "gs_%d" % g], w=["yo_%d" % g])
                    TA("act", lambda e, g=g, gsl=gsl: e.activation(out=junk_s[:, 0:1].broadcast_to([128, 512]), in_=yo[:, gsl], func=AF.Square, accum_out=ssq_s[:, g:g + 1]),
                          r=["yo_%d" % g], w=["ssq_s%d" % g, "junk_s"])
                TA("pool", lambda e: e.tensor_scalar(ts2, ssq_s, 2048.0 * EPS, None, ALU.add), r=["ssq_s0", "ssq_s1"], w=["ts2"])
                TA("pool", lambda e: e.tensor_tensor(rstd_s, ts2, m05[:, 0:2], ALU.pow), r=["ts2", "m05"], w=["rstd_s"])
                for g in range(2):
                    gsl = slice(g * 512, (g + 1) * 512)
                    TA("dve", lambda e, g=g, gsl=gsl: e.tensor_scalar_mul(mix[:, 1024 + g * 512:1024 + (g + 1) * 512], yo[:, gsl], rstd_s[:, g:g + 1]),
                          r=["yo_%d" % g, "rstd_s"], w=["mix_s%d" % g])
            for g in range(2):
                st, stk = next_bank()
                gsl = slice(g * 512, (g + 1) * 512)
                TA("pe", lambda e, g=g, st=st, gsl=gsl: e.matmul(st[:, 0:512], B_tm[:, g * 128:(g + 1) * 128], xde[:, gsl], start=True, stop=True),
                      r=["B_tm", "xde"], w=[stk])
                TA("pool", lambda e, g=g, gsl=gsl: e.tensor_tensor(
                    hst[:, gsl].rearrange("p (r c) -> p r c", r=8), hst[:, gsl].rearrange("p (r c) -> p r c", r=8),
                    cdb[:, g * 8:(g + 1) * 8].unsqueeze(2).broadcast_to([128, 8, 64]), ALU.mult), r=["hst", "cdb", "hbf%d" % g], w=["hst%d" % g])
                TA("dve", lambda e, st=st, gsl=gsl: e.tensor_tensor(hst[:, gsl], hst[:, gsl], st[:, 0:512], ALU.add), r=[stk, "hst%d" % g], w=["hst%d" % g])
                TA("act", lambda e, gsl=gsl: e.activation(out=hbf[:, gsl], in_=hst[:, gsl], func=AF.Copy), r=["hst%d" % g], w=["hbf%d" % g])

            if not full:
                return
            mkeys = ["mix_m%d" % h for h in range(4)] + ["mix_s0", "mix_s1"]
            obs = [next_bank(), next_bank()]
            for half in range(2):
                p, pk = next_pT()

                def tr_m(e, p=p, half=half):
                    for t in range(8):
                        kc = half * 8 + t
                        ins = e.transpose(p[:, t * 128:(t + 1) * 128], mix[:, kc * 128:(kc + 1) * 128], identb[:])
                    return ins
                TA("pe", tr_m, r=(mkeys[0:4] if half == 0 else mkeys[4:6]) + ["identb"], w=[pk])
                TA("dve", lambda e, p=p, half=half: e.tensor_tensor(
                    mixT3[:, 0:8, :], p[:, 0:1024].rearrange("p (k t) -> p k t", k=8),
                    normcat[:, half * 8:(half + 1) * 8].unsqueeze(2).broadcast_to([128, 8, 128]), ALU.mult),
                    r=[pk, "normcat_a", "normcat_b"], w=["mixT_0"])
                for oh in range(2):
                    ob, obk = obs[oh]

                    def f_o(e, ob=ob, half=half, oh=oh):
                        for kc in range(8):
                            ins = e.matmul(ob[:, 0:512], mixT3[:, kc, :], wout3[:, half * 8 + kc, oh * 512:(oh + 1) * 512],
                                           start=(half == 0 and kc == 0), stop=(half == 1 and kc == 7))
                        return ins
                    TA("pe", f_o, r=["mixT_0"] + wout_keys, w=[obk])
            for oh in range(2):
                ob, obk = obs[oh]
                hsl = slice(oh * 512, (oh + 1) * 512)
                TA("dve", lambda e, ob=ob, hsl=hsl: e.tensor_tensor(xb[:, hsl], xb[:, hsl], ob[:, 0:512], ALU.add), r=[obk, xk], w=[xk])
            TA("act", lambda e: e.activation(out=junk_o[:, 0:1].broadcast_to([128, 1024]), in_=xb[:], func=AF.Square, accum_out=ssq_o), r=[xk], w=["ssq_o", "junk_o"])
            TA("pool", lambda e: e.tensor_scalar(to1, ssq_o, 1024.0 * EPS, None, ALU.add), r=["ssq_o"], w=["to1"])
            TA("pool", lambda e: e.tensor_tensor(rstd_o, to1, m05[:, 0:1], ALU.pow), r=["to1", "m05"], w=["rstd_o"])
            TA("dve", lambda e: e.scalar_tensor_tensor(xb[:], xb[:], rstd_o, finalw[:], ALU.mult, ALU.mult), r=[xk, "rstd_o", "finalw2"], w=[xk])
            TA("sp", lambda e: e.dma_start(out=out_d[ci * L:(ci + 1) * L, :], in_=xb[:]), r=[xk], w=["out_dram"], stream="out%d" % slot)

        k0, c0_ = chunk_list[0]
        load_x(src_of(k0), c0_, 0)
        for gi in range(len(chunk_list)):
            chunk(gi)
            if n_pre and gi == n_pre - 1:
                T.add("dve", lambda e: e.tensor_scalar_mul(Cst[:], Cst[:], flag), r=["Cst", "flag"], w=["Cst"])
                T.add("dve", lambda e: e.tensor_scalar_mul(hst[:], hst[:], flag), r=["hst0", "hst1", "flag"], w=["hst0", "hst1", "hst"])
                T.add("dve", lambda e: e.tensor_scalar_mul(hbf[:], hbf[:], flag), r=["hbf0", "hbf1", "flag"], w=["hbf0", "hbf1"])
                T.add("dve", lambda e: e.tensor_scalar_mul(mst, mst, flag[0:4, :]), r=["mst", "flag"], w=["mst"])
                T.add("dve", lambda e: e.tensor_scalar_mul(xbcT3[:, :, 1:4], xbcT3[:, :, 1:4], flag), r=["xbcT_carry", "flag"], w=["xbcT_carry"])

        for nm in debug:
            tile_ap, shape, rkeys = {
                "xnT": (xnT[:], [128, 1024], ["xnT"]),
                "k_tm": (k_tm[:], [128, 512], ["k_tm"]),
                "mix": (mix[:], [128, 2048], ["mix_m0", "mix_m1", "mix_m2", "mix_m3", "mix_s0", "mix_s1"]),
                "Cst": (Cst[:], [128, 4 * 258], ["Cst"]),
                "hst": (hst[:], [128, 1024], ["hst0", "hst1"]),
                "x_tm": (x_tm[:], [128, 1024], ["x_tm"]),
                "sm": (sm[:], [128, 256], ["wf", "dE", "ea", "cdb", "dt16", "acum_sb"]),
                "yo": (yo[:], [128, 1024], ["yo_0", "yo_1"]),
                "dec": (dec[:], [128, 2048], ["dec_0", "dec_1", "dec_2", "dec_3"]),
                "xcT": (xcT[:], [128, 1536], ["xcT_0", "xcT_1", "xcT_2"]),
                "g1": (g1[:], [128, 1024], ["g1_0", "g1_1"]),
                "vaug": (vaug[:], [128, 4 * 258], ["vaug_0", "vaug_2"]),
            }[nm]
            d = nc.dram_tensor("dbg_" + nm, shape, F32, kind="ExternalOutput").ap()
            dbg_out[nm] = d
            q_eng = "sp" if tile_ap.dtype == F32 else "pool"
            T.add(q_eng, lambda e, d=d, tile_ap=tile_ap: e.dma_start(out=d, in_=tile_ap), r=rkeys, w=["dbg_" + nm], stream="dbg")

        import os as _os2
        T.emit(nc, es, same_engine_sync=not _os2.environ.get("K_NOSES"))
    return nc


_CACHE = {}


def kernel(x, norm_w, w_in, b_igate, b_fgate, conv_w, conv_b, dt_bias, a_log, d_skip,
           mlstm_norm_w, ssd_norm_w, w_out, final_norm_w):
    f = lambda a: np.ascontiguousarray(np.asarray(a, dtype=np.float32))
    x = f(x)
    n_half = SEQ // 2 // L
    key = "main"
    if key not in _CACHE:
        _CACHE[key] = build(n_half, n_half)
    nc = _CACHE[key]
    common = {
        "norm_w": f(norm_w)[0], "w_in": f(w_in)[0], "b_igate": f(b_igate)[0], "b_fgate": f(b_fgate)[0],
        "conv_w": f(conv_w)[0], "conv_b": f(conv_b)[0], "dt_bias": f(dt_bias)[0], "a_log": f(a_log)[0],
        "d_skip": f(d_skip)[0],
        "normcat": np.ascontiguousarray(np.concatenate([f(mlstm_norm_w)[0], f(ssd_norm_w)[0]])),
        "w_out": f(w_out)[0], "final_norm_w": f(final_norm_w),
    }
    half = SEQ // 2
    zeros = np.zeros((half, D_MODEL), np.float32)
    in_maps = []
    for core in range(NCORES):
        b, hf = core // 2, core % 2
        m = dict(common)
        m["x"] = np.ascontiguousarray(x[b, hf * half:(hf + 1) * half])
        m["xpre"] = zeros if hf == 0 else np.ascontiguousarray(x[b, 0:half])
        m["flag"] = np.array([float(hf)], np.float32)
        in_maps.append(m)
    res = run_bass_kernel_spmd(nc, in_maps, core_ids=list(range(NCORES)))
    out = np.empty((BATCH, SEQ, D_MODEL), np.float32)
    for core in range(NCORES):
        b, hf = core // 2, core % 2
        out[b, hf * half:(hf + 1) * half] = res.results[core]["out"]
    return out
```

```python
import numpy as np
from contextlib import ExitStack
import concourse.bass as bass
import concourse.mybir as mybir
from concourse.bass_utils import run_bass_kernel_spmd

F32 = mybir.dt.float32
BF16 = mybir.dt.bfloat16
AF = mybir.ActivationFunctionType
ALU = mybir.AluOpType
AX = mybir.AxisListType

D_MODEL = 1024
SEQ = 8192
BATCH = 4
NCOL = 6680
EPS = 1e-6
L = 128
NCORES = 8
SCHED_QNT = 0.0
SCHED_SEED = 0
SCHED_AMP = 0.0


class Tracker:
    def __init__(self):
        self.ops = []
        self.bufs = {}
        self.waitall_streams = set()
        self.regions = {}

    def reg(self, key, arena, off, nbytes, gran=64):
        import os
        if os.environ.get("K_NOALIAS"):
            return
        self.regions[key] = [(arena, g) for g in range(off // gran, (off + nbytes + gran - 1) // gran)]

    def _expand(self, keys):
        out = []
        for k in keys:
            out.extend(self.regions.get(k, [k]))
        return out

    PSUM_BANKS = ("pT0", "pT1", "pm", "pb0", "pb1", "pb2", "pb3", "pb4")

    @classmethod
    def _bank(cls, k):
        if isinstance(k, str):
            if k.startswith("pm_"):
                return "pm"
            if k in cls.PSUM_BANKS:
                return k
        return None

    def add(self, eng, fn, r=(), w=(), stream=None):
        self._lbl = "%s:%s" % (eng, (list(w) + ["?"])[0])
        banks = [self._bank(k) for k in list(r) + list(w)]
        banks = [b for b in banks if b is not None]
        r = [k for k in r if self._bank(k) is None]
        w = [k for k in w if self._bank(k) is None] + sorted(set(banks))
        r = self._expand(r)
        w = self._expand(w)
        deps = set()
        for k in r:
            b = self.bufs.setdefault(k, [None, []])
            if b[0] is not None:
                deps.add(b[0])
        for k in w:
            b = self.bufs.setdefault(k, [None, []])
            if b[0] is not None:
                deps.add(b[0])
            deps.update(b[1])
        idx = len(self.ops)
        deps.discard(idx)
        self.ops.append(dict(eng=eng, fn=fn, deps=deps, stream=stream, idx=idx, has_dep=False, label=self._lbl))
        for k in r:
            self.bufs[k][1].append(idx)
        for k in w:
            self.bufs[k] = [idx, []]
        return idx

    class _Ins:
        def then_inc(self, *a, **k):
            return self

    class _Probe:
        def __init__(self):
            self.calls = []

        def __getattr__(self, name):
            def f(*args, **kw):
                self.calls.append((name, args, kw))
                return Tracker._Ins()
            return f

    @staticmethod
    def _free(ap):
        n = 1
        for d in ap.shape[1:]:
            n *= int(d)
        return n

    def _cost(self, op):
        pr = Tracker._Probe()
        op["fn"](pr)
        eng = op["eng"]
        dur = 0.0
        lat = 0.0
        for name, args, kw in pr.calls:
            out = kw.get("out", args[0] if args else None)
            if name == "dma_start":
                src = kw.get("in_", args[1] if len(args) > 1 else None)
                nbytes = self._free(out) * int(out.shape[0]) * 4
                dur += 80.0
                lat = 2500.0 + nbytes / 160.0
                continue
            n = self._free(out) if out is not None and hasattr(out, "shape") else 64
            if eng == "pe":
                lhs = args[1] if len(args) > 1 else None
                mult = 4.0 if (lhs is not None and lhs.dtype == F32 and name == "matmul") else 1.0
                dur += 16.0 + max(n, 64) * mult / 1.95
            elif eng == "act":
                dur += 230.0 + n / 1.15
            elif eng == "dve":
                dur += 170.0 + n / 0.96
            elif eng == "pool":
                dur += 300.0 + n * 2.0
            else:
                dur += 50.0
        import os
        fr = os.environ.get("K_FREE")
        if fr and any(op["label"].startswith(p) for p in fr.split(",")):
            dur = 20.0
        op["dur"] = max(dur, 20.0)
        op["lat"] = lat

    def schedule(self):
        import heapq
        ops = self.ops
        n = len(ops)
        for op in ops:
            self._cost(op)
        succ = [[] for _ in range(n)]
        for op in ops:
            for d in op["deps"]:
                succ[d].append(op["idx"])
        bl = [0.0] * n
        for i in range(n - 1, -1, -1):
            m = 0.0
            for sidx in succ[i]:
                if bl[sidx] > m:
                    m = bl[sidx]
            bl[i] = m + ops[i]["dur"] + ops[i]["lat"]
        import os
        if os.environ.get("K_CRIT"):
            i = max(range(n), key=lambda j: bl[j])
            print("[crit] DAG critical path %.1f us" % (bl[i] / 1e3))
            agg = {}
            while True:
                agg[ops[i]["label"]] = agg.get(ops[i]["label"], 0.0) + ops[i]["dur"] + ops[i]["lat"]
                nxt = None
                for sidx in succ[i]:
                    if nxt is None or bl[sidx] > bl[nxt]:
                        nxt = sidx
                if nxt is None:
                    break
                i = nxt
            for k, v in sorted(agg.items(), key=lambda kv: -kv[1])[:40]:
                print("     %-28s %.1f us" % (k, v / 1e3))
        ndeps = [len(op["deps"]) for op in ops]
        ready_at = [0.0] * n
        engs = ["pe", "act", "dve", "pool", "sp"]
        avail = {e: [] for e in engs}
        for i in range(n):
            if ndeps[i] == 0:
                avail[ops[i]["eng"]].append(i)
        free_at = {e: 0.0 for e in engs}
        fa_prev = {e: 0.0 for e in engs}
        order = []
        finish = [0.0] * n
        done = 0
        import os
        SYNC_LAT = float(os.environ.get("K_SL", "300"))
        SYNC_LAT_SAME = float(os.environ.get("K_SLS", "150"))
        SLACK = float(os.environ.get("K_SLACK", "0"))
        QNT = float(os.environ.get("K_QNT", str(SCHED_QNT)))
        seed = int(os.environ.get("K_SEED", str(SCHED_SEED)))
        amp = float(os.environ.get("K_AMP", str(SCHED_AMP)))
        import random
        rng = random.Random(seed)
        blp = [b_ * (1.0 + amp * (rng.random() - 0.5)) for b_ in bl] if amp > 0 else bl
        WINDOW = int(os.environ.get("K_WIN", "4000"))
        lowest_unscheduled = 0
        scheduled = [False] * n
        while done < n:
            best = None
            while lowest_unscheduled < n and scheduled[lowest_unscheduled]:
                lowest_unscheduled += 1
            for e in engs:
                lst = avail[e]
                if not lst:
                    continue
                fa = free_at[e]
                cand = None
                for i in lst:
                    if i > lowest_unscheduled + WINDOW:
                        continue
                    st = ready_at[i] if ready_at[i] > fa else fa
                    stq = fa if st - fa <= SLACK else st
                    if QNT > 0:
                        stq = int(stq / QNT)
                    key = (stq, -blp[i], i)
                    if cand is None or key < cand[0]:
                        cand = (key, i, st)
                if cand is None:
                    continue
                if best is None or cand[0] < best[0]:
                    best = cand + (e,)
            assert best is not None, "scheduler stuck"
            _, i, st, e = best
            avail[e].remove(i)
            op = ops[i]
            fin = st + op["dur"]
            free_at[e] = fin
            finish[i] = fin + op["lat"]
            op["t_start"] = st
            op["stall"] = st - fa_prev[e]
            crit = None
            for d in op["deps"]:
                if crit is None or finish[d] > finish[crit]:
                    crit = d
            op["crit"] = crit
            fa_prev[e] = fin
            order.append(i)
            scheduled[i] = True
            done += 1
            for sidx in succ[i]:
                fx = finish[i] + (SYNC_LAT if ops[sidx]["eng"] != e else SYNC_LAT_SAME)
                if fx > ready_at[sidx]:
                    ready_at[sidx] = fx
                ndeps[sidx] -= 1
                if ndeps[sidx] == 0:
                    avail[ops[sidx]["eng"]].append(sidx)
        self.est_makespan_us = max(finish) / 1e3
        busy = {e: 0.0 for e in engs}
        for op in ops:
            busy[op["eng"]] += op["dur"]
        print("[sched] est makespan %.1f us; busy us: %s" % (self.est_makespan_us, {e: round(v / 1e3) for e, v in busy.items()}))
        import os
        if os.environ.get("K_STALLS"):
            t_lo, t_hi = [float(v) * 1e3 for v in os.environ["K_STALLS"].split(",")]
            for e in engs:
                agg = {}
                tot = 0.0
                for op in ops:
                    if op["eng"] == e and t_lo <= op["t_start"] < t_hi and op["stall"] > 1.0 and op["crit"] is not None:
                        k = (op["label"], ops[op["crit"]]["label"])
                        agg[k] = agg.get(k, 0.0) + op["stall"]
                        tot += op["stall"]
                print("[stalls] %s total %.1f us" % (e, tot / 1e3))
                for k, v in sorted(agg.items(), key=lambda kv: -kv[1])[:12]:
                    print("     %-28s waits on %-28s %.1f us" % (k[0], k[1], v / 1e3))
        if os.environ.get("K_BUSY"):
            for e in engs:
                agg = {}
                for op in ops:
                    if op["eng"] == e:
                        k = op["label"].rstrip("0123456789_")
                        agg[k] = agg.get(k, 0.0) + op["dur"]
                print("[busy] %s" % e)
                for k, v in sorted(agg.items(), key=lambda kv: -kv[1])[:22]:
                    print("     %-24s %.1f us" % (k, v / 1e3))
        if os.environ.get("K_TL"):
            eng_, t_lo, t_hi = os.environ["K_TL"].split(",")
            t_lo, t_hi = float(t_lo) * 1e3, float(t_hi) * 1e3
            for i in order:
                op = ops[i]
                if op["eng"] == eng_ and t_lo <= op["t_start"] < t_hi:
                    c = ops[op["crit"]]["label"] if op["crit"] is not None else "-"
                    print("[tl] %8.1f +%6.2f stall %6.2f  %-22s <- %s" % (op["t_start"] / 1e3, op["dur"] / 1e3, op["stall"] / 1e3, op["label"], c))
        remap = {old: new for new, old in enumerate(order)}
        new_ops = []
        for new, old in enumerate(order):
            op = ops[old]
            op["deps"] = {remap[d] for d in op["deps"]}
            op["idx"] = new
            new_ops.append(op)
        self.ops = new_ops

    def emit(self, nc, es, same_engine_sync=True, do_schedule=True):
        if do_schedule:
            self.schedule()
        ops = self.ops
        for op in ops:
            nd = set()
            for d in op["deps"]:
                dop = ops[d]
                if dop["stream"] is None and dop["eng"] == "pe" and op["eng"] == "pe" and op["stream"] is None:
                    continue
                if (not same_engine_sync) and dop["stream"] is None and op["stream"] is None and dop["eng"] == op["eng"]:
                    continue
                nd.add(d)
            op["deps"] = nd
            for d in nd:
                ops[d]["has_dep"] = True
        sems = {}
        engs = ["pe", "act", "dve", "pool", "sp"]
        for e in engs:
            sems[e] = es.enter_context(nc.semaphore("s_" + e))
        streams = sorted({op["stream"] for op in ops if op["stream"] is not None})
        for s in streams:
            sems["d:" + s] = es.enter_context(nc.semaphore("d_" + s))
        cnt = {e: 0 for e in engs}
        scnt = {s: 0 for s in streams}
        for op in ops:
            if op["stream"] is not None:
                scnt[op["stream"]] += 1
                op["sig"] = ("d:" + op["stream"], 16 * scnt[op["stream"]])
            elif op["has_dep"]:
                cnt[op["eng"]] += 1
                op["sig"] = (op["eng"], cnt[op["eng"]])
            else:
                op["sig"] = None
        for op in ops:
            if op["stream"] in self.waitall_streams:
                op["sig"] = ("d:" + op["stream"], 16 * scnt[op["stream"]])
        self.final_counts = {("d:" + s): 16 * scnt[s] for s in streams}
        self.sems = sems
        block = es.enter_context(nc.Block())
        per_eng = {e: [op for op in ops if op["eng"] == e] for e in engs}

        def run(engobj, lst, extra_tail=None):
            waited = {}
            for op in lst:
                need = {}
                for d in op["deps"]:
                    sg = ops[d]["sig"]
                    assert sg is not None
                    if sg[1] > need.get(sg[0], 0):
                        need[sg[0]] = sg[1]
                for sk, v in need.items():
                    if v > waited.get(sk, 0):
                        engobj.wait_ge(sems[sk], v)
                        waited[sk] = v
                ins = op["fn"](engobj)
                if op["stream"] is not None:
                    ins.then_inc(sems["d:" + op["stream"]], 16)
                elif op["sig"] is not None:
                    ins.then_inc(sems[op["eng"]], 1)
            if extra_tail is not None:
                extra_tail(engobj, waited)

        def sp_tail(engobj, waited):
            for sk, v in self.final_counts.items():
                if v > waited.get(sk, 0):
                    engobj.wait_ge(sems[sk], v)

        @block.sync
        def _(e):
            run(e, per_eng["sp"], sp_tail)

        @block.tensor
        def _(e):
            run(e, per_eng["pe"])

        @block.scalar
        def _(e):
            run(e, per_eng["act"])

        @block.vector
        def _(e):
            run(e, per_eng["dve"])

        @block.gpsimd
        def _(e):
            run(e, per_eng["pool"])


PROJ_GROUPS = [
    ("if", 2048, 8), ("dt", 6664, 16), ("k", 512, 512), ("v0", 1024, 512), ("v1", 1536, 512),
    ("xbc0", 5128, 512), ("xbc1", 5640, 512), ("xbc2", 6152, 512), ("q", 0, 512),
    ("zm0", 3080, 512), ("zm1", 3592, 512), ("o0", 2056, 512), ("o1", 2568, 512),
    ("zs0", 4104, 512), ("zs1", 4616, 512),
]
STATE_ONLY = {"if", "dt", "k", "v0", "v1", "xbc0", "xbc1", "xbc2"}


def build(n_pre, n_full, debug=()):
    nc = bass.Bass("TRN2", target_bir_lowering=False)
    T_pre, T_full = n_pre * L, n_full * L
    dr = {}
    dr["x"] = nc.dram_tensor("x", [max(T_full, 1), D_MODEL], F32, kind="ExternalInput").ap()
    if n_pre:
        dr["xpre"] = nc.dram_tensor("xpre", [T_pre, D_MODEL], F32, kind="ExternalInput").ap()
    for nm, shp in [("norm_w", [1024]), ("w_in", [1024, NCOL]), ("b_igate", [4]), ("b_fgate", [4]),
                    ("conv_w", [1536, 4]), ("conv_b", [1536]), ("dt_bias", [16]), ("a_log", [16]),
                    ("d_skip", [16]), ("normcat", [2048]), ("w_out", [2048, 1024]),
                    ("final_norm_w", [1024]), ("flag", [1])]:
        dr[nm] = nc.dram_tensor(nm, shp, F32, kind="ExternalInput").ap()
    out_d = nc.dram_tensor("out", [T_full, D_MODEL], F32, kind="ExternalOutput").ap()
    dbg_out = {}

    T = Tracker()
    T.waitall_streams.add("const")
    es = ExitStack()
    with es:
        def sb(name, shape, dt):
            return es.enter_context(nc.sbuf_tensor(name, shape, dt))

        def ps(name, shape, dt):
            return es.enter_context(nc.psum_tensor(name, shape, dt))

        win = sb("win", [128, 8 * NCOL], BF16)
        wout = sb("wout", [128, 16 * 1024], BF16)
        win3 = win[:].rearrange("p (k n) -> p k n", k=8)
        wout3 = wout[:].rearrange("p (k n) -> p k n", k=16)
        identb = sb("identb", [128, 128], BF16)
        identf = sb("identf", [128, 128], F32)
        tri = sb("tri", [128, 128], F32)
        maskb = sb("maskb", [128, 128], BF16)
        onesf = sb("onesf", [128, 128], F32)
        oh48 = sb("oh48", [128, 16 * 128], BF16)
        oh48_3 = oh48[0:48, :].rearrange("p (r s) -> p r s", r=16)
        cbrow = oh48[64:65, 0:1024]
        onesrow = oh48[64:65, 1024:1152]
        NPE_R = 1
        dgc = sb("dgc", [128, NPE_R * 4 * 4 * 128], BF16)
        dgc4 = dgc[:].rearrange("p (t w c) -> p t w c", t=NPE_R * 4, w=4)
        finalw = sb("finalw", [128, 1024], F32)
        cst = sb("cst", [128, 172], F32)
        cw = cst[:, 0:48].rearrange("p (t w) -> p t w", t=12)
        cb = cst[:, 48:60]
        bias8 = cst[:, 60:68]
        dtb = cst[:, 68:84]
        arep = cst[:, 84:100]
        drep = cst[:, 100:116]
        normw_fm = cst[:, 116:124]
        normcat = cst[:, 124:140]
        m05 = cst[:, 140:148]
        flag = cst[:, 148:149]
        alog = cst[:, 152:168]

        xbuf = [sb("xbuf0", [128, 1024], F32), sb("xbuf1", [128, 1024], F32)]
        ARENA_BYTES = 23552
        arena = sb("arena", [128, ARENA_BYTES // 4], F32)

        def carve(layout_off, name, nbytes, dt, subkeys=None):
            assert layout_off[0] % 4 == 0
            o = layout_off[0]
            layout_off[0] += (nbytes + 3) // 4 * 4
            assert layout_off[0] <= ARENA_BYTES, (name, layout_off[0])
            v = arena[:, o // 4:(o + (nbytes + 3) // 4 * 4) // 4]
            if dt != F32:
                v = v.bitcast(dt)
            if subkeys is None:
                T.reg(name, "A", o, nbytes)
            else:
                n = len(subkeys)
                for i, sk in enumerate(subkeys):
                    T.reg(sk, "A", o + i * (nbytes // n), nbytes // n)
            return v

        lo1 = [0]
        xn = carve(lo1, "xn", 2048, BF16)
        xnT = carve(lo1, "xnT", 2048, BF16)
        xnT3 = xnT[:].rearrange("p (k t) -> p k t", k=8)
        xbc_tm = carve(lo1, "xbc_tm", 3072, BF16, ["xbc_tm0", "xbc_tm1", "xbc_tm2"])
        _c0 = carve(lo1, "cacc0", 2048, F32, ["cacc0_%d" % i for i in range(4)])
        cacc = [_c0, _c0]
        _t0 = carve(lo1, "cth0", 1024, BF16)
        cth = [_t0, _t0]
        q_tm = carve(lo1, "q_tm", 1024, BF16)
        tz = carve(lo1, "tz", 2048, BF16, ["tz0", "tz1"])
        k_tm = carve(lo1, "k_tm", 1024, BF16)
        kp_tm = carve(lo1, "kp_tm", 1024, BF16)
        qT = carve(lo1, "qT", 1024, BF16)
        kT = carve(lo1, "kT", 1024, BF16)
        Pm = carve(lo1, "Pm", 1024, BF16, ["Pm_%d" % i for i in range(4)])
        Pm3 = Pm[:].rearrange("p (h j) -> p h j", h=4)
        Cbf = carve(lo1, "Cbf", 2064, BF16)
        Cbf3 = Cbf[:].rearrange("p (h c) -> p h c", h=4)
        g1 = carve(lo1, "g1", 2048, BF16, ["g1_0", "g1_1"])
        lo2 = [0]
        dec = carve(lo2, "dec", 4096, BF16, ["dec_%d" % i for i in range(4)])
        dec3 = dec[:].rearrange("p (r l) -> p r l", r=16)
        rl = carve(lo2, "rl", 2048, F32)
        CBm = carve(lo2, "CBm", 512, BF16)
        CBm3 = CBm[:].rearrange("p (g l) -> p g l", g=2)
        yo = carve(lo2, "yo", 4096, F32, ["yo_0", "yo_1"])
        ytmp = carve(lo2, "ytmp", 2048, F32)
        x_tm = carve(lo2, "x_tm", 2048, BF16)
        B_tm = carve(lo2, "B_tm", 512, BF16)
        xdt = carve(lo2, "xdt", 2048, BF16)
        xde = carve(lo2, "xde", 2048, BF16)
        mixT = carve(lo2, "mixT", 4096, BF16, ["mixT_0", "mixT_1"])
        mixT3 = mixT[:].rearrange("p (k t) -> p k t", k=16)

        junk = sb("junk", [128, 2], BF16)
        junk_r = sb("junk_r", [128, 2], BF16)
        junk_s = sb("junk_s", [128, 2], BF16)
        junk_o = sb("junk_o", [128, 2], BF16)
        vaug = sb("vaug", [128, 4 * 258], BF16)
        vaug3 = vaug[:].rearrange("p (h c) -> p h c", h=4)
        gs = sb("gs", [128, 1024], BF16)
        xbcT = sb("xbcT", [128, 12 * 132], BF16)
        xbcT3 = xbcT[:].rearrange("p (t c) -> p t c", t=12)
        xcT = sb("xcT", [128, 1536], BF16)
        xcT3 = xcT[:].rearrange("p (t c) -> p t c", t=12)
        Cst = sb("Cst", [128, 4 * 258], F32)
        Cst3 = Cst[:].rearrange("p (h c) -> p h c", h=4)
        hst = sb("hst", [128, 1024], F32)
        hbf = sb("hbf", [128, 1024], BF16)
        mix = sb("mix", [128, 2048], BF16)
        a3 = sb("a3", [128, 48], BF16)
        A48 = sb("A48", [48, 128], BF16)
        sm = sb("sm", [128, 256], F32)
        g8 = sm[:, 0:8]
        t8 = sm[:, 8:16]
        e4 = sm[:, 16:20]
        nlf = sm[:, 20:24]
        li = sm[:, 24:28]
        T1 = sm[:, 28:36]
        T2 = sm[:, 36:44]
        wf = sm[:, 44:52]
        soldb = sm[:, 52:56]
        absden = sm[:, 56:60]
        ssqr = sm[:, 60:64]
        dn4 = sm[:, 64:68]
        rd4 = sm[:, 68:72]
        t4 = sm[:, 72:76]
        rs4 = sm[:, 76:80]
        sc4 = sm[:, 80:84]
        ssq_x = sm[:, 84:85]
        rstd_x = sm[:, 85:86]
        tx1 = sm[:, 86:87]
        ssq_o = sm[:, 87:88]
        rstd_o = sm[:, 88:89]
        to1 = sm[:, 89:90]
        ssq_s = sm[:, 90:92]
        rstd_s = sm[:, 92:94]
        ts2 = sm[:, 94:96]
        dtp = sm[:, 96:112]
        edt = sm[:, 112:128]
        dt16 = sm[:, 128:144]
        a_tm = sm[:, 144:160]
        acum_sb = sm[:, 160:176]
        ea = sm[:, 176:192]
        dEa = sm[:, 192:208]
        dE = sm[:, 208:224]
        cdb = sm[:, 224:240]
        r1 = sm[:, 240:256]
        sm2 = sb("sm2", [128, 16], F32)
        r2 = sm2[:, 0:16]
        gm = sb("gm", [4, 16], F32)
        mst = gm[:, 0:1]
        umax = gm[:, 1:2]
        Rg = gm[:, 2:3]
        dg = gm[:, 3:4]
        soldg = gm[:, 4:5]
        Dg = gm[:, 8:16]
        pT = [ps("pT0", [128, 1024], BF16), ps("pT1", [128, 1024], BF16)]
        pm = ps("pm", [128, 512], F32)
        pb = [ps("pb%d" % i, [128, 512], F32) for i in range(5)]
        print("sbuf bytes remaining:", nc.sbuf_bytes_remaining)

        bank_rr = {"A": 0, "B": 0}
        BANKS = {"A": [0, 1], "B": [2, 3, 4]}

        def next_bank(ph="B"):
            lst = BANKS[ph]
            i = lst[bank_rr[ph] % len(lst)]
            bank_rr[ph] += 1
            return pb[i], "pb%d" % i

        def next_pT(ph="B"):
            i = 0 if ph == "A" else 1
            return pT[i], "pT%d" % i

        def P(fn, r=(), w=()):
            T.add("pool", fn, r=list(r), w=list(w))
        P(lambda e: e.memset(onesf[:], 1.0), w=["onesf"])
        P(lambda e: e.memset(identf[:], 1.0), w=["identf"])
        P(lambda e: e.affine_select(identf[:], identf[:], [[-1, 128]], ALU.is_equal, 0.0, base=0, channel_multiplier=1),
          r=["identf"], w=["identf"])
        P(lambda e: e.tensor_copy(identb[:], identf[:]), r=["identf"], w=["identb"])
        P(lambda e: e.memset(tri[:], 1.0), w=["tri"])
        P(lambda e: e.affine_select(tri[:], tri[:], [[1, 128]], ALU.is_ge, 0.0, base=0, channel_multiplier=-1), r=["tri"], w=["tri"])
        P(lambda e: e.tensor_copy(maskb[:], tri[:]), r=["tri"], w=["maskb"])
        P(lambda e: e.memset(m05, -0.5), w=["m05"])
        P(lambda e: e.memset(vaug[:], 0.0), w=["vaug"])
        P(lambda e: e.memset(vaug3[:, :, 256:257], 1.0), r=["vaug"], w=["vaug"])
        P(lambda e: e.memset(Cst[:], 0.0), w=["Cst"])
        P(lambda e: e.memset(hst[:], 0.0), w=["hst", "hst0", "hst1"])
        P(lambda e: e.memset(hbf[:], 0.0), w=["hbf0", "hbf1"])
        P(lambda e: e.memset(Cbf[:], 0.0), w=["Cbf"])
        P(lambda e: e.memset(mst, 0.0), w=["mst"])
        P(lambda e: e.memset(xbcT[:], 0.0), w=["xbcT_carry", "xbcT_0", "xbcT_1"])
        P(lambda e: e.memset(oh48[0:48, :], 1.0), w=["oh48"])
        P(lambda e: e.memset(onesrow, 1.0), w=["onesrow"])
        P(lambda e: e.affine_select(oh48_3, oh48_3, [[-1, 16], [0, 128]], ALU.is_equal, 0.0, base=0, channel_multiplier=1),
          r=["oh48"], w=["oh48"])
        P(lambda e: e.affine_select(oh48_3, oh48_3, [[-1, 16], [0, 128]], ALU.not_equal, 1.0, base=-16, channel_multiplier=1),
          r=["oh48"], w=["oh48"])
        P(lambda e: e.affine_select(oh48_3, oh48_3, [[-1, 16], [0, 128]], ALU.not_equal, 1.0, base=-32, channel_multiplier=1),
          r=["oh48"], w=["oh48"])

        def cdma(out_ap, in_ap, key, noncontig=False):
            T.add("sp", lambda e: e.dma_start(out=out_ap, in_=in_ap, allow_slow_non_contiguous=noncontig),
                  w=[key], stream="const")

        cdma(normw_fm, dr["norm_w"].rearrange("(k p) -> p k", p=128), "normw_fm", True)
        cdma(normcat, dr["normcat"].rearrange("(k p) -> p k", p=128), "normcat", True)
        cdma(cw, dr["conv_w"].rearrange("(t p) w -> p t w", p=128), "cw")
        cdma(cb, dr["conv_b"].rearrange("(t p) -> p t", p=128), "cb", True)
        cdma(bias8[:, 0:4], dr["b_igate"].partition_broadcast(128), "bias8a")
        cdma(bias8[:, 4:8], dr["b_fgate"].partition_broadcast(128), "bias8b")
        cdma(dtb, dr["dt_bias"].partition_broadcast(128), "dtb")
        cdma(alog, dr["a_log"].partition_broadcast(128), "alog")
        cdma(drep, dr["d_skip"].partition_broadcast(128), "drep")
        cdma(finalw[:], dr["final_norm_w"].partition_broadcast(128), "finalw")
        cdma(flag, dr["flag"].partition_broadcast(128), "flag")

        T.add("act", lambda e: e.activation(out=arep, in_=alog, func=AF.Exp), r=["alog"], w=["arep0"])
        T.add("dve", lambda e: e.tensor_scalar_mul(arep, arep, -1.0), r=["arep0"], w=["arep"])
        T.add("dve", lambda e: e.tensor_scalar_mul(cst[:, 0:60], cst[:, 0:60], 0.5), r=["cw", "cb"], w=["cwb"])
        T.add("dve", lambda e: e.tensor_scalar_mul(finalw[:], finalw[:], 32.0), r=["finalw"], w=["finalw2"])
        T.add("dve", lambda e: e.tensor_scalar_mul(normw_fm, normw_fm, 32.0), r=["normw_fm"], w=["normw2"])
        T.add("dve", lambda e: e.tensor_scalar_mul(normcat[:, 0:8], normcat[:, 0:8], 4.0), r=["normcat"], w=["normcat_a"])
        T.add("dve", lambda e: e.tensor_scalar_mul(normcat[:, 8:16], normcat[:, 8:16], float(np.sqrt(2048.0) / 2.0)),
              r=["normcat"], w=["normcat_b"])

        T.add("pool", lambda e: e.dma_start(out=cbrow, in_=dr["conv_b"][0:1024].rearrange("(o n) -> o n", o=1)), w=["cbrow0"], stream="const2")
        T.add("pool", lambda e: e.tensor_scalar(cbrow, cbrow, 0.5, None, ALU.mult), r=["cbrow0"], w=["cbrow"])
        for t_ in range(NPE_R * 4):
            for w_ in range(4):
                T.add("dve" if (t_ * 4 + w_) % 2 == 0 else "pool",
                      lambda e, t_=t_, w_=w_: e.tensor_scalar(dgc4[:, t_, w_, :], identf[:], cw[:, t_, w_:w_ + 1], None, ALU.mult),
                      r=["cwb", "identf"], w=["dgc_%d_%d" % (t_, w_)])
        dgc_keys = ["dgc_%d_%d" % (t_, w_) for t_ in range(NPE_R * 4) for w_ in range(4)]

        W_PRE = [(512, 1024), (1024, 2048), (2048, 2056), (5128, 6152), (6152, 6680)]
        W_FULL = [(0, 512), (2056, 3080), (3080, 4104), (4104, 5128)]
        wkeys = {}
        piece = [0]
        NSTG = 8
        wout_f = wout[:].bitcast(F32)

        def wpiece(src_ap, dst_ap, scale_ap, scale_key, ncols, stg_ap, stg_key, stream, wkey, extra_w=()):
            i = piece[0]
            piece[0] += 1
            T.add("sp", lambda e: e.dma_start(out=stg_ap[:, 0:ncols], in_=src_ap), w=[stg_key], stream=stream)
            if i % 2 == 1:
                T.add("act", lambda e: e.activation(out=dst_ap, in_=stg_ap[:, 0:ncols], func=AF.Copy, scale=scale_ap),
                      r=[stg_key, scale_key], w=[wkey] + list(extra_w))
            else:
                T.add("dve", lambda e: e.tensor_scalar_mul(dst_ap, stg_ap[:, 0:ncols], scale_ap),
                      r=[stg_key, scale_key], w=[wkey] + list(extra_w))

        for (c0, c1) in W_PRE + W_FULL:
            wkeys[(c0, c1)] = []
            for k in range(8):
                sl_ = piece[0] % NSTG
                wk = "w_%d_%d" % (c0, k)
                wkeys[(c0, c1)].append(wk)
                wpiece(dr["w_in"][k * 128:(k + 1) * 128, c0:c1], win3[:, k, c0:c1], normw_fm[:, k:k + 1], "normw2", c1 - c0,
                       wout_f[:, sl_ * 1024:(sl_ + 1) * 1024], "wstg%d" % sl_, "stg%d" % sl_, wk)

        def wkeys_for(c0, n):
            out = []
            for (a0, a1), ks in wkeys.items():
                if a0 < c0 + n and c0 < a1:
                    out += ks
            return out

        def load_wout():
            for kc in range(16):
                T.add("pool", lambda e, kc=kc: e.dma_start(out=wout3[:, kc, :], in_=dr["w_out"][kc * 128:(kc + 1) * 128, :]),
                      w=["wout_%d" % kc] + ["wstg%d" % j for j in range(NSTG)], stream="wo")
        wout_keys = ["wout_%d" % kc for kc in range(16)]
        load_wout()

        def load_x(src, ci, slot):
            T.add("sp", lambda e: e.dma_start(out=xbuf[slot][:], in_=src[ci * L:(ci + 1) * L, :]),
                  w=["xbuf%d" % slot], stream="xin%d" % slot)

        chunk_list = [("pre", i) for i in range(n_pre)] + [("full", i) for i in range(n_full)]

        def src_of(kind):
            return dr["xpre"] if kind == "pre" else dr["x"]

        import os as _os
        _DBL = set((_os.environ.get("K_DBL") or "").split(",")) - {""}
        _PERSIST = {"Cst", "hst", "hst0", "hst1", "hbf0", "hbf1", "mst", "xbcT_carry", "identb", "identf", "tri", "maskb", "onesf",
                    "m05", "oh48", "cwb", "arep", "drep", "dtb", "bias8a", "bias8b", "finalw2", "flag", "out_dram", "xbuf0", "xbuf1"}
        _par = [0]

        def _km(k):
            if not isinstance(k, str) or Tracker._bank(k) is not None or k in _PERSIST or k.startswith("w_") or k.startswith("wout_"):
                return k
            if "ALL" in _DBL or k in _DBL or k.rstrip("0123456789_") in _DBL:
                return "%s@%d" % (k, _par[0])
            return k

        def TA(eng, fn, r=(), w=(), stream=None):
            return T.add(eng, fn, r=[_km(k) for k in r], w=[_km(k) for k in w], stream=stream)

        def chunk(gi):
            kind, ci = chunk_list[gi]
            _par[0] = gi % 2
            full = kind == "full"
            need_c = full or (gi == n_pre - 1)
            slot = gi % 2
            xb = xbuf[slot]
            xk = "xbuf%d" % slot
            if gi + 1 < len(chunk_list):
                nk, nci = chunk_list[gi + 1]
                load_x(src_of(nk), nci, (gi + 1) % 2)
            TA("act", lambda e: e.activation(out=junk[:, 0:1].broadcast_to([128, 1024]), in_=xb[:], func=AF.Square, accum_out=ssq_x), r=[xk], w=["ssq_x", "junk_x"])
            TA("pool", lambda e: e.tensor_scalar(tx1, ssq_x, 1024.0 * EPS, None, ALU.add), r=["ssq_x"], w=["tx1"])
            TA("pool", lambda e: e.tensor_tensor(rstd_x, tx1, m05[:, 0:1], ALU.pow), r=["tx1", "m05"], w=["rstd_x"])
            TA("dve", lambda e: e.tensor_scalar_mul(xn[:], xb[:], rstd_x), r=[xk, "rstd_x"], w=["xn"])
            p, pk = next_pT("A")

            def tr_x(e, p=p):
                for k in range(8):
                    ins = e.transpose(p[:, k * 128:(k + 1) * 128], xn[:, k * 128:(k + 1) * 128], identb[:])
                return ins
            TA("pe", tr_x, r=["xn", "identb"], w=[pk])
            TA("act", lambda e, p=p: e.activation(out=xnT[:], in_=p[:, 0:1024], func=AF.Copy), r=[pk], w=["xnT"])

            def proj(name, c0, n, out_ap, okey, extra_r=()):
                def f(e):
                    for k in range(8):
                        ins = e.matmul(out_ap, xnT3[:, k, :], win3[:, k, c0:c0 + n], start=(k == 0), stop=(k == 7))
                    return ins
                TA("pe", f, r=["xnT"] + wkeys_for(c0, n) + list(extra_r), w=[okey])

            for name, c0, n in PROJ_GROUPS:
                if not full and name not in STATE_ONLY:
                    continue
                if not need_c and name == "xbc2":
                    n = 256
                if name == "if":
                    b, bk = next_bank("A")
                    proj(name, c0, n, b[:, 0:8], bk)
                    TA("dve", lambda e, b=b: e.tensor_tensor(g8, b[:, 0:8], bias8, ALU.add), r=[bk, "bias8a", "bias8b"], w=["g8"])
                    TA("act", lambda e: e.activation(out=t8, in_=g8, func=AF.Tanh, scale=1.0 / 15.0), r=["g8"], w=["t8"])
                    TA("act", lambda e: e.activation(out=e4, in_=t8[:, 4:8], func=AF.Exp, scale=-15.0), r=["t8"], w=["e4"])
                    TA("act", lambda e: e.activation(out=nlf, in_=e4, func=AF.Ln, bias=1.0), r=["e4"], w=["nlf"])
                    TA("dve", lambda e: e.tensor_scalar_mul(li, t8[:, 0:4], 15.0), r=["t8"], w=["li"])
                elif name == "dt":
                    b, bk = next_bank("A")
                    proj(name, c0, n, b[:, 0:16], bk)
                    TA("dve", lambda e, b=b: e.tensor_tensor(dtp, b[:, 0:16], dtb, ALU.add), r=[bk, "dtb"], w=["dtp"])
                    TA("act", lambda e: e.activation(out=edt, in_=dtp, func=AF.Exp), r=["dtp"], w=["edt"])
                    TA("act", lambda e: e.activation(out=dt16, in_=edt, func=AF.Ln, bias=1.0), r=["edt"], w=["dt16"])
                    TA("dve", lambda e: e.tensor_tensor(a_tm, dt16, arep, ALU.mult), r=["dt16", "arep"], w=["a_tm"])
                else:
                    b, bk = next_bank("A")
                    proj(name, c0, n, b[:, 0:n], bk)
                    if name == "k":
                        TA("act", lambda e, b=b: e.activation(out=k_tm[:], in_=b[:, 0:512], func=AF.Copy), r=[bk], w=["k_tm"])
                    elif name in ("v0", "v1"):
                        h0 = 0 if name == "v0" else 2
                        TA("act", lambda e, b=b, h0=h0: e.activation(
                            out=vaug3[:, h0:h0 + 2, 0:256], in_=b[:, 0:512].rearrange("p (h c) -> p h c", h=2), func=AF.Copy),
                            r=[bk, "vaug"], w=["vaug_%d" % h0])
                    elif name.startswith("xbc"):
                        j = int(name[3])
                        eng = "act"
                        if eng == "act":
                            TA("act", lambda e, b=b, j=j, n=n: e.activation(out=xbc_tm[:, j * 512:j * 512 + n], in_=b[:, 0:n], func=AF.Copy),
                                  r=[bk], w=["xbc_tm%d" % j])
                        else:
                            TA("dve", lambda e, b=b, j=j: e.tensor_copy(xbc_tm[:, j * 512:(j + 1) * 512], b[:, 0:512]),
                                  r=[bk], w=["xbc_tm%d" % j])
                    elif name == "q":
                        TA("act", lambda e, b=b: e.activation(out=q_tm[:], in_=b[:, 0:512], func=AF.Copy, scale=float(128 ** -0.5)),
                              r=[bk], w=["q_tm"])
                    elif name in ("zm0", "zm1"):
                        hh = int(name[2])
                        sl = slice(hh * 512, (hh + 1) * 512)
                        TA("act", lambda e, b=b, sl=sl: e.activation(out=tz[:, sl], in_=b[:, 0:512], func=AF.Tanh, scale=0.5),
                              r=[bk], w=["tz%d" % hh])
                        TA("dve", lambda e, b=b, sl=sl: e.scalar_tensor_tensor(g1[:, sl], tz[:, sl], 1.0, b[:, 0:512], ALU.add, ALU.mult),
                              r=[bk, "tz%d" % hh], w=["g1_%d" % hh])
                    elif name in ("o0", "o1"):
                        hh = int(name[1])
                        sl = slice(hh * 512, (hh + 1) * 512)
                        TA("act", lambda e, b=b, sl=sl: e.activation(out=tz[:, sl], in_=b[:, 0:512], func=AF.Tanh, scale=0.5),
                              r=[bk], w=["tz%d" % hh])
                        TA("dve", lambda e, sl=sl: e.scalar_tensor_tensor(g1[:, sl], tz[:, sl], 1.0, g1[:, sl], ALU.add, ALU.mult),
                              r=["tz%d" % hh, "g1_%d" % hh], w=["g1_%d" % hh])
                    elif name in ("zs0", "zs1"):
                        hh = int(name[2])
                        sl = slice(hh * 512, (hh + 1) * 512)
                        TA("act", lambda e, b=b, sl=sl: e.activation(out=tz[:, sl], in_=b[:, 0:512], func=AF.Tanh, scale=0.5),
                              r=[bk], w=["tz%d" % hh])
                        TA("dve", lambda e, b=b, sl=sl: e.scalar_tensor_tensor(gs[:, sl], tz[:, sl], 1.0, b[:, 0:512], ALU.add, ALU.mult),
                              r=[bk, "tz%d" % hh], w=["gs_%d" % hh])

            def tr4(src, dst, skey, dkey):
                p, pk = next_pT("A")

                def f(e, p=p):
                    for h in range(4):
                        ins = e.transpose(p[:, h * 128:(h + 1) * 128], src[:, h * 128:(h + 1) * 128], identb[:])
                    return ins
                TA("pe", f, r=[skey, "identb"], w=[pk])
                TA("act", lambda e, p=p: e.activation(out=dst[:], in_=p[:, 0:512], func=AF.Copy), r=[pk], w=[dkey])

            if full:
                tr4(k_tm, kT, "k_tm", "kT")
                tr4(q_tm, qT, "q_tm", "qT")
            for half, (t0, nt) in enumerate([(0, 8), (8, 4 if need_c else 2)]):
                p, pk = next_pT("A")

                def f(e, p=p, t0=t0, nt=nt):
                    for t in range(nt):
                        ins = e.transpose(p[:, t * 128:(t + 1) * 128], xbc_tm[:, (t0 + t) * 128:(t0 + t + 1) * 128], identb[:])
                    return ins
                TA("pe", f, r=["xbc_tm0", "xbc_tm1", "xbc_tm2", "identb"], w=[pk])
                TA("act", lambda e, p=p, t0=t0, nt=nt: e.activation(
                    out=xbcT3[:, t0:t0 + nt, 4:132], in_=p[:, 0:nt * 128].rearrange("p (t c) -> p t c", t=nt), func=AF.Copy),
                    r=[pk, "xbcT_carry"], w=["xbcT_%d" % half])

            TA("pe", lambda e: e.matmul(pm[:, 24:28], tri[:], nlf, start=True, stop=True), r=["tri", "nlf"], w=["pm_nb"])

            def f_ugm(e):
                e.matmul(pm[0:4, 128:256], li, identf[:], start=True, stop=False)
                return e.matmul(pm[0:4, 128:256], nlf, tri[:], start=False, stop=True)
            TA("pe", f_ugm, r=["li", "nlf", "identf", "tri"], w=["pm_ugm"])
            TA("pe", lambda e: e.matmul(pm[0:4, 120:121], nlf, onesf[:, 0:1], start=True, stop=True), r=["nlf", "onesf"], w=["pm_nbl"])
            TA("dve", lambda e: e.reduce_max(umax, pm[0:4, 128:256], AX.X), r=["pm_ugm"], w=["umax"])
            TA("dve", lambda e: e.tensor_tensor(Rg, umax, mst, ALU.max), r=["umax", "mst"], w=["Rg"])
            TA("dve", lambda e: e.tensor_tensor(dg, mst, Rg, ALU.subtract), r=["mst", "Rg"], w=["dg"])
            TA("act", lambda e: e.activation(out=soldg, in_=dg, func=AF.Exp), r=["dg"], w=["soldg"])
            TA("dve", lambda e: e.tensor_tensor(mst, Rg, pm[0:4, 120:121], ALU.subtract), r=["Rg", "pm_nbl", "dg"], w=["mst"])
            TA("dve", lambda e: e.tensor_scalar_mul(Dg[:, 0:4], identf[0:4, 0:4], Rg), r=["Rg", "identf"], w=["Dg_a"])
            TA("dve", lambda e: e.tensor_scalar_mul(Dg[:, 4:8], identf[0:4, 0:4], soldg), r=["soldg", "identf"], w=["Dg_b"])
            TA("pe", lambda e: e.matmul(pm[:, 28:36], onesf[0:4, :], Dg, start=True, stop=True), r=["onesf", "Dg_a", "Dg_b"], w=["pm_rs"])
            TA("dve", lambda e: e.tensor_tensor(T1[:, 0:4], li, pm[:, 24:28], ALU.add), r=["li", "pm_nb"], w=["T1a"])
            TA("dve", lambda e: e.tensor_copy(T1[:, 4:8], pm[:, 24:28]), r=["pm_nb"], w=["T1b"])
            TA("dve", lambda e: e.tensor_tensor(
                T2.rearrange("p (a h) -> p a h", a=2), T1.rearrange("p (a h) -> p a h", a=2),
                pm[:, 28:32].unsqueeze(1).broadcast_to([128, 2, 4]), ALU.subtract), r=["T1a", "T1b", "pm_rs"], w=["T2"])
            TA("act", lambda e: e.activation(out=wf, in_=T2, func=AF.Exp), r=["T2"], w=["wf"])
            TA("dve", lambda e: e.tensor_copy(soldb, pm[:, 32:36]), r=["pm_rs"], w=["soldb"])

            TA("pe", lambda e: e.matmul(pm[:, 40:56], tri[:], a_tm, start=True, stop=True), r=["tri", "a_tm"], w=["pm_acum"])
            TA("pe", lambda e: e.matmul(pm[:, 56:72], onesf[:], a_tm, start=True, stop=True), r=["onesf", "a_tm"], w=["pm_alast"])
            TA("dve", lambda e: e.tensor_copy(acum_sb, pm[:, 40:56]), r=["pm_acum"], w=["acum_sb"])
            TA("act", lambda e: e.activation(out=ea, in_=pm[:, 40:56], func=AF.Exp), r=["pm_acum"], w=["ea"])
            TA("dve", lambda e: e.tensor_tensor(dEa, pm[:, 56:72], acum_sb, ALU.subtract), r=["pm_alast", "acum_sb"], w=["dEa"])
            TA("act", lambda e: e.activation(out=dE, in_=dEa, func=AF.Exp), r=["dEa"], w=["dE"])
            TA("act", lambda e: e.activation(out=cdb, in_=pm[:, 56:72], func=AF.Exp), r=["pm_alast"], w=["cdb"])
            if full:
                TA("dve", lambda e: e.tensor_copy(a3[:, 0:16], acum_sb), r=["acum_sb"], w=["a3_0"])
                TA("dve", lambda e: e.tensor_tensor(r1, acum_sb, a3[:, 0:16], ALU.subtract), r=["acum_sb", "a3_0"], w=["r1"])
                TA("dve", lambda e: e.tensor_copy(a3[:, 16:32], r1), r=["r1"], w=["a3_1"])
                TA("dve", lambda e: e.tensor_tensor(r2, r1, a3[:, 16:32], ALU.subtract), r=["r1", "a3_1"], w=["r2"])
                TA("dve", lambda e: e.tensor_copy(a3[:, 32:48], r2), r=["r2"], w=["a3_2"])
                p, pk = next_pT()
                TA("pe", lambda e, p=p: e.transpose(p[0:48, 0:128], a3[:, 0:48], identb[:]), r=["a3_0", "a3_1", "a3_2", "identb"], w=[pk])
                TA("dve", lambda e, p=p: e.tensor_copy(A48[:], p[0:48, 0:128]), r=[pk], w=["A48"])

            for rnd in range(NPE_R):
                cvb, cvk = next_bank()

                def f_cv(e, rnd=rnd, cvb=cvb):
                    for tt in range(4):
                        t = rnd * 4 + tt
                        o_ = cvb[:, tt * 128:(tt + 1) * 128]
                        for w_ in range(4):
                            e.matmul(o_, dgc4[:, t, w_, :], xbcT3[:, t, 1 + w_:129 + w_], start=(w_ == 0), stop=False)
                        ins = e.matmul(o_, cbrow[:, t * 128:(t + 1) * 128], onesrow, start=False, stop=True)
                    return ins
                TA("pe", f_cv, r=["xbcT_0", "xbcT_carry", "cbrow", "onesrow"] + dgc_keys, w=[cvk])
                ct = cth[0]
                TA("act", lambda e, cvb=cvb, ct=ct: e.activation(out=ct[:], in_=cvb[:, 0:512], func=AF.Tanh), r=[cvk], w=["cth0"])
                TA("dve", lambda e, cvb=cvb, ct=ct, rnd=rnd: e.scalar_tensor_tensor(
                    xcT[:, rnd * 512:(rnd + 1) * 512], ct[:], 1.0, cvb[:, 0:512], ALU.add, ALU.mult),
                    r=["cth0", cvk], w=["xcT_%d" % rnd])
            for rnd in range(NPE_R, 3):
                ca = cacc[0]
                ct = cth[0]
                cak = "cacc0"
                ctk = "cth0"
                src_key = "xbcT_0" if rnd < 2 else "xbcT_1"
                for tt in range(4):
                    t = rnd * 4 + tt
                    if not full and t >= 10:
                        continue
                    cslice = ca[:, tt * 128:(tt + 1) * 128]
                    ckey = cak + "_%d" % tt
                    TA("dve", lambda e, t=t, cslice=cslice: e.tensor_scalar(
                        cslice, xbcT3[:, t, 1:129], cw[:, t, 0:1], cb[:, t:t + 1], ALU.mult, ALU.add),
                        r=[src_key, "xbcT_carry", "cwb"], w=[ckey])
                    for w_ in range(1, 4):
                        TA("dve", lambda e, t=t, cslice=cslice, w_=w_: e.scalar_tensor_tensor(
                            cslice, xbcT3[:, t, 1 + w_:129 + w_], cw[:, t, w_:w_ + 1], cslice, ALU.mult, ALU.add),
                            r=[src_key, "xbcT_carry", "cwb", ckey], w=[ckey])
                nv = 512 if (full or rnd < 2) else 256
                TA("act", lambda e, ca=ca, ct=ct, nv=nv: e.activation(out=ct[:, 0:nv], in_=ca[:, 0:nv], func=AF.Tanh),
                   r=[cak + "_%d" % i for i in range(4)], w=[ctk])
                TA("dve", lambda e, ca=ca, ct=ct, rnd=rnd, nv=nv: e.scalar_tensor_tensor(
                    xcT[:, rnd * 512:rnd * 512 + nv], ct[:, 0:nv], 1.0, ca[:, 0:nv], ALU.add, ALU.mult),
                    r=[ctk] + [cak + "_%d" % i for i in range(4)], w=["xcT_%d" % rnd])
            ntc = 12 if need_c else 10
            TA("pool", lambda e, ntc=ntc: e.tensor_copy(xbcT3[:, 0:ntc, 1:4], xbcT3[:, 0:ntc, 129:132]), r=["xbcT_0", "xbcT_1"], w=["xbcT_carry"])

            TA("pool", lambda e: e.tensor_tensor(
                kp_tm[:].rearrange("p (h d) -> p h d", h=4), k_tm[:].rearrange("p (h d) -> p h d", h=4),
                wf[:, 0:4].unsqueeze(2).broadcast_to([128, 4, 128]), ALU.mult), r=["k_tm", "wf"], w=["kp_tm"])
            TA("pool", lambda e: e.tensor_tensor(Cst3[:, :, 0:257], Cst3[:, :, 0:257],
                                                    soldb.unsqueeze(2).broadcast_to([128, 4, 257]), ALU.mult),
                  r=["Cst", "soldb"], w=["Cst"])
            if full:
                TA("act", lambda e: e.activation(out=Cbf3[:, :, 0:257], in_=Cst3[:, :, 0:257], func=AF.Copy), r=["Cst"], w=["Cbf"])
                sb_, sbk = next_bank()

                def f_st(e, sb_=sb_):
                    for h in range(4):
                        ins = e.matmul(sb_[:, h * 128:(h + 1) * 128], kT[:, h * 128:(h + 1) * 128], qT[:, h * 128:(h + 1) * 128],
                                       start=True, stop=True)
                    return ins
                TA("pe", f_st, r=["kT", "qT"], w=[sbk])
                for h in range(4):
                    TA("dve", lambda e, h=h, sb_=sb_: e.scalar_tensor_tensor(
                        Pm3[:, h, :], sb_[:, h * 128:(h + 1) * 128], wf[:, h:h + 1], maskb[:], ALU.mult, ALU.mult),
                        r=[sbk, "wf", "maskb"], w=["Pm_%d" % h])
                brs = []
                for hp in range(2):
                    bb, bbk = next_bank()
                    for h in (2 * hp, 2 * hp + 1):
                        brs.append((bb, bbk, (h % 2) * 256))

                    def f_br(e, hp=hp, bb=bb):
                        for h in (2 * hp, 2 * hp + 1):
                            o_ = (h % 2) * 256
                            e.matmul(bb[:, o_:o_ + 256], Pm3[:, h, :], vaug3[:, h, 0:256], start=True, stop=False)
                            ins = e.matmul(bb[:, o_:o_ + 256], qT[:, h * 128:(h + 1) * 128], Cbf3[:, h, 0:256], start=False, stop=True)
                        return ins
                    TA("pe", f_br, r=["Pm_%d" % (2 * hp), "Pm_%d" % (2 * hp + 1), "vaug_0", "vaug_2", "qT", "Cbf"], w=[bbk])
                    for h in (2 * hp, 2 * hp + 1):
                        o_ = (h % 2) * 256
                        TA("act", lambda e, h=h, bb=bb, o_=o_: e.activation(out=junk_r[:, 0:1].broadcast_to([128, 256]), in_=bb[:, o_:o_ + 256], func=AF.Square,
                                                                            accum_out=ssqr[:, h:h + 1]), r=[bbk], w=["ssqr_%d" % h, "junk_r"])

                def f_den(e):
                    for h in range(4):
                        e.matmul(pm[:, 64 + h:65 + h], Pm3[:, h, :], vaug3[:, h, 256:257], start=True, stop=False)
                        ins = e.matmul(pm[:, 64 + h:65 + h], qT[:, h * 128:(h + 1) * 128], Cbf3[:, h, 256:257], start=False, stop=True)
                    return ins
                TA("pe", f_den, r=["Pm_0", "Pm_1", "Pm_2", "Pm_3", "vaug_0", "vaug_2", "qT", "Cbf"], w=["pm_den"])
                TA("act", lambda e: e.activation(out=absden, in_=pm[:, 64:68], func=AF.Abs), r=["pm_den"], w=["absden"])
                hk = ["absden"]
                sk = ["ssqr_%d" % h for h in range(4)]
                TA("dve", lambda e: e.tensor_tensor(dn4, absden, wf[:, 4:8], ALU.max), r=hk + ["wf"], w=["dn4"])
                TA("dve", lambda e: e.reciprocal(rd4, dn4), r=["dn4"], w=["rd4"])
                TA("dve", lambda e: e.tensor_tensor(t4, rd4, rd4, ALU.mult), r=["rd4"], w=["t4"])
                TA("dve", lambda e: e.tensor_tensor(t4, t4, ssqr, ALU.mult), r=["t4"] + sk, w=["t4"])
                TA("pool", lambda e: e.tensor_scalar(t4, t4, 256.0 * EPS, None, ALU.add), r=["t4"], w=["t4"])
                TA("pool", lambda e: e.tensor_tensor(rs4, t4, m05[:, 0:4], ALU.pow), r=["t4", "m05"], w=["rs4"])
                TA("dve", lambda e: e.tensor_tensor(sc4, rd4, rs4, ALU.mult), r=["rd4", "rs4"], w=["sc4"])
                for h in range(4):
                    bb, bbk, o_ = brs[h]
                    TA("dve", lambda e, h=h, bb=bb, o_=o_: e.scalar_tensor_tensor(
                        mix[:, h * 256:(h + 1) * 256], bb[:, o_:o_ + 256], sc4[:, h:h + 1], g1[:, h * 256:(h + 1) * 256], ALU.mult, ALU.mult),
                        r=[bbk, "sc4", "g1_%d" % (h // 2)], w=["mix_m%d" % h])
            for h in range(4):
                cb_, cbk = next_bank()
                TA("pe", lambda e, h=h, cb_=cb_: e.matmul(cb_[:, 0:257], kp_tm[:, h * 128:(h + 1) * 128], vaug3[:, h, 0:257],
                                                             start=True, stop=True), r=["kp_tm", "vaug_0", "vaug_2"], w=[cbk])
                TA("dve", lambda e, h=h, cb_=cb_: e.tensor_tensor(Cst3[:, h, 0:257], Cst3[:, h, 0:257], cb_[:, 0:257], ALU.add),
                      r=[cbk, "Cst", "Cbf"], w=["Cst"])

            p, pk = next_pT()

            def tr_xc(e, p=p):
                for t in range(8):
                    ins = e.transpose(p[:, t * 128:(t + 1) * 128], xcT3[:, t, :], identb[:])
                return ins
            TA("pe", tr_xc, r=["xcT_0", "xcT_1", "identb"], w=[pk])
            TA("act", lambda e, p=p: e.activation(out=x_tm[:], in_=p[:, 0:1024], func=AF.Copy), r=[pk], w=["x_tm"])
            p, pk = next_pT()

            def tr_B(e, p=p):
                for t in range(2):
                    ins = e.transpose(p[:, t * 128:(t + 1) * 128], xcT3[:, 8 + t, :], identb[:])
                return ins
            TA("pe", tr_B, r=["xcT_2", "identb"], w=[pk])
            TA("act", lambda e, p=p: e.activation(out=B_tm[:], in_=p[:, 0:256], func=AF.Copy), r=[pk], w=["B_tm"])
            TA("pool", lambda e: e.tensor_tensor(
                xdt[:].rearrange("p (r c) -> p r c", r=16), x_tm[:].rearrange("p (r c) -> p r c", r=16),
                dt16.unsqueeze(2).broadcast_to([128, 16, 64]), ALU.mult), r=["x_tm", "dt16"], w=["xdt"])
            TA("pool", lambda e: e.tensor_tensor(
                xde[:].rearrange("p (r c) -> p r c", r=16), xdt[:].rearrange("p (r c) -> p r c", r=16),
                dE.unsqueeze(2).broadcast_to([128, 16, 64]), ALU.mult), r=["xdt", "dE"], w=["xde"])

            if full:
                TA("pe", lambda e: (e.matmul(pm[:, 256:384], xcT3[:, 8, :], xcT3[:, 10, :], start=True, stop=True),
                                       e.matmul(pm[:, 384:512], xcT3[:, 9, :], xcT3[:, 11, :], start=True, stop=True))[1],
                      r=["xcT_2"], w=["pm_cb"])
                TA("dve", lambda e: e.tensor_tensor(CBm3, pm[:, 256:512].rearrange("p (g l) -> p g l", g=2),
                                                       maskb[:].unsqueeze(1).broadcast_to([128, 2, 128]), ALU.mult),
                      r=["pm_cb", "maskb"], w=["CBm"])
                for bq in range(4):
                    ab, abk = next_bank()

                    def f_arg(e, bq=bq, ab=ab):
                        for rr in range(4):
                            ins = e.matmul(ab[:, rr * 128:(rr + 1) * 128], oh48_3[:, bq * 4 + rr, :], A48[:], start=True, stop=True)
                        return ins
                    TA("pe", f_arg, r=["oh48", "A48"], w=[abk])

                    def f_relu(e, bq=bq, ab=ab):
                        for rr in range(4):
                            hd = bq * 4 + rr
                            ins = e.activation(out=rl[:, rr * 128:(rr + 1) * 128], in_=ab[:, rr * 128:(rr + 1) * 128],
                                               func=AF.Relu, bias=acum_sb[:, hd:hd + 1], scale=-1.0)
                        return ins
                    TA("act", f_relu, r=[abk, "acum_sb"], w=["rl"])
                    TA("act", lambda e, bq=bq: e.activation(out=dec[:, bq * 512:(bq + 1) * 512], in_=rl[:], func=AF.Exp, scale=-1.0),
                          r=["rl"], w=["dec_%d" % bq])
                    g = bq // 2
                    TA("dve", lambda e, bq=bq, g=g: e.tensor_tensor(
                        dec3[:, bq * 4:bq * 4 + 4, :], dec3[:, bq * 4:bq * 4 + 4, :],
                        CBm3[:, g:g + 1, :].broadcast_to([128, 4, 128]), ALU.mult), r=["dec_%d" % bq, "CBm"], w=["dec_%d" % bq])
                TA("pool", lambda e: e.tensor_tensor(
                    yo[:].rearrange("p (r c) -> p r c", r=16), x_tm[:].rearrange("p (r c) -> p r c", r=16),
                    drep.unsqueeze(2).broadcast_to([128, 16, 64]), ALU.mult), r=["x_tm", "drep"], w=["yo_0", "yo_1"])
                for g in range(2):
                    yd, ydk = next_bank()

                    def f_yd(e, g=g, yd=yd):
                        for rr in range(8):
                            hd = g * 8 + rr
                            ins = e.matmul(yd[:, rr * 64:(rr + 1) * 64], dec3[:, hd, :], xdt[:, hd * 64:(hd + 1) * 64], start=True, stop=True)
                        return ins
                    TA("pe", f_yd, r=["dec_%d" % (2 * g), "dec_%d" % (2 * g + 1), "xdt"], w=[ydk])
                    yf, yfk = next_bank()
                    TA("pe", lambda e, g=g, yf=yf: e.matmul(yf[:, 0:512], xcT3[:, 10 + g, :], hbf[:, g * 512:(g + 1) * 512], start=True, stop=True),
                          r=["xcT_2", "hbf%d" % g], w=[yfk])
                    gsl = slice(g * 512, (g + 1) * 512)
                    TA("dve", lambda e, g=g, yf=yf: e.tensor_tensor(
                        ytmp[:].rearrange("p (r c) -> p r c", r=8), yf[:, 0:512].rearrange("p (r c) -> p r c", r=8),
                        ea[:, g * 8:(g + 1) * 8].unsqueeze(2).broadcast_to([128, 8, 64]), ALU.mult), r=[yfk, "ea"], w=["ytmp"])
                    TA("dve", lambda e, yd=yd: e.tensor_tensor(ytmp[:], ytmp[:], yd[:, 0:512], ALU.add), r=[ydk, "ytmp"], w=["ytmp"])
                    TA("pool", lambda e, gsl=gsl: e.tensor_tensor(yo[:, gsl], yo[:, gsl], ytmp[:], ALU.add), r=["ytmp", "yo_%d" % g], w=["yo_%d" % g])
                    TA("pool", lambda e, gsl=gsl: e.tensor_tensor(yo[:, gsl], yo[:, gsl], gs[:, gsl], ALU.mult),
                          r=["yo_%d" % g, "gs_%d" % g], w=["yo_%d" % g])
                    TA("act", lambda e, g=g, gsl=gsl: e.activation(out=junk_s[:, 0:1].broadcast_to([128, 512]), in_=yo[:, gsl], func=AF.Square, accum_out=ssq_s[:, g:g + 1]),
                          r=["yo_%d" % g], w=["ssq_s%d" % g, "junk_s"])
                TA("pool", lambda e: e.tensor_scalar(ts2, ssq_s, 2048.0 * EPS, None, ALU.add), r=["ssq_s0", "ssq_s1"], w=["ts2"])
                TA("pool", lambda e: e.tensor_tensor(rstd_s, ts2, m05[:, 0:2], ALU.pow), r=["ts2", "m05"], w=["rstd_s"])
                for g in range(2):
                    gsl = slice(g * 512, (g + 1) * 512)
                    TA("dve", lambda e, g=g, gsl=gsl: e.tensor_scalar_mul(mix[:, 1024 + g * 512:1024 + (g + 1) * 512], yo[:, gsl], rstd_s[:, g:g + 1]),
                          r=["yo_%d" % g, "rstd_s"], w=["mix_s%d" % g])
            for g in range(2):
                st, stk = next_bank()
                gsl = slice(g * 512, (g + 1) * 512)
                TA("pe", lambda e, g=g, st=st, gsl=gsl: e.matmul(st[:, 0:512], B_tm[:, g * 128:(g + 1) * 128], xde[:, gsl], start=True, stop=True),
                      r=["B_tm", "xde"], w=[stk])
                TA("pool", lambda e, g=g, gsl=gsl: e.tensor_tensor(
                    hst[:, gsl].rearrange("p (r c) -> p r c", r=8), hst[:, gsl].rearrange("p (r c) -> p r c", r=8),
                    cdb[:, g * 8:(g + 1) * 8].unsqueeze(2).broadcast_to([128, 8, 64]), ALU.mult), r=["hst", "cdb", "hbf%d" % g], w=["hst%d" % g])
                TA("dve", lambda e, st=st, gsl=gsl: e.tensor_tensor(hst[:, gsl], hst[:, gsl], st[:, 0:512], ALU.add), r=[stk, "hst%d" % g], w=["hst%d" % g])
                TA("act", lambda e, gsl=gsl: e.activation(out=hbf[:, gsl], in_=hst[:, gsl], func=AF.Copy), r=["hst%d" % g], w=["hbf%d" % g])

            if not full:
                return
            mkeys = ["mix_m%d" % h for h in range(4)] + ["mix_s0", "mix_s1"]
            obs = [next_bank(), next_bank()]
            for half in range(2):
                p, pk = next_pT()

                def tr_m(e, p=p, half=half):
                    for t in range(8):
                        kc = half * 8 + t
                        ins = e.transpose(p[:, t * 128:(t + 1) * 128], mix[:, kc * 128:(kc + 1) * 128], identb[:])
                    return ins
                TA("pe", tr_m, r=(mkeys[0:4] if half == 0 else mkeys[4:6]) + ["identb"], w=[pk])
                TA("dve", lambda e, p=p, half=half: e.tensor_tensor(
                    mixT3[:, 0:8, :], p[:, 0:1024].rearrange("p (k t) -> p k t", k=8),
                    normcat[:, half * 8:(half + 1) * 8].unsqueeze(2).broadcast_to([128, 8, 128]), ALU.mult),
                    r=[pk, "normcat_a", "normcat_b"], w=["mixT_0"])
                for oh in range(2):
                    ob, obk = obs[oh]

                    def f_o(e, ob=ob, half=half, oh=oh):
                        for kc in range(8):
                            ins = e.matmul(ob[:, 0:512], mixT3[:, kc, :], wout3[:, half * 8 + kc, oh * 512:(oh + 1) * 512],
                                           start=(half == 0 and kc == 0), stop=(half == 1 and kc == 7))
                        return ins
                    TA("pe", f_o, r=["mixT_0"] + wout_keys, w=[obk])
            for oh in range(2):
                ob, obk = obs[oh]
                hsl = slice(oh * 512, (oh + 1) * 512)
                TA("dve", lambda e, ob=ob, hsl=hsl: e.tensor_tensor(xb[:, hsl], xb[:, hsl], ob[:, 0:512], ALU.add), r=[obk, xk], w=[xk])
            TA("act", lambda e: e.activation(out=junk_o[:, 0:1].broadcast_to([128, 1024]), in_=xb[:], func=AF.Square, accum_out=ssq_o), r=[xk], w=["ssq_o", "junk_o"])
            TA("pool", lambda e: e.tensor_scalar(to1, ssq_o, 1024.0 * EPS, None, ALU.add), r=["ssq_o"], w=["to1"])
            TA("pool", lambda e: e.tensor_tensor(rstd_o, to1, m05[:, 0:1], ALU.pow), r=["to1", "m05"], w=["rstd_o"])
            TA("dve", lambda e: e.scalar_tensor_tensor(xb[:], xb[:], rstd_o, finalw[:], ALU.mult, ALU.mult), r=[xk, "rstd_o", "finalw2"], w=[xk])
            TA("sp", lambda e: e.dma_start(out=out_d[ci * L:(ci + 1) * L, :], in_=xb[:]), r=[xk], w=["out_dram"], stream="out%d" % slot)

        k0, c0_ = chunk_list[0]
        load_x(src_of(k0), c0_, 0)
        for gi in range(len(chunk_list)):
            chunk(gi)
            if n_pre and gi == n_pre - 1:
                T.add("dve", lambda e: e.tensor_scalar_mul(Cst[:], Cst[:], flag), r=["Cst", "flag"], w=["Cst"])
                T.add("dve", lambda e: e.tensor_scalar_mul(hst[:], hst[:], flag), r=["hst0", "hst1", "flag"], w=["hst0", "hst1", "hst"])
                T.add("dve", lambda e: e.tensor_scalar_mul(hbf[:], hbf[:], flag), r=["hbf0", "hbf1", "flag"], w=["hbf0", "hbf1"])
                T.add("dve", lambda e: e.tensor_scalar_mul(mst, mst, flag[0:4, :]), r=["mst", "flag"], w=["mst"])
                T.add("dve", lambda e: e.tensor_scalar_mul(xbcT3[:, :, 1:4], xbcT3[:, :, 1:4], flag), r=["xbcT_carry", "flag"], w=["xbcT_carry"])

        for nm in debug:
            tile_ap, shape, rkeys = {
                "xnT": (xnT[:], [128, 1024], ["xnT"]),
                "k_tm": (k_tm[:], [128, 512], ["k_tm"]),
                "mix": (mix[:], [128, 2048], ["mix_m0", "mix_m1", "mix_m2", "mix_m3", "mix_s0", "mix_s1"]),
                "Cst": (Cst[:], [128, 4 * 258], ["Cst"]),
                "hst": (hst[:], [128, 1024], ["hst0", "hst1"]),
                "x_tm": (x_tm[:], [128, 1024], ["x_tm"]),
                "sm": (sm[:], [128, 256], ["wf", "dE", "ea", "cdb", "dt16", "acum_sb"]),
                "yo": (yo[:], [128, 1024], ["yo_0", "yo_1"]),
                "dec": (dec[:], [128, 2048], ["dec_0", "dec_1", "dec_2", "dec_3"]),
                "xcT": (xcT[:], [128, 1536], ["xcT_0", "xcT_1", "xcT_2"]),
                "g1": (g1[:], [128, 1024], ["g1_0", "g1_1"]),
                "vaug": (vaug[:], [128, 4 * 258], ["vaug_0", "vaug_2"]),
            }[nm]
            d = nc.dram_tensor("dbg_" + nm, shape, F32, kind="ExternalOutput").ap()
            dbg_out[nm] = d
            q_eng = "sp" if tile_ap.dtype == F32 else "pool"
            T.add(q_eng, lambda e, d=d, tile_ap=tile_ap: e.dma_start(out=d, in_=tile_ap), r=rkeys, w=["dbg_" + nm], stream="dbg")

        import os as _os2
        T.emit(nc, es, same_engine_sync=not _os2.environ.get("K_NOSES"))
    return nc


_CACHE = {}


def kernel(x, norm_w, w_in, b_igate, b_fgate, conv_w, conv_b, dt_bias, a_log, d_skip,
           mlstm_norm_w, ssd_norm_w, w_out, final_norm_w):
    f = lambda a: np.ascontiguousarray(np.asarray(a, dtype=np.float32))
    x = f(x)
    n_half = SEQ // 2 // L
    key = "main"
    if key not in _CACHE:
        _CACHE[key] = build(n_half, n_half)
    nc = _CACHE[key]
    common = {
        "norm_w": f(norm_w)[0], "w_in": f(w_in)[0], "b_igate": f(b_igate)[0], "b_fgate": f(b_fgate)[0],
        "conv_w": f(conv_w)[0], "conv_b": f(conv_b)[0], "dt_bias": f(dt_bias)[0], "a_log": f(a_log)[0],
        "d_skip": f(d_skip)[0],
        "normcat": np.ascontiguousarray(np.concatenate([f(mlstm_norm_w)[0], f(ssd_norm_w)[0]])),
        "w_out": f(w_out)[0], "final_norm_w": f(final_norm_w),
    }
    half = SEQ // 2
    zeros = np.zeros((half, D_MODEL), np.float32)
    in_maps = []
    for core in range(NCORES):
        b, hf = core // 2, core % 2
        m = dict(common)
        m["x"] = np.ascontiguousarray(x[b, hf * half:(hf + 1) * half])
        m["xpre"] = zeros if hf == 0 else np.ascontiguousarray(x[b, 0:half])
        m["flag"] = np.array([float(hf)], np.float32)
        in_maps.append(m)
    res = run_bass_kernel_spmd(nc, in_maps, core_ids=list(range(NCORES)))
    out = np.empty((BATCH, SEQ, D_MODEL), np.float32)
    for core in range(NCORES):
        b, hf = core // 2, core % 2
        out[b, hf * half:(hf + 1) * half] = res.results[core]["out"]
    return out
```
